# Optimizing a Trainium2 kernel written in Bass

```python
import math
import jax, jax.numpy as jnp
from jax import lax
import numpy as np

D_MODEL = 1024
BATCH = 8
SEQ = 4096
DEPTH = 1
DEC_BATCH = 8
DEC_SEQ = 64
PAST_LEN = 2048

CHUNK = 64
QBLK = 128
DA_HEADS = 8
DA_HEAD_DIM = 64
DA_WIDTH = DA_HEADS * 2 * DA_HEAD_DIM
MLA_HEADS = 16
MLA_Q_LORA = 256
MLA_KV_LORA = 128
MLA_NOPE = 64
MLA_ROPE = 32
MLA_V = 64
MLA_WIDTH = MLA_HEADS * MLA_V
ROPE_THETA = 10000.0
REL_BUCKETS = 32
REL_MAX_DIST = 128
PEER_HEADS = 8
PEER_N_KEYS = 128
PEER_N_EXPERTS = PEER_N_KEYS * PEER_N_KEYS
PEER_D_KEY = 128
PEER_TOPK = 16
PEER_BLOCK = 128
IN_WIDTH = 3 * DA_WIDTH + MLA_Q_LORA + MLA_KV_LORA + MLA_ROPE + 2 * D_MODEL
NORM_EPS = 1e-6
NEG_INF = -1e30

kernel_name = "hybrid_diffattn_mla_peer_streaming_step"


def rmsnorm(x, g):
    xf = x.astype(jnp.float32)
    y = xf * lax.rsqrt(jnp.mean(xf * xf, axis=-1, keepdims=True) + NORM_EPS)
    return (y * g.astype(jnp.float32)).astype(x.dtype)


def rope(x, pos):
    half = x.shape[-1] // 2
    inv = ROPE_THETA ** (-jnp.arange(half, dtype=jnp.float32) / half)
    ang = pos.astype(jnp.float32)[:, None] * inv
    ang = ang.reshape(ang.shape[0], *([1] * (x.ndim - 3)), half)
    cos, sin = jnp.cos(ang), jnp.sin(ang)
    xf = x.astype(jnp.float32)
    x1, x2 = xf[..., :half], xf[..., half:]
    return jnp.concatenate([x1 * cos - x2 * sin, x1 * sin + x2 * cos], axis=-1).astype(x.dtype)


def t5_bucket(rel):
    nb = REL_BUCKETS // 2
    max_exact = nb // 2
    ret = jnp.where(rel > 0, nb, 0)
    n = jnp.abs(rel)
    large = max_exact + (jnp.log(jnp.maximum(n, 1).astype(jnp.float32) / max_exact)
                         / math.log(REL_MAX_DIST / max_exact) * (nb - max_exact)).astype(jnp.int32)
    large = jnp.minimum(large, nb - 1)
    return ret + jnp.where(n < max_exact, n, large)


def chunk_mask(q_pos, k_pos):
    return (k_pos // CHUNK)[None, :] <= (q_pos // CHUNK)[:, None]


def over_query_blocks(fn, qs, q_pos):
    B, S = qs[0].shape[:2]
    if S <= QBLK or S % QBLK:
        return fn(*qs, q_pos)
    nb = S // QBLK
    qb = tuple(jnp.moveaxis(q.reshape(B, nb, QBLK, *q.shape[2:]), 1, 0) for q in qs)
    out = lax.map(lambda a: fn(*a[0], a[1]), (qb, q_pos.reshape(nb, QBLK)))
    return jnp.moveaxis(out, 0, 1).reshape(B, S, *out.shape[3:])


def diff_attention(q, k, v, q_pos, k_pos, lam, rel_bias):
    s = jnp.einsum('bqhcd,bkhcd->bchqk', q, k).astype(jnp.float32) * (DA_HEAD_DIM ** -0.5)
    bias = jnp.transpose(rel_bias[t5_bucket(k_pos[None, :] - q_pos[:, None])], (2, 0, 1)).astype(jnp.float32)
    s = jnp.where(chunk_mask(q_pos, k_pos), s + bias, NEG_INF)
    p = jax.nn.softmax(s, axis=-1)
    pd = p[:, 0] - lam * p[:, 1]
    return jnp.einsum('bhqk,bkhe->bqhe', pd.astype(v.dtype), v)


def mla_attention(q_lat, q_pe, ckv, kpe, q_pos, k_pos):
    s = (jnp.einsum('bqhr,bkr->bhqk', q_lat, ckv)
         + jnp.einsum('bqhe,bke->bhqk', q_pe, kpe)).astype(jnp.float32) * ((MLA_NOPE + MLA_ROPE) ** -0.5)
    s = jnp.where(chunk_mask(q_pos, k_pos), s, NEG_INF)
    p = jax.nn.softmax(s, axis=-1)
    return jnp.einsum('bhqk,bkr->bqhr', p.astype(ckv.dtype), ckv)


def peer_tokens(h, w_q, keys, u, v):
    T = h.shape[0]
    q = (h @ w_q).reshape(T, PEER_HEADS, 2, PEER_D_KEY // 2)
    s = jnp.einsum('thcd,hcnd->thcn', q, keys).astype(jnp.float32)
    s1, i1 = lax.top_k(s[:, :, 0], PEER_TOPK)
    s2, i2 = lax.top_k(s[:, :, 1], PEER_TOPK)
    cand = (s1[..., :, None] + s2[..., None, :]).reshape(T, PEER_HEADS, PEER_TOPK * PEER_TOPK)
    cidx = (i1[..., :, None] * PEER_N_KEYS + i2[..., None, :]).reshape(T, PEER_HEADS, PEER_TOPK * PEER_TOPK)
    top, sel = lax.top_k(cand, PEER_TOPK)
    idx = jnp.take_along_axis(cidx, sel, axis=-1)
    g = jax.nn.softmax(top, axis=-1).astype(h.dtype)
    act = jax.nn.gelu(jnp.einsum('td,thkd->thk', h, u[idx]), approximate=False)
    return jnp.einsum('thk,thkd->td', g * act, v[idx])


def peer_ffn(h, w_q, keys, u, v):
    B, S, D = h.shape
    t = h.reshape(B * S, D)
    T = t.shape[0]
    if T > PEER_BLOCK and T % PEER_BLOCK == 0:
        out = lax.map(lambda blk: peer_tokens(blk, w_q, keys, u, v), t.reshape(T // PEER_BLOCK, PEER_BLOCK, D))
        out = out.reshape(T, D)
    else:
        out = peer_tokens(t, w_q, keys, u, v)
    return out.reshape(B, S, D)


def trunk_layer(x, pos, past, layer_idx, rel_bias, norm_mix, w_in, diff_lambda, diff_subln,
                mla_q_norm, mla_w_uq, mla_kv_norm, mla_w_uk, mla_w_uv,
                w_branch_a, w_branch_b, w_out, norm_ffn, peer_w_q, peer_keys, peer_u, peer_v):
    B, S, _ = x.shape
    h = rmsnorm(x, norm_mix)
    z = h @ w_in
    c0 = 3 * DA_WIDTH
    cuts = [DA_WIDTH, 2 * DA_WIDTH, c0, c0 + MLA_Q_LORA, c0 + MLA_Q_LORA + MLA_KV_LORA,
            c0 + MLA_Q_LORA + MLA_KV_LORA + MLA_ROPE]
    zq, zk, zv, zcq, zckv, zkr, zg = jnp.split(z, cuts, axis=-1)
    dq = zq.reshape(B, S, DA_HEADS, 2, DA_HEAD_DIM)
    dk = zk.reshape(B, S, DA_HEADS, 2, DA_HEAD_DIM)
    dv = zv.reshape(B, S, DA_HEADS, 2 * DA_HEAD_DIM)
    gates = jax.nn.sigmoid(zg).reshape(B, S, 2, D_MODEL)
    cq = rmsnorm(zcq, mla_q_norm)
    qh = jnp.einsum('bsr,rhe->bshe', cq, mla_w_uq)
    q_nope = qh[..., :MLA_NOPE]
    q_pe = rope(qh[..., MLA_NOPE:], pos)
    q_lat = jnp.einsum('bshn,rhn->bshr', q_nope, mla_w_uk)
    ckv = rmsnorm(zckv, mla_kv_norm)
    kpe = rope(zkr, pos)
    if past is None:
        k_a, v_a, ckv_all, kpe_all, k_pos = dk, dv, ckv, kpe, pos
    else:
        pk, pv, pckv, pkpe = past
        k_pos = jnp.concatenate([jnp.arange(pk.shape[1], dtype=jnp.int32), pos])
        k_a = jnp.concatenate([pk, dk], axis=1)
        v_a = jnp.concatenate([pv, dv], axis=1)
        ckv_all = jnp.concatenate([pckv, ckv], axis=1)
        kpe_all = jnp.concatenate([pkpe, kpe], axis=1)
    lambda_init = 0.8 - 0.6 * math.exp(-0.3 * layer_idx)
    lam_p = diff_lambda.astype(jnp.float32)
    lam = jnp.exp(jnp.sum(lam_p[0] * lam_p[1])) - jnp.exp(jnp.sum(lam_p[2] * lam_p[3])) + lambda_init
    o_a = over_query_blocks(lambda q, qp: diff_attention(q, k_a, v_a, qp, k_pos, lam, rel_bias), (dq,), pos)
    o_a = (rmsnorm(o_a, diff_subln) * (1.0 - lambda_init)).reshape(B, S, DA_WIDTH)
    o_lat = over_query_blocks(lambda ql, qe, qp: mla_attention(ql, qe, ckv_all, kpe_all, qp, k_pos), (q_lat, q_pe), pos)
    o_b = jnp.einsum('bshr,rhv->bshv', o_lat, mla_w_uv).reshape(B, S, MLA_WIDTH)
    merged = gates[:, :, 0] * (o_a @ w_branch_a) + gates[:, :, 1] * (o_b @ w_branch_b)
    x = x + merged @ w_out
    x = x + peer_ffn(rmsnorm(x, norm_ffn), peer_w_q, peer_keys, peer_u, peer_v)
    return x, (dk, dv, ckv, kpe)


def setup_inputs(seed: int = 0) -> dict:
    key = jax.random.key(seed)
    ks = jax.random.split(key, 32)

    def nrm(k, shape, scale):
        return jax.random.normal(k, shape, jnp.float32) * scale

    def gain(k, shape):
        return 1.0 + 0.02 * jax.random.normal(k, shape, jnp.float32)

    return {
        "x_prompt": nrm(ks[0], (BATCH, SEQ, D_MODEL), 1.0),
        "x_sample": nrm(ks[1], (DEC_BATCH, DEC_SEQ, D_MODEL), 1.0),
        "cache_diff_k": nrm(ks[2], (DEPTH, DEC_BATCH, PAST_LEN, DA_HEADS, 2, DA_HEAD_DIM), 1.0),
        "cache_diff_v": nrm(ks[3], (DEPTH, DEC_BATCH, PAST_LEN, DA_HEADS, 2 * DA_HEAD_DIM), 1.0),
        "cache_mla_ckv": nrm(ks[4], (DEPTH, DEC_BATCH, PAST_LEN, MLA_KV_LORA), 1.0),
        "cache_mla_kpe": nrm(ks[5], (DEPTH, DEC_BATCH, PAST_LEN, MLA_ROPE), 1.0),
        "rel_bias": nrm(ks[6], (REL_BUCKETS, DA_HEADS), 0.1),
        "norm_mix": gain(ks[7], (DEPTH, D_MODEL)),
        "w_in": nrm(ks[8], (DEPTH, D_MODEL, IN_WIDTH), D_MODEL ** -0.5),
        "diff_lambda": nrm(ks[9], (DEPTH, 4, DA_HEAD_DIM), 0.1),
        "diff_subln": gain(ks[10], (DEPTH, 2 * DA_HEAD_DIM)),
        "mla_q_norm": gain(ks[11], (DEPTH, MLA_Q_LORA)),
        "mla_w_uq": nrm(ks[12], (DEPTH, MLA_Q_LORA, MLA_HEADS, MLA_NOPE + MLA_ROPE), MLA_Q_LORA ** -0.5),
        "mla_kv_norm": gain(ks[13], (DEPTH, MLA_KV_LORA)),
        "mla_w_uk": nrm(ks[14], (DEPTH, MLA_KV_LORA, MLA_HEADS, MLA_NOPE), MLA_KV_LORA ** -0.5),
        "mla_w_uv": nrm(ks[15], (DEPTH, MLA_KV_LORA, MLA_HEADS, MLA_V), MLA_KV_LORA ** -0.5),
        "w_branch_a": nrm(ks[16], (DEPTH, DA_WIDTH, D_MODEL), DA_WIDTH ** -0.5),
        "w_branch_b": nrm(ks[17], (DEPTH, MLA_WIDTH, D_MODEL), MLA_WIDTH ** -0.5),
        "w_out": nrm(ks[18], (DEPTH, D_MODEL, D_MODEL), D_MODEL ** -0.5),
        "norm_ffn": gain(ks[19], (DEPTH, D_MODEL)),
        "peer_w_q": nrm(ks[20], (DEPTH, D_MODEL, PEER_HEADS * PEER_D_KEY), D_MODEL ** -0.5),
        "peer_keys": nrm(ks[21], (DEPTH, PEER_HEADS, 2, PEER_N_KEYS, PEER_D_KEY // 2), (PEER_D_KEY // 2) ** -0.5),
        "peer_u": nrm(ks[22], (DEPTH, PEER_N_EXPERTS, D_MODEL), D_MODEL ** -0.5),
        "peer_v": nrm(ks[23], (DEPTH, PEER_N_EXPERTS, D_MODEL), 0.1),
        "norm_final": gain(ks[24], (D_MODEL,)),
    }


def reference(x_prompt, x_sample, cache_diff_k, cache_diff_v, cache_mla_ckv, cache_mla_kpe,
              rel_bias, norm_mix, w_in, diff_lambda, diff_subln, mla_q_norm, mla_w_uq, mla_kv_norm,
              mla_w_uk, mla_w_uv, w_branch_a, w_branch_b, w_out, norm_ffn, peer_w_q, peer_keys,
              peer_u, peer_v, norm_final):
    pos_p = jnp.arange(x_prompt.shape[1], dtype=jnp.int32)
    pos_s = cache_diff_k.shape[2] + jnp.arange(x_sample.shape[1], dtype=jnp.int32)
    xp, xs = x_prompt, x_sample
    kp, vp, cp, ep = [], [], [], []
    ks_, vs_, cs_, es_ = [], [], [], []
    for l in range(DEPTH):
        w = (norm_mix[l], w_in[l], diff_lambda[l], diff_subln[l], mla_q_norm[l], mla_w_uq[l],
             mla_kv_norm[l], mla_w_uk[l], mla_w_uv[l], w_branch_a[l], w_branch_b[l], w_out[l],
             norm_ffn[l], peer_w_q[l], peer_keys[l], peer_u[l], peer_v[l])
        xp, (a, b, c, d) = trunk_layer(xp, pos_p, None, l, rel_bias, *w)
        kp.append(a); vp.append(b); cp.append(c); ep.append(d)
        past = (cache_diff_k[l], cache_diff_v[l], cache_mla_ckv[l], cache_mla_kpe[l])
        xs, (a, b, c, d) = trunk_layer(xs, pos_s, past, l, rel_bias, *w)
        ks_.append(a); vs_.append(b); cs_.append(c); es_.append(d)
    y_prompt = rmsnorm(xp, norm_final)
    y_sample = rmsnorm(xs, norm_final)
    return (y_prompt, y_sample,
            jnp.stack(kp), jnp.stack(vp), jnp.stack(cp), jnp.stack(ep),
            jnp.stack(ks_), jnp.stack(vs_), jnp.stack(cs_), jnp.stack(es_))
```

```python
import math
from contextlib import ExitStack

import numpy as np
import ml_dtypes

import concourse.bass as bass
import concourse.mybir as mybir
from concourse.bass_utils import run_bass_kernel_spmd

F32 = mybir.dt.float32
BF16 = mybir.dt.bfloat16
U32 = mybir.dt.uint32
I32 = mybir.dt.int32
ALU = mybir.AluOpType
AF = mybir.ActivationFunctionType

NCORES = 8
D = 1024
SP = 4096
SS = 64
PAST = 2048
INW = 5536
NQ = SP + 128
NT = SP + PAST + 128
KS0 = SP
EPS = 1e-6
LAMBDA_INIT = 0.8 - 0.6 * math.exp(0.0)


def L(name, *a, **kw):
    return (name, a, kw)


class Sched:
    ENGS = ['tensor', 'vector', 'scalar', 'gpsimd', 'sync']

    def __init__(self):
        self.prog = {e: [] for e in self.ENGS}
        self.cnt = {e: 0 for e in self.ENGS}
        self.waited = {e: {} for e in self.ENGS}
        self.last_w = {}
        self.readers = {}
        self.semkeys = list(self.ENGS)
        self.phys_of = {}
        self.free = {}
        self.phys_q = {}
        self.nphys = 0
        self.nops = 0
        self.single = ()

    banks = ()

    def with_banks(self, reads, writes):
        extra = []
        for k in list(reads) + list(writes):
            for b in self.banks:
                if k.startswith(b):
                    bk = 'BANK:' + b
                    if bk not in extra:
                        extra.append(bk)
                    break
        return list(writes) + extra

    def canon(self, keys):
        out = []
        for k in keys:
            for p in self.single:
                if k.startswith(p + '1'):
                    k = p + '0' + k[len(p) + 1:]
                    break
            out.append(k)
        return out

    def _need(self, eng, tok, same_ok=False):
        if tok is None:
            return
        key, val = tok
        if same_ok and key == eng and eng == 'tensor':
            return
        if self.waited[eng].get(key, 0) >= val:
            return
        self.waited[eng][key] = val
        self.prog[eng].append(('wait', key, val))

    def _deps(self, eng, reads, writes):
        for r in reads:
            self._need(eng, self.last_w.get(r))
        for w in writes:
            self._need(eng, self.last_w.get(w), same_ok=True)
            for tok in self.readers.get(w, ()):
                self._need(eng, tok, same_ok=True)

    def _commit(self, tok, reads, writes):
        for r in reads:
            lst = self.readers.setdefault(r, [])
            lst.append(tok)
            if len(lst) > 48:
                best = {}
                for k, v in lst:
                    best[k] = max(best.get(k, 0), v)
                self.readers[r] = list(best.items())
        for w in writes:
            self.last_w[w] = tok
            self.readers[w] = []

    max_ops = 10 ** 9

    def op(self, eng, fn, reads=(), writes=()):
        if self.nops >= self.max_ops:
            return None
        reads = self.canon(reads); writes = self.with_banks(reads, self.canon(writes))
        self._deps(eng, reads, writes)
        self.cnt[eng] += 1
        tok = (eng, self.cnt[eng])
        self.prog[eng].append(('op', fn, eng, 1))
        self._commit(tok, reads, writes)
        self.nops += 1
        return tok

    def dma(self, q, semkey, fn, reads=(), writes=()):
        if self.nops >= self.max_ops:
            return None
        phys = self.phys_of.get(semkey)
        if phys is None:
            if self.free.get(q):
                phys = self.free[q].pop()
            else:
                phys = 'dma%s%d' % (q[0], self.nphys)
                self.nphys += 1
                self.cnt[phys] = 0
                self.semkeys.append(phys)
            self.phys_of[semkey] = phys
            self.phys_q[phys] = q
        reads = self.canon(reads); writes = self.with_banks(reads, self.canon(writes))
        self._deps(q, reads, writes)
        self.cnt[phys] += 16
        tok = (phys, self.cnt[phys])
        self.prog[q].append(('op', fn, phys, 16))
        self._commit(tok, reads, writes)
        self.nops += 1
        return tok

    def wait_dma(self, eng, semkey):
        phys = self.phys_of.get(semkey)
        if phys is not None:
            self._need(eng, (phys, self.cnt[phys]))

    def end_phase(self):
        for k in sorted(self.phys_of):
            phys = self.phys_of[k]
            self._need('sync', (phys, self.cnt[phys]))
            self.free.setdefault(self.phys_q[phys], []).append(phys)
        self.phys_of = {}

    def emit(self, nc, sems):
        prog = self.prog
        self.prog = {e: [] for e in self.ENGS}
        self.last_w = {}
        self.readers = {}

        def run(engname, e):
            for it in prog[engname]:
                if it[0] == 'wait':
                    e.wait_ge(sems[it[1]], it[2])
                else:
                    _, fn, key, inc = it
                    name, a, kw = fn
                    getattr(e, name)(*a, **kw).then_inc(sems[key], inc)

        with nc.Block() as block:
            @block.tensor
            def _(e):
                run('tensor', e)

            @block.vector
            def _(e):
                run('vector', e)

            @block.scalar
            def _(e):
                run('scalar', e)

            @block.gpsimd
            def _(e):
                run('gpsimd', e)

            @block.sync
            def _(e):
                run('sync', e)


def _rope_table():
    half = 16
    inv = (np.float32(10000.0) ** (-np.arange(half, dtype=np.float32) / np.float32(half))).astype(np.float32)
    pos = np.concatenate([np.arange(SP), PAST + np.arange(128)]).astype(np.float32)
    ang = (pos[:, None] * inv[None, :]).astype(np.float32)
    return np.concatenate([np.cos(ang), np.sin(ang)], axis=1).astype(np.float32)


def _t5_bucket(rel):
    nb = 16
    max_exact = 8
    ret = np.where(rel > 0, nb, 0)
    n = np.abs(rel)
    lg = (np.log(np.maximum(n, 1).astype(np.float32) / np.float32(max_exact)) / np.float32(math.log(128 / max_exact))
          * np.float32(nb - max_exact)).astype(np.float32)
    large = max_exact + lg.astype(np.int32)
    large = np.minimum(large, nb - 1)
    return ret + np.where(n < max_exact, n, large)


def _bias_consts():
    rel = np.arange(383) - 255
    bk = _t5_bucket(rel)
    oh = np.zeros((32, 384), np.float32)
    oh[bk, np.arange(383)] = 1.0
    kk = np.arange(128)[:, None]
    qq = np.arange(256)[None, :]
    maskw = ((kk // 64) <= (qq // 64)).astype(np.float32)
    return oh, maskw


def build_program(phases=('A', 'B', 'C'), cfg=None):
    cfg = cfg or {}
    nc = bass.Bass("TRN2", target_bir_lowering=False)
    S = Sched()
    S.max_ops = cfg.get('max_ops', 10 ** 9)

    def din(name, shape, dt=F32):
        return nc.dram_tensor(name, list(shape), dt, kind="ExternalInput")

    def dout(name, shape, dt=F32):
        return nc.dram_tensor(name, list(shape), dt, kind="ExternalOutput")

    def dscr(name, shape, dt=BF16):
        if cfg.get('dbg_scratch') and name in ('oa_s', 'ob_s', 'gates_s'):
            return nc.dram_tensor(name, list(shape), dt, kind="ExternalOutput")
        return nc.dram_tensor(name, list(shape), dt)

    xp = din("xp", [SP, D]); xs = din("xs", [SS, D])
    cdk = din("cdk", [PAST, D]); cdv = din("cdv", [PAST, D])
    cckv = din("cckv", [PAST, 128]); ckpe = din("ckpe", [PAST, 32])
    rel_bias = din("rel_bias", [32, 8])
    norm_mix = din("norm_mix", [D]); w_in = din("w_in", [D, INW])
    diff_lambda = din("diff_lambda", [4 * 64]); diff_subln = din("diff_subln", [128])
    mla_q_norm = din("mla_q_norm", [256]); w_uq = din("w_uq", [256, 1536])
    mla_kv_norm = din("mla_kv_norm", [128]); w_uk = din("w_uk", [128, 1024]); w_uv = din("w_uv", [128, 1024])
    w_ba = din("w_ba", [D, D]); w_bb = din("w_bb", [D, D]); w_o = din("w_o", [D, D])
    norm_ffn = din("norm_ffn", [D]); peer_wq = din("peer_wq", [D, D])
    peer_keys = din("peer_keys", [16, 128, 64])
    peer_u = din("peer_u", [16384, D]); peer_v = din("peer_v", [16384, D])
    norm_final = din("norm_final", [D])
    c_ident = din("c_ident", [128, 128]); c_rope = din("c_rope", [NQ, 32])
    c_oh = din("c_oh", [32, 384]); c_maskw = din("c_maskw", [128, 256])
    c_iota = din("c_iota", [128, 256])

    yp = dout("yp", [SP, D]); ys = dout("ys", [SS, D])
    kp = dout("kp", [SP, D]); vp = dout("vp", [SP, D]); cp = dout("cp", [SP, 128]); ep = dout("ep", [SP, 32])
    ks_ = dout("ks", [SS, D]); vs_ = dout("vs", [SS, D]); cs_ = dout("cs", [SS, 128]); es_ = dout("es", [SS, 32])

    dqT_s = dscr("dqT_s", [128, 8, NQ]); dkT_s = dscr("dkT_s", [128, 8, NT]); dv_s = dscr("dv_s", [NT, D])
    mqT_s = dscr("mqT_s", [96, 16, NQ]); mkT_s = dscr("mkT_s", [128, 8, NT]); mkpeT_s = dscr("mkpeT_s", [32, NT])
    mv_s = dscr("mv_s", [NT, D]); gates_s = dscr("gates_s", [NQ, 2 * D])
    oa_s = dscr("oa_s", [NQ, D]); ob_s = dscr("ob_s", [NQ, D])
    fbias_s = dscr("fbias_s", [8, 384], F32)
    DENSE = cfg.get('peer', 'dense') == 'dense'
    TG = cfg.get('tg', 256)
    UT_s = dscr("UT_s", [64, 128, 2, 1024]); V_s = dscr("V_s", [64, 128, 2, 1024])
    x2_s = dscr("x2_s", [NQ, D], F32); h2T_s = dscr("h2T_s", [128, 8, NQ])
    r_s = dscr("r_s", [128, 3, NQ], F32)

    with ExitStack() as gs:
        sems = {}

        def ensure_sems():
            for k in S.semkeys:
                if k not in sems:
                    sems[k] = gs.enter_context(nc.semaphore("s_" + k))

        if 'A' in phases:
            with ExitStack() as es:
                def sb(name, shape, dt):
                    return es.enter_context(nc.sbuf_tensor(name, list(shape), dt))

                def ps(name, shape, dt):
                    return es.enter_context(nc.psum_tensor(name, list(shape), dt))

                def sb1(name, shape, dt):
                    t = sb(name, shape, dt)
                    return [t, t]
                S.banks = ('pz0', 'pz1', 'pTa', 'pTb0', 'pTb1', 'pmm0', 'pmm1', 'pTs')
                S.single = ('zk_f', 'zv_f', 'gates_b', 'qh_f', 'rtmp', 'hb', 'cq_b', 'vm_b')

                win_sb = sb("win_sb", [128, 8, INW], BF16)
                wuq_sb = sb("wuq_sb", [128, 2, 1536], BF16)
                wuk_sb = sb("wuk_sb", [128, 1024], BF16)
                wuv_sb = sb("wuv_sb", [128, 1024], BF16)
                identf = sb("identf", [128, 128], F32)
                identb = sb("identb", [128, 128], BF16)
                gmix = sb("gmix", [128, D], F32)
                gq = sb("gq", [128, 256], F32)
                gkv = sb("gkv", [128, 128], F32)
                rope_sb = sb("rope_sb", [128, 33, 32], F32)
                xt = [sb("xt%d" % i, [128, D], F32) for i in range(2)]
                junk = sb("junk", [128, D], BF16)
                sgtmp = sb("sgtmp", [128, D], F32)
                st = [sb("st%d" % i, [128, 8], F32) for i in range(2)]
                hb = sb1("hb", [128, D], BF16)
                hT = [sb("hT%d" % i, [128, 8, 128], BF16) for i in range(2)]
                zq_b = [sb("zq_b%d" % i, [128, D], BF16) for i in range(2)]
                zk_f = sb1("zk_f", [128, D], F32)
                zk_b = [sb("zk_b%d" % i, [128, D], BF16) for i in range(2)]
                zv_f = sb1("zv_f", [128, D], F32)
                zv_b = [sb("zv_b%d" % i, [128, D], BF16) for i in range(2)]
                zm = [sb("zm%d" % i, [128, 416], F32) for i in range(2)]
                gates_b = sb1("gates_b", [128, 2 * D], BF16)
                qT_sb = [sb("qT_sb%d" % i, [128, 8, 128], BF16) for i in range(2)]
                kT_sb = [sb("kT_sb%d" % i, [128, 8, 128], BF16) for i in range(2)]
                cq_b = sb1("cq_b", [128, 256], BF16)
                cqT = [sb("cqT%d" % i, [128, 2, 128], BF16) for i in range(2)]
                qh_f = sb1("qh_f", [128, 16, 96], F32)
                qh_b = [sb("qh_b%d" % i, [128, 16, 96], BF16) for i in range(2)]
                rtmp = sb1("rtmp", [128, 4, 16, 16], F32)
                mqT_sb = [sb("mqT_sb%d" % i, [128, 16, 128], BF16) for i in range(2)]
                ckv_f = [sb("ckv_f%d" % i, [128, 128], F32) for i in range(2)]
                ckv_b = [sb("ckv_b%d" % i, [128, 128], BF16) for i in range(2)]
                ckvT = [sb("ckvT%d" % i, [128, 128], BF16) for i in range(2)]
                kn_b = [sb("kn_b%d" % i, [128, D], BF16) for i in range(2)]
                vm_b = sb1("vm_b", [128, D], BF16)
                knT_sb = [sb("knT_sb%d" % i, [128, 8, 128], BF16) for i in range(2)]
                kpe_f = [sb("kpe_f%d" % i, [128, 32], F32) for i in range(2)]
                kpe_b = [sb("kpe_b%d" % i, [128, 32], BF16) for i in range(2)]
                kpeT_sb = [sb("kpeT_sb%d" % i, [32, 128], BF16) for i in range(2)]
                pTa = ps("pTa", [128, 8, 128], BF16)
                pz = [ps("pz%d" % i, [128, 512], F32) for i in range(2)]
                pTb = [ps("pTb%d" % i, [128, 8, 128], BF16) for i in range(2)]
                pmm = [ps("pmm%d" % i, [128, 512], F32) for i in range(2)]
                pTs = ps("pTs", [128, 4, 128], BF16)

                for kc in range(8):
                    for (n0, nw) in [(0, 2048), (2048, 2048), (4096, INW - 4096)]:
                        S.dma('gpsimd', 'w_in', L('dma_start', out=win_sb[:, kc, n0:n0 + nw], in_=w_in[kc * 128:(kc + 1) * 128, n0:n0 + nw]),
                            writes=['win'])
                for kc in range(2):
                    S.dma('gpsimd', 'w_uq', L('dma_start', out=wuq_sb[:, kc, :], in_=w_uq[kc * 128:(kc + 1) * 128, :]), writes=['wuq'])
                S.dma('gpsimd', 'w_uk', L('dma_start', out=wuk_sb[:], in_=w_uk.ap()), writes=['wuk'])
                S.dma('gpsimd', 'w_uv', L('dma_start', out=wuv_sb[:], in_=w_uv.ap()), writes=['wuv'])
                S.dma('sync', 'c_identf', L('dma_start', out=identf[:], in_=c_ident.ap()), writes=['identf'])
                S.dma('sync', 'c_gmix', L('dma_start', out=gmix[:], in_=bass.AP(norm_mix, 0, [[0, 128], [1, D]])), writes=['gmix'])
                S.dma('sync', 'c_gq', L('dma_start', out=gq[:], in_=bass.AP(mla_q_norm, 0, [[0, 128], [1, 256]])), writes=['gq'])
                S.dma('sync', 'c_gkv', L('dma_start', out=gkv[:], in_=bass.AP(mla_kv_norm, 0, [[0, 128], [1, 128]])), writes=['gkv'])
                S.dma('sync', 'c_rope', L('dma_start', out=rope_sb[:], in_=c_rope.ap().rearrange("(t p) c -> p t c", p=128)), writes=['rope'])
                S.op('vector', L('tensor_copy', out=identb[:], in_=identf[:]), reads=['identf'], writes=['identb'])

                evac_rr = [0]

                def evac(out, in_, reads, writes, func=AF.Copy, eng=None):
                    if eng is None:
                        eng = 'scalar' if (evac_rr[0] % 2 == 0) else 'vector'
                        evac_rr[0] += 1
                    if eng == 'scalar':
                        S.op('scalar', L('activation', out=out, in_=in_, func=func), reads=reads, writes=writes)
                    else:
                        S.op('vector', L('tensor_copy', out=out, in_=in_), reads=reads, writes=writes)

                def scale_gain(out, in0, scalar, in1, tmp, reads, writes):
                    S.op('vector', L('tensor_scalar', out=tmp, in0=in0, scalar1=scalar, scalar2=None, op0=ALU.mult),
                         reads=reads, writes=['sgtmp'])
                    S.op('vector', L('tensor_tensor', out=out, in0=tmp, in1=in1, op=ALU.mult),
                         reads=['sgtmp'] + list(reads), writes=writes)

                def rms_stats(src_ap, ncols, stt, col, sl):
                    key = 'st%d_%d' % (sl, col)
                    S.op('scalar', L('activation', out=junk[:, 0:ncols], in_=src_ap, func=AF.Square,
                                                          accum_out=stt[:, col:col + 1]),
                         reads=rms_stats.reads, writes=['junk', key])
                    S.op('vector', L('tensor_scalar', out=stt[:, col:col + 1], in0=stt[:, col:col + 1],
                                                             scalar1=1.0 / ncols, scalar2=EPS, op0=ALU.mult, op1=ALU.add),
                         reads=[key], writes=[key])
                    S.op('scalar', L('activation', out=stt[:, col:col + 1], in_=stt[:, col:col + 1], func=AF.Sqrt),
                         reads=[key], writes=[key])
                    S.op('vector', L('reciprocal', out=stt[:, col:col + 1], in_=stt[:, col:col + 1]),
                         reads=[key], writes=[key])
                    return key

                def transposes(src_fn, n, pst, pkey, reads, rows=128):
                    for j in range(n):
                        S.op('tensor', L('transpose', out=pst[0:rows, j, :], in_=src_fn(j), identity=identb[:]),
                             reads=list(reads) + ['identb'], writes=['%s_%d' % (pkey, j)])

                def k_side(sl, kcol, zkb_keys, zvb_keys, ckvb_key, kpeb_key):
                    transposes(lambda j: zk_b[sl][:, j * 128:(j + 1) * 128], 8, pTb[1], 'pTb1', zkb_keys)
                    evac(kT_sb[sl][:], pTb[1][:], ['pTb1_%d' % j for j in range(8)], ['kT_sb%d' % sl])
                    S.dma('gpsimd', 'st_kT%d' % sl, L('dma_start', out=dkT_s[:, :, kcol:kcol + 128], in_=kT_sb[sl][:]),
                          reads=['kT_sb%d' % sl])
                    S.dma('gpsimd', 'st_v%d' % sl, L('dma_start', out=dv_s[kcol:kcol + 128, :], in_=zv_b[sl][:]),
                          reads=zvb_keys)
                    S.op('tensor', L('transpose', out=pTs[:, 2, :], in_=ckv_b[sl][:], identity=identb[:]),
                         reads=[ckvb_key, 'identb'], writes=['pTs_2'])
                    evac(ckvT[sl][:], pTs[:, 2, :], ['pTs_2'], ['ckvT%d' % sl])
                    for (wsb, wkey, dst, dkey) in [(wuk_sb, 'wuk', kn_b, 'kn_b'), (wuv_sb, 'wuv', vm_b, 'vm_b')]:
                        for c in range(2):
                            S.op('tensor', L('matmul', pmm[c][:], lhsT=ckvT[sl][:], rhs=wsb[:, c * 512:(c + 1) * 512],
                                                                    start=True, stop=True),
                                 reads=['ckvT%d' % sl, wkey], writes=['pmm%d' % c])
                            evac(dst[sl][:, c * 512:(c + 1) * 512], pmm[c][:], ['pmm%d' % c], ['%s%d_%d' % (dkey, sl, c)])
                    S.dma('gpsimd', 'st_mv%d' % sl, L('dma_start', out=mv_s[kcol:kcol + 128, :], in_=vm_b[sl][:]),
                          reads=['vm_b%d_0' % sl, 'vm_b%d_1' % sl])
                    transposes(lambda j: kn_b[sl][:, j * 128:(j + 1) * 128], 8, pTb[0], 'pTb0',
                               ['kn_b%d_0' % sl, 'kn_b%d_1' % sl])
                    evac(knT_sb[sl][:], pTb[0][:], ['pTb0_%d' % j for j in range(8)], ['knT_sb%d' % sl])
                    S.dma('gpsimd', 'st_mk%d' % sl, L('dma_start', out=mkT_s[:, :, kcol:kcol + 128], in_=knT_sb[sl][:]),
                          reads=['knT_sb%d' % sl])
                    S.op('tensor', L('transpose', out=pTs[0:32, 3, :], in_=kpe_b[sl][:], identity=identb[:]),
                         reads=[kpeb_key, 'identb'], writes=['pTs_3'])
                    evac(kpeT_sb[sl][:], pTs[0:32, 3, :], ['pTs_3'], ['kpeT_sb%d' % sl])
                    S.dma('gpsimd', 'st_kpe%d' % sl, L('dma_start', out=mkpeT_s[:, kcol:kcol + 128], in_=kpeT_sb[sl][:]),
                          reads=['kpeT_sb%d' % sl])

                tiles = [('p', i) for i in range(32)] + [('s', 0)] + [('c', i) for i in range(16)]
                tiles = cfg.get('a_tiles', tiles)
                for tix, (kind, i) in enumerate(tiles):
                    sl = tix % 2
                    if kind == 'c':
                        kcol = KS0 + i * 128
                        r0 = i * 128
                        S.dma('gpsimd', 'ld_ck%d' % sl, L('dma_start', out=zk_b[sl][:], in_=cdk[r0:r0 + 128, :]),
                              writes=['zk_b%d_2' % sl, 'zk_b%d_3' % sl])
                        S.dma('gpsimd', 'ld_cv%d' % sl, L('dma_start', out=zv_b[sl][:], in_=cdv[r0:r0 + 128, :]),
                              writes=['zv_b%d_4' % sl, 'zv_b%d_5' % sl])
                        S.dma('gpsimd', 'ld_cc%d' % sl, L('dma_start', out=ckv_b[sl][:], in_=cckv[r0:r0 + 128, :]),
                              writes=['ckv_b%d' % sl])
                        S.dma('gpsimd', 'ld_ce%d' % sl, L('dma_start', out=kpe_b[sl][:], in_=ckpe[r0:r0 + 128, :]),
                              writes=['kpe_b%d' % sl])
                        k_side(sl, kcol, ['zk_b%d_2' % sl, 'zk_b%d_3' % sl], ['zv_b%d_4' % sl, 'zv_b%d_5' % sl], 'ckv_b%d' % sl, 'kpe_b%d' % sl)
                        continue
                    if kind == 'p':
                        rows = 128; qcol = i * 128; kcol = i * 128; rt = i
                        xsrc = xp[i * 128:(i + 1) * 128, :]
                        o_k, o_v, o_c, o_e = kp[qcol:qcol + 128, :], vp[qcol:qcol + 128, :], cp[qcol:qcol + 128, :], ep[qcol:qcol + 128, :]
                    else:
                        rows = 64; qcol = SP; kcol = KS0 + PAST; rt = 32
                        xsrc = xs.ap()
                        o_k, o_v, o_c, o_e = ks_.ap(), vs_.ap(), cs_.ap(), es_.ap()
                        S.op('vector', L('memset', xt[sl][:], 0.0), writes=['xt%d' % sl])
                    S.dma('sync', 'ld_x%d' % sl, L('dma_start', out=xt[sl][0:rows, :], in_=xsrc),
                          writes=['xt%d' % sl])
                    rms_stats.reads = ['xt%d' % sl]
                    k0 = rms_stats(xt[sl][:], D, st[sl], 0, sl)
                    scale_gain(hb[sl][:], xt[sl][:], st[sl][:, 0:1], gmix[:], sgtmp[:, 0:1024], ['xt%d' % sl, k0, 'gmix'], ['hb%d' % sl])
                    transposes(lambda j: hb[sl][:, j * 128:(j + 1) * 128], 8, pTa, 'pTa', ['hb%d' % sl])
                    evac(hT[sl][:], pTa[:], ['pTa_%d' % j for j in range(8)], ['hT%d' % sl])

                    chunks = [(0, 512), (512, 512), (1024, 512), (1536, 512), (2048, 512), (2560, 512), (3072, 416),
                              (3488, 512), (4000, 512), (4512, 512), (5024, 512)]
                    for ci, (n0, nw) in enumerate(chunks):
                        pb = ci % 2
                        for kc in range(8):
                            S.op('tensor', L('matmul', pz[pb][:, 0:nw], lhsT=hT[sl][:, kc, :], rhs=win_sb[:, kc, n0:n0 + nw],
                                start=(kc == 0), stop=(kc == 7)),
                                reads=['hT%d' % sl, 'win'], writes=['pz%d' % pb])
                        if ci < 2:
                            evac(zq_b[sl][:, n0:n0 + 512], pz[pb][:], ['pz%d' % pb], ['zq_b%d_%d' % (sl, ci)])
                        elif ci < 4:
                            c0 = n0 - 1024
                            evac(zk_f[sl][:, c0:c0 + 512], pz[pb][:], ['pz%d' % pb], ['zk_f%d_%d' % (sl, ci)], eng='scalar')
                            S.op('vector', L('tensor_copy', out=zk_b[sl][:, c0:c0 + 512], in_=pz[pb][:]),
                                 reads=['pz%d' % pb], writes=['zk_b%d_%d' % (sl, ci)])
                        elif ci < 6:
                            c0 = n0 - 2048
                            evac(zv_f[sl][:, c0:c0 + 512], pz[pb][:], ['pz%d' % pb], ['zv_f%d_%d' % (sl, ci)], eng='scalar')
                            S.op('vector', L('tensor_copy', out=zv_b[sl][:, c0:c0 + 512], in_=pz[pb][:]),
                                 reads=['pz%d' % pb], writes=['zv_b%d_%d' % (sl, ci)])
                        elif ci == 6:
                            evac(zm[sl][:], pz[pb][:, 0:416], ['pz%d' % pb], ['zm%d' % sl], eng='vector')
                        else:
                            c0 = n0 - 3488
                            S.op('scalar', L('activation', out=gates_b[sl][:, c0:c0 + 512], in_=pz[pb][:],
                                                                                 func=AF.Sigmoid),
                                 reads=['pz%d' % pb], writes=['gates_b%d_%d' % (sl, ci)])
                    S.dma('gpsimd', 'st_ok%d' % sl, L('dma_start', out=o_k, in_=zk_f[sl][0:rows, :]),
                          reads=['zk_f%d_2' % sl, 'zk_f%d_3' % sl])
                    S.dma('gpsimd', 'st_ov%d' % sl, L('dma_start', out=o_v, in_=zv_f[sl][0:rows, :]),
                          reads=['zv_f%d_4' % sl, 'zv_f%d_5' % sl])
                    S.dma('gpsimd', 'st_g%d' % sl, L('dma_start', out=gates_s[qcol:qcol + 128, :], in_=gates_b[sl][:]),
                          reads=['gates_b%d_%d' % (sl, c) for c in range(7, 11)])
                    if cfg.get('a_stage', 9) < 2:
                        continue
                    transposes(lambda j: zq_b[sl][:, j * 128:(j + 1) * 128], 8, pTb[0], 'pTb0', ['zq_b%d_0' % sl, 'zq_b%d_1' % sl])
                    evac(qT_sb[sl][:], pTb[0][:], ['pTb0_%d' % j for j in range(8)], ['qT_sb%d' % sl])
                    S.dma('gpsimd', 'st_qT%d' % sl, L('dma_start', out=dqT_s[:, :, qcol:qcol + 128], in_=qT_sb[sl][:]),
                          reads=['qT_sb%d' % sl])
                    if cfg.get('a_stage', 9) < 3:
                        continue
                    rms_stats.reads = ['zm%d' % sl]
                    k1 = rms_stats(zm[sl][:, 0:256], 256, st[sl], 1, sl)
                    scale_gain(cq_b[sl][:], zm[sl][:, 0:256], st[sl][:, 1:2], gq[:], sgtmp[:, 0:256], ['zm%d' % sl, k1, 'gq'], ['cq_b%d' % sl])
                    for j in range(2):
                        S.op('tensor', L('transpose', out=pTs[:, j, :], in_=cq_b[sl][:, j * 128:(j + 1) * 128], identity=identb[:]),
                             reads=['cq_b%d' % sl, 'identb'], writes=['pTs_%d' % j])
                    evac(cqT[sl][:], pTs[:, 0:2, :], ['pTs_0', 'pTs_1'], ['cqT%d' % sl])
                    qh_flat = qh_f[sl][:].rearrange("p h e -> p (h e)")
                    for c in range(3):
                        pb = c % 2
                        for kc in range(2):
                            S.op('tensor', L('matmul', pmm[pb][:], lhsT=cqT[sl][:, kc, :], rhs=wuq_sb[:, kc, c * 512:(c + 1) * 512],
                                start=(kc == 0), stop=(kc == 1)),
                                reads=['cqT%d' % sl, 'wuq'], writes=['pmm%d' % pb])
                        evac(qh_flat[:, c * 512:(c + 1) * 512], pmm[pb][:], ['pmm%d' % pb], ['qh_f%d_%d' % (sl, c)])
                    qhr = ['qh_f%d_%d' % (sl, c) for c in range(3)]
                    cosb = bass.AP(rope_sb, rt * 32, [[33 * 32, 128], [0, 16], [1, 16]])
                    sinb = bass.AP(rope_sb, rt * 32 + 16, [[33 * 32, 128], [0, 16], [1, 16]])
                    x1 = qh_f[sl][:, :, 64:80]
                    x2 = qh_f[sl][:, :, 80:96]
                    tmpk = 'rtmp%d' % sl
                    for ti_, (a, b_) in enumerate([(x1, cosb), (x2, sinb), (x1, sinb), (x2, cosb)]):
                        S.op('vector', L('tensor_tensor', out=rtmp[sl][:, ti_, :, :], in0=a, in1=b_, op=ALU.mult),
                             reads=qhr + ['rope'], writes=[tmpk + '_%d' % ti_])
                    S.op('vector', L('tensor_copy', out=qh_b[sl][:, :, 0:64], in_=qh_f[sl][:, :, 0:64]),
                         reads=qhr, writes=['qh_b%d_n' % sl])
                    S.op('vector', L('tensor_tensor', out=qh_b[sl][:, :, 64:80], in0=rtmp[sl][:, 0, :, :], in1=rtmp[sl][:, 1, :, :],
                                                             op=ALU.subtract),
                         reads=[tmpk + '_0', tmpk + '_1'], writes=['qh_b%d_a' % sl])
                    S.op('vector', L('tensor_tensor', out=qh_b[sl][:, :, 80:96], in0=rtmp[sl][:, 2, :, :], in1=rtmp[sl][:, 3, :, :],
                                                             op=ALU.add),
                         reads=[tmpk + '_2', tmpk + '_3'], writes=['qh_b%d_b' % sl])
                    qhb_keys = ['qh_b%d_n' % sl, 'qh_b%d_a' % sl, 'qh_b%d_b' % sl]
                    for half in range(2):
                        transposes(lambda j, half=half: qh_b[sl][:, half * 8 + j, :], 8, pTb[half], 'pTb%d' % half, qhb_keys, rows=96)
                        evac(mqT_sb[sl][0:96, half * 8:(half + 1) * 8, :], pTb[half][0:96, :, :],
                             ['pTb%d_%d' % (half, j) for j in range(8)], ['mqT_sb%d_%d' % (sl, half)])
                    S.dma('gpsimd', 'st_mq%d' % sl, L('dma_start', out=mqT_s[:, :, qcol:qcol + 128], in_=mqT_sb[sl][0:96, :, :]),
                          reads=['mqT_sb%d_0' % sl, 'mqT_sb%d_1' % sl])
                    if cfg.get('a_stage', 9) < 4:
                        continue
                    k2 = rms_stats(zm[sl][:, 256:384], 128, st[sl], 2, sl)
                    scale_gain(ckv_f[sl][:], zm[sl][:, 256:384], st[sl][:, 2:3], gkv[:], sgtmp[:, 0:128], ['zm%d' % sl, k2, 'gkv'], ['ckv_f%d' % sl])
                    S.op('vector', L('tensor_copy', out=ckv_b[sl][:], in_=ckv_f[sl][:]), reads=['ckv_f%d' % sl], writes=['ckv_b%d' % sl])
                    S.dma('gpsimd', 'st_oc%d' % sl, L('dma_start', out=o_c, in_=ckv_f[sl][0:rows, :]),
                          reads=['ckv_f%d' % sl])
                    cos1 = bass.AP(rope_sb, rt * 32, [[33 * 32, 128], [1, 16]])
                    sin1 = bass.AP(rope_sb, rt * 32 + 16, [[33 * 32, 128], [1, 16]])
                    y1 = zm[sl][:, 384:400]
                    y2 = zm[sl][:, 400:416]
                    for ti_, (a, b_) in enumerate([(y1, cos1), (y2, sin1), (y1, sin1), (y2, cos1)]):
                        S.op('vector', L('tensor_tensor', out=rtmp[sl][:, ti_, 0, :], in0=a, in1=b_, op=ALU.mult),
                             reads=['zm%d' % sl, 'rope'], writes=[tmpk + '_%d' % ti_])
                    S.op('vector', L('tensor_tensor', out=kpe_f[sl][:, 0:16], in0=rtmp[sl][:, 0, 0, :], in1=rtmp[sl][:, 1, 0, :], op=ALU.subtract),
                         reads=[tmpk + '_0', tmpk + '_1'], writes=['kpe_f%d_a' % sl])
                    S.op('vector', L('tensor_tensor', out=kpe_f[sl][:, 16:32], in0=rtmp[sl][:, 2, 0, :], in1=rtmp[sl][:, 3, 0, :], op=ALU.add),
                         reads=[tmpk + '_2', tmpk + '_3'], writes=['kpe_f%d_b' % sl])
                    S.op('vector', L('tensor_copy', out=kpe_b[sl][:], in_=kpe_f[sl][:]),
                         reads=['kpe_f%d_a' % sl, 'kpe_f%d_b' % sl], writes=['kpe_b%d' % sl])
                    S.dma('gpsimd', 'st_oe%d' % sl, L('dma_start', out=o_e, in_=kpe_f[sl][0:rows, :]),
                          reads=['kpe_f%d_a' % sl, 'kpe_f%d_b' % sl])
                    if cfg.get('a_stage', 9) < 5:
                        continue
                    k_side(sl, kcol, ['zk_b%d_2' % sl, 'zk_b%d_3' % sl], ['zv_b%d_4' % sl, 'zv_b%d_5' % sl], 'ckv_b%d' % sl, 'kpe_b%d' % sl)

                S.end_phase()
                ensure_sems()
                S.emit(nc, sems)

        if 'B' in phases:
            with ExitStack() as es:
                def sb(name, shape, dt):
                    return es.enter_context(nc.sbuf_tensor(name, list(shape), dt))

                def ps(name, shape, dt):
                    return es.enter_context(nc.psum_tensor(name, list(shape), dt))
                S.single = ()
                S.banks = ('sc0', 'sc1', 'oacc0', 'oacc1', 'oacc2', 'oacc3', 'pset')
                NKT = NT // 128
                QT = [sb("QT%d" % i, [128, NQ], BF16) for i in range(2)]
                KT = [sb("KT%d" % i, [128, NT], BF16) for i in range(2)]
                VA = [sb("VA%d" % i, [128, NKT, 129], BF16) for i in range(2)]
                VM = [sb("VM%d" % i, [128, NKT, 65], BF16) for i in range(2)]
                et = [sb("et%d" % i, [128, 512], BF16) for i in range(3)]
                rb_sb = sb("rb_sb", [32, 8], F32)
                oh_sb = sb("oh_sb", [32, 384], F32)
                rb15 = sb("rb15", [8, 1], F32)
                fexp = sb("fexp", [8, 384], F32)
                hank = sb("hank", [128, 8, 256], F32)
                maskw = sb("maskw", [128, 256], F32)
                ebm = sb("ebm", [128, 8, 256], BF16)
                dl = sb("dl", [128, 256], F32)
                dlj = sb("dlj", [128, 64], F32)
                lam = sb("lam", [128, 4], F32)
                gsub = sb("gsub", [128, 128], F32)
                o0 = [sb("o0_%d" % i, [128, 129], F32) for i in range(4)]
                sm = [sb("smB%d" % i, [128, 8], F32) for i in range(4)]
                of = [sb("of%d" % i, [128, 128], F32) for i in range(4)]
                of2 = [sb("of2_%d" % i, [128, 128], F32) for i in range(4)]
                jk = sb("jkB", [128, 128], BF16)
                oa_t = [sb("oa_t%d" % i, [128, 128], BF16) for i in range(4)]
                ob_t = [sb("ob_t%d" % i, [128, 64], BF16) for i in range(4)]
                sc = [ps("sc%d" % i, [128, 512], F32) for i in range(2)]
                oacc = [ps("oacc%d" % i, [128, 512], F32) for i in range(4)]
                pset = ps("pset", [128, 512], F32)

                S.dma('sync', 'b_rb', L('dma_start', out=rb_sb[:], in_=rel_bias.ap()), writes=['rb_sb'])
                S.dma('sync', 'b_oh', L('dma_start', out=oh_sb[:], in_=c_oh.ap()), writes=['oh_sb'])
                S.dma('sync', 'b_rb15', L('dma_start', out=rb15[:], in_=rel_bias[15:16, :].rearrange("a h -> h a")), writes=['rb15'])
                S.dma('sync', 'b_mw', L('dma_start', out=maskw[:], in_=c_maskw.ap()), writes=['maskw'])
                S.dma('sync', 'b_dl', L('dma_start', out=dl[:], in_=bass.AP(diff_lambda, 0, [[0, 128], [1, 256]])), writes=['dl'])
                S.dma('sync', 'b_gs', L('dma_start', out=gsub[:], in_=bass.AP(diff_subln, 0, [[0, 128], [1, 128]])), writes=['gsub'])
                S.op('tensor', L('matmul', pset[0:8, 0:384], lhsT=rb_sb[:], rhs=oh_sb[:], start=True, stop=True),
                     reads=['rb_sb', 'oh_sb'], writes=['pset'])
                S.op('vector', L('tensor_scalar', out=fexp[:], in0=pset[0:8, 0:384], scalar1=rb15[:, 0:1], scalar2=None, op0=ALU.subtract),
                     reads=['pset', 'rb15'], writes=['fexp'])
                S.op('scalar', L('activation', out=fexp[:], in_=fexp[:], func=AF.Exp), reads=['fexp'], writes=['fexp'])
                S.dma('sync', 'b_fs', L('dma_start', out=fbias_s.ap(), in_=fexp[:]), reads=['fexp'], writes=['fbias_s'])
                for h in range(8):
                    S.dma('sync', 'b_hk', L('dma_start', out=hank[:, h, :], in_=bass.AP(fbias_s, h * 384, [[1, 128], [1, 256]])),
                          reads=['fbias_s'], writes=['hank'])
                for h in range(8):
                    S.op('vector', L('tensor_tensor', out=ebm[:, h, :], in0=hank[:, h, ::-1], in1=maskw[:], op=ALU.mult),
                         reads=['hank', 'maskw'], writes=['ebm'])
                for i_, (a, b_) in enumerate([(0, 1), (2, 3)]):
                    S.op('vector', L('tensor_tensor', out=dlj[:], in0=dl[:, a * 64:(a + 1) * 64], in1=dl[:, b_ * 64:(b_ + 1) * 64], op=ALU.mult),
                         reads=['dl'], writes=['dlj'])
                    S.op('vector', L('reduce_sum', out=lam[:, i_:i_ + 1], in_=dlj[:], axis=mybir.AxisListType.X),
                         reads=['dlj'], writes=['lam%d' % i_])
                    S.op('scalar', L('activation', out=lam[:, i_:i_ + 1], in_=lam[:, i_:i_ + 1], func=AF.Exp),
                         reads=['lam%d' % i_], writes=['lam%d' % i_])
                S.op('vector', L('tensor_tensor', out=lam[:, 2:3], in0=lam[:, 1:2], in1=lam[:, 0:1], op=ALU.subtract),
                     reads=['lam0', 'lam1'], writes=['lam2'])
                S.op('vector', L('tensor_scalar', out=lam[:, 2:3], in0=lam[:, 2:3], scalar1=-LAMBDA_INIT, scalar2=None, op0=ALU.add),
                     reads=['lam2'], writes=['lam2'])
                S.op('vector', L('tensor_scalar', out=gsub[:], in0=gsub[:], scalar1=1.0 - LAMBDA_INIT, scalar2=None, op0=ALU.mult),
                     reads=['gsub'], writes=['gsub'])
                for i in range(2):
                    S.op('gpsimd', L('memset', VA[i][:, :, 128:129], 1.0), writes=['VA%d_one' % i])
                    S.op('gpsimd', L('memset', VA[i][64:128, NKT - 1, 128:129], 0.0), writes=['VA%d_one' % i])
                    S.op('gpsimd', L('memset', VM[i][:, :, 64:65], 1.0), writes=['VM%d_one' % i])
                    S.op('gpsimd', L('memset', VM[i][64:128, NKT - 1, 64:65], 0.0), writes=['VM%d_one' % i])

                groups = [dict(qcol0=g * 512, nq=4, kts=list(range(4 * g + 4)), qabs0=4 * g) for g in range(8)]
                groups.append(dict(qcol0=SP, nq=1, kts=list(range(32, 49)), qabs0=48))
                groups = cfg.get('b_groups', groups)
                st_ctr = [0]
                sc_ctr = [0]
                scb = [sc[0], sc[1], pset]
                sckey = ['sc0', 'sc1', 'pset']

                def attention(kind, h, slot, r0, r1, E, scale, vt, vkey, grp, finalize):
                    nq = grp['nq']; kts = grp['kts']; qabs0 = grp['qabs0']; qcol0 = grp['qcol0']

                    def qlo_of(kt):
                        return max(kt - qabs0, 0) if nq > 1 else 0

                    def score(idx):
                        kt = kts[idx]; qlo = qlo_of(kt); N = (nq - qlo) * 128
                        s = sc_ctr[0] % 3
                        sc_ctr[0] += 1
                        S.op('tensor', L('matmul', scb[s][:, 0:N], lhsT=KT[slot][r0:r1, kt * 128:(kt + 1) * 128],
                                         rhs=QT[slot][r0:r1, qcol0 + qlo * 128:qcol0 + nq * 128], start=True, stop=True),
                             reads=['KT%d' % slot, 'KT%d_pe' % slot, 'QT%d' % slot], writes=[sckey[s]])
                        return s
                    pend = [score(i_) for i_ in range(min(2, len(kts)))]
                    for idx, kt in enumerate(kts):
                        s = pend.pop(0)
                        st_ctr[0] += 1
                        if idx + 2 < len(kts):
                            pend.append(score(idx + 2))
                        qlo = qlo_of(kt); N = (nq - qlo) * 128
                        t = st_ctr[0] % 3
                        S.op('scalar', L('activation', out=et[t][:, 0:N], in_=scb[s][:, 0:N], func=AF.Exp, scale=scale),
                             reads=[sckey[s]], writes=['et%d' % t])
                        d = qabs0 + qlo - kt
                        if kind == 'd' and d in (0, 1):
                            W = min(256 - d * 128, N)
                            S.op('vector', L('tensor_tensor', out=et[t][:, 0:W], in0=et[t][:, 0:W], in1=ebm[:, h, d * 128:d * 128 + W], op=ALU.mult),
                                 reads=['et%d' % t, 'ebm'], writes=['et%d' % t])
                        if kind == 'm' and d == 0:
                            S.op('gpsimd', L('memset', et[t][64:128, 0:64], 0.0), reads=['et%d' % t], writes=['et%d' % t])
                        for j in range(qlo, nq):
                            last = (qabs0 + j) if nq > 1 else kts[-1]
                            S.op('tensor', L('matmul', oacc[j][:, 0:E + 1], lhsT=et[t][:, (j - qlo) * 128:(j - qlo + 1) * 128],
                                             rhs=vt[:, kt, 0:E + 1], start=(idx == 0), stop=(kt == last)),
                                 reads=['et%d' % t, vkey, vkey + '_one'], writes=['oacc%d' % j])
                    for j in range(nq):
                        finalize(j, qcol0 + j * 128)

                def rstd_small(smt, col, key, n):
                    S.op('vector', L('tensor_scalar', out=smt[:, col:col + 1], in0=smt[:, col:col + 1], scalar1=1.0 / n, scalar2=EPS,
                                     op0=ALU.mult, op1=ALU.add), reads=[key], writes=[key])
                    S.op('scalar', L('activation', out=smt[:, col:col + 1], in_=smt[:, col:col + 1], func=AF.Sqrt), reads=[key], writes=[key])
                    S.op('vector', L('reciprocal', out=smt[:, col:col + 1], in_=smt[:, col:col + 1]), reads=[key], writes=[key])

                dheads = cfg.get('b_dheads', list(range(8)))
                for hi, h in enumerate(dheads):
                    slot = hi % 2
                    S.dma('sync', 'b_q%d' % slot, L('dma_start', out=QT[slot][:], in_=dqT_s[:, h, :]), writes=['QT%d' % slot])
                    S.dma('sync', 'b_k%d' % slot, L('dma_start', out=KT[slot][:], in_=dkT_s[:, h, :]), writes=['KT%d' % slot, 'KT%d_pe' % slot])
                    for t0 in range(0, NKT, 13):
                        t1 = min(t0 + 13, NKT)
                        S.dma('sync', 'b_v%d' % slot, L('dma_start', out=VA[slot][:, t0:t1, 0:128],
                                                         in_=dv_s.ap().rearrange("(t p) f -> p t f", p=128)[:, t0:t1, h * 128:(h + 1) * 128]),
                              writes=['VA%d' % slot])
                    for grp in groups:
                        def fin0(j, qrow):
                            S.op('vector', L('tensor_copy', out=o0[j][:], in_=oacc[j][:, 0:129]), reads=['oacc%d' % j], writes=['o0_%d' % j])

                        def fin1(j, qrow, h=h):
                            k_ = 'smB%d' % j
                            S.op('vector', L('reciprocal', out=sm[j][:, 0:1], in_=o0[j][:, 128:129]), reads=['o0_%d' % j], writes=[k_ + 'a'])
                            S.op('vector', L('reciprocal', out=sm[j][:, 1:2], in_=oacc[j][:, 128:129]), reads=['oacc%d' % j], writes=[k_ + 'b'])
                            S.op('vector', L('tensor_tensor', out=sm[j][:, 1:2], in0=sm[j][:, 1:2], in1=lam[:, 2:3], op=ALU.mult),
                                 reads=[k_ + 'b', 'lam2'], writes=[k_ + 'b'])
                            S.op('vector', L('tensor_scalar', out=of[j][:], in0=o0[j][:, 0:128], scalar1=sm[j][:, 0:1], scalar2=None, op0=ALU.mult),
                                 reads=['o0_%d' % j, k_ + 'a'], writes=['of%d' % j])
                            S.op('vector', L('tensor_scalar', out=of2[j][:], in0=oacc[j][:, 0:128], scalar1=sm[j][:, 1:2], scalar2=None, op0=ALU.mult),
                                 reads=['oacc%d' % j, k_ + 'b'], writes=['of2_%d' % j])
                            S.op('vector', L('tensor_tensor', out=of[j][:], in0=of[j][:], in1=of2[j][:], op=ALU.add),
                                 reads=['of%d' % j, 'of2_%d' % j], writes=['of%d' % j])
                            S.op('scalar', L('activation', out=jk[:], in_=of[j][:], func=AF.Square, accum_out=sm[j][:, 2:3]),
                                 reads=['of%d' % j], writes=['jkB', k_ + 'c'])
                            rstd_small(sm[j], 2, k_ + 'c', 128)
                            S.op('vector', L('tensor_scalar', out=of[j][:], in0=of[j][:], scalar1=sm[j][:, 2:3], scalar2=None, op0=ALU.mult),
                                 reads=['of%d' % j, k_ + 'c'], writes=['of%d' % j])
                            S.op('vector', L('tensor_tensor', out=oa_t[j][:], in0=of[j][:], in1=gsub[:], op=ALU.mult),
                                 reads=['of%d' % j, 'gsub'], writes=['oa_t%d' % j])
                            S.dma('gpsimd', 'b_so%d' % j, L('dma_start', out=oa_s[qrow:qrow + 128, h * 128:(h + 1) * 128], in_=oa_t[j][:]),
                                  reads=['oa_t%d' % j])
                        attention('d', h, slot, 0, 64, 128, 0.125, VA[slot], 'VA%d' % slot, grp, fin0)
                        attention('d', h, slot, 64, 128, 128, 0.125, VA[slot], 'VA%d' % slot, grp, fin1)

                mheads = cfg.get('b_mheads', list(range(16)))
                for hi, h in enumerate(mheads):
                    slot = hi % 2
                    S.dma('sync', 'b_q%d' % slot, L('dma_start', out=QT[slot][0:96, :], in_=mqT_s[:, h, :]), writes=['QT%d' % slot])
                    S.dma('sync', 'b_k%d' % slot, L('dma_start', out=KT[slot][0:64, :], in_=mkT_s[(h % 2) * 64:(h % 2) * 64 + 64, h // 2, :]),
                          writes=['KT%d' % slot])
                    if hi < 2:
                        S.dma('sync', 'b_kpe%d' % slot, L('dma_start', out=KT[slot][64:96, :], in_=mkpeT_s.ap()), writes=['KT%d_pe' % slot])
                    for t0 in range(0, NKT, 13):
                        t1 = min(t0 + 13, NKT)
                        S.dma('sync', 'b_vm%d' % slot, L('dma_start', out=VM[slot][:, t0:t1, 0:64],
                                                          in_=mv_s.ap().rearrange("(t p) f -> p t f", p=128)[:, t0:t1, h * 64:(h + 1) * 64]),
                              writes=['VM%d' % slot])
                    for grp in groups:
                        def finm(j, qrow, h=h):
                            k_ = 'smB%d' % j
                            S.op('vector', L('reciprocal', out=sm[j][:, 0:1], in_=oacc[j][:, 64:65]), reads=['oacc%d' % j], writes=[k_ + 'a'])
                            S.op('vector', L('tensor_scalar', out=ob_t[j][:], in0=oacc[j][:, 0:64], scalar1=sm[j][:, 0:1], scalar2=None, op0=ALU.mult),
                                 reads=['oacc%d' % j, k_ + 'a'], writes=['ob_t%d' % j])
                            S.dma('gpsimd', 'b_sb%d' % j, L('dma_start', out=ob_s[qrow:qrow + 128, h * 64:(h + 1) * 64], in_=ob_t[j][:]),
                                  reads=['ob_t%d' % j])
                        attention('m', h, slot, 0, 96, 64, 96.0 ** -0.5, VM[slot], 'VM%d' % slot, grp, finm)

                S.end_phase()
                ensure_sems()
                S.emit(nc, sems)

        if 'C' in phases:
            with ExitStack() as es:
                def sb(name, shape, dt):
                    return es.enter_context(nc.sbuf_tensor(name, list(shape), dt))

                def ps(name, shape, dt):
                    return es.enter_context(nc.psum_tensor(name, list(shape), dt))
                S.single = ()
                S.banks = ('pT0', 'pT1', 'pacc0', 'pacc1', 'pacc2', 'pacc3', 'psS0', 'psS1')
                NB = 4
                wa_sb = sb("wa_sb", [128, 8, D], BF16); wb_sb = sb("wb_sb", [128, 8, D], BF16)
                wo_sb = sb("wo_sb", [128, 8, D], BF16); wq_sb = sb("wq_sb", [128, 8, D], BF16)
                identf = sb("identfC", [128, 128], F32); identb = sb("identbC", [128, 128], BF16)
                gffn = sb("gffn", [128, D], F32); gfin = sb("gfin", [128, D], F32)
                keys_f = sb("keys_f", [128, 16, 64], F32); keys_b = sb("keys_b", [128, 16, 64], BF16)
                keysT = sb("keysT", [128, 8, 128], BF16)
                iota_t = sb("iota_t", [128, 256], F32)
                thr = sb("thr", [128, 16], F32)
                oa_t = sb("oa_tC", [128, D], BF16); ob_t = sb("ob_tC", [128, D], BF16)
                gt = sb("gtC", [128, 2 * D], BF16); xt = sb("xtC", [128, D], F32)
                oaT = sb("oaT", [128, 8, 128], BF16); obT = sb("obT", [128, 8, 128], BF16)
                m1 = sb("m1", [128, D], F32); m2 = sb("m2", [128, D], F32); mb = sb("mb", [128, D], BF16)
                mT = sb("mT", [128, 8, 128], BF16)
                h2f = sb("h2f", [128, D], F32); h2b = sb("h2b", [128, D], BF16)
                pq_b = sb("pq_b", [128, D], BF16); pqT = sb("pqT", [128, 8, 128], BF16)
                S_all = sb("S_all", [128, 16, 128], F32); S_wk = sb("S_wk", [128, 256], F32)
                s_top = sb("s_top", [128, 16, 16], F32); i_top = sb("i_top", [128, 16, 16], U32); i_topf = sb("i_topf", [128, 16, 16], F32)
                cand = sb("cand", [128, 256], F32)
                c_top = sb("c_top", [128, 16], F32); c_pos = sb("c_pos", [128, 16], U32); c_posf = sb("c_posf", [128, 16], F32)
                t3 = sb("t3", [128, 16, 16], F32)
                av = sb("av", [128, 16], F32); bv = sb("bv", [128, 16], F32)
                i1v = sb("i1v", [128, 16], F32); i2v = sb("i2v", [128, 16], F32)
                eidf = sb("eidf", [128, 128], F32); eidi = sb("eidi", [128, 128], I32)
                g_all = sb("g_all", [128, 128], F32)
                smc = sb("smc", [128, 16], F32)
                act = sb("act", [128, 128], F32); ga = sb("ga", [128, 128], F32)
                i1_all = sb("i1_all", [128, 128], F32); i2_all = sb("i2_all", [128, 128], F32)
                rT = sb("rT", [128, 3, 128], F32)
                junkb = sb("junkbC", [128, D], BF16)
                pT = [ps("pT%d" % i, [128, 8, 128], BF16) for i in range(2)]
                pacc = [ps("pacc%d" % i, [128, 512], F32) for i in range(4)]
                psS = [ps("psS%d" % i, [128, 4, 128], F32) for i in range(2)]

                for (wsb, wsrc, wk) in [(wa_sb, w_ba, 'wa'), (wb_sb, w_bb, 'wb'), (wo_sb, w_o, 'wo'), (wq_sb, peer_wq, 'wq')]:
                    for kc in range(8):
                        S.dma('gpsimd', 'c_' + wk, L('dma_start', out=wsb[:, kc, :], in_=wsrc[kc * 128:(kc + 1) * 128, :]), writes=[wk])
                S.dma('sync', 'c_identf', L('dma_start', out=identf[:], in_=c_ident.ap()), writes=['identf'])
                S.dma('sync', 'c_gffn', L('dma_start', out=gffn[:], in_=bass.AP(norm_ffn, 0, [[0, 128], [1, D]])), writes=['gffn'])
                S.dma('sync', 'c_gfin', L('dma_start', out=gfin[:], in_=bass.AP(norm_final, 0, [[0, 128], [1, D]])), writes=['gfin'])
                S.dma('sync', 'c_keys', L('dma_start', out=keys_f[:], in_=peer_keys.ap().rearrange("a n d -> n a d")), writes=['keys_f'])
                S.dma('sync', 'c_iota', L('dma_start', out=iota_t[:], in_=c_iota.ap()), writes=['iota'])
                S.op('vector', L('tensor_copy', out=identb[:], in_=identf[:]), reads=['identf'], writes=['identb'])
                S.op('vector', L('tensor_copy', out=keys_b[:], in_=keys_f[:]), reads=['keys_f'], writes=['keys_b'])
                S.op('vector', L('tensor_scalar', out=thr[:], in0=iota_t[:, 0:16], scalar1=16.0, scalar2=None, op0=ALU.mult),
                     reads=['iota'], writes=['thr'])
                for h in range(8):
                    S.op('tensor', L('transpose', out=pT[0][:, h, :], in_=keys_b[:, 2 * h:2 * h + 2, :].rearrange("p a d -> p (a d)"), identity=identb[:]),
                         reads=['keys_b', 'identb'], writes=['pT0_%d' % h])
                S.op('vector', L('tensor_copy', out=keysT[:], in_=pT[0][:]), reads=['pT0_%d' % h for h in range(8)], writes=['keysT'])

                def transposes8(src, skey, pti, dst, dkey, eng):
                    for j in range(8):
                        S.op('tensor', L('transpose', out=pT[pti][:, j, :], in_=src[:, j * 128:(j + 1) * 128], identity=identb[:]),
                             reads=list(skey) + ['identb'], writes=['pT%d_%d' % (pti, j)])
                    rk = ['pT%d_%d' % (pti, j) for j in range(8)]
                    if eng == 'scalar':
                        S.op('scalar', L('activation', out=dst[:], in_=pT[pti][:], func=AF.Copy), reads=rk, writes=[dkey])
                    else:
                        S.op('vector', L('tensor_copy', out=dst[:], in_=pT[pti][:]), reads=rk, writes=[dkey])

                def proj(srcT, skey, wsb, wk, c, pb):
                    for kc in range(8):
                        S.op('tensor', L('matmul', pacc[pb][:], lhsT=srcT[:, kc, :], rhs=wsb[:, kc, c * 512:(c + 1) * 512],
                                         start=(kc == 0), stop=(kc == 7)), reads=[skey, wk], writes=['pacc%d' % pb])

                def rstd_c(col, key, n):
                    S.op('vector', L('tensor_scalar', out=smc[:, col:col + 1], in0=smc[:, col:col + 1], scalar1=1.0 / n, scalar2=EPS,
                                     op0=ALU.mult, op1=ALU.add), reads=[key], writes=[key])
                    S.op('scalar', L('activation', out=smc[:, col:col + 1], in_=smc[:, col:col + 1], func=AF.Sqrt), reads=[key], writes=[key])
                    S.op('vector', L('reciprocal', out=smc[:, col:col + 1], in_=smc[:, col:col + 1]), reads=[key], writes=[key])

                def top16(src_ap, skey, vals, vkey, idxs, ikey):
                    S.op('vector', L('max', out=vals[:, 0:8], in_=src_ap), reads=[skey], writes=[vkey + 'a'])
                    S.op('vector', L('max_index', out=idxs[:, 0:8], in_max=vals[:, 0:8], in_values=src_ap), reads=[skey, vkey + 'a'], writes=[ikey + 'a'])
                    n = src_ap.shape[-1]
                    S.op('vector', L('match_replace', out=S_wk[:, 0:n], in_to_replace=vals[:, 0:8], in_values=src_ap, imm_value=-1e30),
                         reads=[skey, vkey + 'a'], writes=['S_wk'])
                    S.op('vector', L('max', out=vals[:, 8:16], in_=S_wk[:, 0:n]), reads=['S_wk'], writes=[vkey + 'b'])
                    S.op('vector', L('max_index', out=idxs[:, 8:16], in_max=vals[:, 8:16], in_values=S_wk[:, 0:n]), reads=['S_wk', vkey + 'b'], writes=[ikey + 'b'])

                ctiles = [('p', i) for i in range(32)] + [('s', 0)]
                ctiles = cfg.get('c_tiles', ctiles)
                mhalf = sb("mhalf", [128, 1], F32)
                S.op('gpsimd', L('memset', mhalf[:], -0.5), writes=['mhalf'])
                x2d = [sb("x2d%d" % i, [128, D], F32) for i in range(2)]
                h2Td = [sb("h2Td%d" % i, [128, 8, 128], BF16) for i in range(2)]
                S_alld = [S_all, sb("S_all1", [128, 16, 128], F32)]
                c_top_all = sb("c_top_all", [128, 8, 16], F32); c_pos_all = sb("c_pos_all", [128, 8, 16], U32)
                c_posf_all = sb("c_posf_all", [128, 128], F32)
                t3b = sb("t3b", [128, 128, 16], F32)
                av_all = sb("av_all", [128, 128], F32); bv_all = sb("bv_all", [128, 128], F32)
                zsum = sb("zsum", [128, 8], F32)
                psX = psS[1]
                S.banks = ('pT0', 'pT1', 'pacc0', 'pacc1', 'pacc2', 'pacc3', 'psS0', 'psS1')

                def t8(src, skey, dst, dkey):
                    for j in range(8):
                        S.op('tensor', L('transpose', out=pT[0][:, j, :], in_=src[:, j * 128:(j + 1) * 128], identity=identb[:]),
                             reads=list(skey) + ['identb'], writes=['pT0_%d' % j])
                    S.op('scalar', L('activation', out=dst[:], in_=pT[0][:], func=AF.Copy), reads=['pT0_%d' % j for j in range(8)], writes=[dkey])

                def front(ti):
                    kind, i = ctiles[ti]
                    sl = ti % 2
                    if kind == 'p':
                        rows = 128; qrow = i * 128; xsrc = xp[qrow:qrow + 128, :]
                    else:
                        rows = 64; qrow = SP; xsrc = xs.ap()
                        S.op('gpsimd', L('memset', xt[:], 0.0), writes=['xt'])
                    S.dma('sync', 'cl_x', L('dma_start', out=xt[0:rows, :], in_=xsrc), writes=['xt'])
                    S.dma('sync', 'cl_oa', L('dma_start', out=oa_t[:], in_=oa_s[qrow:qrow + 128, :]), writes=['oa_t'])
                    S.dma('sync', 'cl_ob', L('dma_start', out=ob_t[:], in_=ob_s[qrow:qrow + 128, :]), writes=['ob_t'])
                    S.dma('sync', 'cl_g', L('dma_start', out=gt[:], in_=gates_s[qrow:qrow + 128, :]), writes=['gt'])
                    t8(oa_t, ['oa_t'], oaT, 'oaT')
                    t8(ob_t, ['ob_t'], obT, 'obT')
                    for (srcT, skey, wsb, wk, dst, dkey, pb0) in [(oaT, 'oaT', wa_sb, 'wa', m1, 'm1', 0), (obT, 'obT', wb_sb, 'wb', m2, 'm2', 2)]:
                        for c in range(2):
                            proj(srcT, skey, wsb, wk, c, pb0 + c)
                            S.op('scalar', L('activation', out=dst[:, c * 512:(c + 1) * 512], in_=pacc[pb0 + c][:], func=AF.Copy),
                                 reads=['pacc%d' % (pb0 + c)], writes=['%s_%d' % (dkey, c)])
                    S.op('gpsimd', L('tensor_tensor', out=m1[:], in0=m1[:], in1=gt[:, 0:D], op=ALU.mult), reads=['m1_0', 'm1_1', 'gt'], writes=['m1_0', 'm1_1'])
                    S.op('gpsimd', L('tensor_tensor', out=m2[:], in0=m2[:], in1=gt[:, D:2 * D], op=ALU.mult), reads=['m2_0', 'm2_1', 'gt'], writes=['m2_0', 'm2_1'])
                    S.op('gpsimd', L('tensor_tensor', out=mb[:], in0=m1[:], in1=m2[:], op=ALU.add),
                         reads=['m1_0', 'm1_1', 'm2_0', 'm2_1'], writes=['mb'])
                    t8(mb, ['mb'], mT, 'mT')
                    x2 = x2d[sl]
                    for c in range(2):
                        proj(mT, 'mT', wo_sb, 'wo', c, c)
                        S.op('scalar', L('activation', out=x2[:, c * 512:(c + 1) * 512], in_=pacc[c][:], func=AF.Copy),
                             reads=['pacc%d' % c], writes=['x2d%d_%d' % (sl, c)])
                    x2k = ['x2d%d_0' % sl, 'x2d%d_1' % sl]
                    S.op('gpsimd', L('tensor_tensor', out=x2[:], in0=x2[:], in1=xt[:], op=ALU.add), reads=x2k + ['xt'], writes=x2k)
                    S.op('scalar', L('activation', out=junkb[:], in_=x2[:], func=AF.Square, accum_out=smc[:, 0:1]), reads=x2k, writes=['junkb', 'smc0'])
                    S.op('gpsimd', L('tensor_scalar', out=smc[:, 0:1], in0=smc[:, 0:1], scalar1=1.0 / D, scalar2=EPS, op0=ALU.mult, op1=ALU.add),
                         reads=['smc0'], writes=['smc0'])
                    S.op('gpsimd', L('tensor_tensor', out=smc[:, 0:1], in0=smc[:, 0:1], in1=mhalf[:], op=ALU.pow), reads=['smc0', 'mhalf'], writes=['smc0'])
                    S.op('scalar', L('activation', out=h2f[:], in_=x2[:], func=AF.Copy, scale=smc[:, 0:1]), reads=x2k + ['smc0'], writes=['h2f'])
                    S.op('gpsimd', L('tensor_tensor', out=h2b[:], in0=h2f[:], in1=gffn[:], op=ALU.mult), reads=['h2f', 'gffn'], writes=['h2b'])
                    t8(h2b, ['h2b'], h2Td[sl], 'h2Td%d' % sl)
                    for c in range(2):
                        proj(h2Td[sl], 'h2Td%d' % sl, wq_sb, 'wq', c, 2 + c)
                        S.op('scalar', L('activation', out=pq_b[:, c * 512:(c + 1) * 512], in_=pacc[2 + c][:], func=AF.Copy),
                             reads=['pacc%d' % (2 + c)], writes=['pq_b%d' % c])
                    t8(pq_b, ['pq_b0', 'pq_b1'], pqT, 'pqT')
                    for q4 in range(4):
                        c = q4 % 2; h0 = (q4 // 2) * 4
                        bank, bkey = (psS[0], 'psS0') if c == 0 else (pacc[3].rearrange("p (a b) -> p a b", a=4), 'pacc3')
                        for i4 in range(4):
                            h = h0 + i4
                            S.op('tensor', L('matmul', bank[:, i4, :], lhsT=pqT[c * 64:(c + 1) * 64, h, :], rhs=keysT[c * 64:(c + 1) * 64, h, :],
                                             start=True, stop=True), reads=['pqT', 'keysT'], writes=[bkey + ('_%d' % i4 if c == 0 else '')])
                        lo = 2 * h0 + c
                        S.op('scalar', L('activation', out=S_alld[sl][:, lo:min(lo + 8, 16):2, :], in_=bank[:, :, :], func=AF.Copy),
                             reads=([bkey + '_%d' % i4 for i4 in range(4)] if c == 0 else [bkey]),
                             writes=['S_all%d_hc%d' % (sl, lo + 2 * i4) for i4 in range(4)])

                def back(ti):
                    sl = ti % 2
                    Sa = S_alld[sl]
                    for hc in range(16):
                        top16(Sa[:, hc, :], 'S_all%d_hc%d' % (sl, hc), s_top[:, hc, :], 's_top%d' % hc, i_top[:, hc, :], 'i_top%d' % hc)
                    S.op('vector', L('tensor_copy', out=i_topf[:], in_=i_top[:]),
                         reads=['i_top%d%s' % (hc, ab) for hc in range(16) for ab in 'ab'], writes=['i_topf'])
                    pstr = 16 * 16
                    for h in range(8):
                        stk = ['s_top%d%s' % (hc, ab) for hc in (2 * h, 2 * h + 1) for ab in 'ab']
                        in0 = bass.AP(s_top, (2 * h) * 16, [[pstr, 128], [1, 16], [0, 16]])
                        in1 = bass.AP(s_top, (2 * h + 1) * 16, [[pstr, 128], [0, 16], [1, 16]])
                        S.op('vector', L('tensor_tensor', out=cand[:].rearrange("p (a b) -> p a b", a=16), in0=in0, in1=in1, op=ALU.add),
                             reads=stk, writes=['cand'])
                        top16(cand[:], 'cand', c_top_all[:, h, :], 'c_top%d' % h, c_pos_all[:, h, :], 'c_pos%d' % h)
                    ctk = ['c_top%d%s' % (h, ab) for h in range(8) for ab in 'ab']
                    cpk = ['c_pos%d%s' % (h, ab) for h in range(8) for ab in 'ab']
                    S.op('vector', L('tensor_copy', out=c_posf_all[:], in_=c_pos_all[:].rearrange("p h k -> p (h k)")), reads=cpk, writes=['c_posf_all'])
                    S.op('vector', L('tensor_tensor', out=t3b[:, :, 0:15], in0=bass.AP(c_posf_all, 0, [[128, 128], [1, 128], [0, 15]]),
                                     in1=bass.AP(thr, 1, [[16, 128], [0, 128], [1, 15]]), op=ALU.is_ge), reads=['c_posf_all', 'thr'], writes=['t3b'])
                    S.op('vector', L('reduce_sum', out=av_all[:], in_=t3b[:, :, 0:15], axis=mybir.AxisListType.X), reads=['t3b'], writes=['av_all'])
                    S.op('vector', L('scalar_tensor_tensor', out=bv_all[:], in0=av_all[:], scalar=-16.0, in1=c_posf_all[:], op0=ALU.mult, op1=ALU.add),
                         reads=['av_all', 'c_posf_all'], writes=['bv_all'])
                    for (sel, skey, c, dst, dkey) in [(av_all, 'av_all', 0, i1_all, 'i1_all'), (bv_all, 'bv_all', 1, i2_all, 'i2_all')]:
                        S.op('vector', L('tensor_tensor', out=t3b[:], in0=bass.AP(sel, 0, [[128, 128], [1, 128], [0, 16]]),
                                         in1=bass.AP(iota_t, 0, [[256, 128], [0, 128], [1, 16]]), op=ALU.is_equal), reads=[skey, 'iota'], writes=['t3b'])
                        S.op('vector', L('tensor_tensor', out=t3b[:].rearrange("p (h k) a -> p h k a", h=8), in0=t3b[:].rearrange("p (h k) a -> p h k a", h=8),
                                         in1=bass.AP(i_topf, c * 16, [[pstr, 128], [32, 8], [0, 16], [1, 16]]), op=ALU.mult),
                             reads=['t3b', 'i_topf'], writes=['t3b'])
                        S.op('vector', L('reduce_sum', out=dst[:], in_=t3b[:], axis=mybir.AxisListType.X), reads=['t3b'], writes=[dkey])
                    S.op('vector', L('tensor_tensor', out=g_all[:].rearrange("p (h k) -> p h k", h=8), in0=c_top_all[:],
                                     in1=bass.AP(c_top_all, 0, [[128, 128], [16, 8], [0, 16]]), op=ALU.subtract), reads=ctk, writes=['g_all'])
                    S.op('scalar', L('activation', out=g_all[:], in_=g_all[:], func=AF.Exp), reads=['g_all'], writes=['g_all'])
                    S.op('vector', L('reduce_sum', out=zsum[:], in_=g_all[:].rearrange("p (h k) -> p h k", h=8), axis=mybir.AxisListType.X),
                         reads=['g_all'], writes=['zsum'])
                    S.op('vector', L('reciprocal', out=zsum[:], in_=zsum[:]), reads=['zsum'], writes=['zsum'])
                    S.op('vector', L('tensor_tensor', out=g_all[:].rearrange("p (h k) -> p h k", h=8), in0=g_all[:].rearrange("p (h k) -> p h k", h=8),
                                     in1=bass.AP(zsum, 0, [[8, 128], [1, 8], [0, 16]]), op=ALU.mult), reads=['g_all', 'zsum'], writes=['g_all'])

                def export(ti):
                    kind, i = ctiles[ti]
                    sl = ti % 2
                    qrow = i * 128 if kind == 'p' else SP
                    for ri, (src, key_) in enumerate([(i1_all, 'i1_all'), (i2_all, 'i2_all'), (g_all, 'g_all')]):
                        S.op('tensor', L('transpose', out=psX[:, ri, :], in_=src[:], identity=identf[:]),
                             reads=[key_, 'identf'], writes=['psS1_%d' % ri])
                    S.op('scalar', L('activation', out=rT[:], in_=psX[:, 0:3, :], func=AF.Copy), reads=['psS1_0', 'psS1_1', 'psS1_2'], writes=['rT'])
                    S.dma('sync', 'cs_r', L('dma_start', out=r_s[:, :, qrow:qrow + 128], in_=rT[:]), reads=['rT'])
                    S.dma('sync', 'cs_x2', L('dma_start', out=x2_s[qrow:qrow + 128, :], in_=x2d[sl][:]), reads=['x2d%d_0' % sl, 'x2d%d_1' % sl])
                    S.dma('sync', 'cs_h2', L('dma_start', out=h2T_s[:, :, qrow:qrow + 128], in_=h2Td[sl][:]), reads=['h2Td%d' % sl])

                if ctiles:
                    front(0)
                for ti in range(len(ctiles)):
                    if ti + 1 < len(ctiles):
                        front(ti + 1)
                    back(ti)
                    export(ti)

                S.end_phase()
                ensure_sems()
                S.emit(nc, sems)

        if 'C' in phases and DENSE:
            n_ec = cfg.get('n_ec', 128)
            with ExitStack() as es:
                def sb(name, shape, dt):
                    return es.enter_context(nc.sbuf_tensor(name, list(shape), dt))

                def ps(name, shape, dt):
                    return es.enter_context(nc.psum_tensor(name, list(shape), dt))
                S.single = ()
                S.banks = ('ppT0', 'ppT1')
                identf = sb("identfP", [128, 128], F32); identb = sb("identbP", [128, 128], BF16)
                ub = [sb("ub%d" % i, [128, D], BF16) for i in range(3)]
                vb = [sb("vbD%d" % i, [128, D], BF16) for i in range(3)]
                uT = [sb("uT%d" % i, [128, 8, 128], BF16) for i in range(3)]
                ppT = [ps("ppT%d" % i, [128, 8, 128], BF16) for i in range(2)]
                S.dma('sync', 'p_identf', L('dma_start', out=identf[:], in_=c_ident.ap()), writes=['identf'])
                S.op('vector', L('tensor_copy', out=identb[:], in_=identf[:]), reads=['identf'], writes=['identb'])
                for ec in range(n_ec):
                    s2 = ec % 3
                    pp = ec % 2
                    S.dma('gpsimd', 'p_u%d' % s2, L('dma_start', out=ub[s2][:], in_=peer_u[ec * 128:(ec + 1) * 128, :]), writes=['ub%d' % s2])
                    S.dma('gpsimd', 'p_v%d' % s2, L('dma_start', out=vb[s2][:], in_=peer_v[ec * 128:(ec + 1) * 128, :]), writes=['vb%d' % s2])
                    for dc in range(8):
                        S.op('tensor', L('transpose', out=ppT[pp][:, dc, :], in_=ub[s2][:, dc * 128:(dc + 1) * 128], identity=identb[:]),
                             reads=['ub%d' % s2, 'identb'], writes=['ppT%d_%d' % (pp, dc)])
                    if ec % 2 == 0:
                        S.op('vector', L('tensor_copy', out=uT[s2][:], in_=ppT[pp][:]), reads=['ppT%d_%d' % (pp, dc) for dc in range(8)], writes=['uT%d' % s2])
                    else:
                        S.op('scalar', L('activation', out=uT[s2][:], in_=ppT[pp][:], func=AF.Copy), reads=['ppT%d_%d' % (pp, dc) for dc in range(8)], writes=['uT%d' % s2])
                    S.dma('sync', 'p_su%d' % s2, L('dma_start', out=UT_s[ec // 2][:, ec % 2, :].rearrange("p (a b) -> p a b", a=8), in_=uT[s2][:]),
                          reads=['uT%d' % s2], writes=['UT_s'])
                    S.dma('sync', 'p_sv%d' % s2, L('dma_start', out=V_s[ec // 2][:, ec % 2, :], in_=vb[s2][:]), reads=['vb%d' % s2], writes=['V_s'])
                S.end_phase()
                ensure_sems()
                S.emit(nc, sems)
            with ExitStack() as es:
                def sb(name, shape, dt):
                    return es.enter_context(nc.sbuf_tensor(name, list(shape), dt))

                def ps(name, shape, dt):
                    return es.enter_context(nc.psum_tensor(name, list(shape), dt))
                S.single = ()
                S.banks = ('pact0', 'pact1', 'pout0', 'pout1', 'pout2', 'pout3', 'pout4', 'pout5', 'pw0', 'pw1')
                NTT = TG // 128
                gfin = sb("gfinD", [128, D], F32)
                iota_b = sb("iota_b", [128, 128], F32)
                NBUF = 4
                utp = [sb("utp%d" % i, [128, 2, 8, 128], BF16) for i in range(NBUF)]
                vtp = [sb("vtp%d" % i, [128, 2, D], BF16) for i in range(NBUF)]
                h2g = [sb("h2g%d" % i, [128, 8, TG], BF16) for i in range(2)]
                rg = [sb("rg%d" % i, [128, 3, TG], F32) for i in range(2)]
                oh2 = [sb("oh2_%d" % i, [128, 128], BF16) for i in range(2)]
                oh1 = [sb("oh1_%d" % i, [128, 128], BF16) for i in range(2)]
                WT = [sb("WT%d" % i, [128, TG, 128], BF16) for i in range(2)]
                gl = [sb("gl%d" % i, [128, TG], BF16) for i in range(2)]
                gad = [sb("gad%d" % i, [128, TG], BF16) for i in range(2)]
                x2t = sb("x2t", [128, D], F32); accd = sb("accd", [128, D], F32); ytd = sb("ytd", [128, D], F32)
                junkd = sb("junkd", [128, D], BF16); smd = sb("smd", [128, 4], F32)
                npout = 2 * NTT
                pact = [ps("pact%d" % i, [128, 512], F32) for i in range(2 if npout <= 4 else 1)]
                pout = [ps("pout%d" % i, [128, 512], F32) for i in range(npout)]
                pw = [ps("pw%d" % i, [128, 4, 128], F32) for i in range(1)]

                S.dma('sync', 'd_gfin', L('dma_start', out=gfin[:], in_=bass.AP(norm_final, 0, [[0, 128], [1, D]])), writes=['gfin'])
                S.dma('sync', 'd_iota', L('dma_start', out=iota_b[:], in_=c_iota[:, 0:128]), writes=['iota_b'])
                dgroups = [(g * TG, TG, [('p', g * TG + j * 128) for j in range(NTT)]) for g in range(SP // TG)] + [(SP, 128, [('s', SP)])]
                dgroups = cfg.get('d_groups', dgroups)
                pairs_total = len(dgroups) * (n_ec // 2)
                issued = [0]

                def ensure_loaded(upto):
                    while issued[0] < min(upto, pairs_total):
                        gp = issued[0]; p = gp % (n_ec // 2); b = gp % NBUF
                        S.dma('sync', 'd_ut%d' % b, L('dma_start', out=utp[b][:].rearrange("p j a b -> p j (a b)"), in_=UT_s[p]), writes=['utp%d' % b])
                        S.dma('sync', 'd_vt%d' % b, L('dma_start', out=vtp[b][:], in_=V_s[p]), writes=['vtp%d' % b])
                        issued[0] += 1

                def load_group(gi):
                    q0, tg, _ = dgroups[gi]
                    w = gi % 2
                    S.dma('sync', 'd_h2_%d' % w, L('dma_start', out=h2g[w][:, :, 0:tg], in_=h2T_s[:, :, q0:q0 + tg]), writes=['h2g%d' % w])
                    S.dma('sync', 'd_r%d' % w, L('dma_start', out=rg[w][:, :, 0:tg], in_=r_s[:, :, q0:q0 + tg]), writes=['rg%d' % w])

                def wbuild(gi, t):
                    w = gi % 2
                    o = t % 2
                    S.op('vector', L('tensor_scalar', out=oh2[o][:], in0=iota_b[:], scalar1=rg[w][:, 1, t:t + 1], scalar2=None, op0=ALU.is_equal),
                         reads=['iota_b', 'rg%d' % w], writes=['oh2_%d' % o])
                    S.op('vector', L('tensor_scalar', out=oh1[o][:], in0=iota_b[:], scalar1=rg[w][:, 0, t:t + 1], scalar2=rg[w][:, 2, t:t + 1],
                                     op0=ALU.is_equal, op1=ALU.mult), reads=['iota_b', 'rg%d' % w], writes=['oh1_%d' % o])
                    S.op('tensor', L('matmul', pw[0][:, t % 4, :], lhsT=oh2[o][:], rhs=oh1[o][:], start=True, stop=True),
                         reads=['oh2_%d' % o, 'oh1_%d' % o], writes=['pw0_%d' % (t % 4)])
                    if t % 4 == 3:
                        S.op('scalar', L('activation', out=WT[w][:, t - 3:t + 1, :], in_=pw[0][:], func=AF.Copy),
                             reads=['pw0_%d' % j for j in range(4)], writes=['WT%d' % w])

                if dgroups:
                    ensure_loaded(3)
                    load_group(0)
                    for t in range(dgroups[0][1]):
                        wbuild(0, t)
                for gi, (q0, tg, ttiles) in enumerate(dgroups):
                    ntt = tg // 128
                    w = gi % 2
                    nxt = gi + 1 if gi + 1 < len(dgroups) else None
                    if nxt is not None:
                        load_group(nxt)
                        ntok_next = dgroups[nxt][1]
                        per = (ntok_next + n_ec - 1) // n_ec
                    wb_t = [0]

                    def act_mm(ec, gi=gi, w=w, tg=tg):
                        gp = gi * (n_ec // 2) + ec // 2
                        b = gp % NBUF; j = ec % 2
                        pa = ec % len(pact)
                        for dc in range(8):
                            S.op('tensor', L('matmul', pact[pa][:, 0:tg], lhsT=utp[b][:, j, dc, :], rhs=h2g[w][:, dc, 0:tg], start=(dc == 0), stop=(dc == 7)),
                                 reads=['utp%d' % b, 'h2g%d' % w], writes=['pact%d' % pa])
                        return (b, j)
                    pend_b = act_mm(0) if n_ec > 0 else None
                    for ec in range(n_ec):
                        b = pend_b
                        pa = ec % len(pact)
                        gb = ec % 2
                        S.op('scalar', L('activation', out=gl[gb][:, 0:tg], in_=pact[pa][:, 0:tg], func=AF.Gelu), reads=['pact%d' % pa], writes=['gl%d' % gb])
                        S.op('vector', L('tensor_tensor', out=gad[gb][:, 0:tg], in0=gl[gb][:, 0:tg], in1=WT[w][:, 0:tg, ec], op=ALU.mult),
                             reads=['gl%d' % gb, 'WT%d' % w], writes=['gad%d' % gb])
                        if ec + 1 < n_ec:
                            pend_b = act_mm(ec + 1)
                        for tt in range(ntt):
                            for dh in range(2):
                                S.op('tensor', L('matmul', pout[tt * 2 + dh][:], lhsT=gad[gb][:, tt * 128:(tt + 1) * 128], rhs=vtp[b[0]][:, b[1], dh * 512:(dh + 1) * 512],
                                                 start=(ec == 0), stop=(ec == n_ec - 1)),
                                     reads=['gad%d' % gb, 'vtp%d' % b[0]], writes=['pout%d' % (tt * 2 + dh)])
                        if ec % 2 == 1:
                            ensure_loaded(gi * (n_ec // 2) + ec // 2 + 4)
                        if nxt is not None:
                            for _ in range(per):
                                if wb_t[0] < ntok_next:
                                    wbuild(nxt, wb_t[0])
                                    wb_t[0] += 1
                    if nxt is not None:
                        while wb_t[0] < ntok_next:
                            wbuild(nxt, wb_t[0])
                            wb_t[0] += 1
                    for tt, (kind, qrow) in enumerate(ttiles):
                        rows = 128 if kind == 'p' else 64
                        ydst = yp[qrow:qrow + 128, :] if kind == 'p' else ys.ap()
                        S.dma('sync', 'd_x2', L('dma_start', out=x2t[:], in_=x2_s[qrow:qrow + 128, :]), writes=['x2t'])
                        for dh in range(2):
                            S.op('vector', L('tensor_tensor', out=accd[:, dh * 512:(dh + 1) * 512], in0=pout[tt * 2 + dh][:], in1=x2t[:, dh * 512:(dh + 1) * 512], op=ALU.add),
                                 reads=['pout%d' % (tt * 2 + dh), 'x2t'], writes=['accd%d' % dh])
                        S.op('scalar', L('activation', out=junkd[:], in_=accd[:], func=AF.Square, accum_out=smd[:, 0:1]),
                             reads=['accd0', 'accd1'], writes=['junkd', 'smd0'])
                        S.op('vector', L('tensor_scalar', out=smd[:, 0:1], in0=smd[:, 0:1], scalar1=1.0 / D, scalar2=EPS, op0=ALU.mult, op1=ALU.add),
                             reads=['smd0'], writes=['smd0'])
                        S.op('scalar', L('activation', out=smd[:, 0:1], in_=smd[:, 0:1], func=AF.Sqrt), reads=['smd0'], writes=['smd0'])
                        S.op('vector', L('reciprocal', out=smd[:, 0:1], in_=smd[:, 0:1]), reads=['smd0'], writes=['smd0'])
                        S.op('vector', L('tensor_scalar', out=ytd[:], in0=accd[:], scalar1=smd[:, 0:1], scalar2=None, op0=ALU.mult),
                             reads=['accd0', 'accd1', 'smd0'], writes=['ytd'])
                        S.op('gpsimd', L('tensor_tensor', out=ytd[:], in0=ytd[:], in1=gfin[:], op=ALU.mult), reads=['ytd', 'gfin'], writes=['ytd'])
                        S.dma('sync', 'd_y', L('dma_start', out=ydst, in_=ytd[0:rows, :]), reads=['ytd'])

                S.end_phase()
                ensure_sems()
                S.emit(nc, sems)

    return nc


_CACHE = {}


def _consts():
    if 'c' not in _CACHE:
        oh, maskw = _bias_consts()
        _CACHE['c'] = dict(
            c_ident=np.eye(128, dtype=np.float32),
            c_rope=_rope_table(),
            c_oh=oh, c_maskw=maskw,
            c_iota=np.tile(np.arange(256, dtype=np.float32)[None, :], (128, 1)),
        )
    return _CACHE['c']


def kernel(x_prompt, x_sample, cache_diff_k, cache_diff_v, cache_mla_ckv, cache_mla_kpe,
           rel_bias, norm_mix, w_in, diff_lambda, diff_subln, mla_q_norm, mla_w_uq, mla_kv_norm,
           mla_w_uk, mla_w_uv, w_branch_a, w_branch_b, w_out, norm_ffn, peer_w_q, peer_keys,
           peer_u, peer_v, norm_final):
    f = lambda a: np.ascontiguousarray(np.asarray(a, dtype=np.float32))
    shared = dict(
        rel_bias=f(rel_bias), norm_mix=f(norm_mix).reshape(D), w_in=f(w_in).reshape(D, INW),
        diff_lambda=f(diff_lambda).reshape(256), diff_subln=f(diff_subln).reshape(128),
        mla_q_norm=f(mla_q_norm).reshape(256), w_uq=f(mla_w_uq).reshape(256, 1536),
        mla_kv_norm=f(mla_kv_norm).reshape(128), w_uk=f(mla_w_uk).reshape(128, 1024), w_uv=f(mla_w_uv).reshape(128, 1024),
        w_ba=f(w_branch_a).reshape(D, D), w_bb=f(w_branch_b).reshape(D, D), w_o=f(w_out).reshape(D, D),
        norm_ffn=f(norm_ffn).reshape(D), peer_wq=f(peer_w_q).reshape(D, D),
        peer_keys=f(peer_keys).reshape(16, 128, 64), peer_u=f(peer_u).reshape(16384, D), peer_v=f(peer_v).reshape(16384, D),
        norm_final=f(norm_final).reshape(D),
    )
    shared.update(_consts())
    xpf = f(x_prompt); xsf = f(x_sample)
    cdkf = f(cache_diff_k).reshape(NCORES, PAST, D); cdvf = f(cache_diff_v).reshape(NCORES, PAST, D)
    cckvf = f(cache_mla_ckv).reshape(NCORES, PAST, 128); ckpef = f(cache_mla_kpe).reshape(NCORES, PAST, 32)
    in_maps = []
    for c in range(NCORES):
        m = dict(shared)
        m.update(xp=xpf[c], xs=xsf[c], cdk=cdkf[c], cdv=cdvf[c], cckv=cckvf[c], ckpe=ckpef[c])
        in_maps.append(m)
    nc = build_program()
    res = run_bass_kernel_spmd(nc, in_maps, core_ids=list(range(NCORES)))
    R = res.results

    def g(name, shape):
        return np.stack([np.asarray(R[c][name], dtype=np.float32) for c in range(NCORES)], 0).reshape(shape)
    return (g("yp", (8, SP, D)), g("ys", (8, SS, D)),
            g("kp", (1, 8, SP, 8, 2, 64)), g("vp", (1, 8, SP, 8, 128)), g("cp", (1, 8, SP, 128)), g("ep", (1, 8, SP, 32)),
            g("ks", (1, 8, SS, 8, 2, 64)), g("vs", (1, 8, SS, 8, 128)), g("cs", (1, 8, SS, 128)), g("es", (1, 8, SS, 32)))
```

```python
import math
from contextlib import ExitStack

import numpy as np
import ml_dtypes

import concourse.bass as bass
import concourse.mybir as mybir
from concourse.bass_utils import run_bass_kernel_spmd

F32 = mybir.dt.float32
BF16 = mybir.dt.bfloat16
U32 = mybir.dt.uint32
I32 = mybir.dt.int32
ALU = mybir.AluOpType
AF = mybir.ActivationFunctionType

NCORES = 8
D = 1024
SP = 4096
SS = 64
PAST = 2048
INW = 5536
NQ = SP + 128
NT = SP + PAST + 128
KS0 = SP
EPS = 1e-6
LAMBDA_INIT = 0.8 - 0.6 * math.exp(0.0)


def L(name, *a, **kw):
    return (name, a, kw)


class Sched:
    ENGS = ['tensor', 'vector', 'scalar', 'gpsimd', 'sync']

    def __init__(self):
        self.prog = {e: [] for e in self.ENGS}
        self.cnt = {e: 0 for e in self.ENGS}
        self.waited = {e: {} for e in self.ENGS}
        self.last_w = {}
        self.readers = {}
        self.semkeys = list(self.ENGS)
        self.phys_of = {}
        self.free = {}
        self.phys_q = {}
        self.nphys = 0
        self.nops = 0
        self.single = ()

    banks = ()

    def with_banks(self, reads, writes):
        extra = []
        for k in list(reads) + list(writes):
            for b in self.banks:
                if k.startswith(b):
                    bk = 'BANK:' + b
                    if bk not in extra:
                        extra.append(bk)
                    break
        return list(writes) + extra

    def canon(self, keys):
        out = []
        for k in keys:
            for p in self.single:
                if k.startswith(p + '1'):
                    k = p + '0' + k[len(p) + 1:]
                    break
            out.append(k)
        return out

    def _need(self, eng, tok, same_ok=False):
        if tok is None:
            return
        key, val = tok
        if same_ok and key == eng and eng == 'tensor':
            return
        if self.waited[eng].get(key, 0) >= val:
            return
        self.waited[eng][key] = val
        self.prog[eng].append(('wait', key, val))

    def _deps(self, eng, reads, writes):
        for r in reads:
            self._need(eng, self.last_w.get(r))
        for w in writes:
            self._need(eng, self.last_w.get(w), same_ok=True)
            for tok in self.readers.get(w, ()):
                self._need(eng, tok, same_ok=True)

    def _commit(self, tok, reads, writes):
        for r in reads:
            lst = self.readers.setdefault(r, [])
            lst.append(tok)
            if len(lst) > 48:
                best = {}
                for k, v in lst:
                    best[k] = max(best.get(k, 0), v)
                self.readers[r] = list(best.items())
        for w in writes:
            self.last_w[w] = tok
            self.readers[w] = []

    max_ops = 10 ** 9

    def op(self, eng, fn, reads=(), writes=()):
        if self.nops >= self.max_ops:
            return None
        reads = self.canon(reads); writes = self.with_banks(reads, self.canon(writes))
        self._deps(eng, reads, writes)
        self.cnt[eng] += 1
        tok = (eng, self.cnt[eng])
        self.prog[eng].append(('op', fn, eng, 1))
        self._commit(tok, reads, writes)
        self.nops += 1
        return tok

    def dma(self, q, semkey, fn, reads=(), writes=()):
        if self.nops >= self.max_ops:
            return None
        phys = self.phys_of.get(semkey)
        if phys is None:
            if self.free.get(q):
                phys = self.free[q].pop()
            else:
                phys = 'dma%s%d' % (q[0], self.nphys)
                self.nphys += 1
                self.cnt[phys] = 0
                self.semkeys.append(phys)
            self.phys_of[semkey] = phys
            self.phys_q[phys] = q
        reads = self.canon(reads); writes = self.with_banks(reads, self.canon(writes))
        self._deps(q, reads, writes)
        self.cnt[phys] += 16
        tok = (phys, self.cnt[phys])
        self.prog[q].append(('op', fn, phys, 16))
        self._commit(tok, reads, writes)
        self.nops += 1
        return tok

    def wait_dma(self, eng, semkey):
        phys = self.phys_of.get(semkey)
        if phys is not None:
            self._need(eng, (phys, self.cnt[phys]))

    def end_phase(self):
        for k in sorted(self.phys_of):
            phys = self.phys_of[k]
            self._need('sync', (phys, self.cnt[phys]))
            self.free.setdefault(self.phys_q[phys], []).append(phys)
        self.phys_of = {}

    def emit(self, nc, sems):
        prog = self.prog
        self.prog = {e: [] for e in self.ENGS}
        self.last_w = {}
        self.readers = {}

        def run(engname, e):
            for it in prog[engname]:
                if it[0] == 'wait':
                    e.wait_ge(sems[it[1]], it[2])
                else:
                    _, fn, key, inc = it
                    name, a, kw = fn
                    getattr(e, name)(*a, **kw).then_inc(sems[key], inc)

        with nc.Block() as block:
            @block.tensor
            def _(e):
                run('tensor', e)

            @block.vector
            def _(e):
                run('vector', e)

            @block.scalar
            def _(e):
                run('scalar', e)

            @block.gpsimd
            def _(e):
                run('gpsimd', e)

            @block.sync
            def _(e):
                run('sync', e)


def _rope_table():
    half = 16
    inv = (np.float32(10000.0) ** (-np.arange(half, dtype=np.float32) / np.float32(half))).astype(np.float32)
    pos = np.concatenate([np.arange(SP), PAST + np.arange(128)]).astype(np.float32)
    ang = (pos[:, None] * inv[None, :]).astype(np.float32)
    return np.concatenate([np.cos(ang), np.sin(ang)], axis=1).astype(np.float32)


def _t5_bucket(rel):
    nb = 16
    max_exact = 8
    ret = np.where(rel > 0, nb, 0)
    n = np.abs(rel)
    lg = (np.log(np.maximum(n, 1).astype(np.float32) / np.float32(max_exact)) / np.float32(math.log(128 / max_exact))
          * np.float32(nb - max_exact)).astype(np.float32)
    large = max_exact + lg.astype(np.int32)
    large = np.minimum(large, nb - 1)
    return ret + np.where(n < max_exact, n, large)


def _bias_consts():
    rel = np.arange(383) - 255
    bk = _t5_bucket(rel)
    oh = np.zeros((32, 384), np.float32)
    oh[bk, np.arange(383)] = 1.0
    kk = np.arange(128)[:, None]
    qq = np.arange(256)[None, :]
    maskw = ((kk // 64) <= (qq // 64)).astype(np.float32)
    return oh, maskw


def build_program(phases=('A', 'B', 'C'), cfg=None):
    cfg = cfg or {}
    nc = bass.Bass("TRN2", target_bir_lowering=False)
    S = Sched()
    S.max_ops = cfg.get('max_ops', 10 ** 9)

    def din(name, shape, dt=F32):
        return nc.dram_tensor(name, list(shape), dt, kind="ExternalInput")

    def dout(name, shape, dt=F32):
        return nc.dram_tensor(name, list(shape), dt, kind="ExternalOutput")

    def dscr(name, shape, dt=BF16):
        if cfg.get('dbg_scratch') and name in ('oa_s', 'ob_s', 'gates_s'):
            return nc.dram_tensor(name, list(shape), dt, kind="ExternalOutput")
        return nc.dram_tensor(name, list(shape), dt)

    xp = din("xp", [SP, D]); xs = din("xs", [SS, D])
    cdk = din("cdk", [PAST, D]); cdv = din("cdv", [PAST, D])
    cckv = din("cckv", [PAST, 128]); ckpe = din("ckpe", [PAST, 32])
    rel_bias = din("rel_bias", [32, 8])
    norm_mix = din("norm_mix", [D]); w_in = din("w_in", [D, INW])
    diff_lambda = din("diff_lambda", [4 * 64]); diff_subln = din("diff_subln", [128])
    mla_q_norm = din("mla_q_norm", [256]); w_uq = din("w_uq", [256, 1536])
    mla_kv_norm = din("mla_kv_norm", [128]); w_uk = din("w_uk", [128, 1024]); w_uv = din("w_uv", [128, 1024])
    w_ba = din("w_ba", [D, D]); w_bb = din("w_bb", [D, D]); w_o = din("w_o", [D, D])
    norm_ffn = din("norm_ffn", [D]); peer_wq = din("peer_wq", [D, D])
    peer_keys = din("peer_keys", [16, 128, 64])
    peer_u = din("peer_u", [16384, D]); peer_v = din("peer_v", [16384, D])
    norm_final = din("norm_final", [D])
    c_ident = din("c_ident", [128, 128]); c_rope = din("c_rope", [NQ, 32])
    c_oh = din("c_oh", [32, 384]); c_maskw = din("c_maskw", [128, 256])
    c_iota = din("c_iota", [128, 256])

    yp = dout("yp", [SP, D]); ys = dout("ys", [SS, D])
    kp = dout("kp", [SP, D]); vp = dout("vp", [SP, D]); cp = dout("cp", [SP, 128]); ep = dout("ep", [SP, 32])
    ks_ = dout("ks", [SS, D]); vs_ = dout("vs", [SS, D]); cs_ = dout("cs", [SS, 128]); es_ = dout("es", [SS, 32])

    dqT_s = dscr("dqT_s", [128, 8, NQ]); dkT_s = dscr("dkT_s", [128, 8, NT]); dv_s = dscr("dv_s", [NT, D])
    mqT_s = dscr("mqT_s", [96, 16, NQ]); mkT_s = dscr("mkT_s", [128, 8, NT]); mkpeT_s = dscr("mkpeT_s", [32, NT])
    mv_s = dscr("mv_s", [NT, D]); gates_s = dscr("gates_s", [NQ, 2 * D])
    oa_s = dscr("oa_s", [NQ, D]); ob_s = dscr("ob_s", [NQ, D])
    fbias_s = dscr("fbias_s", [8, 384], F32)
    DENSE = cfg.get('peer', 'dense') == 'dense'
    TG = cfg.get('tg', 256)
    UT_s = dscr("UT_s", [64, 128, 2, 1024]); V_s = dscr("V_s", [64, 128, 2, 1024])
    x2_s = dscr("x2_s", [NQ, D], F32); h2T_s = dscr("h2T_s", [128, 8, NQ])
    r_s = dscr("r_s", [128, 3, NQ], F32)

    with ExitStack() as gs:
        sems = {}

        def ensure_sems():
            for k in S.semkeys:
                if k not in sems:
                    sems[k] = gs.enter_context(nc.semaphore("s_" + k))

        if 'A' in phases:
            with ExitStack() as es:
                def sb(name, shape, dt):
                    return es.enter_context(nc.sbuf_tensor(name, list(shape), dt))

                def ps(name, shape, dt):
                    return es.enter_context(nc.psum_tensor(name, list(shape), dt))

                def sb1(name, shape, dt):
                    t = sb(name, shape, dt)
                    return [t, t]
                S.banks = ('pz0', 'pz1', 'pTa', 'pTb0', 'pTb1', 'pmm0', 'pmm1', 'pTs')
                S.single = ('zk_f', 'zv_f', 'gates_b', 'qh_f', 'rtmp', 'hb', 'cq_b', 'vm_b')

                win_sb = sb("win_sb", [128, 8, INW], BF16)
                wuq_sb = sb("wuq_sb", [128, 2, 1536], BF16)
                wuk_sb = sb("wuk_sb", [128, 1024], BF16)
                wuv_sb = sb("wuv_sb", [128, 1024], BF16)
                identf = sb("identf", [128, 128], F32)
                identb = sb("identb", [128, 128], BF16)
                gmix = sb("gmix", [128, D], F32)
                gq = sb("gq", [128, 256], F32)
                gkv = sb("gkv", [128, 128], F32)
                rope_sb = sb("rope_sb", [128, 33, 32], F32)
                xt = [sb("xt%d" % i, [128, D], F32) for i in range(2)]
                junk = sb("junk", [128, D], BF16)
                sgtmp = sb("sgtmp", [128, D], F32)
                st = [sb("st%d" % i, [128, 8], F32) for i in range(2)]
                hb = sb1("hb", [128, D], BF16)
                hT = [sb("hT%d" % i, [128, 8, 128], BF16) for i in range(2)]
                zq_b = [sb("zq_b%d" % i, [128, D], BF16) for i in range(2)]
                zk_f = sb1("zk_f", [128, D], F32)
                zk_b = [sb("zk_b%d" % i, [128, D], BF16) for i in range(2)]
                zv_f = sb1("zv_f", [128, D], F32)
                zv_b = [sb("zv_b%d" % i, [128, D], BF16) for i in range(2)]
                zm = [sb("zm%d" % i, [128, 416], F32) for i in range(2)]
                gates_b = sb1("gates_b", [128, 2 * D], BF16)
                qT_sb = [sb("qT_sb%d" % i, [128, 8, 128], BF16) for i in range(2)]
                kT_sb = [sb("kT_sb%d" % i, [128, 8, 128], BF16) for i in range(2)]
                cq_b = sb1("cq_b", [128, 256], BF16)
                cqT = [sb("cqT%d" % i, [128, 2, 128], BF16) for i in range(2)]
                qh_f = sb1("qh_f", [128, 16, 96], F32)
                qh_b = [sb("qh_b%d" % i, [128, 16, 96], BF16) for i in range(2)]
                rtmp = sb1("rtmp", [128, 4, 16, 16], F32)
                mqT_sb = [sb("mqT_sb%d" % i, [128, 16, 128], BF16) for i in range(2)]
                ckv_f = [sb("ckv_f%d" % i, [128, 128], F32) for i in range(2)]
                ckv_b = [sb("ckv_b%d" % i, [128, 128], BF16) for i in range(2)]
                ckvT = [sb("ckvT%d" % i, [128, 128], BF16) for i in range(2)]
                kn_b = [sb("kn_b%d" % i, [128, D], BF16) for i in range(2)]
                vm_b = sb1("vm_b", [128, D], BF16)
                knT_sb = [sb("knT_sb%d" % i, [128, 8, 128], BF16) for i in range(2)]
                kpe_f = [sb("kpe_f%d" % i, [128, 32], F32) for i in range(2)]
                kpe_b = [sb("kpe_b%d" % i, [128, 32], BF16) for i in range(2)]
                kpeT_sb = [sb("kpeT_sb%d" % i, [32, 128], BF16) for i in range(2)]
                pTa = ps("pTa", [128, 8, 128], BF16)
                pz = [ps("pz%d" % i, [128, 512], F32) for i in range(2)]
                pTb = [ps("pTb%d" % i, [128, 8, 128], BF16) for i in range(2)]
                pmm = [ps("pmm%d" % i, [128, 512], F32) for i in range(2)]
                pTs = ps("pTs", [128, 4, 128], BF16)

                for kc in range(8):
                    for (n0, nw) in [(0, 2048), (2048, 2048), (4096, INW - 4096)]:
                        S.dma('gpsimd', 'w_in', L('dma_start', out=win_sb[:, kc, n0:n0 + nw], in_=w_in[kc * 128:(kc + 1) * 128, n0:n0 + nw]),
                            writes=['win'])
                for kc in range(2):
                    S.dma('gpsimd', 'w_uq', L('dma_start', out=wuq_sb[:, kc, :], in_=w_uq[kc * 128:(kc + 1) * 128, :]), writes=['wuq'])
                S.dma('gpsimd', 'w_uk', L('dma_start', out=wuk_sb[:], in_=w_uk.ap()), writes=['wuk'])
                S.dma('gpsimd', 'w_uv', L('dma_start', out=wuv_sb[:], in_=w_uv.ap()), writes=['wuv'])
                S.dma('sync', 'c_identf', L('dma_start', out=identf[:], in_=c_ident.ap()), writes=['identf'])
                S.dma('sync', 'c_gmix', L('dma_start', out=gmix[:], in_=bass.AP(norm_mix, 0, [[0, 128], [1, D]])), writes=['gmix'])
                S.dma('sync', 'c_gq', L('dma_start', out=gq[:], in_=bass.AP(mla_q_norm, 0, [[0, 128], [1, 256]])), writes=['gq'])
                S.dma('sync', 'c_gkv', L('dma_start', out=gkv[:], in_=bass.AP(mla_kv_norm, 0, [[0, 128], [1, 128]])), writes=['gkv'])
                S.dma('sync', 'c_rope', L('dma_start', out=rope_sb[:], in_=c_rope.ap().rearrange("(t p) c -> p t c", p=128)), writes=['rope'])
                S.op('vector', L('tensor_copy', out=identb[:], in_=identf[:]), reads=['identf'], writes=['identb'])

                evac_rr = [0]

                def evac(out, in_, reads, writes, func=AF.Copy, eng=None):
                    if eng is None:
                        eng = 'scalar' if (evac_rr[0] % 2 == 0) else 'vector'
                        evac_rr[0] += 1
                    if eng == 'scalar':
                        S.op('scalar', L('activation', out=out, in_=in_, func=func), reads=reads, writes=writes)
                    else:
                        S.op('vector', L('tensor_copy', out=out, in_=in_), reads=reads, writes=writes)

                def scale_gain(out, in0, scalar, in1, tmp, reads, writes):
                    S.op('vector', L('tensor_scalar', out=tmp, in0=in0, scalar1=scalar, scalar2=None, op0=ALU.mult),
                         reads=reads, writes=['sgtmp'])
                    S.op('vector', L('tensor_tensor', out=out, in0=tmp, in1=in1, op=ALU.mult),
                         reads=['sgtmp'] + list(reads), writes=writes)

                def rms_stats(src_ap, ncols, stt, col, sl):
                    key = 'st%d_%d' % (sl, col)
                    S.op('scalar', L('activation', out=junk[:, 0:ncols], in_=src_ap, func=AF.Square,
                                                          accum_out=stt[:, col:col + 1]),
                         reads=rms_stats.reads, writes=['junk', key])
                    S.op('vector', L('tensor_scalar', out=stt[:, col:col + 1], in0=stt[:, col:col + 1],
                                                             scalar1=1.0 / ncols, scalar2=EPS, op0=ALU.mult, op1=ALU.add),
                         reads=[key], writes=[key])
                    S.op('scalar', L('activation', out=stt[:, col:col + 1], in_=stt[:, col:col + 1], func=AF.Sqrt),
                         reads=[key], writes=[key])
                    S.op('vector', L('reciprocal', out=stt[:, col:col + 1], in_=stt[:, col:col + 1]),
                         reads=[key], writes=[key])
                    return key

                def transposes(src_fn, n, pst, pkey, reads, rows=128):
                    for j in range(n):
                        S.op('tensor', L('transpose', out=pst[0:rows, j, :], in_=src_fn(j), identity=identb[:]),
                             reads=list(reads) + ['identb'], writes=['%s_%d' % (pkey, j)])

                def k_side(sl, kcol, zkb_keys, zvb_keys, ckvb_key, kpeb_key):
                    transposes(lambda j: zk_b[sl][:, j * 128:(j + 1) * 128], 8, pTb[1], 'pTb1', zkb_keys)
                    evac(kT_sb[sl][:], pTb[1][:], ['pTb1_%d' % j for j in range(8)], ['kT_sb%d' % sl])
                    S.dma('gpsimd', 'st_kT%d' % sl, L('dma_start', out=dkT_s[:, :, kcol:kcol + 128], in_=kT_sb[sl][:]),
                          reads=['kT_sb%d' % sl])
                    S.dma('gpsimd', 'st_v%d' % sl, L('dma_start', out=dv_s[kcol:kcol + 128, :], in_=zv_b[sl][:]),
                          reads=zvb_keys)
                    yield
                    S.op('tensor', L('transpose', out=pTs[:, 2, :], in_=ckv_b[sl][:], identity=identb[:]),
                         reads=[ckvb_key, 'identb'], writes=['pTs_2'])
                    evac(ckvT[sl][:], pTs[:, 2, :], ['pTs_2'], ['ckvT%d' % sl])
                    yield
                    for (wsb, wkey, dst, dkey) in [(wuk_sb, 'wuk', kn_b, 'kn_b'), (wuv_sb, 'wuv', vm_b, 'vm_b')]:
                        for c in range(2):
                            S.op('tensor', L('matmul', pmm[c][:], lhsT=ckvT[sl][:], rhs=wsb[:, c * 512:(c + 1) * 512],
                                                                    start=True, stop=True),
                                 reads=['ckvT%d' % sl, wkey], writes=['pmm%d' % c])
                            evac(dst[sl][:, c * 512:(c + 1) * 512], pmm[c][:], ['pmm%d' % c], ['%s%d_%d' % (dkey, sl, c)])
                    S.dma('gpsimd', 'st_mv%d' % sl, L('dma_start', out=mv_s[kcol:kcol + 128, :], in_=vm_b[sl][:]),
                          reads=['vm_b%d_0' % sl, 'vm_b%d_1' % sl])
                    yield
                    transposes(lambda j: kn_b[sl][:, j * 128:(j + 1) * 128], 8, pTb[0], 'pTb0',
                               ['kn_b%d_0' % sl, 'kn_b%d_1' % sl])
                    evac(knT_sb[sl][:], pTb[0][:], ['pTb0_%d' % j for j in range(8)], ['knT_sb%d' % sl])
                    S.dma('gpsimd', 'st_mk%d' % sl, L('dma_start', out=mkT_s[:, :, kcol:kcol + 128], in_=knT_sb[sl][:]),
                          reads=['knT_sb%d' % sl])
                    S.op('tensor', L('transpose', out=pTs[0:32, 3, :], in_=kpe_b[sl][:], identity=identb[:]),
                         reads=[kpeb_key, 'identb'], writes=['pTs_3'])
                    evac(kpeT_sb[sl][:], pTs[0:32, 3, :], ['pTs_3'], ['kpeT_sb%d' % sl])
                    S.dma('gpsimd', 'st_kpe%d' % sl, L('dma_start', out=mkpeT_s[:, kcol:kcol + 128], in_=kpeT_sb[sl][:]),
                          reads=['kpeT_sb%d' % sl])

                tiles = [('p', i) for i in range(32)] + [('s', 0)] + [('c', i) for i in range(16)]
                tiles = cfg.get('a_tiles', tiles)
                pending = [None]

                def step_pending():
                    if pending[0] is not None:
                        try:
                            next(pending[0])
                        except StopIteration:
                            pending[0] = None

                def drain():
                    while pending[0] is not None:
                        step_pending()
                for tix, (kind, i) in enumerate(tiles):
                    sl = tix % 2
                    if kind == 'c':
                        kcol = KS0 + i * 128
                        r0 = i * 128
                        S.dma('gpsimd', 'ld_ck%d' % sl, L('dma_start', out=zk_b[sl][:], in_=cdk[r0:r0 + 128, :]),
                              writes=['zk_b%d_2' % sl, 'zk_b%d_3' % sl])
                        S.dma('gpsimd', 'ld_cv%d' % sl, L('dma_start', out=zv_b[sl][:], in_=cdv[r0:r0 + 128, :]),
                              writes=['zv_b%d_4' % sl, 'zv_b%d_5' % sl])
                        S.dma('gpsimd', 'ld_cc%d' % sl, L('dma_start', out=ckv_b[sl][:], in_=cckv[r0:r0 + 128, :]),
                              writes=['ckv_b%d' % sl])
                        S.dma('gpsimd', 'ld_ce%d' % sl, L('dma_start', out=kpe_b[sl][:], in_=ckpe[r0:r0 + 128, :]),
                              writes=['kpe_b%d' % sl])
                        drain()
                        for _ in k_side(sl, kcol, ['zk_b%d_2' % sl, 'zk_b%d_3' % sl], ['zv_b%d_4' % sl, 'zv_b%d_5' % sl], 'ckv_b%d' % sl, 'kpe_b%d' % sl):
                            pass
                        continue
                    if kind == 'p':
                        rows = 128; qcol = i * 128; kcol = i * 128; rt = i
                        xsrc = xp[i * 128:(i + 1) * 128, :]
                        o_k, o_v, o_c, o_e = kp[qcol:qcol + 128, :], vp[qcol:qcol + 128, :], cp[qcol:qcol + 128, :], ep[qcol:qcol + 128, :]
                    else:
                        rows = 64; qcol = SP; kcol = KS0 + PAST; rt = 32
                        xsrc = xs.ap()
                        o_k, o_v, o_c, o_e = ks_.ap(), vs_.ap(), cs_.ap(), es_.ap()
                        S.op('vector', L('memset', xt[sl][:], 0.0), writes=['xt%d' % sl])
                    S.dma('sync', 'ld_x%d' % sl, L('dma_start', out=xt[sl][0:rows, :], in_=xsrc),
                          writes=['xt%d' % sl])
                    rms_stats.reads = ['xt%d' % sl]
                    k0 = rms_stats(xt[sl][:], D, st[sl], 0, sl)
                    scale_gain(hb[sl][:], xt[sl][:], st[sl][:, 0:1], gmix[:], sgtmp[:, 0:1024], ['xt%d' % sl, k0, 'gmix'], ['hb%d' % sl])
                    transposes(lambda j: hb[sl][:, j * 128:(j + 1) * 128], 8, pTa, 'pTa', ['hb%d' % sl])
                    evac(hT[sl][:], pTa[:], ['pTa_%d' % j for j in range(8)], ['hT%d' % sl])

                    chunks = [(0, 512), (512, 512), (1024, 512), (1536, 512), (2048, 512), (2560, 512), (3072, 416),
                              (3488, 512), (4000, 512), (4512, 512), (5024, 512)]
                    for ci, (n0, nw) in enumerate(chunks):
                        pb = ci % 2
                        for kc in range(8):
                            S.op('tensor', L('matmul', pz[pb][:, 0:nw], lhsT=hT[sl][:, kc, :], rhs=win_sb[:, kc, n0:n0 + nw],
                                start=(kc == 0), stop=(kc == 7)),
                                reads=['hT%d' % sl, 'win'], writes=['pz%d' % pb])
                        if ci < 2:
                            evac(zq_b[sl][:, n0:n0 + 512], pz[pb][:], ['pz%d' % pb], ['zq_b%d_%d' % (sl, ci)])
                        elif ci < 4:
                            c0 = n0 - 1024
                            evac(zk_f[sl][:, c0:c0 + 512], pz[pb][:], ['pz%d' % pb], ['zk_f%d_%d' % (sl, ci)], eng='scalar')
                            S.op('vector', L('tensor_copy', out=zk_b[sl][:, c0:c0 + 512], in_=pz[pb][:]),
                                 reads=['pz%d' % pb], writes=['zk_b%d_%d' % (sl, ci)])
                        elif ci < 6:
                            c0 = n0 - 2048
                            evac(zv_f[sl][:, c0:c0 + 512], pz[pb][:], ['pz%d' % pb], ['zv_f%d_%d' % (sl, ci)], eng='scalar')
                            S.op('vector', L('tensor_copy', out=zv_b[sl][:, c0:c0 + 512], in_=pz[pb][:]),
                                 reads=['pz%d' % pb], writes=['zv_b%d_%d' % (sl, ci)])
                        elif ci == 6:
                            evac(zm[sl][:], pz[pb][:, 0:416], ['pz%d' % pb], ['zm%d' % sl], eng='vector')
                        else:
                            c0 = n0 - 3488
                            S.op('scalar', L('activation', out=gates_b[sl][:, c0:c0 + 512], in_=pz[pb][:],
                                                                                 func=AF.Sigmoid),
                                 reads=['pz%d' % pb], writes=['gates_b%d_%d' % (sl, ci)])
                        step_pending()
                    drain()
                    S.dma('gpsimd', 'st_ok%d' % sl, L('dma_start', out=o_k, in_=zk_f[sl][0:rows, :]),
                          reads=['zk_f%d_2' % sl, 'zk_f%d_3' % sl])
                    S.dma('gpsimd', 'st_ov%d' % sl, L('dma_start', out=o_v, in_=zv_f[sl][0:rows, :]),
                          reads=['zv_f%d_4' % sl, 'zv_f%d_5' % sl])
                    S.dma('gpsimd', 'st_g%d' % sl, L('dma_start', out=gates_s[qcol:qcol + 128, :], in_=gates_b[sl][:]),
                          reads=['gates_b%d_%d' % (sl, c) for c in range(7, 11)])
                    def epi(sl=sl, qcol=qcol, kcol=kcol, rows=rows, rt=rt, o_c=o_c, o_e=o_e):
                        tmpk = 'rtmp%d' % sl
                        rms_stats.reads = ['zm%d' % sl]
                        k1 = rms_stats(zm[sl][:, 0:256], 256, st[sl], 1, sl)
                        scale_gain(cq_b[sl][:], zm[sl][:, 0:256], st[sl][:, 1:2], gq[:], sgtmp[:, 0:256], ['zm%d' % sl, k1, 'gq'], ['cq_b%d' % sl])
                        k2 = rms_stats(zm[sl][:, 256:384], 128, st[sl], 2, sl)
                        scale_gain(ckv_f[sl][:], zm[sl][:, 256:384], st[sl][:, 2:3], gkv[:], sgtmp[:, 0:128], ['zm%d' % sl, k2, 'gkv'], ['ckv_f%d' % sl])
                        S.op('vector', L('tensor_copy', out=ckv_b[sl][:], in_=ckv_f[sl][:]), reads=['ckv_f%d' % sl], writes=['ckv_b%d' % sl])
                        S.dma('gpsimd', 'st_oc%d' % sl, L('dma_start', out=o_c, in_=ckv_f[sl][0:rows, :]),
                              reads=['ckv_f%d' % sl])
                        cos1 = bass.AP(rope_sb, rt * 32, [[33 * 32, 128], [1, 16]])
                        sin1 = bass.AP(rope_sb, rt * 32 + 16, [[33 * 32, 128], [1, 16]])
                        y1 = zm[sl][:, 384:400]
                        y2 = zm[sl][:, 400:416]
                        for ti_, (a, b_) in enumerate([(y1, cos1), (y2, sin1), (y1, sin1), (y2, cos1)]):
                            S.op('vector', L('tensor_tensor', out=rtmp[sl][:, ti_, 0, :], in0=a, in1=b_, op=ALU.mult),
                                 reads=['zm%d' % sl, 'rope'], writes=[tmpk + '_%d' % ti_])
                        S.op('vector', L('tensor_tensor', out=kpe_f[sl][:, 0:16], in0=rtmp[sl][:, 0, 0, :], in1=rtmp[sl][:, 1, 0, :], op=ALU.subtract),
                             reads=[tmpk + '_0', tmpk + '_1'], writes=['kpe_f%d_a' % sl])
                        S.op('vector', L('tensor_tensor', out=kpe_f[sl][:, 16:32], in0=rtmp[sl][:, 2, 0, :], in1=rtmp[sl][:, 3, 0, :], op=ALU.add),
                             reads=[tmpk + '_2', tmpk + '_3'], writes=['kpe_f%d_b' % sl])
                        S.op('vector', L('tensor_copy', out=kpe_b[sl][:], in_=kpe_f[sl][:]),
                             reads=['kpe_f%d_a' % sl, 'kpe_f%d_b' % sl], writes=['kpe_b%d' % sl])
                        S.dma('gpsimd', 'st_oe%d' % sl, L('dma_start', out=o_e, in_=kpe_f[sl][0:rows, :]),
                              reads=['kpe_f%d_a' % sl, 'kpe_f%d_b' % sl])
                        yield
                        transposes(lambda j: zq_b[sl][:, j * 128:(j + 1) * 128], 8, pTb[0], 'pTb0', ['zq_b%d_0' % sl, 'zq_b%d_1' % sl])
                        evac(qT_sb[sl][:], pTb[0][:], ['pTb0_%d' % j for j in range(8)], ['qT_sb%d' % sl])
                        S.dma('gpsimd', 'st_qT%d' % sl, L('dma_start', out=dqT_s[:, :, qcol:qcol + 128], in_=qT_sb[sl][:]),
                              reads=['qT_sb%d' % sl])
                        yield
                        for j in range(2):
                            S.op('tensor', L('transpose', out=pTs[:, j, :], in_=cq_b[sl][:, j * 128:(j + 1) * 128], identity=identb[:]),
                                 reads=['cq_b%d' % sl, 'identb'], writes=['pTs_%d' % j])
                        evac(cqT[sl][:], pTs[:, 0:2, :], ['pTs_0', 'pTs_1'], ['cqT%d' % sl])
                        yield
                        qh_flat = qh_f[sl][:].rearrange("p h e -> p (h e)")
                        for c in range(3):
                            pb = c % 2
                            for kc in range(2):
                                S.op('tensor', L('matmul', pmm[pb][:], lhsT=cqT[sl][:, kc, :], rhs=wuq_sb[:, kc, c * 512:(c + 1) * 512],
                                    start=(kc == 0), stop=(kc == 1)),
                                    reads=['cqT%d' % sl, 'wuq'], writes=['pmm%d' % pb])
                            evac(qh_flat[:, c * 512:(c + 1) * 512], pmm[pb][:], ['pmm%d' % pb], ['qh_f%d_%d' % (sl, c)])
                        yield
                        qhr = ['qh_f%d_%d' % (sl, c) for c in range(3)]
                        cosb = bass.AP(rope_sb, rt * 32, [[33 * 32, 128], [0, 16], [1, 16]])
                        sinb = bass.AP(rope_sb, rt * 32 + 16, [[33 * 32, 128], [0, 16], [1, 16]])
                        x1 = qh_f[sl][:, :, 64:80]
                        x2 = qh_f[sl][:, :, 80:96]
                        tmpk = 'rtmp%d' % sl
                        for ti_, (a, b_) in enumerate([(x1, cosb), (x2, sinb), (x1, sinb), (x2, cosb)]):
                            S.op('vector', L('tensor_tensor', out=rtmp[sl][:, ti_, :, :], in0=a, in1=b_, op=ALU.mult),
                                 reads=qhr + ['rope'], writes=[tmpk + '_%d' % ti_])
                        S.op('vector', L('tensor_copy', out=qh_b[sl][:, :, 0:64], in_=qh_f[sl][:, :, 0:64]),
                             reads=qhr, writes=['qh_b%d_n' % sl])
                        S.op('vector', L('tensor_tensor', out=qh_b[sl][:, :, 64:80], in0=rtmp[sl][:, 0, :, :], in1=rtmp[sl][:, 1, :, :],
                                                                 op=ALU.subtract),
                             reads=[tmpk + '_0', tmpk + '_1'], writes=['qh_b%d_a' % sl])
                        S.op('vector', L('tensor_tensor', out=qh_b[sl][:, :, 80:96], in0=rtmp[sl][:, 2, :, :], in1=rtmp[sl][:, 3, :, :],
                                                                 op=ALU.add),
                             reads=[tmpk + '_2', tmpk + '_3'], writes=['qh_b%d_b' % sl])
                        yield
                        ks = k_side(sl, kcol, ['zk_b%d_2' % sl, 'zk_b%d_3' % sl], ['zv_b%d_4' % sl, 'zv_b%d_5' % sl], 'ckv_b%d' % sl, 'kpe_b%d' % sl)
                        next(ks, None)
                        yield
                        qhb_keys = ['qh_b%d_n' % sl, 'qh_b%d_a' % sl, 'qh_b%d_b' % sl]
                        for half in range(2):
                            transposes(lambda j, half=half: qh_b[sl][:, half * 8 + j, :], 8, pTb[half], 'pTb%d' % half, qhb_keys, rows=96)
                            evac(mqT_sb[sl][0:96, half * 8:(half + 1) * 8, :], pTb[half][0:96, :, :],
                                 ['pTb%d_%d' % (half, j) for j in range(8)], ['mqT_sb%d_%d' % (sl, half)])
                        S.dma('gpsimd', 'st_mq%d' % sl, L('dma_start', out=mqT_s[:, :, qcol:qcol + 128], in_=mqT_sb[sl][0:96, :, :]),
                              reads=['mqT_sb%d_0' % sl, 'mqT_sb%d_1' % sl])
                        yield
                        for _ in ks:
                            yield
                    pending[0] = epi()

                drain()
                S.end_phase()
                ensure_sems()
                S.emit(nc, sems)

        if 'B' in phases:
            with ExitStack() as es:
                def sb(name, shape, dt):
                    return es.enter_context(nc.sbuf_tensor(name, list(shape), dt))

                def ps(name, shape, dt):
                    return es.enter_context(nc.psum_tensor(name, list(shape), dt))
                S.single = ()
                S.banks = ('sc0', 'sc1', 'oacc0', 'oacc1', 'oacc2', 'oacc3', 'pset')
                NKT = NT // 128
                QT = [sb("QT%d" % i, [128, NQ], BF16) for i in range(2)]
                KT = [sb("KT%d" % i, [128, NT], BF16) for i in range(2)]
                VA = [sb("VA%d" % i, [128, NKT, 129], BF16) for i in range(2)]
                VM = [sb("VM%d" % i, [128, NKT, 65], BF16) for i in range(2)]
                et = [sb("et%d" % i, [128, 512], BF16) for i in range(3)]
                rb_sb = sb("rb_sb", [32, 8], F32)
                oh_sb = sb("oh_sb", [32, 384], F32)
                rb15 = sb("rb15", [8, 1], F32)
                fexp = sb("fexp", [8, 384], F32)
                hank = sb("hank", [128, 8, 256], F32)
                maskw = sb("maskw", [128, 256], F32)
                ebm = sb("ebm", [128, 8, 256], BF16)
                dl = sb("dl", [128, 256], F32)
                dlj = sb("dlj", [128, 64], F32)
                lam = sb("lam", [128, 4], F32)
                gsub = sb("gsub", [128, 128], F32)
                o0 = [sb("o0_%d" % i, [128, 129], F32) for i in range(4)]
                o1 = [sb("o1_%d" % i, [128, 129], F32) for i in range(4)]
                mhalfB = sb("mhalfB", [128, 1], F32)
                sm = [sb("smB%d" % i, [128, 8], F32) for i in range(4)]
                of = [sb("of%d" % i, [128, 128], F32) for i in range(4)]
                of2 = [sb("of2_%d" % i, [128, 128], F32) for i in range(4)]
                jk = sb("jkB", [128, 128], BF16)
                oa_t = [sb("oa_t%d" % i, [128, 128], BF16) for i in range(4)]
                ob_t = [sb("ob_t%d" % i, [128, 64], BF16) for i in range(4)]
                sc = [ps("sc%d" % i, [128, 512], F32) for i in range(2)]
                oacc = [ps("oacc%d" % i, [128, 512], F32) for i in range(4)]
                pset = ps("pset", [128, 512], F32)

                S.dma('sync', 'b_rb', L('dma_start', out=rb_sb[:], in_=rel_bias.ap()), writes=['rb_sb'])
                S.dma('sync', 'b_oh', L('dma_start', out=oh_sb[:], in_=c_oh.ap()), writes=['oh_sb'])
                S.dma('sync', 'b_rb15', L('dma_start', out=rb15[:], in_=rel_bias[15:16, :].rearrange("a h -> h a")), writes=['rb15'])
                S.dma('sync', 'b_mw', L('dma_start', out=maskw[:], in_=c_maskw.ap()), writes=['maskw'])
                S.dma('sync', 'b_dl', L('dma_start', out=dl[:], in_=bass.AP(diff_lambda, 0, [[0, 128], [1, 256]])), writes=['dl'])
                S.dma('sync', 'b_gs', L('dma_start', out=gsub[:], in_=bass.AP(diff_subln, 0, [[0, 128], [1, 128]])), writes=['gsub'])
                S.op('tensor', L('matmul', pset[0:8, 0:384], lhsT=rb_sb[:], rhs=oh_sb[:], start=True, stop=True),
                     reads=['rb_sb', 'oh_sb'], writes=['pset'])
                S.op('vector', L('tensor_scalar', out=fexp[:], in0=pset[0:8, 0:384], scalar1=rb15[:, 0:1], scalar2=None, op0=ALU.subtract),
                     reads=['pset', 'rb15'], writes=['fexp'])
                S.op('scalar', L('activation', out=fexp[:], in_=fexp[:], func=AF.Exp), reads=['fexp'], writes=['fexp'])
                S.dma('sync', 'b_fs', L('dma_start', out=fbias_s.ap(), in_=fexp[:]), reads=['fexp'], writes=['fbias_s'])
                for h in range(8):
                    S.dma('sync', 'b_hk', L('dma_start', out=hank[:, h, :], in_=bass.AP(fbias_s, h * 384, [[1, 128], [1, 256]])),
                          reads=['fbias_s'], writes=['hank'])
                for h in range(8):
                    S.op('vector', L('tensor_tensor', out=ebm[:, h, :], in0=hank[:, h, ::-1], in1=maskw[:], op=ALU.mult),
                         reads=['hank', 'maskw'], writes=['ebm'])
                for i_, (a, b_) in enumerate([(0, 1), (2, 3)]):
                    S.op('vector', L('tensor_tensor', out=dlj[:], in0=dl[:, a * 64:(a + 1) * 64], in1=dl[:, b_ * 64:(b_ + 1) * 64], op=ALU.mult),
                         reads=['dl'], writes=['dlj'])
                    S.op('vector', L('reduce_sum', out=lam[:, i_:i_ + 1], in_=dlj[:], axis=mybir.AxisListType.X),
                         reads=['dlj'], writes=['lam%d' % i_])
                    S.op('scalar', L('activation', out=lam[:, i_:i_ + 1], in_=lam[:, i_:i_ + 1], func=AF.Exp),
                         reads=['lam%d' % i_], writes=['lam%d' % i_])
                S.op('vector', L('tensor_tensor', out=lam[:, 2:3], in0=lam[:, 1:2], in1=lam[:, 0:1], op=ALU.subtract),
                     reads=['lam0', 'lam1'], writes=['lam2'])
                S.op('vector', L('tensor_scalar', out=lam[:, 2:3], in0=lam[:, 2:3], scalar1=-LAMBDA_INIT, scalar2=None, op0=ALU.add),
                     reads=['lam2'], writes=['lam2'])
                S.op('vector', L('tensor_scalar', out=gsub[:], in0=gsub[:], scalar1=1.0 - LAMBDA_INIT, scalar2=None, op0=ALU.mult),
                     reads=['gsub'], writes=['gsub'])
                S.op('gpsimd', L('memset', mhalfB[:], -0.5), writes=['mhalfB'])
                for i in range(2):
                    S.op('gpsimd', L('memset', VA[i][:, :, 128:129], 1.0), writes=['VA%d_one' % i])
                    S.op('gpsimd', L('memset', VA[i][64:128, NKT - 1, 128:129], 0.0), writes=['VA%d_one' % i])
                    S.op('gpsimd', L('memset', VM[i][:, :, 64:65], 1.0), writes=['VM%d_one' % i])
                    S.op('gpsimd', L('memset', VM[i][64:128, NKT - 1, 64:65], 0.0), writes=['VM%d_one' % i])

                groups = [dict(qcol0=g * 512, nq=4, kts=list(range(4 * g + 4)), qabs0=4 * g) for g in range(8)]
                groups.append(dict(qcol0=SP, nq=1, kts=list(range(32, 49)), qabs0=48))
                groups = cfg.get('b_groups', groups)
                st_ctr = [0]
                sc_ctr = [0]
                scb = [sc[0], sc[1], pset]
                sckey = ['sc0', 'sc1', 'pset']

                def attention(kind, h, slot, r0, r1, E, scale, vt, vkey, grp, finalize):
                    nq = grp['nq']; kts = grp['kts']; qabs0 = grp['qabs0']; qcol0 = grp['qcol0']

                    def qlo_of(kt):
                        return max(kt - qabs0, 0) if nq > 1 else 0

                    def score(idx):
                        kt = kts[idx]; qlo = qlo_of(kt); N = (nq - qlo) * 128
                        s = sc_ctr[0] % 3
                        sc_ctr[0] += 1
                        S.op('tensor', L('matmul', scb[s][:, 0:N], lhsT=KT[slot][r0:r1, kt * 128:(kt + 1) * 128],
                                         rhs=QT[slot][r0:r1, qcol0 + qlo * 128:qcol0 + nq * 128], start=True, stop=True),
                             reads=['KT%d' % slot, 'KT%d_pe' % slot, 'QT%d' % slot], writes=[sckey[s]])
                        return s
                    pend = [score(i_) for i_ in range(min(2, len(kts)))]
                    for idx, kt in enumerate(kts):
                        s = pend.pop(0)
                        st_ctr[0] += 1
                        if idx + 2 < len(kts):
                            pend.append(score(idx + 2))
                        qlo = qlo_of(kt); N = (nq - qlo) * 128
                        t = st_ctr[0] % 3
                        S.op('scalar', L('activation', out=et[t][:, 0:N], in_=scb[s][:, 0:N], func=AF.Exp, scale=scale),
                             reads=[sckey[s]], writes=['et%d' % t])
                        d = qabs0 + qlo - kt
                        if kind == 'd' and d in (0, 1):
                            W = min(256 - d * 128, N)
                            S.op('vector', L('tensor_tensor', out=et[t][:, 0:W], in0=et[t][:, 0:W], in1=ebm[:, h, d * 128:d * 128 + W], op=ALU.mult),
                                 reads=['et%d' % t, 'ebm'], writes=['et%d' % t])
                        if kind == 'm' and d == 0:
                            S.op('gpsimd', L('memset', et[t][64:128, 0:64], 0.0), reads=['et%d' % t], writes=['et%d' % t])
                        for j in range(qlo, nq):
                            last = (qabs0 + j) if nq > 1 else kts[-1]
                            S.op('tensor', L('matmul', oacc[j][:, 0:E + 1], lhsT=et[t][:, (j - qlo) * 128:(j - qlo + 1) * 128],
                                             rhs=vt[:, kt, 0:E + 1], start=(idx == 0), stop=(kt == last)),
                                 reads=['et%d' % t, vkey, vkey + '_one'], writes=['oacc%d' % j])
                    finalize([(j, qcol0 + j * 128) for j in range(nq)])

                def rstd_small(smt, col, key, n):
                    S.op('vector', L('tensor_scalar', out=smt[:, col:col + 1], in0=smt[:, col:col + 1], scalar1=1.0 / n, scalar2=EPS,
                                     op0=ALU.mult, op1=ALU.add), reads=[key], writes=[key])
                    S.op('scalar', L('activation', out=smt[:, col:col + 1], in_=smt[:, col:col + 1], func=AF.Sqrt), reads=[key], writes=[key])
                    S.op('vector', L('reciprocal', out=smt[:, col:col + 1], in_=smt[:, col:col + 1]), reads=[key], writes=[key])

                dheads = cfg.get('b_dheads', list(range(8)))
                for hi, h in enumerate(dheads):
                    slot = hi % 2
                    S.dma('sync', 'b_q%d' % slot, L('dma_start', out=QT[slot][:], in_=dqT_s[:, h, :]), writes=['QT%d' % slot])
                    S.dma('sync', 'b_k%d' % slot, L('dma_start', out=KT[slot][:], in_=dkT_s[:, h, :]), writes=['KT%d' % slot, 'KT%d_pe' % slot])
                    for t0 in range(0, NKT, 13):
                        t1 = min(t0 + 13, NKT)
                        S.dma('sync', 'b_v%d' % slot, L('dma_start', out=VA[slot][:, t0:t1, 0:128],
                                                         in_=dv_s.ap().rearrange("(t p) f -> p t f", p=128)[:, t0:t1, h * 128:(h + 1) * 128]),
                              writes=['VA%d' % slot])
                    for grp in groups:
                        def fin0(items):
                            for (j, qrow) in items:
                                S.op('vector', L('tensor_copy', out=o0[j][:], in_=oacc[j][:, 0:129]), reads=['oacc%d' % j], writes=['o0_%d' % j])

                        def fin1(items, h=h):
                            for (j, qrow) in items:
                                S.op('vector', L('tensor_copy', out=o1[j][:], in_=oacc[j][:, 0:129]), reads=['oacc%d' % j], writes=['o1_%d' % j])
                            for (j, qrow) in items:
                                k_ = 'smB%d' % j
                                S.op('vector', L('reciprocal', out=sm[j][:, 0:1], in_=o0[j][:, 128:129]), reads=['o0_%d' % j], writes=[k_ + 'a'])
                                S.op('vector', L('reciprocal', out=sm[j][:, 1:2], in_=o1[j][:, 128:129]), reads=['o1_%d' % j], writes=[k_ + 'b'])
                                S.op('vector', L('tensor_tensor', out=sm[j][:, 1:2], in0=sm[j][:, 1:2], in1=lam[:, 2:3], op=ALU.mult),
                                     reads=[k_ + 'b', 'lam2'], writes=[k_ + 'b'])
                                S.op('vector', L('tensor_scalar', out=of[j][:], in0=o0[j][:, 0:128], scalar1=sm[j][:, 0:1], scalar2=None, op0=ALU.mult),
                                     reads=['o0_%d' % j, k_ + 'a'], writes=['of%d' % j])
                                S.op('vector', L('tensor_scalar', out=of2[j][:], in0=o1[j][:, 0:128], scalar1=sm[j][:, 1:2], scalar2=None, op0=ALU.mult),
                                     reads=['o1_%d' % j, k_ + 'b'], writes=['of2_%d' % j])
                                S.op('vector', L('tensor_tensor', out=of[j][:], in0=of[j][:], in1=of2[j][:], op=ALU.add),
                                     reads=['of%d' % j, 'of2_%d' % j], writes=['of%d' % j])
                                S.op('vector', L('tensor_tensor', out=of2[j][:], in0=of[j][:], in1=of[j][:], op=ALU.mult),
                                     reads=['of%d' % j], writes=['of2_%d' % j])
                                S.op('vector', L('reduce_sum', out=sm[j][:, 2:3], in_=of2[j][:], axis=mybir.AxisListType.X),
                                     reads=['of2_%d' % j], writes=[k_ + 'c'])
                                S.op('gpsimd', L('tensor_scalar', out=sm[j][:, 2:3], in0=sm[j][:, 2:3], scalar1=1.0 / 128, scalar2=EPS, op0=ALU.mult, op1=ALU.add),
                                     reads=[k_ + 'c'], writes=[k_ + 'c'])
                                S.op('gpsimd', L('tensor_tensor', out=sm[j][:, 2:3], in0=sm[j][:, 2:3], in1=mhalfB[:], op=ALU.pow),
                                     reads=[k_ + 'c', 'mhalfB'], writes=[k_ + 'c'])
                                S.op('vector', L('tensor_scalar', out=of[j][:], in0=of[j][:], scalar1=sm[j][:, 2:3], scalar2=None, op0=ALU.mult),
                                     reads=['of%d' % j, k_ + 'c'], writes=['of%d' % j])
                                S.op('vector', L('tensor_tensor', out=oa_t[j][:], in0=of[j][:], in1=gsub[:], op=ALU.mult),
                                     reads=['of%d' % j, 'gsub'], writes=['oa_t%d' % j])
                                S.dma('gpsimd', 'b_so%d' % j, L('dma_start', out=oa_s[qrow:qrow + 128, h * 128:(h + 1) * 128], in_=oa_t[j][:]),
                                      reads=['oa_t%d' % j])
                        attention('d', h, slot, 0, 64, 128, 0.125, VA[slot], 'VA%d' % slot, grp, fin0)
                        attention('d', h, slot, 64, 128, 128, 0.125, VA[slot], 'VA%d' % slot, grp, fin1)

                mheads = cfg.get('b_mheads', list(range(16)))
                for hi, h in enumerate(mheads):
                    slot = hi % 2
                    S.dma('sync', 'b_q%d' % slot, L('dma_start', out=QT[slot][0:96, :], in_=mqT_s[:, h, :]), writes=['QT%d' % slot])
                    S.dma('sync', 'b_k%d' % slot, L('dma_start', out=KT[slot][0:64, :], in_=mkT_s[(h % 2) * 64:(h % 2) * 64 + 64, h // 2, :]),
                          writes=['KT%d' % slot])
                    if hi < 2:
                        S.dma('sync', 'b_kpe%d' % slot, L('dma_start', out=KT[slot][64:96, :], in_=mkpeT_s.ap()), writes=['KT%d_pe' % slot])
                    for t0 in range(0, NKT, 13):
                        t1 = min(t0 + 13, NKT)
                        S.dma('sync', 'b_vm%d' % slot, L('dma_start', out=VM[slot][:, t0:t1, 0:64],
                                                          in_=mv_s.ap().rearrange("(t p) f -> p t f", p=128)[:, t0:t1, h * 64:(h + 1) * 64]),
                              writes=['VM%d' % slot])
                    for grp in groups:
                        def finm(items, h=h):
                            for (j, qrow) in items:
                                S.op('vector', L('tensor_copy', out=o1[j][:, 0:65], in_=oacc[j][:, 0:65]), reads=['oacc%d' % j], writes=['o1_%d' % j])
                            for (j, qrow) in items:
                                k_ = 'smB%d' % j
                                S.op('vector', L('reciprocal', out=sm[j][:, 0:1], in_=o1[j][:, 64:65]), reads=['o1_%d' % j], writes=[k_ + 'a'])
                                S.op('vector', L('tensor_scalar', out=ob_t[j][:], in0=o1[j][:, 0:64], scalar1=sm[j][:, 0:1], scalar2=None, op0=ALU.mult),
                                     reads=['o1_%d' % j, k_ + 'a'], writes=['ob_t%d' % j])
                                S.dma('gpsimd', 'b_sb%d' % j, L('dma_start', out=ob_s[qrow:qrow + 128, h * 64:(h + 1) * 64], in_=ob_t[j][:]),
                                      reads=['ob_t%d' % j])
                        attention('m', h, slot, 0, 96, 64, 96.0 ** -0.5, VM[slot], 'VM%d' % slot, grp, finm)

                S.end_phase()
                ensure_sems()
                S.emit(nc, sems)

        if 'C' in phases:
            with ExitStack() as es:
                def sb(name, shape, dt):
                    return es.enter_context(nc.sbuf_tensor(name, list(shape), dt))

                def ps(name, shape, dt):
                    return es.enter_context(nc.psum_tensor(name, list(shape), dt))
                S.single = ()
                S.banks = ('pT0', 'pT1', 'pacc0', 'pacc1', 'pacc2', 'pacc3', 'psS0', 'psS1')
                NB = 4
                wa_sb = sb("wa_sb", [128, 8, D], BF16); wb_sb = sb("wb_sb", [128, 8, D], BF16)
                wo_sb = sb("wo_sb", [128, 8, D], BF16); wq_sb = sb("wq_sb", [128, 8, D], BF16)
                identf = sb("identfC", [128, 128], F32); identb = sb("identbC", [128, 128], BF16)
                gffn = sb("gffn", [128, D], F32); gfin = sb("gfin", [128, D], F32)
                keys_f = sb("keys_f", [128, 16, 64], F32); keys_b = sb("keys_b", [128, 16, 64], BF16)
                keysT = sb("keysT", [128, 8, 128], BF16)
                iota_t = sb("iota_t", [128, 256], F32)
                thr = sb("thr", [128, 16], F32)
                oa_t = sb("oa_tC", [128, D], BF16); ob_t = sb("ob_tC", [128, D], BF16)
                gt = sb("gtC", [128, 2 * D], BF16); xt = sb("xtC", [128, D], F32)
                oaT = sb("oaT", [128, 8, 128], BF16); obT = sb("obT", [128, 8, 128], BF16)
                m1 = sb("m1", [128, D], F32); m2 = sb("m2", [128, D], F32); mb = sb("mb", [128, D], BF16)
                mT = sb("mT", [128, 8, 128], BF16)
                h2f = sb("h2f", [128, D], F32); h2b = sb("h2b", [128, D], BF16)
                pq_b = sb("pq_b", [128, D], BF16); pqT = sb("pqT", [128, 8, 128], BF16)
                S_all = sb("S_all", [128, 16, 128], F32); S_wk = sb("S_wk", [128, 256], F32)
                s_top = sb("s_top", [128, 16, 16], F32); i_top = sb("i_top", [128, 16, 16], U32); i_topf = sb("i_topf", [128, 16, 16], F32)
                cand = sb("cand", [128, 256], F32)
                c_top = sb("c_top", [128, 16], F32); c_pos = sb("c_pos", [128, 16], U32); c_posf = sb("c_posf", [128, 16], F32)
                t3 = sb("t3", [128, 16, 16], F32)
                av = sb("av", [128, 16], F32); bv = sb("bv", [128, 16], F32)
                i1v = sb("i1v", [128, 16], F32); i2v = sb("i2v", [128, 16], F32)
                eidf = sb("eidf", [128, 128], F32); eidi = sb("eidi", [128, 128], I32)
                g_all = sb("g_all", [128, 128], F32)
                smc = sb("smc", [128, 16], F32)
                act = sb("act", [128, 128], F32); ga = sb("ga", [128, 128], F32)
                i1_all = sb("i1_all", [128, 128], F32); i2_all = sb("i2_all", [128, 128], F32)
                rT = sb("rT", [128, 3, 128], F32)
                junkb = sb("junkbC", [128, D], BF16)
                pT = [ps("pT%d" % i, [128, 8, 128], BF16) for i in range(2)]
                pacc = [ps("pacc%d" % i, [128, 512], F32) for i in range(4)]
                psS = [ps("psS%d" % i, [128, 4, 128], F32) for i in range(2)]

                for (wsb, wsrc, wk) in [(wa_sb, w_ba, 'wa'), (wb_sb, w_bb, 'wb'), (wo_sb, w_o, 'wo'), (wq_sb, peer_wq, 'wq')]:
                    for kc in range(8):
                        S.dma('gpsimd', 'c_' + wk, L('dma_start', out=wsb[:, kc, :], in_=wsrc[kc * 128:(kc + 1) * 128, :]), writes=[wk])
                S.dma('sync', 'c_identf', L('dma_start', out=identf[:], in_=c_ident.ap()), writes=['identf'])
                S.dma('sync', 'c_gffn', L('dma_start', out=gffn[:], in_=bass.AP(norm_ffn, 0, [[0, 128], [1, D]])), writes=['gffn'])
                S.dma('sync', 'c_gfin', L('dma_start', out=gfin[:], in_=bass.AP(norm_final, 0, [[0, 128], [1, D]])), writes=['gfin'])
                S.dma('sync', 'c_keys', L('dma_start', out=keys_f[:], in_=peer_keys.ap().rearrange("a n d -> n a d")), writes=['keys_f'])
                S.dma('sync', 'c_iota', L('dma_start', out=iota_t[:], in_=c_iota.ap()), writes=['iota'])
                S.op('vector', L('tensor_copy', out=identb[:], in_=identf[:]), reads=['identf'], writes=['identb'])
                S.op('vector', L('tensor_copy', out=keys_b[:], in_=keys_f[:]), reads=['keys_f'], writes=['keys_b'])
                S.op('vector', L('tensor_scalar', out=thr[:], in0=iota_t[:, 0:16], scalar1=16.0, scalar2=None, op0=ALU.mult),
                     reads=['iota'], writes=['thr'])
                for h in range(8):
                    S.op('tensor', L('transpose', out=pT[0][:, h, :], in_=keys_b[:, 2 * h:2 * h + 2, :].rearrange("p a d -> p (a d)"), identity=identb[:]),
                         reads=['keys_b', 'identb'], writes=['pT0_%d' % h])
                S.op('vector', L('tensor_copy', out=keysT[:], in_=pT[0][:]), reads=['pT0_%d' % h for h in range(8)], writes=['keysT'])

                def transposes8(src, skey, pti, dst, dkey, eng):
                    for j in range(8):
                        S.op('tensor', L('transpose', out=pT[pti][:, j, :], in_=src[:, j * 128:(j + 1) * 128], identity=identb[:]),
                             reads=list(skey) + ['identb'], writes=['pT%d_%d' % (pti, j)])
                    rk = ['pT%d_%d' % (pti, j) for j in range(8)]
                    if eng == 'scalar':
                        S.op('scalar', L('activation', out=dst[:], in_=pT[pti][:], func=AF.Copy), reads=rk, writes=[dkey])
                    else:
                        S.op('vector', L('tensor_copy', out=dst[:], in_=pT[pti][:]), reads=rk, writes=[dkey])

                def proj(srcT, skey, wsb, wk, c, pb):
                    for kc in range(8):
                        S.op('tensor', L('matmul', pacc[pb][:], lhsT=srcT[:, kc, :], rhs=wsb[:, kc, c * 512:(c + 1) * 512],
                                         start=(kc == 0), stop=(kc == 7)), reads=[skey, wk], writes=['pacc%d' % pb])

                def rstd_c(col, key, n):
                    S.op('vector', L('tensor_scalar', out=smc[:, col:col + 1], in0=smc[:, col:col + 1], scalar1=1.0 / n, scalar2=EPS,
                                     op0=ALU.mult, op1=ALU.add), reads=[key], writes=[key])
                    S.op('scalar', L('activation', out=smc[:, col:col + 1], in_=smc[:, col:col + 1], func=AF.Sqrt), reads=[key], writes=[key])
                    S.op('vector', L('reciprocal', out=smc[:, col:col + 1], in_=smc[:, col:col + 1]), reads=[key], writes=[key])

                def top16(src_ap, skey, vals, vkey, idxs, ikey):
                    S.op('vector', L('max', out=vals[:, 0:8], in_=src_ap), reads=[skey], writes=[vkey + 'a'])
                    S.op('vector', L('max_index', out=idxs[:, 0:8], in_max=vals[:, 0:8], in_values=src_ap), reads=[skey, vkey + 'a'], writes=[ikey + 'a'])
                    n = src_ap.shape[-1]
                    S.op('vector', L('match_replace', out=S_wk[:, 0:n], in_to_replace=vals[:, 0:8], in_values=src_ap, imm_value=-1e30),
                         reads=[skey, vkey + 'a'], writes=['S_wk'])
                    S.op('vector', L('max', out=vals[:, 8:16], in_=S_wk[:, 0:n]), reads=['S_wk'], writes=[vkey + 'b'])
                    S.op('vector', L('max_index', out=idxs[:, 8:16], in_max=vals[:, 8:16], in_values=S_wk[:, 0:n]), reads=['S_wk', vkey + 'b'], writes=[ikey + 'b'])

                ctiles = [('p', i) for i in range(32)] + [('s', 0)]
                ctiles = cfg.get('c_tiles', ctiles)
                mhalf = sb("mhalf", [128, 1], F32)
                S.op('gpsimd', L('memset', mhalf[:], -0.5), writes=['mhalf'])
                x2d = [sb("x2d%d" % i, [128, D], F32) for i in range(2)]
                h2Td = [sb("h2Td%d" % i, [128, 8, 128], BF16) for i in range(2)]
                S_alld = [S_all, sb("S_all1", [128, 16, 128], F32)]
                c_top_all = sb("c_top_all", [128, 8, 16], F32); c_pos_all = sb("c_pos_all", [128, 8, 16], U32)
                c_posf_all = sb("c_posf_all", [128, 128], F32)
                t3b = sb("t3b", [128, 128, 16], F32)
                av_all = sb("av_all", [128, 128], F32); bv_all = sb("bv_all", [128, 128], F32)
                zsum = sb("zsum", [128, 8], F32)
                psX = psS[1]
                S.banks = ('pT0', 'pT1', 'pacc0', 'pacc1', 'pacc2', 'pacc3', 'psS0', 'psS1')

                def t8(src, skey, dst, dkey):
                    for j in range(8):
                        S.op('tensor', L('transpose', out=pT[0][:, j, :], in_=src[:, j * 128:(j + 1) * 128], identity=identb[:]),
                             reads=list(skey) + ['identb'], writes=['pT0_%d' % j])
                    S.op('scalar', L('activation', out=dst[:], in_=pT[0][:], func=AF.Copy), reads=['pT0_%d' % j for j in range(8)], writes=[dkey])

                def front(ti):
                    kind, i = ctiles[ti]
                    sl = ti % 2
                    if kind == 'p':
                        rows = 128; qrow = i * 128; xsrc = xp[qrow:qrow + 128, :]
                    else:
                        rows = 64; qrow = SP; xsrc = xs.ap()
                        S.op('gpsimd', L('memset', xt[:], 0.0), writes=['xt'])
                    S.dma('sync', 'cl_x', L('dma_start', out=xt[0:rows, :], in_=xsrc), writes=['xt'])
                    S.dma('sync', 'cl_oa', L('dma_start', out=oa_t[:], in_=oa_s[qrow:qrow + 128, :]), writes=['oa_t'])
                    S.dma('sync', 'cl_ob', L('dma_start', out=ob_t[:], in_=ob_s[qrow:qrow + 128, :]), writes=['ob_t'])
                    S.dma('sync', 'cl_g', L('dma_start', out=gt[:], in_=gates_s[qrow:qrow + 128, :]), writes=['gt'])
                    t8(oa_t, ['oa_t'], oaT, 'oaT')
                    t8(ob_t, ['ob_t'], obT, 'obT')
                    for (srcT, skey, wsb, wk, dst, dkey, pb0) in [(oaT, 'oaT', wa_sb, 'wa', m1, 'm1', 0), (obT, 'obT', wb_sb, 'wb', m2, 'm2', 2)]:
                        for c in range(2):
                            proj(srcT, skey, wsb, wk, c, pb0 + c)
                            S.op('scalar', L('activation', out=dst[:, c * 512:(c + 1) * 512], in_=pacc[pb0 + c][:], func=AF.Copy),
                                 reads=['pacc%d' % (pb0 + c)], writes=['%s_%d' % (dkey, c)])
                    S.op('gpsimd', L('tensor_tensor', out=m1[:], in0=m1[:], in1=gt[:, 0:D], op=ALU.mult), reads=['m1_0', 'm1_1', 'gt'], writes=['m1_0', 'm1_1'])
                    S.op('gpsimd', L('tensor_tensor', out=m2[:], in0=m2[:], in1=gt[:, D:2 * D], op=ALU.mult), reads=['m2_0', 'm2_1', 'gt'], writes=['m2_0', 'm2_1'])
                    S.op('gpsimd', L('tensor_tensor', out=mb[:], in0=m1[:], in1=m2[:], op=ALU.add),
                         reads=['m1_0', 'm1_1', 'm2_0', 'm2_1'], writes=['mb'])
                    t8(mb, ['mb'], mT, 'mT')
                    x2 = x2d[sl]
                    for c in range(2):
                        proj(mT, 'mT', wo_sb, 'wo', c, c)
                        S.op('scalar', L('activation', out=x2[:, c * 512:(c + 1) * 512], in_=pacc[c][:], func=AF.Copy),
                             reads=['pacc%d' % c], writes=['x2d%d_%d' % (sl, c)])
                    x2k = ['x2d%d_0' % sl, 'x2d%d_1' % sl]
                    S.op('gpsimd', L('tensor_tensor', out=x2[:], in0=x2[:], in1=xt[:], op=ALU.add), reads=x2k + ['xt'], writes=x2k)
                    S.op('scalar', L('activation', out=junkb[:], in_=x2[:], func=AF.Square, accum_out=smc[:, 0:1]), reads=x2k, writes=['junkb', 'smc0'])
                    S.op('gpsimd', L('tensor_scalar', out=smc[:, 0:1], in0=smc[:, 0:1], scalar1=1.0 / D, scalar2=EPS, op0=ALU.mult, op1=ALU.add),
                         reads=['smc0'], writes=['smc0'])
                    S.op('gpsimd', L('tensor_tensor', out=smc[:, 0:1], in0=smc[:, 0:1], in1=mhalf[:], op=ALU.pow), reads=['smc0', 'mhalf'], writes=['smc0'])
                    S.op('scalar', L('activation', out=h2f[:], in_=x2[:], func=AF.Copy, scale=smc[:, 0:1]), reads=x2k + ['smc0'], writes=['h2f'])
                    S.op('gpsimd', L('tensor_tensor', out=h2b[:], in0=h2f[:], in1=gffn[:], op=ALU.mult), reads=['h2f', 'gffn'], writes=['h2b'])
                    t8(h2b, ['h2b'], h2Td[sl], 'h2Td%d' % sl)
                    for c in range(2):
                        proj(h2Td[sl], 'h2Td%d' % sl, wq_sb, 'wq', c, 2 + c)
                        S.op('scalar', L('activation', out=pq_b[:, c * 512:(c + 1) * 512], in_=pacc[2 + c][:], func=AF.Copy),
                             reads=['pacc%d' % (2 + c)], writes=['pq_b%d' % c])
                    t8(pq_b, ['pq_b0', 'pq_b1'], pqT, 'pqT')
                    for q4 in range(4):
                        c = q4 % 2; h0 = (q4 // 2) * 4
                        bank, bkey = (psS[0], 'psS0') if c == 0 else (pacc[3].rearrange("p (a b) -> p a b", a=4), 'pacc3')
                        for i4 in range(4):
                            h = h0 + i4
                            S.op('tensor', L('matmul', bank[:, i4, :], lhsT=pqT[c * 64:(c + 1) * 64, h, :], rhs=keysT[c * 64:(c + 1) * 64, h, :],
                                             start=True, stop=True), reads=['pqT', 'keysT'], writes=[bkey + ('_%d' % i4 if c == 0 else '')])
                        lo = 2 * h0 + c
                        S.op('scalar', L('activation', out=S_alld[sl][:, lo:min(lo + 8, 16):2, :], in_=bank[:, :, :], func=AF.Copy),
                             reads=([bkey + '_%d' % i4 for i4 in range(4)] if c == 0 else [bkey]),
                             writes=['S_all%d_hc%d' % (sl, lo + 2 * i4) for i4 in range(4)])

                def back(ti):
                    sl = ti % 2
                    Sa = S_alld[sl]
                    for hc in range(16):
                        top16(Sa[:, hc, :], 'S_all%d_hc%d' % (sl, hc), s_top[:, hc, :], 's_top%d' % hc, i_top[:, hc, :], 'i_top%d' % hc)
                    S.op('vector', L('tensor_copy', out=i_topf[:], in_=i_top[:]),
                         reads=['i_top%d%s' % (hc, ab) for hc in range(16) for ab in 'ab'], writes=['i_topf'])
                    pstr = 16 * 16
                    for h in range(8):
                        stk = ['s_top%d%s' % (hc, ab) for hc in (2 * h, 2 * h + 1) for ab in 'ab']
                        in0 = bass.AP(s_top, (2 * h) * 16, [[pstr, 128], [1, 16], [0, 16]])
                        in1 = bass.AP(s_top, (2 * h + 1) * 16, [[pstr, 128], [0, 16], [1, 16]])
                        S.op('vector', L('tensor_tensor', out=cand[:].rearrange("p (a b) -> p a b", a=16), in0=in0, in1=in1, op=ALU.add),
                             reads=stk, writes=['cand'])
                        top16(cand[:], 'cand', c_top_all[:, h, :], 'c_top%d' % h, c_pos_all[:, h, :], 'c_pos%d' % h)
                    ctk = ['c_top%d%s' % (h, ab) for h in range(8) for ab in 'ab']
                    cpk = ['c_pos%d%s' % (h, ab) for h in range(8) for ab in 'ab']
                    S.op('vector', L('tensor_copy', out=c_posf_all[:], in_=c_pos_all[:].rearrange("p h k -> p (h k)")), reads=cpk, writes=['c_posf_all'])
                    S.op('vector', L('tensor_tensor', out=t3b[:, :, 0:15], in0=bass.AP(c_posf_all, 0, [[128, 128], [1, 128], [0, 15]]),
                                     in1=bass.AP(thr, 1, [[16, 128], [0, 128], [1, 15]]), op=ALU.is_ge), reads=['c_posf_all', 'thr'], writes=['t3b'])
                    S.op('vector', L('reduce_sum', out=av_all[:], in_=t3b[:, :, 0:15], axis=mybir.AxisListType.X), reads=['t3b'], writes=['av_all'])
                    S.op('vector', L('scalar_tensor_tensor', out=bv_all[:], in0=av_all[:], scalar=-16.0, in1=c_posf_all[:], op0=ALU.mult, op1=ALU.add),
                         reads=['av_all', 'c_posf_all'], writes=['bv_all'])
                    for (sel, skey, c, dst, dkey) in [(av_all, 'av_all', 0, i1_all, 'i1_all'), (bv_all, 'bv_all', 1, i2_all, 'i2_all')]:
                        S.op('vector', L('tensor_tensor', out=t3b[:], in0=bass.AP(sel, 0, [[128, 128], [1, 128], [0, 16]]),
                                         in1=bass.AP(iota_t, 0, [[256, 128], [0, 128], [1, 16]]), op=ALU.is_equal), reads=[skey, 'iota'], writes=['t3b'])
                        S.op('vector', L('tensor_tensor', out=t3b[:].rearrange("p (h k) a -> p h k a", h=8), in0=t3b[:].rearrange("p (h k) a -> p h k a", h=8),
                                         in1=bass.AP(i_topf, c * 16, [[pstr, 128], [32, 8], [0, 16], [1, 16]]), op=ALU.mult),
                             reads=['t3b', 'i_topf'], writes=['t3b'])
                        S.op('vector', L('reduce_sum', out=dst[:], in_=t3b[:], axis=mybir.AxisListType.X), reads=['t3b'], writes=[dkey])
                    S.op('vector', L('tensor_tensor', out=g_all[:].rearrange("p (h k) -> p h k", h=8), in0=c_top_all[:],
                                     in1=bass.AP(c_top_all, 0, [[128, 128], [16, 8], [0, 16]]), op=ALU.subtract), reads=ctk, writes=['g_all'])
                    S.op('scalar', L('activation', out=g_all[:], in_=g_all[:], func=AF.Exp), reads=['g_all'], writes=['g_all'])
                    S.op('vector', L('reduce_sum', out=zsum[:], in_=g_all[:].rearrange("p (h k) -> p h k", h=8), axis=mybir.AxisListType.X),
                         reads=['g_all'], writes=['zsum'])
                    S.op('vector', L('reciprocal', out=zsum[:], in_=zsum[:]), reads=['zsum'], writes=['zsum'])
                    S.op('vector', L('tensor_tensor', out=g_all[:].rearrange("p (h k) -> p h k", h=8), in0=g_all[:].rearrange("p (h k) -> p h k", h=8),
                                     in1=bass.AP(zsum, 0, [[8, 128], [1, 8], [0, 16]]), op=ALU.mult), reads=['g_all', 'zsum'], writes=['g_all'])

                def export(ti):
                    kind, i = ctiles[ti]
                    sl = ti % 2
                    qrow = i * 128 if kind == 'p' else SP
                    for ri, (src, key_) in enumerate([(i1_all, 'i1_all'), (i2_all, 'i2_all'), (g_all, 'g_all')]):
                        S.op('tensor', L('transpose', out=psX[:, ri, :], in_=src[:], identity=identf[:]),
                             reads=[key_, 'identf'], writes=['psS1_%d' % ri])
                    S.op('scalar', L('activation', out=rT[:], in_=psX[:, 0:3, :], func=AF.Copy), reads=['psS1_0', 'psS1_1', 'psS1_2'], writes=['rT'])
                    S.dma('sync', 'cs_r', L('dma_start', out=r_s[:, :, qrow:qrow + 128], in_=rT[:]), reads=['rT'])
                    S.dma('sync', 'cs_x2', L('dma_start', out=x2_s[qrow:qrow + 128, :], in_=x2d[sl][:]), reads=['x2d%d_0' % sl, 'x2d%d_1' % sl])
                    S.dma('sync', 'cs_h2', L('dma_start', out=h2T_s[:, :, qrow:qrow + 128], in_=h2Td[sl][:]), reads=['h2Td%d' % sl])

                if ctiles:
                    front(0)
                for ti in range(len(ctiles)):
                    if ti + 1 < len(ctiles):
                        front(ti + 1)
                    back(ti)
                    export(ti)

                S.end_phase()
                ensure_sems()
                S.emit(nc, sems)

        if 'C' in phases and DENSE:
            n_ec = cfg.get('n_ec', 128)
            with ExitStack() as es:
                def sb(name, shape, dt):
                    return es.enter_context(nc.sbuf_tensor(name, list(shape), dt))

                def ps(name, shape, dt):
                    return es.enter_context(nc.psum_tensor(name, list(shape), dt))
                S.single = ()
                S.banks = ('ppT0', 'ppT1')
                identf = sb("identfP", [128, 128], F32); identb = sb("identbP", [128, 128], BF16)
                ub = [sb("ub%d" % i, [128, D], BF16) for i in range(3)]
                vb = [sb("vbD%d" % i, [128, D], BF16) for i in range(3)]
                uT = [sb("uT%d" % i, [128, 8, 128], BF16) for i in range(3)]
                ppT = [ps("ppT%d" % i, [128, 8, 128], BF16) for i in range(2)]
                S.dma('sync', 'p_identf', L('dma_start', out=identf[:], in_=c_ident.ap()), writes=['identf'])
                S.op('vector', L('tensor_copy', out=identb[:], in_=identf[:]), reads=['identf'], writes=['identb'])
                for ec in range(n_ec):
                    s2 = ec % 3
                    pp = ec % 2
                    S.dma('gpsimd', 'p_u%d' % s2, L('dma_start', out=ub[s2][:], in_=peer_u[ec * 128:(ec + 1) * 128, :]), writes=['ub%d' % s2])
                    S.dma('gpsimd', 'p_v%d' % s2, L('dma_start', out=vb[s2][:], in_=peer_v[ec * 128:(ec + 1) * 128, :]), writes=['vb%d' % s2])
                    for dc in range(8):
                        S.op('tensor', L('transpose', out=ppT[pp][:, dc, :], in_=ub[s2][:, dc * 128:(dc + 1) * 128], identity=identb[:]),
                             reads=['ub%d' % s2, 'identb'], writes=['ppT%d_%d' % (pp, dc)])
                    if ec % 2 == 0:
                        S.op('vector', L('tensor_copy', out=uT[s2][:], in_=ppT[pp][:]), reads=['ppT%d_%d' % (pp, dc) for dc in range(8)], writes=['uT%d' % s2])
                    else:
                        S.op('scalar', L('activation', out=uT[s2][:], in_=ppT[pp][:], func=AF.Copy), reads=['ppT%d_%d' % (pp, dc) for dc in range(8)], writes=['uT%d' % s2])
                    S.dma('sync', 'p_su%d' % s2, L('dma_start', out=UT_s[ec // 2][:, ec % 2, :].rearrange("p (a b) -> p a b", a=8), in_=uT[s2][:]),
                          reads=['uT%d' % s2], writes=['UT_s'])
                    S.dma('sync', 'p_sv%d' % s2, L('dma_start', out=V_s[ec // 2][:, ec % 2, :], in_=vb[s2][:]), reads=['vb%d' % s2], writes=['V_s'])
                S.end_phase()
                ensure_sems()
                S.emit(nc, sems)
            with ExitStack() as es:
                def sb(name, shape, dt):
                    return es.enter_context(nc.sbuf_tensor(name, list(shape), dt))

                def ps(name, shape, dt):
                    return es.enter_context(nc.psum_tensor(name, list(shape), dt))
                S.single = ()
                S.banks = ('pact0', 'pact1', 'pout0', 'pout1', 'pout2', 'pout3', 'pout4', 'pout5', 'pw0', 'pw1')
                NTT = TG // 128
                gfin = sb("gfinD", [128, D], F32)
                iota_b = sb("iota_b", [128, 128], F32)
                NBUF = 4
                utp = [sb("utp%d" % i, [128, 2, 8, 128], BF16) for i in range(NBUF)]
                vtp = [sb("vtp%d" % i, [128, 2, D], BF16) for i in range(NBUF)]
                h2g = [sb("h2g%d" % i, [128, 8, TG], BF16) for i in range(2)]
                rg = [sb("rg%d" % i, [128, 3, TG], F32) for i in range(2)]
                oh2 = [sb("oh2_%d" % i, [128, 128], BF16) for i in range(2)]
                oh1 = [sb("oh1_%d" % i, [128, 128], BF16) for i in range(2)]
                WT = [sb("WT%d" % i, [128, TG, 128], BF16) for i in range(2)]
                gl = [sb("gl%d" % i, [128, TG], BF16) for i in range(2)]
                gad = [sb("gad%d" % i, [128, TG], BF16) for i in range(2)]
                x2t = sb("x2t", [128, D], F32); accd = sb("accd", [128, D], F32); ytd = sb("ytd", [128, D], F32)
                junkd = sb("junkd", [128, D], BF16); smd = sb("smd", [128, 4], F32)
                npout = 2 * NTT
                pact = [ps("pact%d" % i, [128, 512], F32) for i in range(2 if npout <= 4 else 1)]
                pout = [ps("pout%d" % i, [128, 512], F32) for i in range(npout)]
                pw = [ps("pw%d" % i, [128, 4, 128], F32) for i in range(1)]

                S.dma('sync', 'd_gfin', L('dma_start', out=gfin[:], in_=bass.AP(norm_final, 0, [[0, 128], [1, D]])), writes=['gfin'])
                S.dma('sync', 'd_iota', L('dma_start', out=iota_b[:], in_=c_iota[:, 0:128]), writes=['iota_b'])
                dgroups = [(g * TG, TG, [('p', g * TG + j * 128) for j in range(NTT)]) for g in range(SP // TG)] + [(SP, 128, [('s', SP)])]
                dgroups = cfg.get('d_groups', dgroups)
                pairs_total = len(dgroups) * (n_ec // 2)
                issued = [0]

                def ensure_loaded(upto):
                    while issued[0] < min(upto, pairs_total):
                        gp = issued[0]; p = gp % (n_ec // 2); b = gp % NBUF
                        S.dma('sync', 'd_ut%d' % b, L('dma_start', out=utp[b][:].rearrange("p j a b -> p j (a b)"), in_=UT_s[p]), writes=['utp%d' % b])
                        S.dma('sync', 'd_vt%d' % b, L('dma_start', out=vtp[b][:], in_=V_s[p]), writes=['vtp%d' % b])
                        issued[0] += 1

                def load_group(gi):
                    q0, tg, _ = dgroups[gi]
                    w = gi % 2
                    S.dma('sync', 'd_h2_%d' % w, L('dma_start', out=h2g[w][:, :, 0:tg], in_=h2T_s[:, :, q0:q0 + tg]), writes=['h2g%d' % w])
                    S.dma('sync', 'd_r%d' % w, L('dma_start', out=rg[w][:, :, 0:tg], in_=r_s[:, :, q0:q0 + tg]), writes=['rg%d' % w])

                def wbuild(gi, t):
                    w = gi % 2
                    o = t % 2
                    S.op('vector', L('tensor_scalar', out=oh2[o][:], in0=iota_b[:], scalar1=rg[w][:, 1, t:t + 1], scalar2=None, op0=ALU.is_equal),
                         reads=['iota_b', 'rg%d' % w], writes=['oh2_%d' % o])
                    S.op('vector', L('tensor_scalar', out=oh1[o][:], in0=iota_b[:], scalar1=rg[w][:, 0, t:t + 1], scalar2=rg[w][:, 2, t:t + 1],
                                     op0=ALU.is_equal, op1=ALU.mult), reads=['iota_b', 'rg%d' % w], writes=['oh1_%d' % o])
                    S.op('tensor', L('matmul', pw[0][:, t % 4, :], lhsT=oh2[o][:], rhs=oh1[o][:], start=True, stop=True),
                         reads=['oh2_%d' % o, 'oh1_%d' % o], writes=['pw0_%d' % (t % 4)])
                    if t % 4 == 3:
                        S.op('scalar', L('activation', out=WT[w][:, t - 3:t + 1, :], in_=pw[0][:], func=AF.Copy),
                             reads=['pw0_%d' % j for j in range(4)], writes=['WT%d' % w])

                if dgroups:
                    ensure_loaded(3)
                    load_group(0)
                    for t in range(dgroups[0][1]):
                        wbuild(0, t)
                for gi, (q0, tg, ttiles) in enumerate(dgroups):
                    ntt = tg // 128
                    w = gi % 2
                    nxt = gi + 1 if gi + 1 < len(dgroups) else None
                    if nxt is not None:
                        load_group(nxt)
                        ntok_next = dgroups[nxt][1]
                        per = (ntok_next + n_ec - 1) // n_ec
                    wb_t = [0]

                    def act_mm(ec, gi=gi, w=w, tg=tg):
                        gp = gi * (n_ec // 2) + ec // 2
                        b = gp % NBUF; j = ec % 2
                        pa = ec % len(pact)
                        for dc in range(8):
                            S.op('tensor', L('matmul', pact[pa][:, 0:tg], lhsT=utp[b][:, j, dc, :], rhs=h2g[w][:, dc, 0:tg], start=(dc == 0), stop=(dc == 7)),
                                 reads=['utp%d' % b, 'h2g%d' % w], writes=['pact%d' % pa])
                        return (b, j)
                    pend_b = act_mm(0) if n_ec > 0 else None
                    for ec in range(n_ec):
                        b = pend_b
                        pa = ec % len(pact)
                        gb = ec % 2
                        S.op('scalar', L('activation', out=gl[gb][:, 0:tg], in_=pact[pa][:, 0:tg], func=AF.Gelu), reads=['pact%d' % pa], writes=['gl%d' % gb])
                        S.op('vector', L('tensor_tensor', out=gad[gb][:, 0:tg], in0=gl[gb][:, 0:tg], in1=WT[w][:, 0:tg, ec], op=ALU.mult),
                             reads=['gl%d' % gb, 'WT%d' % w], writes=['gad%d' % gb])
                        if ec + 1 < n_ec:
                            pend_b = act_mm(ec + 1)
                        for tt in range(ntt):
                            for dh in range(2):
                                S.op('tensor', L('matmul', pout[tt * 2 + dh][:], lhsT=gad[gb][:, tt * 128:(tt + 1) * 128], rhs=vtp[b[0]][:, b[1], dh * 512:(dh + 1) * 512],
                                                 start=(ec == 0), stop=(ec == n_ec - 1)),
                                     reads=['gad%d' % gb, 'vtp%d' % b[0]], writes=['pout%d' % (tt * 2 + dh)])
                        if ec % 2 == 1:
                            ensure_loaded(gi * (n_ec // 2) + ec // 2 + 4)
                        if nxt is not None:
                            for _ in range(per):
                                if wb_t[0] < ntok_next:
                                    wbuild(nxt, wb_t[0])
                                    wb_t[0] += 1
                    if nxt is not None:
                        while wb_t[0] < ntok_next:
                            wbuild(nxt, wb_t[0])
                            wb_t[0] += 1
                    for tt, (kind, qrow) in enumerate(ttiles):
                        rows = 128 if kind == 'p' else 64
                        ydst = yp[qrow:qrow + 128, :] if kind == 'p' else ys.ap()
                        S.dma('sync', 'd_x2', L('dma_start', out=x2t[:], in_=x2_s[qrow:qrow + 128, :]), writes=['x2t'])
                        for dh in range(2):
                            S.op('vector', L('tensor_tensor', out=accd[:, dh * 512:(dh + 1) * 512], in0=pout[tt * 2 + dh][:], in1=x2t[:, dh * 512:(dh + 1) * 512], op=ALU.add),
                                 reads=['pout%d' % (tt * 2 + dh), 'x2t'], writes=['accd%d' % dh])
                        S.op('scalar', L('activation', out=junkd[:], in_=accd[:], func=AF.Square, accum_out=smd[:, 0:1]),
                             reads=['accd0', 'accd1'], writes=['junkd', 'smd0'])
                        S.op('vector', L('tensor_scalar', out=smd[:, 0:1], in0=smd[:, 0:1], scalar1=1.0 / D, scalar2=EPS, op0=ALU.mult, op1=ALU.add),
                             reads=['smd0'], writes=['smd0'])
                        S.op('scalar', L('activation', out=smd[:, 0:1], in_=smd[:, 0:1], func=AF.Sqrt), reads=['smd0'], writes=['smd0'])
                        S.op('vector', L('reciprocal', out=smd[:, 0:1], in_=smd[:, 0:1]), reads=['smd0'], writes=['smd0'])
                        S.op('vector', L('tensor_scalar', out=ytd[:], in0=accd[:], scalar1=smd[:, 0:1], scalar2=None, op0=ALU.mult),
                             reads=['accd0', 'accd1', 'smd0'], writes=['ytd'])
                        S.op('gpsimd', L('tensor_tensor', out=ytd[:], in0=ytd[:], in1=gfin[:], op=ALU.mult), reads=['ytd', 'gfin'], writes=['ytd'])
                        S.dma('sync', 'd_y', L('dma_start', out=ydst, in_=ytd[0:rows, :]), reads=['ytd'])

                S.end_phase()
                ensure_sems()
                S.emit(nc, sems)

    return nc


_CACHE = {}


def _consts():
    if 'c' not in _CACHE:
        oh, maskw = _bias_consts()
        _CACHE['c'] = dict(
            c_ident=np.eye(128, dtype=np.float32),
            c_rope=_rope_table(),
            c_oh=oh, c_maskw=maskw,
            c_iota=np.tile(np.arange(256, dtype=np.float32)[None, :], (128, 1)),
        )
    return _CACHE['c']


def kernel(x_prompt, x_sample, cache_diff_k, cache_diff_v, cache_mla_ckv, cache_mla_kpe,
           rel_bias, norm_mix, w_in, diff_lambda, diff_subln, mla_q_norm, mla_w_uq, mla_kv_norm,
           mla_w_uk, mla_w_uv, w_branch_a, w_branch_b, w_out, norm_ffn, peer_w_q, peer_keys,
           peer_u, peer_v, norm_final):
    f = lambda a: np.ascontiguousarray(np.asarray(a, dtype=np.float32))
    shared = dict(
        rel_bias=f(rel_bias), norm_mix=f(norm_mix).reshape(D), w_in=f(w_in).reshape(D, INW),
        diff_lambda=f(diff_lambda).reshape(256), diff_subln=f(diff_subln).reshape(128),
        mla_q_norm=f(mla_q_norm).reshape(256), w_uq=f(mla_w_uq).reshape(256, 1536),
        mla_kv_norm=f(mla_kv_norm).reshape(128), w_uk=f(mla_w_uk).reshape(128, 1024), w_uv=f(mla_w_uv).reshape(128, 1024),
        w_ba=f(w_branch_a).reshape(D, D), w_bb=f(w_branch_b).reshape(D, D), w_o=f(w_out).reshape(D, D),
        norm_ffn=f(norm_ffn).reshape(D), peer_wq=f(peer_w_q).reshape(D, D),
        peer_keys=f(peer_keys).reshape(16, 128, 64), peer_u=f(peer_u).reshape(16384, D), peer_v=f(peer_v).reshape(16384, D),
        norm_final=f(norm_final).reshape(D),
    )
    shared.update(_consts())
    xpf = f(x_prompt); xsf = f(x_sample)
    cdkf = f(cache_diff_k).reshape(NCORES, PAST, D); cdvf = f(cache_diff_v).reshape(NCORES, PAST, D)
    cckvf = f(cache_mla_ckv).reshape(NCORES, PAST, 128); ckpef = f(cache_mla_kpe).reshape(NCORES, PAST, 32)
    in_maps = []
    for c in range(NCORES):
        m = dict(shared)
        m.update(xp=xpf[c], xs=xsf[c], cdk=cdkf[c], cdv=cdvf[c], cckv=cckvf[c], ckpe=ckpef[c])
        in_maps.append(m)
    nc = build_program()
    res = run_bass_kernel_spmd(nc, in_maps, core_ids=list(range(NCORES)))
    R = res.results

    def g(name, shape):
        return np.stack([np.asarray(R[c][name], dtype=np.float32) for c in range(NCORES)], 0).reshape(shape)
    return (g("yp", (8, SP, D)), g("ys", (8, SS, D)),
            g("kp", (1, 8, SP, 8, 2, 64)), g("vp", (1, 8, SP, 8, 128)), g("cp", (1, 8, SP, 128)), g("ep", (1, 8, SP, 32)),
            g("ks", (1, 8, SS, 8, 2, 64)), g("vs", (1, 8, SS, 8, 128)), g("cs", (1, 8, SS, 128)), g("es", (1, 8, SS, 32)))
```

```python
import math
from contextlib import ExitStack

import numpy as np
import ml_dtypes

import concourse.bass as bass
import concourse.mybir as mybir
from concourse.bass_utils import run_bass_kernel_spmd

F32 = mybir.dt.float32
BF16 = mybir.dt.bfloat16
U32 = mybir.dt.uint32
I32 = mybir.dt.int32
ALU = mybir.AluOpType
AF = mybir.ActivationFunctionType

NCORES = 8
D = 1024
SP = 4096
SS = 64
PAST = 2048
INW = 5536
NQ = SP + 128
NT = SP + PAST + 128
KS0 = SP
EPS = 1e-6
LAMBDA_INIT = 0.8 - 0.6 * math.exp(0.0)


def L(name, *a, **kw):
    return (name, a, kw)


class Sched:
    ENGS = ['tensor', 'vector', 'scalar', 'gpsimd', 'sync']

    def __init__(self):
        self.prog = {e: [] for e in self.ENGS}
        self.cnt = {e: 0 for e in self.ENGS}
        self.waited = {e: {} for e in self.ENGS}
        self.last_w = {}
        self.readers = {}
        self.semkeys = list(self.ENGS)
        self.phys_of = {}
        self.free = {}
        self.phys_q = {}
        self.nphys = 0
        self.nops = 0
        self.single = ()

    banks = ()

    def with_banks(self, reads, writes):
        extra = []
        for k in list(reads) + list(writes):
            for b in self.banks:
                if k.startswith(b):
                    bk = 'BANK:' + b
                    if bk not in extra:
                        extra.append(bk)
                    break
        return list(writes) + extra

    def canon(self, keys):
        out = []
        for k in keys:
            for p in self.single:
                if k.startswith(p + '1'):
                    k = p + '0' + k[len(p) + 1:]
                    break
            out.append(k)
        return out

    def _need(self, eng, tok, same_ok=False):
        if tok is None:
            return
        key, val = tok
        if same_ok and key == eng and eng == 'tensor':
            return
        if self.waited[eng].get(key, 0) >= val:
            return
        self.waited[eng][key] = val
        self.prog[eng].append(('wait', key, val))

    def _deps(self, eng, reads, writes):
        for r in reads:
            self._need(eng, self.last_w.get(r))
        for w in writes:
            self._need(eng, self.last_w.get(w), same_ok=True)
            for tok in self.readers.get(w, ()):
                self._need(eng, tok, same_ok=True)

    def _commit(self, tok, reads, writes):
        for r in reads:
            lst = self.readers.setdefault(r, [])
            lst.append(tok)
            if len(lst) > 48:
                best = {}
                for k, v in lst:
                    best[k] = max(best.get(k, 0), v)
                self.readers[r] = list(best.items())
        for w in writes:
            self.last_w[w] = tok
            self.readers[w] = []

    max_ops = 10 ** 9

    def op(self, eng, fn, reads=(), writes=()):
        if self.nops >= self.max_ops:
            return None
        reads = self.canon(reads); writes = self.with_banks(reads, self.canon(writes))
        self._deps(eng, reads, writes)
        self.cnt[eng] += 1
        tok = (eng, self.cnt[eng])
        self.prog[eng].append(('op', fn, eng, 1))
        self._commit(tok, reads, writes)
        self.nops += 1
        return tok

    def dma(self, q, semkey, fn, reads=(), writes=()):
        if self.nops >= self.max_ops:
            return None
        phys = self.phys_of.get(semkey)
        if phys is None:
            if self.free.get(q):
                phys = self.free[q].pop()
            else:
                phys = 'dma%s%d' % (q[0], self.nphys)
                self.nphys += 1
                self.cnt[phys] = 0
                self.semkeys.append(phys)
            self.phys_of[semkey] = phys
            self.phys_q[phys] = q
        reads = self.canon(reads); writes = self.with_banks(reads, self.canon(writes))
        self._deps(q, reads, writes)
        self.cnt[phys] += 16
        tok = (phys, self.cnt[phys])
        self.prog[q].append(('op', fn, phys, 16))
        self._commit(tok, reads, writes)
        self.nops += 1
        return tok

    def wait_dma(self, eng, semkey):
        phys = self.phys_of.get(semkey)
        if phys is not None:
            self._need(eng, (phys, self.cnt[phys]))

    def end_phase(self):
        for k in sorted(self.phys_of):
            phys = self.phys_of[k]
            self._need('sync', (phys, self.cnt[phys]))
            self.free.setdefault(self.phys_q[phys], []).append(phys)
        self.phys_of = {}

    def emit(self, nc, sems):
        prog = self.prog
        self.prog = {e: [] for e in self.ENGS}
        self.last_w = {}
        self.readers = {}

        def run(engname, e):
            for it in prog[engname]:
                if it[0] == 'wait':
                    e.wait_ge(sems[it[1]], it[2])
                else:
                    _, fn, key, inc = it
                    name, a, kw = fn
                    getattr(e, name)(*a, **kw).then_inc(sems[key], inc)

        with nc.Block() as block:
            @block.tensor
            def _(e):
                run('tensor', e)

            @block.vector
            def _(e):
                run('vector', e)

            @block.scalar
            def _(e):
                run('scalar', e)

            @block.gpsimd
            def _(e):
                run('gpsimd', e)

            @block.sync
            def _(e):
                run('sync', e)


def _rope_table():
    half = 16
    inv = (np.float32(10000.0) ** (-np.arange(half, dtype=np.float32) / np.float32(half))).astype(np.float32)
    pos = np.concatenate([np.arange(SP), PAST + np.arange(128)]).astype(np.float32)
    ang = (pos[:, None] * inv[None, :]).astype(np.float32)
    return np.concatenate([np.cos(ang), np.sin(ang)], axis=1).astype(np.float32)


def _t5_bucket(rel):
    nb = 16
    max_exact = 8
    ret = np.where(rel > 0, nb, 0)
    n = np.abs(rel)
    lg = (np.log(np.maximum(n, 1).astype(np.float32) / np.float32(max_exact)) / np.float32(math.log(128 / max_exact))
          * np.float32(nb - max_exact)).astype(np.float32)
    large = max_exact + lg.astype(np.int32)
    large = np.minimum(large, nb - 1)
    return ret + np.where(n < max_exact, n, large)


def _bias_consts():
    rel = np.arange(383) - 255
    bk = _t5_bucket(rel)
    oh = np.zeros((32, 384), np.float32)
    oh[bk, np.arange(383)] = 1.0
    kk = np.arange(128)[:, None]
    qq = np.arange(256)[None, :]
    maskw = ((kk // 64) <= (qq // 64)).astype(np.float32)
    return oh, maskw


def build_program(phases=('A', 'B', 'C'), cfg=None):
    cfg = cfg or {}
    nc = bass.Bass("TRN2", target_bir_lowering=False)
    S = Sched()
    S.max_ops = cfg.get('max_ops', 10 ** 9)

    def din(name, shape, dt=F32):
        return nc.dram_tensor(name, list(shape), dt, kind="ExternalInput")

    def dout(name, shape, dt=F32):
        return nc.dram_tensor(name, list(shape), dt, kind="ExternalOutput")

    def dscr(name, shape, dt=BF16):
        if cfg.get('dbg_scratch') and name in ('oa_s', 'ob_s', 'gates_s'):
            return nc.dram_tensor(name, list(shape), dt, kind="ExternalOutput")
        return nc.dram_tensor(name, list(shape), dt)

    xp = din("xp", [SP, D]); xs = din("xs", [SS, D])
    cdk = din("cdk", [PAST, D]); cdv = din("cdv", [PAST, D])
    cckv = din("cckv", [PAST, 128]); ckpe = din("ckpe", [PAST, 32])
    rel_bias = din("rel_bias", [32, 8])
    norm_mix = din("norm_mix", [D]); w_in = din("w_in", [D, INW])
    diff_lambda = din("diff_lambda", [4 * 64]); diff_subln = din("diff_subln", [128])
    mla_q_norm = din("mla_q_norm", [256]); w_uq = din("w_uq", [256, 1536])
    mla_kv_norm = din("mla_kv_norm", [128]); w_uk = din("w_uk", [128, 1024]); w_uv = din("w_uv", [128, 1024])
    w_ba = din("w_ba", [D, D]); w_bb = din("w_bb", [D, D]); w_o = din("w_o", [D, D])
    norm_ffn = din("norm_ffn", [D]); peer_wq = din("peer_wq", [D, D])
    peer_keys = din("peer_keys", [16, 128, 64])
    peer_u = din("peer_u", [16384, D]); peer_v = din("peer_v", [16384, D])
    norm_final = din("norm_final", [D])
    c_ident = din("c_ident", [128, 128]); c_rope = din("c_rope", [NQ, 32])
    c_oh = din("c_oh", [32, 384]); c_maskw = din("c_maskw", [128, 256])
    c_iota = din("c_iota", [128, 256])

    yp = dout("yp", [SP, D]); ys = dout("ys", [SS, D])
    kp = dout("kp", [SP, D]); vp = dout("vp", [SP, D]); cp = dout("cp", [SP, 128]); ep = dout("ep", [SP, 32])
    ks_ = dout("ks", [SS, D]); vs_ = dout("vs", [SS, D]); cs_ = dout("cs", [SS, 128]); es_ = dout("es", [SS, 32])

    dqT_s = dscr("dqT_s", [128, 8, NQ]); dkT_s = dscr("dkT_s", [128, 8, NT]); dv_s = dscr("dv_s", [NT, D])
    mqT_s = dscr("mqT_s", [96, 16, NQ]); mkT_s = dscr("mkT_s", [128, 8, NT]); mkpeT_s = dscr("mkpeT_s", [32, NT])
    mv_s = dscr("mv_s", [NT, D]); gates_s = dscr("gates_s", [NQ, 2 * D])
    oa_s = dscr("oa_s", [NQ, D]); ob_s = dscr("ob_s", [NQ, D])
    fbias_s = dscr("fbias_s", [8, 384], F32)
    DENSE = cfg.get('peer', 'dense') == 'dense'
    TG = cfg.get('tg', 256)
    UT_s = dscr("UT_s", [64, 128, 2, 1024]); V_s = dscr("V_s", [64, 128, 2, 1024])
    x2_s = dscr("x2_s", [NQ, D], F32); h2T_s = dscr("h2T_s", [128, 8, NQ])
    r_s = dscr("r_s", [128, 3, NQ], F32)

    with ExitStack() as gs:
        sems = {}

        def ensure_sems():
            for k in S.semkeys:
                if k not in sems:
                    sems[k] = gs.enter_context(nc.semaphore("s_" + k))

        if 'A' in phases:
            with ExitStack() as es:
                def sb(name, shape, dt):
                    return es.enter_context(nc.sbuf_tensor(name, list(shape), dt))

                def ps(name, shape, dt):
                    return es.enter_context(nc.psum_tensor(name, list(shape), dt))

                def sb1(name, shape, dt):
                    t = sb(name, shape, dt)
                    return [t, t]
                S.banks = ('pz0', 'pz1', 'pTa', 'pTb0', 'pTb1', 'pmm0', 'pmm1', 'pTs')
                S.single = ('zk_f', 'zv_f', 'gates_b', 'qh_f', 'rtmp', 'hb', 'cq_b', 'vm_b')

                win_sb = sb("win_sb", [128, 8, INW], BF16)
                wuq_sb = sb("wuq_sb", [128, 2, 1536], BF16)
                wuk_sb = sb("wuk_sb", [128, 1024], BF16)
                wuv_sb = sb("wuv_sb", [128, 1024], BF16)
                identf = sb("identf", [128, 128], F32)
                identb = sb("identb", [128, 128], BF16)
                gmix = sb("gmix", [128, D], F32)
                gq = sb("gq", [128, 256], F32)
                gkv = sb("gkv", [128, 128], F32)
                rope_sb = sb("rope_sb", [128, 33, 32], F32)
                xt = [sb("xt%d" % i, [128, D], F32) for i in range(2)]
                junk = sb("junk", [128, D], BF16)
                sgtmp = sb("sgtmp", [128, D], F32)
                st = [sb("st%d" % i, [128, 8], F32) for i in range(2)]
                hb = sb1("hb", [128, D], BF16)
                hT = [sb("hT%d" % i, [128, 8, 128], BF16) for i in range(2)]
                zq_b = [sb("zq_b%d" % i, [128, D], BF16) for i in range(2)]
                zk_f = sb1("zk_f", [128, D], F32)
                zk_b = [sb("zk_b%d" % i, [128, D], BF16) for i in range(2)]
                zv_f = sb1("zv_f", [128, D], F32)
                zv_b = [sb("zv_b%d" % i, [128, D], BF16) for i in range(2)]
                zm = [sb("zm%d" % i, [128, 416], F32) for i in range(2)]
                gates_b = sb1("gates_b", [128, 2 * D], BF16)
                qT_sb = [sb("qT_sb%d" % i, [128, 8, 128], BF16) for i in range(2)]
                kT_sb = [sb("kT_sb%d" % i, [128, 8, 128], BF16) for i in range(2)]
                cq_b = sb1("cq_b", [128, 256], BF16)
                cqT = [sb("cqT%d" % i, [128, 2, 128], BF16) for i in range(2)]
                qh_f = sb1("qh_f", [128, 16, 96], F32)
                qh_b = [sb("qh_b%d" % i, [128, 16, 96], BF16) for i in range(2)]
                rtmp = sb1("rtmp", [128, 4, 16, 16], F32)
                mqT_sb = [sb("mqT_sb%d" % i, [128, 16, 128], BF16) for i in range(2)]
                ckv_f = [sb("ckv_f%d" % i, [128, 128], F32) for i in range(2)]
                ckv_b = [sb("ckv_b%d" % i, [128, 128], BF16) for i in range(2)]
                ckvT = [sb("ckvT%d" % i, [128, 128], BF16) for i in range(2)]
                kn_b = [sb("kn_b%d" % i, [128, D], BF16) for i in range(2)]
                vm_b = sb1("vm_b", [128, D], BF16)
                knT_sb = [sb("knT_sb%d" % i, [128, 8, 128], BF16) for i in range(2)]
                kpe_f = [sb("kpe_f%d" % i, [128, 32], F32) for i in range(2)]
                kpe_b = [sb("kpe_b%d" % i, [128, 32], BF16) for i in range(2)]
                kpeT_sb = [sb("kpeT_sb%d" % i, [32, 128], BF16) for i in range(2)]
                pTa = ps("pTa", [128, 8, 128], BF16)
                pz = [ps("pz%d" % i, [128, 512], F32) for i in range(2)]
                pTb = [ps("pTb%d" % i, [128, 8, 128], BF16) for i in range(2)]
                pmm = [ps("pmm%d" % i, [128, 512], F32) for i in range(2)]
                pTs = ps("pTs", [128, 4, 128], BF16)

                for kc in range(8):
                    for (n0, nw) in [(0, 2048), (2048, 2048), (4096, INW - 4096)]:
                        S.dma('gpsimd', 'w_in', L('dma_start', out=win_sb[:, kc, n0:n0 + nw], in_=w_in[kc * 128:(kc + 1) * 128, n0:n0 + nw]),
                            writes=['win'])
                for kc in range(2):
                    S.dma('gpsimd', 'w_uq', L('dma_start', out=wuq_sb[:, kc, :], in_=w_uq[kc * 128:(kc + 1) * 128, :]), writes=['wuq'])
                S.dma('gpsimd', 'w_uk', L('dma_start', out=wuk_sb[:], in_=w_uk.ap()), writes=['wuk'])
                S.dma('gpsimd', 'w_uv', L('dma_start', out=wuv_sb[:], in_=w_uv.ap()), writes=['wuv'])
                S.dma('sync', 'c_identf', L('dma_start', out=identf[:], in_=c_ident.ap()), writes=['identf'])
                S.dma('sync', 'c_gmix', L('dma_start', out=gmix[:], in_=bass.AP(norm_mix, 0, [[0, 128], [1, D]])), writes=['gmix'])
                S.dma('sync', 'c_gq', L('dma_start', out=gq[:], in_=bass.AP(mla_q_norm, 0, [[0, 128], [1, 256]])), writes=['gq'])
                S.dma('sync', 'c_gkv', L('dma_start', out=gkv[:], in_=bass.AP(mla_kv_norm, 0, [[0, 128], [1, 128]])), writes=['gkv'])
                S.dma('sync', 'c_rope', L('dma_start', out=rope_sb[:], in_=c_rope.ap().rearrange("(t p) c -> p t c", p=128)), writes=['rope'])
                S.op('vector', L('tensor_copy', out=identb[:], in_=identf[:]), reads=['identf'], writes=['identb'])

                evac_rr = [0]

                def evac(out, in_, reads, writes, func=AF.Copy, eng=None):
                    if eng is None:
                        eng = 'scalar' if (evac_rr[0] % 2 == 0) else 'vector'
                        evac_rr[0] += 1
                    if eng == 'scalar':
                        S.op('scalar', L('activation', out=out, in_=in_, func=func), reads=reads, writes=writes)
                    else:
                        S.op('vector', L('tensor_copy', out=out, in_=in_), reads=reads, writes=writes)

                def scale_gain(out, in0, scalar, in1, tmp, reads, writes):
                    S.op('vector', L('tensor_scalar', out=tmp, in0=in0, scalar1=scalar, scalar2=None, op0=ALU.mult),
                         reads=reads, writes=['sgtmp'])
                    S.op('vector', L('tensor_tensor', out=out, in0=tmp, in1=in1, op=ALU.mult),
                         reads=['sgtmp'] + list(reads), writes=writes)

                def rms_stats(src_ap, ncols, stt, col, sl):
                    key = 'st%d_%d' % (sl, col)
                    S.op('scalar', L('activation', out=junk[:, 0:ncols], in_=src_ap, func=AF.Square,
                                                          accum_out=stt[:, col:col + 1]),
                         reads=rms_stats.reads, writes=['junk', key])
                    S.op('vector', L('tensor_scalar', out=stt[:, col:col + 1], in0=stt[:, col:col + 1],
                                                             scalar1=1.0 / ncols, scalar2=EPS, op0=ALU.mult, op1=ALU.add),
                         reads=[key], writes=[key])
                    S.op('scalar', L('activation', out=stt[:, col:col + 1], in_=stt[:, col:col + 1], func=AF.Sqrt),
                         reads=[key], writes=[key])
                    S.op('vector', L('reciprocal', out=stt[:, col:col + 1], in_=stt[:, col:col + 1]),
                         reads=[key], writes=[key])
                    return key

                def transposes(src_fn, n, pst, pkey, reads, rows=128):
                    for j in range(n):
                        S.op('tensor', L('transpose', out=pst[0:rows, j, :], in_=src_fn(j), identity=identb[:]),
                             reads=list(reads) + ['identb'], writes=['%s_%d' % (pkey, j)])

                def k_side(sl, kcol, zkb_keys, zvb_keys, ckvb_key, kpeb_key):
                    transposes(lambda j: zk_b[sl][:, j * 128:(j + 1) * 128], 8, pTb[1], 'pTb1', zkb_keys)
                    evac(kT_sb[sl][:], pTb[1][:], ['pTb1_%d' % j for j in range(8)], ['kT_sb%d' % sl])
                    S.dma('gpsimd', 'st_kT%d' % sl, L('dma_start', out=dkT_s[:, :, kcol:kcol + 128], in_=kT_sb[sl][:]),
                          reads=['kT_sb%d' % sl])
                    S.dma('gpsimd', 'st_v%d' % sl, L('dma_start', out=dv_s[kcol:kcol + 128, :], in_=zv_b[sl][:]),
                          reads=zvb_keys)
                    yield
                    S.op('tensor', L('transpose', out=pTs[:, 2, :], in_=ckv_b[sl][:], identity=identb[:]),
                         reads=[ckvb_key, 'identb'], writes=['pTs_2'])
                    evac(ckvT[sl][:], pTs[:, 2, :], ['pTs_2'], ['ckvT%d' % sl])
                    yield
                    for (wsb, wkey, dst, dkey) in [(wuk_sb, 'wuk', kn_b, 'kn_b'), (wuv_sb, 'wuv', vm_b, 'vm_b')]:
                        for c in range(2):
                            S.op('tensor', L('matmul', pmm[c][:], lhsT=ckvT[sl][:], rhs=wsb[:, c * 512:(c + 1) * 512],
                                                                    start=True, stop=True),
                                 reads=['ckvT%d' % sl, wkey], writes=['pmm%d' % c])
                            evac(dst[sl][:, c * 512:(c + 1) * 512], pmm[c][:], ['pmm%d' % c], ['%s%d_%d' % (dkey, sl, c)])
                    S.dma('gpsimd', 'st_mv%d' % sl, L('dma_start', out=mv_s[kcol:kcol + 128, :], in_=vm_b[sl][:]),
                          reads=['vm_b%d_0' % sl, 'vm_b%d_1' % sl])
                    yield
                    transposes(lambda j: kn_b[sl][:, j * 128:(j + 1) * 128], 8, pTb[0], 'pTb0',
                               ['kn_b%d_0' % sl, 'kn_b%d_1' % sl])
                    evac(knT_sb[sl][:], pTb[0][:], ['pTb0_%d' % j for j in range(8)], ['knT_sb%d' % sl])
                    S.dma('gpsimd', 'st_mk%d' % sl, L('dma_start', out=mkT_s[:, :, kcol:kcol + 128], in_=knT_sb[sl][:]),
                          reads=['knT_sb%d' % sl])
                    S.op('tensor', L('transpose', out=pTs[0:32, 3, :], in_=kpe_b[sl][:], identity=identb[:]),
                         reads=[kpeb_key, 'identb'], writes=['pTs_3'])
                    evac(kpeT_sb[sl][:], pTs[0:32, 3, :], ['pTs_3'], ['kpeT_sb%d' % sl])
                    S.dma('gpsimd', 'st_kpe%d' % sl, L('dma_start', out=mkpeT_s[:, kcol:kcol + 128], in_=kpeT_sb[sl][:]),
                          reads=['kpeT_sb%d' % sl])

                tiles = [('p', i) for i in range(32)] + [('s', 0)] + [('c', i) for i in range(16)]
                tiles = cfg.get('a_tiles', tiles)
                pending = [None]

                def step_pending():
                    if pending[0] is not None:
                        try:
                            next(pending[0])
                        except StopIteration:
                            pending[0] = None

                def drain():
                    while pending[0] is not None:
                        step_pending()
                for tix, (kind, i) in enumerate(tiles):
                    sl = tix % 2
                    if kind == 'c':
                        kcol = KS0 + i * 128
                        r0 = i * 128
                        S.dma('gpsimd', 'ld_ck%d' % sl, L('dma_start', out=zk_b[sl][:], in_=cdk[r0:r0 + 128, :]),
                              writes=['zk_b%d_2' % sl, 'zk_b%d_3' % sl])
                        S.dma('gpsimd', 'ld_cv%d' % sl, L('dma_start', out=zv_b[sl][:], in_=cdv[r0:r0 + 128, :]),
                              writes=['zv_b%d_4' % sl, 'zv_b%d_5' % sl])
                        S.dma('gpsimd', 'ld_cc%d' % sl, L('dma_start', out=ckv_b[sl][:], in_=cckv[r0:r0 + 128, :]),
                              writes=['ckv_b%d' % sl])
                        S.dma('gpsimd', 'ld_ce%d' % sl, L('dma_start', out=kpe_b[sl][:], in_=ckpe[r0:r0 + 128, :]),
                              writes=['kpe_b%d' % sl])
                        drain()
                        for _ in k_side(sl, kcol, ['zk_b%d_2' % sl, 'zk_b%d_3' % sl], ['zv_b%d_4' % sl, 'zv_b%d_5' % sl], 'ckv_b%d' % sl, 'kpe_b%d' % sl):
                            pass
                        continue
                    if kind == 'p':
                        rows = 128; qcol = i * 128; kcol = i * 128; rt = i
                        xsrc = xp[i * 128:(i + 1) * 128, :]
                        o_k, o_v, o_c, o_e = kp[qcol:qcol + 128, :], vp[qcol:qcol + 128, :], cp[qcol:qcol + 128, :], ep[qcol:qcol + 128, :]
                    else:
                        rows = 64; qcol = SP; kcol = KS0 + PAST; rt = 32
                        xsrc = xs.ap()
                        o_k, o_v, o_c, o_e = ks_.ap(), vs_.ap(), cs_.ap(), es_.ap()
                        S.op('vector', L('memset', xt[sl][:], 0.0), writes=['xt%d' % sl])
                    S.dma('sync', 'ld_x%d' % sl, L('dma_start', out=xt[sl][0:rows, :], in_=xsrc),
                          writes=['xt%d' % sl])
                    rms_stats.reads = ['xt%d' % sl]
                    k0 = rms_stats(xt[sl][:], D, st[sl], 0, sl)
                    scale_gain(hb[sl][:], xt[sl][:], st[sl][:, 0:1], gmix[:], sgtmp[:, 0:1024], ['xt%d' % sl, k0, 'gmix'], ['hb%d' % sl])
                    transposes(lambda j: hb[sl][:, j * 128:(j + 1) * 128], 8, pTa, 'pTa', ['hb%d' % sl])
                    evac(hT[sl][:], pTa[:], ['pTa_%d' % j for j in range(8)], ['hT%d' % sl])

                    chunks = [(0, 512), (512, 512), (1024, 512), (1536, 512), (2048, 512), (2560, 512), (3072, 416),
                              (3488, 512), (4000, 512), (4512, 512), (5024, 512)]
                    for ci, (n0, nw) in enumerate(chunks):
                        pb = ci % 2
                        for kc in range(8):
                            S.op('tensor', L('matmul', pz[pb][:, 0:nw], lhsT=hT[sl][:, kc, :], rhs=win_sb[:, kc, n0:n0 + nw],
                                start=(kc == 0), stop=(kc == 7)),
                                reads=['hT%d' % sl, 'win'], writes=['pz%d' % pb])
                        if ci < 2:
                            evac(zq_b[sl][:, n0:n0 + 512], pz[pb][:], ['pz%d' % pb], ['zq_b%d_%d' % (sl, ci)])
                        elif ci < 4:
                            c0 = n0 - 1024
                            evac(zk_f[sl][:, c0:c0 + 512], pz[pb][:], ['pz%d' % pb], ['zk_f%d_%d' % (sl, ci)], eng='scalar')
                            S.op('vector', L('tensor_copy', out=zk_b[sl][:, c0:c0 + 512], in_=pz[pb][:]),
                                 reads=['pz%d' % pb], writes=['zk_b%d_%d' % (sl, ci)])
                        elif ci < 6:
                            c0 = n0 - 2048
                            evac(zv_f[sl][:, c0:c0 + 512], pz[pb][:], ['pz%d' % pb], ['zv_f%d_%d' % (sl, ci)], eng='scalar')
                            S.op('vector', L('tensor_copy', out=zv_b[sl][:, c0:c0 + 512], in_=pz[pb][:]),
                                 reads=['pz%d' % pb], writes=['zv_b%d_%d' % (sl, ci)])
                        elif ci == 6:
                            evac(zm[sl][:], pz[pb][:, 0:416], ['pz%d' % pb], ['zm%d' % sl], eng='vector')
                        else:
                            c0 = n0 - 3488
                            S.op('scalar', L('activation', out=gates_b[sl][:, c0:c0 + 512], in_=pz[pb][:],
                                                                                 func=AF.Sigmoid),
                                 reads=['pz%d' % pb], writes=['gates_b%d_%d' % (sl, ci)])
                        step_pending()
                    drain()
                    S.dma('gpsimd', 'st_ok%d' % sl, L('dma_start', out=o_k, in_=zk_f[sl][0:rows, :]),
                          reads=['zk_f%d_2' % sl, 'zk_f%d_3' % sl])
                    S.dma('gpsimd', 'st_ov%d' % sl, L('dma_start', out=o_v, in_=zv_f[sl][0:rows, :]),
                          reads=['zv_f%d_4' % sl, 'zv_f%d_5' % sl])
                    S.dma('gpsimd', 'st_g%d' % sl, L('dma_start', out=gates_s[qcol:qcol + 128, :], in_=gates_b[sl][:]),
                          reads=['gates_b%d_%d' % (sl, c) for c in range(7, 11)])
                    def epi(sl=sl, qcol=qcol, kcol=kcol, rows=rows, rt=rt, o_c=o_c, o_e=o_e):
                        tmpk = 'rtmp%d' % sl
                        rms_stats.reads = ['zm%d' % sl]
                        k1 = rms_stats(zm[sl][:, 0:256], 256, st[sl], 1, sl)
                        scale_gain(cq_b[sl][:], zm[sl][:, 0:256], st[sl][:, 1:2], gq[:], sgtmp[:, 0:256], ['zm%d' % sl, k1, 'gq'], ['cq_b%d' % sl])
                        k2 = rms_stats(zm[sl][:, 256:384], 128, st[sl], 2, sl)
                        scale_gain(ckv_f[sl][:], zm[sl][:, 256:384], st[sl][:, 2:3], gkv[:], sgtmp[:, 0:128], ['zm%d' % sl, k2, 'gkv'], ['ckv_f%d' % sl])
                        S.op('vector', L('tensor_copy', out=ckv_b[sl][:], in_=ckv_f[sl][:]), reads=['ckv_f%d' % sl], writes=['ckv_b%d' % sl])
                        S.dma('gpsimd', 'st_oc%d' % sl, L('dma_start', out=o_c, in_=ckv_f[sl][0:rows, :]),
                              reads=['ckv_f%d' % sl])
                        cos1 = bass.AP(rope_sb, rt * 32, [[33 * 32, 128], [1, 16]])
                        sin1 = bass.AP(rope_sb, rt * 32 + 16, [[33 * 32, 128], [1, 16]])
                        y1 = zm[sl][:, 384:400]
                        y2 = zm[sl][:, 400:416]
                        for ti_, (a, b_) in enumerate([(y1, cos1), (y2, sin1), (y1, sin1), (y2, cos1)]):
                            S.op('vector', L('tensor_tensor', out=rtmp[sl][:, ti_, 0, :], in0=a, in1=b_, op=ALU.mult),
                                 reads=['zm%d' % sl, 'rope'], writes=[tmpk + '_%d' % ti_])
                        S.op('vector', L('tensor_tensor', out=kpe_f[sl][:, 0:16], in0=rtmp[sl][:, 0, 0, :], in1=rtmp[sl][:, 1, 0, :], op=ALU.subtract),
                             reads=[tmpk + '_0', tmpk + '_1'], writes=['kpe_f%d_a' % sl])
                        S.op('vector', L('tensor_tensor', out=kpe_f[sl][:, 16:32], in0=rtmp[sl][:, 2, 0, :], in1=rtmp[sl][:, 3, 0, :], op=ALU.add),
                             reads=[tmpk + '_2', tmpk + '_3'], writes=['kpe_f%d_b' % sl])
                        S.op('vector', L('tensor_copy', out=kpe_b[sl][:], in_=kpe_f[sl][:]),
                             reads=['kpe_f%d_a' % sl, 'kpe_f%d_b' % sl], writes=['kpe_b%d' % sl])
                        S.dma('gpsimd', 'st_oe%d' % sl, L('dma_start', out=o_e, in_=kpe_f[sl][0:rows, :]),
                              reads=['kpe_f%d_a' % sl, 'kpe_f%d_b' % sl])
                        yield
                        transposes(lambda j: zq_b[sl][:, j * 128:(j + 1) * 128], 8, pTb[0], 'pTb0', ['zq_b%d_0' % sl, 'zq_b%d_1' % sl])
                        evac(qT_sb[sl][:], pTb[0][:], ['pTb0_%d' % j for j in range(8)], ['qT_sb%d' % sl])
                        S.dma('gpsimd', 'st_qT%d' % sl, L('dma_start', out=dqT_s[:, :, qcol:qcol + 128], in_=qT_sb[sl][:]),
                              reads=['qT_sb%d' % sl])
                        yield
                        for j in range(2):
                            S.op('tensor', L('transpose', out=pTs[:, j, :], in_=cq_b[sl][:, j * 128:(j + 1) * 128], identity=identb[:]),
                                 reads=['cq_b%d' % sl, 'identb'], writes=['pTs_%d' % j])
                        evac(cqT[sl][:], pTs[:, 0:2, :], ['pTs_0', 'pTs_1'], ['cqT%d' % sl])
                        yield
                        qh_flat = qh_f[sl][:].rearrange("p h e -> p (h e)")
                        for c in range(3):
                            pb = c % 2
                            for kc in range(2):
                                S.op('tensor', L('matmul', pmm[pb][:], lhsT=cqT[sl][:, kc, :], rhs=wuq_sb[:, kc, c * 512:(c + 1) * 512],
                                    start=(kc == 0), stop=(kc == 1)),
                                    reads=['cqT%d' % sl, 'wuq'], writes=['pmm%d' % pb])
                            evac(qh_flat[:, c * 512:(c + 1) * 512], pmm[pb][:], ['pmm%d' % pb], ['qh_f%d_%d' % (sl, c)])
                        yield
                        qhr = ['qh_f%d_%d' % (sl, c) for c in range(3)]
                        cosb = bass.AP(rope_sb, rt * 32, [[33 * 32, 128], [0, 16], [1, 16]])
                        sinb = bass.AP(rope_sb, rt * 32 + 16, [[33 * 32, 128], [0, 16], [1, 16]])
                        x1 = qh_f[sl][:, :, 64:80]
                        x2 = qh_f[sl][:, :, 80:96]
                        tmpk = 'rtmp%d' % sl
                        for ti_, (a, b_) in enumerate([(x1, cosb), (x2, sinb), (x1, sinb), (x2, cosb)]):
                            S.op('vector', L('tensor_tensor', out=rtmp[sl][:, ti_, :, :], in0=a, in1=b_, op=ALU.mult),
                                 reads=qhr + ['rope'], writes=[tmpk + '_%d' % ti_])
                        S.op('vector', L('tensor_copy', out=qh_b[sl][:, :, 0:64], in_=qh_f[sl][:, :, 0:64]),
                             reads=qhr, writes=['qh_b%d_n' % sl])
                        S.op('vector', L('tensor_tensor', out=qh_b[sl][:, :, 64:80], in0=rtmp[sl][:, 0, :, :], in1=rtmp[sl][:, 1, :, :],
                                                                 op=ALU.subtract),
                             reads=[tmpk + '_0', tmpk + '_1'], writes=['qh_b%d_a' % sl])
                        S.op('vector', L('tensor_tensor', out=qh_b[sl][:, :, 80:96], in0=rtmp[sl][:, 2, :, :], in1=rtmp[sl][:, 3, :, :],
                                                                 op=ALU.add),
                             reads=[tmpk + '_2', tmpk + '_3'], writes=['qh_b%d_b' % sl])
                        yield
                        ks = k_side(sl, kcol, ['zk_b%d_2' % sl, 'zk_b%d_3' % sl], ['zv_b%d_4' % sl, 'zv_b%d_5' % sl], 'ckv_b%d' % sl, 'kpe_b%d' % sl)
                        next(ks, None)
                        yield
                        qhb_keys = ['qh_b%d_n' % sl, 'qh_b%d_a' % sl, 'qh_b%d_b' % sl]
                        for half in range(2):
                            transposes(lambda j, half=half: qh_b[sl][:, half * 8 + j, :], 8, pTb[half], 'pTb%d' % half, qhb_keys, rows=96)
                            evac(mqT_sb[sl][0:96, half * 8:(half + 1) * 8, :], pTb[half][0:96, :, :],
                                 ['pTb%d_%d' % (half, j) for j in range(8)], ['mqT_sb%d_%d' % (sl, half)])
                        S.dma('gpsimd', 'st_mq%d' % sl, L('dma_start', out=mqT_s[:, :, qcol:qcol + 128], in_=mqT_sb[sl][0:96, :, :]),
                              reads=['mqT_sb%d_0' % sl, 'mqT_sb%d_1' % sl])
                        yield
                        for _ in ks:
                            yield
                    pending[0] = epi()

                drain()
                S.end_phase()
                ensure_sems()
                S.emit(nc, sems)

        if 'B' in phases:
            with ExitStack() as es:
                def sb(name, shape, dt):
                    return es.enter_context(nc.sbuf_tensor(name, list(shape), dt))

                def ps(name, shape, dt):
                    return es.enter_context(nc.psum_tensor(name, list(shape), dt))
                S.single = ()
                S.banks = ('sc0', 'sc1', 'oacc0', 'oacc1', 'oacc2', 'oacc3', 'pset')
                NKT = NT // 128
                QT = [sb("QT%d" % i, [128, NQ], BF16) for i in range(2)]
                KT = [sb("KT%d" % i, [128, NT], BF16) for i in range(2)]
                VA = [sb("VA%d" % i, [128, NKT, 129], BF16) for i in range(2)]
                VM = [sb("VM%d" % i, [128, NKT, 65], BF16) for i in range(2)]
                et = [sb("et%d" % i, [128, 512], BF16) for i in range(3)]
                rb_sb = sb("rb_sb", [32, 8], F32)
                oh_sb = sb("oh_sb", [32, 384], F32)
                rb15 = sb("rb15", [8, 1], F32)
                fexp = sb("fexp", [8, 384], F32)
                hank = sb("hank", [128, 8, 256], F32)
                maskw = sb("maskw", [128, 256], F32)
                ebm = sb("ebm", [128, 8, 256], BF16)
                dl = sb("dl", [128, 256], F32)
                dlj = sb("dlj", [128, 64], F32)
                lam = sb("lam", [128, 4], F32)
                gsub = sb("gsub", [128, 128], F32)
                o0 = [sb("o0_%d" % i, [128, 129], F32) for i in range(4)]
                o1 = [sb("o1_%d" % i, [128, 129], F32) for i in range(4)]
                mhalfB = sb("mhalfB", [128, 1], F32)
                sm = [sb("smB%d" % i, [128, 8], F32) for i in range(4)]
                of = [sb("of%d" % i, [128, 128], F32) for i in range(4)]
                of2 = [sb("of2_%d" % i, [128, 128], F32) for i in range(4)]
                jk = sb("jkB", [128, 128], BF16)
                oa_t = [sb("oa_t%d" % i, [128, 128], BF16) for i in range(4)]
                ob_t = [sb("ob_t%d" % i, [128, 64], BF16) for i in range(4)]
                sc = [ps("sc%d" % i, [128, 512], F32) for i in range(2)]
                oacc = [ps("oacc%d" % i, [128, 512], F32) for i in range(4)]
                pset = ps("pset", [128, 512], F32)
                P_IN_B = DENSE and 'C' in phases and cfg.get('p_in_b', True)
                n_ecB = cfg.get('n_ec', 128) if P_IN_B else 0
                if P_IN_B:
                    S.banks = S.banks + ('ppTB',)
                    identfB = sb("identfB", [128, 128], F32); identbB = sb("identbB", [128, 128], BF16)
                    ubB = [sb("ubB%d" % i, [128, D], BF16) for i in range(3)]
                    vbB = [sb("vbB%d" % i, [128, D], BF16) for i in range(3)]
                    uTB = [sb("uTB%d" % i, [128, 8, 128], BF16) for i in range(3)]
                    ppTB = ps("ppTB", [128, 8, 128], BF16)
                    S.dma('sync', 'pb_identf', L('dma_start', out=identfB[:], in_=c_ident.ap()), writes=['identfB'])
                    S.op('vector', L('tensor_copy', out=identbB[:], in_=identfB[:]), reads=['identfB'], writes=['identbB'])
                p_state = [0, 0]

                def p_load():
                    ec = p_state[0]
                    if ec >= n_ecB:
                        return
                    p_state[0] += 1
                    s3 = ec % 3
                    S.dma('gpsimd', 'pb_u%d' % s3, L('dma_start', out=ubB[s3][:], in_=peer_u[ec * 128:(ec + 1) * 128, :]), writes=['ubB%d' % s3])
                    S.dma('gpsimd', 'pb_v%d' % s3, L('dma_start', out=vbB[s3][:], in_=peer_v[ec * 128:(ec + 1) * 128, :]), writes=['vbB%d' % s3])

                def p_proc():
                    ec = p_state[1]
                    if ec >= n_ecB:
                        return
                    p_state[1] += 1
                    s3 = ec % 3
                    for dc in range(8):
                        S.op('tensor', L('transpose', out=ppTB[:, dc, :], in_=ubB[s3][:, dc * 128:(dc + 1) * 128], identity=identbB[:]),
                             reads=['ubB%d' % s3, 'identbB'], writes=['ppTB_%d' % dc])
                    S.op('vector', L('tensor_copy', out=uTB[s3][:], in_=ppTB[:]), reads=['ppTB_%d' % dc for dc in range(8)], writes=['uTB%d' % s3])
                    S.dma('sync', 'pb_su%d' % s3, L('dma_start', out=UT_s[ec // 2][:, ec % 2, :].rearrange("p (a b) -> p a b", a=8), in_=uTB[s3][:]),
                          reads=['uTB%d' % s3])
                    S.dma('sync', 'pb_sv%d' % s3, L('dma_start', out=V_s[ec // 2][:, ec % 2, :], in_=vbB[s3][:]), reads=['vbB%d' % s3])

                def p_iter():
                    if p_state[1] < n_ecB:
                        while p_state[0] < min(p_state[1] + 3, n_ecB):
                            p_load()
                        p_proc()

                S.dma('sync', 'b_rb', L('dma_start', out=rb_sb[:], in_=rel_bias.ap()), writes=['rb_sb'])
                S.dma('sync', 'b_oh', L('dma_start', out=oh_sb[:], in_=c_oh.ap()), writes=['oh_sb'])
                S.dma('sync', 'b_rb15', L('dma_start', out=rb15[:], in_=rel_bias[15:16, :].rearrange("a h -> h a")), writes=['rb15'])
                S.dma('sync', 'b_mw', L('dma_start', out=maskw[:], in_=c_maskw.ap()), writes=['maskw'])
                S.dma('sync', 'b_dl', L('dma_start', out=dl[:], in_=bass.AP(diff_lambda, 0, [[0, 128], [1, 256]])), writes=['dl'])
                S.dma('sync', 'b_gs', L('dma_start', out=gsub[:], in_=bass.AP(diff_subln, 0, [[0, 128], [1, 128]])), writes=['gsub'])
                S.op('tensor', L('matmul', pset[0:8, 0:384], lhsT=rb_sb[:], rhs=oh_sb[:], start=True, stop=True),
                     reads=['rb_sb', 'oh_sb'], writes=['pset'])
                S.op('vector', L('tensor_scalar', out=fexp[:], in0=pset[0:8, 0:384], scalar1=rb15[:, 0:1], scalar2=None, op0=ALU.subtract),
                     reads=['pset', 'rb15'], writes=['fexp'])
                S.op('scalar', L('activation', out=fexp[:], in_=fexp[:], func=AF.Exp), reads=['fexp'], writes=['fexp'])
                S.dma('sync', 'b_fs', L('dma_start', out=fbias_s.ap(), in_=fexp[:]), reads=['fexp'], writes=['fbias_s'])
                for h in range(8):
                    S.dma('sync', 'b_hk', L('dma_start', out=hank[:, h, :], in_=bass.AP(fbias_s, h * 384, [[1, 128], [1, 256]])),
                          reads=['fbias_s'], writes=['hank'])
                for h in range(8):
                    S.op('vector', L('tensor_tensor', out=ebm[:, h, :], in0=hank[:, h, ::-1], in1=maskw[:], op=ALU.mult),
                         reads=['hank', 'maskw'], writes=['ebm'])
                for i_, (a, b_) in enumerate([(0, 1), (2, 3)]):
                    S.op('vector', L('tensor_tensor', out=dlj[:], in0=dl[:, a * 64:(a + 1) * 64], in1=dl[:, b_ * 64:(b_ + 1) * 64], op=ALU.mult),
                         reads=['dl'], writes=['dlj'])
                    S.op('vector', L('reduce_sum', out=lam[:, i_:i_ + 1], in_=dlj[:], axis=mybir.AxisListType.X),
                         reads=['dlj'], writes=['lam%d' % i_])
                    S.op('scalar', L('activation', out=lam[:, i_:i_ + 1], in_=lam[:, i_:i_ + 1], func=AF.Exp),
                         reads=['lam%d' % i_], writes=['lam%d' % i_])
                S.op('vector', L('tensor_tensor', out=lam[:, 2:3], in0=lam[:, 1:2], in1=lam[:, 0:1], op=ALU.subtract),
                     reads=['lam0', 'lam1'], writes=['lam2'])
                S.op('vector', L('tensor_scalar', out=lam[:, 2:3], in0=lam[:, 2:3], scalar1=-LAMBDA_INIT, scalar2=None, op0=ALU.add),
                     reads=['lam2'], writes=['lam2'])
                S.op('vector', L('tensor_scalar', out=gsub[:], in0=gsub[:], scalar1=1.0 - LAMBDA_INIT, scalar2=None, op0=ALU.mult),
                     reads=['gsub'], writes=['gsub'])
                S.op('gpsimd', L('memset', mhalfB[:], -0.5), writes=['mhalfB'])
                for i in range(2):
                    S.op('gpsimd', L('memset', VA[i][:, :, 128:129], 1.0), writes=['VA%d_one' % i])
                    S.op('gpsimd', L('memset', VA[i][64:128, NKT - 1, 128:129], 0.0), writes=['VA%d_one' % i])
                    S.op('gpsimd', L('memset', VM[i][:, :, 64:65], 1.0), writes=['VM%d_one' % i])
                    S.op('gpsimd', L('memset', VM[i][64:128, NKT - 1, 64:65], 0.0), writes=['VM%d_one' % i])

                groups = [dict(qcol0=g * 512, nq=4, kts=list(range(4 * g + 4)), qabs0=4 * g) for g in range(8)]
                groups.append(dict(qcol0=SP, nq=1, kts=list(range(32, 49)), qabs0=48))
                groups = cfg.get('b_groups', groups)
                st_ctr = [0]
                sc_ctr = [0]
                scb = [sc[0], sc[1], pset]
                sckey = ['sc0', 'sc1', 'pset']

                def attention(kind, h, slot, r0, r1, E, scale, vt, vkey, grp, finalize):
                    nq = grp['nq']; kts = grp['kts']; qabs0 = grp['qabs0']; qcol0 = grp['qcol0']
                    p_iter()

                    def qlo_of(kt):
                        return max(kt - qabs0, 0) if nq > 1 else 0

                    def score(idx):
                        kt = kts[idx]; qlo = qlo_of(kt); N = (nq - qlo) * 128
                        s = sc_ctr[0] % 3
                        sc_ctr[0] += 1
                        S.op('tensor', L('matmul', scb[s][:, 0:N], lhsT=KT[slot][r0:r1, kt * 128:(kt + 1) * 128],
                                         rhs=QT[slot][r0:r1, qcol0 + qlo * 128:qcol0 + nq * 128], start=True, stop=True),
                             reads=['KT%d' % slot, 'KT%d_pe' % slot, 'QT%d' % slot], writes=[sckey[s]])
                        return s
                    pend = [score(i_) for i_ in range(min(2, len(kts)))]
                    for idx, kt in enumerate(kts):
                        s = pend.pop(0)
                        st_ctr[0] += 1
                        if idx + 2 < len(kts):
                            pend.append(score(idx + 2))
                        qlo = qlo_of(kt); N = (nq - qlo) * 128
                        t = st_ctr[0] % 3
                        S.op('scalar', L('activation', out=et[t][:, 0:N], in_=scb[s][:, 0:N], func=AF.Exp, scale=scale),
                             reads=[sckey[s]], writes=['et%d' % t])
                        d = qabs0 + qlo - kt
                        if kind == 'd' and d in (0, 1):
                            W = min(256 - d * 128, N)
                            S.op('vector', L('tensor_tensor', out=et[t][:, 0:W], in0=et[t][:, 0:W], in1=ebm[:, h, d * 128:d * 128 + W], op=ALU.mult),
                                 reads=['et%d' % t, 'ebm'], writes=['et%d' % t])
                        if kind == 'm' and d == 0:
                            S.op('gpsimd', L('memset', et[t][64:128, 0:64], 0.0), reads=['et%d' % t], writes=['et%d' % t])
                        for j in range(qlo, nq):
                            last = (qabs0 + j) if nq > 1 else kts[-1]
                            S.op('tensor', L('matmul', oacc[j][:, 0:E + 1], lhsT=et[t][:, (j - qlo) * 128:(j - qlo + 1) * 128],
                                             rhs=vt[:, kt, 0:E + 1], start=(idx == 0), stop=(kt == last)),
                                 reads=['et%d' % t, vkey, vkey + '_one'], writes=['oacc%d' % j])
                    finalize([(j, qcol0 + j * 128) for j in range(nq)])

                def rstd_small(smt, col, key, n):
                    S.op('vector', L('tensor_scalar', out=smt[:, col:col + 1], in0=smt[:, col:col + 1], scalar1=1.0 / n, scalar2=EPS,
                                     op0=ALU.mult, op1=ALU.add), reads=[key], writes=[key])
                    S.op('scalar', L('activation', out=smt[:, col:col + 1], in_=smt[:, col:col + 1], func=AF.Sqrt), reads=[key], writes=[key])
                    S.op('vector', L('reciprocal', out=smt[:, col:col + 1], in_=smt[:, col:col + 1]), reads=[key], writes=[key])

                dheads = cfg.get('b_dheads', list(range(8)))
                for hi, h in enumerate(dheads):
                    slot = hi % 2
                    S.dma('sync', 'b_q%d' % slot, L('dma_start', out=QT[slot][:], in_=dqT_s[:, h, :]), writes=['QT%d' % slot])
                    S.dma('sync', 'b_k%d' % slot, L('dma_start', out=KT[slot][:], in_=dkT_s[:, h, :]), writes=['KT%d' % slot, 'KT%d_pe' % slot])
                    for t0 in range(0, NKT, 13):
                        t1 = min(t0 + 13, NKT)
                        S.dma('sync', 'b_v%d' % slot, L('dma_start', out=VA[slot][:, t0:t1, 0:128],
                                                         in_=dv_s.ap().rearrange("(t p) f -> p t f", p=128)[:, t0:t1, h * 128:(h + 1) * 128]),
                              writes=['VA%d' % slot])
                    for grp in groups:
                        def fin0(items):
                            for (j, qrow) in items:
                                S.op('vector', L('tensor_copy', out=o0[j][:], in_=oacc[j][:, 0:129]), reads=['oacc%d' % j], writes=['o0_%d' % j])

                        def fin1(items, h=h):
                            for (j, qrow) in items:
                                S.op('vector', L('tensor_copy', out=o1[j][:], in_=oacc[j][:, 0:129]), reads=['oacc%d' % j], writes=['o1_%d' % j])
                            for (j, qrow) in items:
                                k_ = 'smB%d' % j
                                S.op('vector', L('reciprocal', out=sm[j][:, 0:1], in_=o0[j][:, 128:129]), reads=['o0_%d' % j], writes=[k_ + 'a'])
                                S.op('vector', L('reciprocal', out=sm[j][:, 1:2], in_=o1[j][:, 128:129]), reads=['o1_%d' % j], writes=[k_ + 'b'])
                                S.op('vector', L('tensor_tensor', out=sm[j][:, 1:2], in0=sm[j][:, 1:2], in1=lam[:, 2:3], op=ALU.mult),
                                     reads=[k_ + 'b', 'lam2'], writes=[k_ + 'b'])
                                S.op('vector', L('tensor_scalar', out=of[j][:], in0=o0[j][:, 0:128], scalar1=sm[j][:, 0:1], scalar2=None, op0=ALU.mult),
                                     reads=['o0_%d' % j, k_ + 'a'], writes=['of%d' % j])
                                S.op('vector', L('tensor_scalar', out=of2[j][:], in0=o1[j][:, 0:128], scalar1=sm[j][:, 1:2], scalar2=None, op0=ALU.mult),
                                     reads=['o1_%d' % j, k_ + 'b'], writes=['of2_%d' % j])
                                S.op('vector', L('tensor_tensor', out=of[j][:], in0=of[j][:], in1=of2[j][:], op=ALU.add),
                                     reads=['of%d' % j, 'of2_%d' % j], writes=['of%d' % j])
                                S.op('vector', L('tensor_tensor', out=of2[j][:], in0=of[j][:], in1=of[j][:], op=ALU.mult),
                                     reads=['of%d' % j], writes=['of2_%d' % j])
                                S.op('vector', L('reduce_sum', out=sm[j][:, 2:3], in_=of2[j][:], axis=mybir.AxisListType.X),
                                     reads=['of2_%d' % j], writes=[k_ + 'c'])
                                S.op('gpsimd', L('tensor_scalar', out=sm[j][:, 2:3], in0=sm[j][:, 2:3], scalar1=1.0 / 128, scalar2=EPS, op0=ALU.mult, op1=ALU.add),
                                     reads=[k_ + 'c'], writes=[k_ + 'c'])
                                S.op('gpsimd', L('tensor_tensor', out=sm[j][:, 2:3], in0=sm[j][:, 2:3], in1=mhalfB[:], op=ALU.pow),
                                     reads=[k_ + 'c', 'mhalfB'], writes=[k_ + 'c'])
                                S.op('vector', L('tensor_scalar', out=of[j][:], in0=of[j][:], scalar1=sm[j][:, 2:3], scalar2=None, op0=ALU.mult),
                                     reads=['of%d' % j, k_ + 'c'], writes=['of%d' % j])
                                S.op('vector', L('tensor_tensor', out=oa_t[j][:], in0=of[j][:], in1=gsub[:], op=ALU.mult),
                                     reads=['of%d' % j, 'gsub'], writes=['oa_t%d' % j])
                                S.dma('gpsimd', 'b_so%d' % j, L('dma_start', out=oa_s[qrow:qrow + 128, h * 128:(h + 1) * 128], in_=oa_t[j][:]),
                                      reads=['oa_t%d' % j])
                        attention('d', h, slot, 0, 64, 128, 0.125, VA[slot], 'VA%d' % slot, grp, fin0)
                        attention('d', h, slot, 64, 128, 128, 0.125, VA[slot], 'VA%d' % slot, grp, fin1)

                mheads = cfg.get('b_mheads', list(range(16)))
                for hi, h in enumerate(mheads):
                    slot = hi % 2
                    S.dma('sync', 'b_q%d' % slot, L('dma_start', out=QT[slot][0:96, :], in_=mqT_s[:, h, :]), writes=['QT%d' % slot])
                    S.dma('sync', 'b_k%d' % slot, L('dma_start', out=KT[slot][0:64, :], in_=mkT_s[(h % 2) * 64:(h % 2) * 64 + 64, h // 2, :]),
                          writes=['KT%d' % slot])
                    if hi < 2:
                        S.dma('sync', 'b_kpe%d' % slot, L('dma_start', out=KT[slot][64:96, :], in_=mkpeT_s.ap()), writes=['KT%d_pe' % slot])
                    for t0 in range(0, NKT, 13):
                        t1 = min(t0 + 13, NKT)
                        S.dma('sync', 'b_vm%d' % slot, L('dma_start', out=VM[slot][:, t0:t1, 0:64],
                                                          in_=mv_s.ap().rearrange("(t p) f -> p t f", p=128)[:, t0:t1, h * 64:(h + 1) * 64]),
                              writes=['VM%d' % slot])
                    for grp in groups:
                        def finm(items, h=h):
                            for (j, qrow) in items:
                                S.op('vector', L('tensor_copy', out=o1[j][:, 0:65], in_=oacc[j][:, 0:65]), reads=['oacc%d' % j], writes=['o1_%d' % j])
                            for (j, qrow) in items:
                                k_ = 'smB%d' % j
                                S.op('vector', L('reciprocal', out=sm[j][:, 0:1], in_=o1[j][:, 64:65]), reads=['o1_%d' % j], writes=[k_ + 'a'])
                                S.op('vector', L('tensor_scalar', out=ob_t[j][:], in0=o1[j][:, 0:64], scalar1=sm[j][:, 0:1], scalar2=None, op0=ALU.mult),
                                     reads=['o1_%d' % j, k_ + 'a'], writes=['ob_t%d' % j])
                                S.dma('gpsimd', 'b_sb%d' % j, L('dma_start', out=ob_s[qrow:qrow + 128, h * 64:(h + 1) * 64], in_=ob_t[j][:]),
                                      reads=['ob_t%d' % j])
                        attention('m', h, slot, 0, 96, 64, 96.0 ** -0.5, VM[slot], 'VM%d' % slot, grp, finm)

                while p_state[1] < n_ecB:
                    p_iter()
                S.end_phase()
                ensure_sems()
                S.emit(nc, sems)

        if 'C' in phases:
            with ExitStack() as es:
                def sb(name, shape, dt):
                    return es.enter_context(nc.sbuf_tensor(name, list(shape), dt))

                def ps(name, shape, dt):
                    return es.enter_context(nc.psum_tensor(name, list(shape), dt))
                S.single = ()
                S.banks = ('pT0', 'pT1', 'pacc0', 'pacc1', 'pacc2', 'pacc3', 'psS0', 'psS1')
                NB = 4
                wa_sb = sb("wa_sb", [128, 8, D], BF16); wb_sb = sb("wb_sb", [128, 8, D], BF16)
                wo_sb = sb("wo_sb", [128, 8, D], BF16); wq_sb = sb("wq_sb", [128, 8, D], BF16)
                identf = sb("identfC", [128, 128], F32); identb = sb("identbC", [128, 128], BF16)
                gffn = sb("gffn", [128, D], F32); gfin = sb("gfin", [128, D], F32)
                keys_f = sb("keys_f", [128, 16, 64], F32); keys_b = sb("keys_b", [128, 16, 64], BF16)
                keysT = sb("keysT", [128, 8, 128], BF16)
                iota_t = sb("iota_t", [128, 256], F32)
                thr = sb("thr", [128, 16], F32)
                oa_t = sb("oa_tC", [128, D], BF16); ob_t = sb("ob_tC", [128, D], BF16)
                gt = sb("gtC", [128, 2 * D], BF16); xt = sb("xtC", [128, D], F32)
                oaT = sb("oaT", [128, 8, 128], BF16); obT = sb("obT", [128, 8, 128], BF16)
                m1 = sb("m1", [128, D], F32); m2 = sb("m2", [128, D], F32); mb = sb("mb", [128, D], BF16)
                mT = sb("mT", [128, 8, 128], BF16)
                h2f = sb("h2f", [128, D], F32); h2b = sb("h2b", [128, D], BF16)
                pq_b = sb("pq_b", [128, D], BF16); pqT = sb("pqT", [128, 8, 128], BF16)
                S_all = sb("S_all", [128, 16, 128], F32); S_wk = sb("S_wk", [128, 256], F32)
                s_top = sb("s_top", [128, 16, 16], F32); i_top = sb("i_top", [128, 16, 16], U32); i_topf = sb("i_topf", [128, 16, 16], F32)
                cand = sb("cand", [128, 256], F32)
                c_top = sb("c_top", [128, 16], F32); c_pos = sb("c_pos", [128, 16], U32); c_posf = sb("c_posf", [128, 16], F32)
                t3 = sb("t3", [128, 16, 16], F32)
                av = sb("av", [128, 16], F32); bv = sb("bv", [128, 16], F32)
                i1v = sb("i1v", [128, 16], F32); i2v = sb("i2v", [128, 16], F32)
                eidf = sb("eidf", [128, 128], F32); eidi = sb("eidi", [128, 128], I32)
                g_all = sb("g_all", [128, 128], F32)
                smc = sb("smc", [128, 16], F32)
                act = sb("act", [128, 128], F32); ga = sb("ga", [128, 128], F32)
                i1_all = sb("i1_all", [128, 128], F32); i2_all = sb("i2_all", [128, 128], F32)
                rT = sb("rT", [128, 3, 128], F32)
                junkb = sb("junkbC", [128, D], BF16)
                pT = [ps("pT%d" % i, [128, 8, 128], BF16) for i in range(2)]
                pacc = [ps("pacc%d" % i, [128, 512], F32) for i in range(4)]
                psS = [ps("psS%d" % i, [128, 4, 128], F32) for i in range(2)]

                for (wsb, wsrc, wk) in [(wa_sb, w_ba, 'wa'), (wb_sb, w_bb, 'wb'), (wo_sb, w_o, 'wo'), (wq_sb, peer_wq, 'wq')]:
                    for kc in range(8):
                        S.dma('gpsimd', 'c_' + wk, L('dma_start', out=wsb[:, kc, :], in_=wsrc[kc * 128:(kc + 1) * 128, :]), writes=[wk])
                S.dma('sync', 'c_identf', L('dma_start', out=identf[:], in_=c_ident.ap()), writes=['identf'])
                S.dma('sync', 'c_gffn', L('dma_start', out=gffn[:], in_=bass.AP(norm_ffn, 0, [[0, 128], [1, D]])), writes=['gffn'])
                S.dma('sync', 'c_gfin', L('dma_start', out=gfin[:], in_=bass.AP(norm_final, 0, [[0, 128], [1, D]])), writes=['gfin'])
                S.dma('sync', 'c_keys', L('dma_start', out=keys_f[:], in_=peer_keys.ap().rearrange("a n d -> n a d")), writes=['keys_f'])
                S.dma('sync', 'c_iota', L('dma_start', out=iota_t[:], in_=c_iota.ap()), writes=['iota'])
                S.op('vector', L('tensor_copy', out=identb[:], in_=identf[:]), reads=['identf'], writes=['identb'])
                S.op('vector', L('tensor_copy', out=keys_b[:], in_=keys_f[:]), reads=['keys_f'], writes=['keys_b'])
                S.op('vector', L('tensor_scalar', out=thr[:], in0=iota_t[:, 0:16], scalar1=16.0, scalar2=None, op0=ALU.mult),
                     reads=['iota'], writes=['thr'])
                for h in range(8):
                    S.op('tensor', L('transpose', out=pT[0][:, h, :], in_=keys_b[:, 2 * h:2 * h + 2, :].rearrange("p a d -> p (a d)"), identity=identb[:]),
                         reads=['keys_b', 'identb'], writes=['pT0_%d' % h])
                S.op('vector', L('tensor_copy', out=keysT[:], in_=pT[0][:]), reads=['pT0_%d' % h for h in range(8)], writes=['keysT'])

                def transposes8(src, skey, pti, dst, dkey, eng):
                    for j in range(8):
                        S.op('tensor', L('transpose', out=pT[pti][:, j, :], in_=src[:, j * 128:(j + 1) * 128], identity=identb[:]),
                             reads=list(skey) + ['identb'], writes=['pT%d_%d' % (pti, j)])
                    rk = ['pT%d_%d' % (pti, j) for j in range(8)]
                    if eng == 'scalar':
                        S.op('scalar', L('activation', out=dst[:], in_=pT[pti][:], func=AF.Copy), reads=rk, writes=[dkey])
                    else:
                        S.op('vector', L('tensor_copy', out=dst[:], in_=pT[pti][:]), reads=rk, writes=[dkey])

                def proj(srcT, skey, wsb, wk, c, pb):
                    for kc in range(8):
                        S.op('tensor', L('matmul', pacc[pb][:], lhsT=srcT[:, kc, :], rhs=wsb[:, kc, c * 512:(c + 1) * 512],
                                         start=(kc == 0), stop=(kc == 7)), reads=[skey, wk], writes=['pacc%d' % pb])

                def rstd_c(col, key, n):
                    S.op('vector', L('tensor_scalar', out=smc[:, col:col + 1], in0=smc[:, col:col + 1], scalar1=1.0 / n, scalar2=EPS,
                                     op0=ALU.mult, op1=ALU.add), reads=[key], writes=[key])
                    S.op('scalar', L('activation', out=smc[:, col:col + 1], in_=smc[:, col:col + 1], func=AF.Sqrt), reads=[key], writes=[key])
                    S.op('vector', L('reciprocal', out=smc[:, col:col + 1], in_=smc[:, col:col + 1]), reads=[key], writes=[key])

                def top16(src_ap, skey, vals, vkey, idxs, ikey):
                    S.op('vector', L('max', out=vals[:, 0:8], in_=src_ap), reads=[skey], writes=[vkey + 'a'])
                    S.op('vector', L('max_index', out=idxs[:, 0:8], in_max=vals[:, 0:8], in_values=src_ap), reads=[skey, vkey + 'a'], writes=[ikey + 'a'])
                    n = src_ap.shape[-1]
                    S.op('vector', L('match_replace', out=S_wk[:, 0:n], in_to_replace=vals[:, 0:8], in_values=src_ap, imm_value=-1e30),
                         reads=[skey, vkey + 'a'], writes=['S_wk'])
                    S.op('vector', L('max', out=vals[:, 8:16], in_=S_wk[:, 0:n]), reads=['S_wk'], writes=[vkey + 'b'])
                    S.op('vector', L('max_index', out=idxs[:, 8:16], in_max=vals[:, 8:16], in_values=S_wk[:, 0:n]), reads=['S_wk', vkey + 'b'], writes=[ikey + 'b'])

                ctiles = [('p', i) for i in range(32)] + [('s', 0)]
                ctiles = cfg.get('c_tiles', ctiles)
                mhalf = sb("mhalf", [128, 1], F32)
                S.op('gpsimd', L('memset', mhalf[:], -0.5), writes=['mhalf'])
                x2d = [sb("x2d%d" % i, [128, D], F32) for i in range(2)]
                h2Td = [sb("h2Td%d" % i, [128, 8, 128], BF16) for i in range(2)]
                S_alld = [S_all, sb("S_all1", [128, 16, 128], F32)]
                c_top_all = sb("c_top_all", [128, 8, 16], F32); c_pos_all = sb("c_pos_all", [128, 8, 16], U32)
                c_posf_all = sb("c_posf_all", [128, 128], F32)
                t3b = sb("t3b", [128, 128, 16], F32)
                av_all = sb("av_all", [128, 128], F32); bv_all = sb("bv_all", [128, 128], F32)
                zsum = sb("zsum", [128, 8], F32)
                psX = psS[1]
                S.banks = ('pT0', 'pT1', 'pacc0', 'pacc1', 'pacc2', 'pacc3', 'psS0', 'psS1')

                def t8(src, skey, dst, dkey):
                    for j in range(8):
                        S.op('tensor', L('transpose', out=pT[0][:, j, :], in_=src[:, j * 128:(j + 1) * 128], identity=identb[:]),
                             reads=list(skey) + ['identb'], writes=['pT0_%d' % j])
                    S.op('scalar', L('activation', out=dst[:], in_=pT[0][:], func=AF.Copy), reads=['pT0_%d' % j for j in range(8)], writes=[dkey])

                def front(ti):
                    kind, i = ctiles[ti]
                    sl = ti % 2
                    if kind == 'p':
                        rows = 128; qrow = i * 128; xsrc = xp[qrow:qrow + 128, :]
                    else:
                        rows = 64; qrow = SP; xsrc = xs.ap()
                        S.op('gpsimd', L('memset', xt[:], 0.0), writes=['xt'])
                    S.dma('sync', 'cl_x', L('dma_start', out=xt[0:rows, :], in_=xsrc), writes=['xt'])
                    S.dma('sync', 'cl_oa', L('dma_start', out=oa_t[:], in_=oa_s[qrow:qrow + 128, :]), writes=['oa_t'])
                    S.dma('sync', 'cl_ob', L('dma_start', out=ob_t[:], in_=ob_s[qrow:qrow + 128, :]), writes=['ob_t'])
                    S.dma('sync', 'cl_g', L('dma_start', out=gt[:], in_=gates_s[qrow:qrow + 128, :]), writes=['gt'])
                    t8(oa_t, ['oa_t'], oaT, 'oaT')
                    t8(ob_t, ['ob_t'], obT, 'obT')
                    for (srcT, skey, wsb, wk, dst, dkey, pb0) in [(oaT, 'oaT', wa_sb, 'wa', m1, 'm1', 0), (obT, 'obT', wb_sb, 'wb', m2, 'm2', 2)]:
                        for c in range(2):
                            proj(srcT, skey, wsb, wk, c, pb0 + c)
                            S.op('scalar', L('activation', out=dst[:, c * 512:(c + 1) * 512], in_=pacc[pb0 + c][:], func=AF.Copy),
                                 reads=['pacc%d' % (pb0 + c)], writes=['%s_%d' % (dkey, c)])
                    S.op('gpsimd', L('tensor_tensor', out=m1[:], in0=m1[:], in1=gt[:, 0:D], op=ALU.mult), reads=['m1_0', 'm1_1', 'gt'], writes=['m1_0', 'm1_1'])
                    S.op('gpsimd', L('tensor_tensor', out=m2[:], in0=m2[:], in1=gt[:, D:2 * D], op=ALU.mult), reads=['m2_0', 'm2_1', 'gt'], writes=['m2_0', 'm2_1'])
                    S.op('gpsimd', L('tensor_tensor', out=mb[:], in0=m1[:], in1=m2[:], op=ALU.add),
                         reads=['m1_0', 'm1_1', 'm2_0', 'm2_1'], writes=['mb'])
                    t8(mb, ['mb'], mT, 'mT')
                    x2 = x2d[sl]
                    for c in range(2):
                        proj(mT, 'mT', wo_sb, 'wo', c, c)
                        S.op('scalar', L('activation', out=x2[:, c * 512:(c + 1) * 512], in_=pacc[c][:], func=AF.Copy),
                             reads=['pacc%d' % c], writes=['x2d%d_%d' % (sl, c)])
                    x2k = ['x2d%d_0' % sl, 'x2d%d_1' % sl]
                    S.op('gpsimd', L('tensor_tensor', out=x2[:], in0=x2[:], in1=xt[:], op=ALU.add), reads=x2k + ['xt'], writes=x2k)
                    S.op('scalar', L('activation', out=junkb[:], in_=x2[:], func=AF.Square, accum_out=smc[:, 0:1]), reads=x2k, writes=['junkb', 'smc0'])
                    S.op('gpsimd', L('tensor_scalar', out=smc[:, 0:1], in0=smc[:, 0:1], scalar1=1.0 / D, scalar2=EPS, op0=ALU.mult, op1=ALU.add),
                         reads=['smc0'], writes=['smc0'])
                    S.op('gpsimd', L('tensor_tensor', out=smc[:, 0:1], in0=smc[:, 0:1], in1=mhalf[:], op=ALU.pow), reads=['smc0', 'mhalf'], writes=['smc0'])
                    S.op('scalar', L('activation', out=h2f[:], in_=x2[:], func=AF.Copy, scale=smc[:, 0:1]), reads=x2k + ['smc0'], writes=['h2f'])
                    S.op('gpsimd', L('tensor_tensor', out=h2b[:], in0=h2f[:], in1=gffn[:], op=ALU.mult), reads=['h2f', 'gffn'], writes=['h2b'])
                    t8(h2b, ['h2b'], h2Td[sl], 'h2Td%d' % sl)
                    for c in range(2):
                        proj(h2Td[sl], 'h2Td%d' % sl, wq_sb, 'wq', c, 2 + c)
                        S.op('scalar', L('activation', out=pq_b[:, c * 512:(c + 1) * 512], in_=pacc[2 + c][:], func=AF.Copy),
                             reads=['pacc%d' % (2 + c)], writes=['pq_b%d' % c])
                    t8(pq_b, ['pq_b0', 'pq_b1'], pqT, 'pqT')
                    for q4 in range(4):
                        c = q4 % 2; h0 = (q4 // 2) * 4
                        bank, bkey = (psS[0], 'psS0') if c == 0 else (pacc[3].rearrange("p (a b) -> p a b", a=4), 'pacc3')
                        for i4 in range(4):
                            h = h0 + i4
                            S.op('tensor', L('matmul', bank[:, i4, :], lhsT=pqT[c * 64:(c + 1) * 64, h, :], rhs=keysT[c * 64:(c + 1) * 64, h, :],
                                             start=True, stop=True), reads=['pqT', 'keysT'], writes=[bkey + ('_%d' % i4 if c == 0 else '')])
                        lo = 2 * h0 + c
                        S.op('scalar', L('activation', out=S_alld[sl][:, lo:min(lo + 8, 16):2, :], in_=bank[:, :, :], func=AF.Copy),
                             reads=([bkey + '_%d' % i4 for i4 in range(4)] if c == 0 else [bkey]),
                             writes=['S_all%d_hc%d' % (sl, lo + 2 * i4) for i4 in range(4)])

                def back(ti):
                    sl = ti % 2
                    Sa = S_alld[sl]
                    for hc in range(16):
                        top16(Sa[:, hc, :], 'S_all%d_hc%d' % (sl, hc), s_top[:, hc, :], 's_top%d' % hc, i_top[:, hc, :], 'i_top%d' % hc)
                    S.op('vector', L('tensor_copy', out=i_topf[:], in_=i_top[:]),
                         reads=['i_top%d%s' % (hc, ab) for hc in range(16) for ab in 'ab'], writes=['i_topf'])
                    pstr = 16 * 16
                    for h in range(8):
                        stk = ['s_top%d%s' % (hc, ab) for hc in (2 * h, 2 * h + 1) for ab in 'ab']
                        in0 = bass.AP(s_top, (2 * h) * 16, [[pstr, 128], [1, 16], [0, 16]])
                        in1 = bass.AP(s_top, (2 * h + 1) * 16, [[pstr, 128], [0, 16], [1, 16]])
                        S.op('vector', L('tensor_tensor', out=cand[:].rearrange("p (a b) -> p a b", a=16), in0=in0, in1=in1, op=ALU.add),
                             reads=stk, writes=['cand'])
                        top16(cand[:], 'cand', c_top_all[:, h, :], 'c_top%d' % h, c_pos_all[:, h, :], 'c_pos%d' % h)
                    ctk = ['c_top%d%s' % (h, ab) for h in range(8) for ab in 'ab']
                    cpk = ['c_pos%d%s' % (h, ab) for h in range(8) for ab in 'ab']
                    S.op('vector', L('tensor_copy', out=c_posf_all[:], in_=c_pos_all[:].rearrange("p h k -> p (h k)")), reads=cpk, writes=['c_posf_all'])
                    S.op('vector', L('tensor_tensor', out=t3b[:, :, 0:15], in0=bass.AP(c_posf_all, 0, [[128, 128], [1, 128], [0, 15]]),
                                     in1=bass.AP(thr, 1, [[16, 128], [0, 128], [1, 15]]), op=ALU.is_ge), reads=['c_posf_all', 'thr'], writes=['t3b'])
                    S.op('vector', L('reduce_sum', out=av_all[:], in_=t3b[:, :, 0:15], axis=mybir.AxisListType.X), reads=['t3b'], writes=['av_all'])
                    S.op('vector', L('scalar_tensor_tensor', out=bv_all[:], in0=av_all[:], scalar=-16.0, in1=c_posf_all[:], op0=ALU.mult, op1=ALU.add),
                         reads=['av_all', 'c_posf_all'], writes=['bv_all'])
                    for (sel, skey, c, dst, dkey) in [(av_all, 'av_all', 0, i1_all, 'i1_all'), (bv_all, 'bv_all', 1, i2_all, 'i2_all')]:
                        S.op('vector', L('tensor_tensor', out=t3b[:], in0=bass.AP(sel, 0, [[128, 128], [1, 128], [0, 16]]),
                                         in1=bass.AP(iota_t, 0, [[256, 128], [0, 128], [1, 16]]), op=ALU.is_equal), reads=[skey, 'iota'], writes=['t3b'])
                        S.op('vector', L('tensor_tensor', out=t3b[:].rearrange("p (h k) a -> p h k a", h=8), in0=t3b[:].rearrange("p (h k) a -> p h k a", h=8),
                                         in1=bass.AP(i_topf, c * 16, [[pstr, 128], [32, 8], [0, 16], [1, 16]]), op=ALU.mult),
                             reads=['t3b', 'i_topf'], writes=['t3b'])
                        S.op('vector', L('reduce_sum', out=dst[:], in_=t3b[:], axis=mybir.AxisListType.X), reads=['t3b'], writes=[dkey])
                    S.op('vector', L('tensor_tensor', out=g_all[:].rearrange("p (h k) -> p h k", h=8), in0=c_top_all[:],
                                     in1=bass.AP(c_top_all, 0, [[128, 128], [16, 8], [0, 16]]), op=ALU.subtract), reads=ctk, writes=['g_all'])
                    S.op('scalar', L('activation', out=g_all[:], in_=g_all[:], func=AF.Exp), reads=['g_all'], writes=['g_all'])
                    S.op('vector', L('reduce_sum', out=zsum[:], in_=g_all[:].rearrange("p (h k) -> p h k", h=8), axis=mybir.AxisListType.X),
                         reads=['g_all'], writes=['zsum'])
                    S.op('vector', L('reciprocal', out=zsum[:], in_=zsum[:]), reads=['zsum'], writes=['zsum'])
                    S.op('vector', L('tensor_tensor', out=g_all[:].rearrange("p (h k) -> p h k", h=8), in0=g_all[:].rearrange("p (h k) -> p h k", h=8),
                                     in1=bass.AP(zsum, 0, [[8, 128], [1, 8], [0, 16]]), op=ALU.mult), reads=['g_all', 'zsum'], writes=['g_all'])

                def export(ti):
                    kind, i = ctiles[ti]
                    sl = ti % 2
                    qrow = i * 128 if kind == 'p' else SP
                    for ri, (src, key_) in enumerate([(i1_all, 'i1_all'), (i2_all, 'i2_all'), (g_all, 'g_all')]):
                        S.op('tensor', L('transpose', out=psX[:, ri, :], in_=src[:], identity=identf[:]),
                             reads=[key_, 'identf'], writes=['psS1_%d' % ri])
                    S.op('scalar', L('activation', out=rT[:], in_=psX[:, 0:3, :], func=AF.Copy), reads=['psS1_0', 'psS1_1', 'psS1_2'], writes=['rT'])
                    S.dma('sync', 'cs_r', L('dma_start', out=r_s[:, :, qrow:qrow + 128], in_=rT[:]), reads=['rT'])
                    S.dma('sync', 'cs_x2', L('dma_start', out=x2_s[qrow:qrow + 128, :], in_=x2d[sl][:]), reads=['x2d%d_0' % sl, 'x2d%d_1' % sl])
                    S.dma('sync', 'cs_h2', L('dma_start', out=h2T_s[:, :, qrow:qrow + 128], in_=h2Td[sl][:]), reads=['h2Td%d' % sl])

                if ctiles:
                    front(0)
                for ti in range(len(ctiles)):
                    if ti + 1 < len(ctiles):
                        front(ti + 1)
                    back(ti)
                    export(ti)

                S.end_phase()
                ensure_sems()
                S.emit(nc, sems)

        if 'C' in phases and DENSE:
            n_ec = cfg.get('n_ec', 128)
            with ExitStack() as es:
                def sb(name, shape, dt):
                    return es.enter_context(nc.sbuf_tensor(name, list(shape), dt))

                def ps(name, shape, dt):
                    return es.enter_context(nc.psum_tensor(name, list(shape), dt))
                S.single = ()
                S.banks = ('ppT0', 'ppT1')
                identf = sb("identfP", [128, 128], F32); identb = sb("identbP", [128, 128], BF16)
                ub = [sb("ub%d" % i, [128, D], BF16) for i in range(3)]
                vb = [sb("vbD%d" % i, [128, D], BF16) for i in range(3)]
                uT = [sb("uT%d" % i, [128, 8, 128], BF16) for i in range(3)]
                ppT = [ps("ppT%d" % i, [128, 8, 128], BF16) for i in range(2)]
                S.dma('sync', 'p_identf', L('dma_start', out=identf[:], in_=c_ident.ap()), writes=['identf'])
                S.op('vector', L('tensor_copy', out=identb[:], in_=identf[:]), reads=['identf'], writes=['identb'])
                for ec in range(0 if ('B' in phases and cfg.get('p_in_b', True)) else n_ec):
                    s2 = ec % 3
                    pp = ec % 2
                    S.dma('gpsimd', 'p_u%d' % s2, L('dma_start', out=ub[s2][:], in_=peer_u[ec * 128:(ec + 1) * 128, :]), writes=['ub%d' % s2])
                    S.dma('gpsimd', 'p_v%d' % s2, L('dma_start', out=vb[s2][:], in_=peer_v[ec * 128:(ec + 1) * 128, :]), writes=['vb%d' % s2])
                    for dc in range(8):
                        S.op('tensor', L('transpose', out=ppT[pp][:, dc, :], in_=ub[s2][:, dc * 128:(dc + 1) * 128], identity=identb[:]),
                             reads=['ub%d' % s2, 'identb'], writes=['ppT%d_%d' % (pp, dc)])
                    if ec % 2 == 0:
                        S.op('vector', L('tensor_copy', out=uT[s2][:], in_=ppT[pp][:]), reads=['ppT%d_%d' % (pp, dc) for dc in range(8)], writes=['uT%d' % s2])
                    else:
                        S.op('scalar', L('activation', out=uT[s2][:], in_=ppT[pp][:], func=AF.Copy), reads=['ppT%d_%d' % (pp, dc) for dc in range(8)], writes=['uT%d' % s2])
                    S.dma('sync', 'p_su%d' % s2, L('dma_start', out=UT_s[ec // 2][:, ec % 2, :].rearrange("p (a b) -> p a b", a=8), in_=uT[s2][:]),
                          reads=['uT%d' % s2], writes=['UT_s'])
                    S.dma('sync', 'p_sv%d' % s2, L('dma_start', out=V_s[ec // 2][:, ec % 2, :], in_=vb[s2][:]), reads=['vb%d' % s2], writes=['V_s'])
                S.end_phase()
                ensure_sems()
                S.emit(nc, sems)
            with ExitStack() as es:
                def sb(name, shape, dt):
                    return es.enter_context(nc.sbuf_tensor(name, list(shape), dt))

                def ps(name, shape, dt):
                    return es.enter_context(nc.psum_tensor(name, list(shape), dt))
                S.single = ()
                S.banks = ('pact0', 'pact1', 'pout0', 'pout1', 'pout2', 'pout3', 'pout4', 'pout5', 'pw0', 'pw1')
                NTT = TG // 128
                gfin = sb("gfinD", [128, D], F32)
                iota_b = sb("iota_b", [128, 128], F32)
                NBUF = 4
                utp = [sb("utp%d" % i, [128, 2, 8, 128], BF16) for i in range(NBUF)]
                vtp = [sb("vtp%d" % i, [128, 2, D], BF16) for i in range(NBUF)]
                h2g = [sb("h2g%d" % i, [128, 8, TG], BF16) for i in range(2)]
                rg = [sb("rg%d" % i, [128, 3, TG], F32) for i in range(2)]
                oh2 = [sb("oh2_%d" % i, [128, 128], BF16) for i in range(2)]
                oh1 = [sb("oh1_%d" % i, [128, 128], BF16) for i in range(2)]
                WT = [sb("WT%d" % i, [128, TG, 128], BF16) for i in range(2)]
                gl = [sb("gl%d" % i, [128, TG], BF16) for i in range(2)]
                gad = [sb("gad%d" % i, [128, TG], BF16) for i in range(2)]
                x2t = sb("x2t", [128, D], F32); accd = sb("accd", [128, D], F32); ytd = sb("ytd", [128, D], F32)
                junkd = sb("junkd", [128, D], BF16); smd = sb("smd", [128, 4], F32)
                npout = 2 * NTT
                pact = [ps("pact%d" % i, [128, 512], F32) for i in range(2 if npout <= 4 else 1)]
                pout = [ps("pout%d" % i, [128, 512], F32) for i in range(npout)]
                pw = [ps("pw%d" % i, [128, 4, 128], F32) for i in range(1)]

                S.dma('sync', 'd_gfin', L('dma_start', out=gfin[:], in_=bass.AP(norm_final, 0, [[0, 128], [1, D]])), writes=['gfin'])
                S.dma('sync', 'd_iota', L('dma_start', out=iota_b[:], in_=c_iota[:, 0:128]), writes=['iota_b'])
                dgroups = [(g * TG, TG, [('p', g * TG + j * 128) for j in range(NTT)]) for g in range(SP // TG)] + [(SP, 128, [('s', SP)])]
                dgroups = cfg.get('d_groups', dgroups)
                pairs_total = len(dgroups) * (n_ec // 2)
                issued = [0]

                def ensure_loaded(upto):
                    while issued[0] < min(upto, pairs_total):
                        gp = issued[0]; p = gp % (n_ec // 2); b = gp % NBUF
                        S.dma('sync', 'd_ut%d' % b, L('dma_start', out=utp[b][:].rearrange("p j a b -> p j (a b)"), in_=UT_s[p]), writes=['utp%d' % b])
                        S.dma('sync', 'd_vt%d' % b, L('dma_start', out=vtp[b][:], in_=V_s[p]), writes=['vtp%d' % b])
                        issued[0] += 1

                def load_group(gi):
                    q0, tg, _ = dgroups[gi]
                    w = gi % 2
                    S.dma('sync', 'd_h2_%d' % w, L('dma_start', out=h2g[w][:, :, 0:tg], in_=h2T_s[:, :, q0:q0 + tg]), writes=['h2g%d' % w])
                    S.dma('sync', 'd_r%d' % w, L('dma_start', out=rg[w][:, :, 0:tg], in_=r_s[:, :, q0:q0 + tg]), writes=['rg%d' % w])

                def wbuild(gi, t):
                    w = gi % 2
                    o = t % 2
                    S.op('vector', L('tensor_scalar', out=oh2[o][:], in0=iota_b[:], scalar1=rg[w][:, 1, t:t + 1], scalar2=None, op0=ALU.is_equal),
                         reads=['iota_b', 'rg%d' % w], writes=['oh2_%d' % o])
                    S.op('vector', L('tensor_scalar', out=oh1[o][:], in0=iota_b[:], scalar1=rg[w][:, 0, t:t + 1], scalar2=rg[w][:, 2, t:t + 1],
                                     op0=ALU.is_equal, op1=ALU.mult), reads=['iota_b', 'rg%d' % w], writes=['oh1_%d' % o])
                    S.op('tensor', L('matmul', pw[0][:, t % 4, :], lhsT=oh2[o][:], rhs=oh1[o][:], start=True, stop=True),
                         reads=['oh2_%d' % o, 'oh1_%d' % o], writes=['pw0_%d' % (t % 4)])
                    if t % 4 == 3:
                        S.op('scalar', L('activation', out=WT[w][:, t - 3:t + 1, :], in_=pw[0][:], func=AF.Copy),
                             reads=['pw0_%d' % j for j in range(4)], writes=['WT%d' % w])

                if dgroups:
                    ensure_loaded(3)
                    load_group(0)
                    for t in range(dgroups[0][1]):
                        wbuild(0, t)
                for gi, (q0, tg, ttiles) in enumerate(dgroups):
                    ntt = tg // 128
                    w = gi % 2
                    nxt = gi + 1 if gi + 1 < len(dgroups) else None
                    if nxt is not None:
                        load_group(nxt)
                        ntok_next = dgroups[nxt][1]
                        per = (ntok_next + n_ec - 1) // n_ec
                    wb_t = [0]

                    def act_mm(ec, gi=gi, w=w, tg=tg):
                        gp = gi * (n_ec // 2) + ec // 2
                        b = gp % NBUF; j = ec % 2
                        pa = ec % len(pact)
                        for dc in range(8):
                            S.op('tensor', L('matmul', pact[pa][:, 0:tg], lhsT=utp[b][:, j, dc, :], rhs=h2g[w][:, dc, 0:tg], start=(dc == 0), stop=(dc == 7)),
                                 reads=['utp%d' % b, 'h2g%d' % w], writes=['pact%d' % pa])
                        return (b, j)
                    pend_b = act_mm(0) if n_ec > 0 else None
                    for ec in range(n_ec):
                        b = pend_b
                        pa = ec % len(pact)
                        gb = ec % 2
                        S.op('scalar', L('activation', out=gl[gb][:, 0:tg], in_=pact[pa][:, 0:tg], func=AF.Gelu), reads=['pact%d' % pa], writes=['gl%d' % gb])
                        S.op('vector', L('tensor_tensor', out=gad[gb][:, 0:tg], in0=gl[gb][:, 0:tg], in1=WT[w][:, 0:tg, ec], op=ALU.mult),
                             reads=['gl%d' % gb, 'WT%d' % w], writes=['gad%d' % gb])
                        if ec + 1 < n_ec:
                            pend_b = act_mm(ec + 1)
                        for tt in range(ntt):
                            for dh in range(2):
                                S.op('tensor', L('matmul', pout[tt * 2 + dh][:], lhsT=gad[gb][:, tt * 128:(tt + 1) * 128], rhs=vtp[b[0]][:, b[1], dh * 512:(dh + 1) * 512],
                                                 start=(ec == 0), stop=(ec == n_ec - 1)),
                                     reads=['gad%d' % gb, 'vtp%d' % b[0]], writes=['pout%d' % (tt * 2 + dh)])
                        if ec % 2 == 1:
                            ensure_loaded(gi * (n_ec // 2) + ec // 2 + 4)
                        if nxt is not None:
                            for _ in range(per):
                                if wb_t[0] < ntok_next:
                                    wbuild(nxt, wb_t[0])
                                    wb_t[0] += 1
                    if nxt is not None:
                        while wb_t[0] < ntok_next:
                            wbuild(nxt, wb_t[0])
                            wb_t[0] += 1
                    for tt, (kind, qrow) in enumerate(ttiles):
                        rows = 128 if kind == 'p' else 64
                        ydst = yp[qrow:qrow + 128, :] if kind == 'p' else ys.ap()
                        S.dma('sync', 'd_x2', L('dma_start', out=x2t[:], in_=x2_s[qrow:qrow + 128, :]), writes=['x2t'])
                        for dh in range(2):
                            S.op('vector', L('tensor_tensor', out=accd[:, dh * 512:(dh + 1) * 512], in0=pout[tt * 2 + dh][:], in1=x2t[:, dh * 512:(dh + 1) * 512], op=ALU.add),
                                 reads=['pout%d' % (tt * 2 + dh), 'x2t'], writes=['accd%d' % dh])
                        S.op('scalar', L('activation', out=junkd[:], in_=accd[:], func=AF.Square, accum_out=smd[:, 0:1]),
                             reads=['accd0', 'accd1'], writes=['junkd', 'smd0'])
                        S.op('vector', L('tensor_scalar', out=smd[:, 0:1], in0=smd[:, 0:1], scalar1=1.0 / D, scalar2=EPS, op0=ALU.mult, op1=ALU.add),
                             reads=['smd0'], writes=['smd0'])
                        S.op('scalar', L('activation', out=smd[:, 0:1], in_=smd[:, 0:1], func=AF.Sqrt), reads=['smd0'], writes=['smd0'])
                        S.op('vector', L('reciprocal', out=smd[:, 0:1], in_=smd[:, 0:1]), reads=['smd0'], writes=['smd0'])
                        S.op('vector', L('tensor_scalar', out=ytd[:], in0=accd[:], scalar1=smd[:, 0:1], scalar2=None, op0=ALU.mult),
                             reads=['accd0', 'accd1', 'smd0'], writes=['ytd'])
                        S.op('gpsimd', L('tensor_tensor', out=ytd[:], in0=ytd[:], in1=gfin[:], op=ALU.mult), reads=['ytd', 'gfin'], writes=['ytd'])
                        S.dma('sync', 'd_y', L('dma_start', out=ydst, in_=ytd[0:rows, :]), reads=['ytd'])

                S.end_phase()
                ensure_sems()
                S.emit(nc, sems)

    return nc


_CACHE = {}


def _consts():
    if 'c' not in _CACHE:
        oh, maskw = _bias_consts()
        _CACHE['c'] = dict(
            c_ident=np.eye(128, dtype=np.float32),
            c_rope=_rope_table(),
            c_oh=oh, c_maskw=maskw,
            c_iota=np.tile(np.arange(256, dtype=np.float32)[None, :], (128, 1)),
        )
    return _CACHE['c']


def kernel(x_prompt, x_sample, cache_diff_k, cache_diff_v, cache_mla_ckv, cache_mla_kpe,
           rel_bias, norm_mix, w_in, diff_lambda, diff_subln, mla_q_norm, mla_w_uq, mla_kv_norm,
           mla_w_uk, mla_w_uv, w_branch_a, w_branch_b, w_out, norm_ffn, peer_w_q, peer_keys,
           peer_u, peer_v, norm_final):
    f = lambda a: np.ascontiguousarray(np.asarray(a, dtype=np.float32))
    shared = dict(
        rel_bias=f(rel_bias), norm_mix=f(norm_mix).reshape(D), w_in=f(w_in).reshape(D, INW),
        diff_lambda=f(diff_lambda).reshape(256), diff_subln=f(diff_subln).reshape(128),
        mla_q_norm=f(mla_q_norm).reshape(256), w_uq=f(mla_w_uq).reshape(256, 1536),
        mla_kv_norm=f(mla_kv_norm).reshape(128), w_uk=f(mla_w_uk).reshape(128, 1024), w_uv=f(mla_w_uv).reshape(128, 1024),
        w_ba=f(w_branch_a).reshape(D, D), w_bb=f(w_branch_b).reshape(D, D), w_o=f(w_out).reshape(D, D),
        norm_ffn=f(norm_ffn).reshape(D), peer_wq=f(peer_w_q).reshape(D, D),
        peer_keys=f(peer_keys).reshape(16, 128, 64), peer_u=f(peer_u).reshape(16384, D), peer_v=f(peer_v).reshape(16384, D),
        norm_final=f(norm_final).reshape(D),
    )
    shared.update(_consts())
    xpf = f(x_prompt); xsf = f(x_sample)
    cdkf = f(cache_diff_k).reshape(NCORES, PAST, D); cdvf = f(cache_diff_v).reshape(NCORES, PAST, D)
    cckvf = f(cache_mla_ckv).reshape(NCORES, PAST, 128); ckpef = f(cache_mla_kpe).reshape(NCORES, PAST, 32)
    in_maps = []
    for c in range(NCORES):
        m = dict(shared)
        m.update(xp=xpf[c], xs=xsf[c], cdk=cdkf[c], cdv=cdvf[c], cckv=cckvf[c], ckpe=ckpef[c])
        in_maps.append(m)
    nc = build_program()
    res = run_bass_kernel_spmd(nc, in_maps, core_ids=list(range(NCORES)))
    R = res.results

    def g(name, shape):
        return np.stack([np.asarray(R[c][name], dtype=np.float32) for c in range(NCORES)], 0).reshape(shape)
    return (g("yp", (8, SP, D)), g("ys", (8, SS, D)),
            g("kp", (1, 8, SP, 8, 2, 64)), g("vp", (1, 8, SP, 8, 128)), g("cp", (1, 8, SP, 128)), g("ep", (1, 8, SP, 32)),
            g("ks", (1, 8, SS, 8, 2, 64)), g("vs", (1, 8, SS, 8, 128)), g("cs", (1, 8, SS, 128)), g("es", (1, 8, SS, 32)))
```

```python
import math
from contextlib import ExitStack

import numpy as np
import ml_dtypes

import concourse.bass as bass
import concourse.mybir as mybir
from concourse.bass_utils import run_bass_kernel_spmd

F32 = mybir.dt.float32
BF16 = mybir.dt.bfloat16
U32 = mybir.dt.uint32
I32 = mybir.dt.int32
ALU = mybir.AluOpType
AF = mybir.ActivationFunctionType

NCORES = 8
D = 1024
SP = 4096
SS = 64
PAST = 2048
INW = 5536
NQ = SP + 128
NT = SP + PAST + 128
KS0 = SP
EPS = 1e-6
LAMBDA_INIT = 0.8 - 0.6 * math.exp(0.0)


def L(name, *a, **kw):
    return (name, a, kw)


class Sched:
    ENGS = ['tensor', 'vector', 'scalar', 'gpsimd', 'sync']

    def __init__(self):
        self.prog = {e: [] for e in self.ENGS}
        self.cnt = {e: 0 for e in self.ENGS}
        self.waited = {e: {} for e in self.ENGS}
        self.last_w = {}
        self.readers = {}
        self.semkeys = list(self.ENGS)
        self.phys_of = {}
        self.free = {}
        self.phys_q = {}
        self.nphys = 0
        self.nops = 0
        self.single = ()

    banks = ()

    def with_banks(self, reads, writes):
        extra = []
        for k in list(reads) + list(writes):
            for b in self.banks:
                if k.startswith(b):
                    bk = 'BANK:' + b
                    if bk not in extra:
                        extra.append(bk)
                    break
        return list(writes) + extra

    def canon(self, keys):
        out = []
        for k in keys:
            for p in self.single:
                if k.startswith(p + '1'):
                    k = p + '0' + k[len(p) + 1:]
                    break
            out.append(k)
        return out

    def _need(self, eng, tok, same_ok=False):
        if tok is None:
            return
        key, val = tok
        if same_ok and key == eng and eng == 'tensor':
            return
        if self.waited[eng].get(key, 0) >= val:
            return
        self.waited[eng][key] = val
        self.prog[eng].append(('wait', key, val))

    def _deps(self, eng, reads, writes):
        for r in reads:
            self._need(eng, self.last_w.get(r))
        for w in writes:
            self._need(eng, self.last_w.get(w), same_ok=True)
            for tok in self.readers.get(w, ()):
                self._need(eng, tok, same_ok=True)

    def _commit(self, tok, reads, writes):
        for r in reads:
            lst = self.readers.setdefault(r, [])
            lst.append(tok)
            if len(lst) > 48:
                best = {}
                for k, v in lst:
                    best[k] = max(best.get(k, 0), v)
                self.readers[r] = list(best.items())
        for w in writes:
            self.last_w[w] = tok
            self.readers[w] = []

    max_ops = 10 ** 9

    def op(self, eng, fn, reads=(), writes=()):
        if self.nops >= self.max_ops:
            return None
        reads = self.canon(reads); writes = self.with_banks(reads, self.canon(writes))
        self._deps(eng, reads, writes)
        self.cnt[eng] += 1
        tok = (eng, self.cnt[eng])
        self.prog[eng].append(('op', fn, eng, 1))
        self._commit(tok, reads, writes)
        self.nops += 1
        return tok

    def dma(self, q, semkey, fn, reads=(), writes=()):
        if self.nops >= self.max_ops:
            return None
        phys = self.phys_of.get(semkey)
        if phys is None:
            if self.free.get(q):
                phys = self.free[q].pop()
            else:
                phys = 'dma%s%d' % (q[0], self.nphys)
                self.nphys += 1
                self.cnt[phys] = 0
                self.semkeys.append(phys)
            self.phys_of[semkey] = phys
            self.phys_q[phys] = q
        reads = self.canon(reads); writes = self.with_banks(reads, self.canon(writes))
        self._deps(q, reads, writes)
        self.cnt[phys] += 16
        tok = (phys, self.cnt[phys])
        self.prog[q].append(('op', fn, phys, 16))
        self._commit(tok, reads, writes)
        self.nops += 1
        return tok

    def wait_dma(self, eng, semkey):
        phys = self.phys_of.get(semkey)
        if phys is not None:
            self._need(eng, (phys, self.cnt[phys]))

    def end_phase(self):
        for k in sorted(self.phys_of):
            phys = self.phys_of[k]
            self._need('sync', (phys, self.cnt[phys]))
            self.free.setdefault(self.phys_q[phys], []).append(phys)
        self.phys_of = {}

    def emit(self, nc, sems):
        prog = self.prog
        self.prog = {e: [] for e in self.ENGS}
        self.last_w = {}
        self.readers = {}

        def run(engname, e):
            for it in prog[engname]:
                if it[0] == 'wait':
                    e.wait_ge(sems[it[1]], it[2])
                else:
                    _, fn, key, inc = it
                    name, a, kw = fn
                    getattr(e, name)(*a, **kw).then_inc(sems[key], inc)

        with nc.Block() as block:
            @block.tensor
            def _(e):
                run('tensor', e)

            @block.vector
            def _(e):
                run('vector', e)

            @block.scalar
            def _(e):
                run('scalar', e)

            @block.gpsimd
            def _(e):
                run('gpsimd', e)

            @block.sync
            def _(e):
                run('sync', e)


def _rope_table():
    half = 16
    inv = (np.float32(10000.0) ** (-np.arange(half, dtype=np.float32) / np.float32(half))).astype(np.float32)
    pos = np.concatenate([np.arange(SP), PAST + np.arange(128)]).astype(np.float32)
    ang = (pos[:, None] * inv[None, :]).astype(np.float32)
    return np.concatenate([np.cos(ang), np.sin(ang)], axis=1).astype(np.float32)


def _t5_bucket(rel):
    nb = 16
    max_exact = 8
    ret = np.where(rel > 0, nb, 0)
    n = np.abs(rel)
    lg = (np.log(np.maximum(n, 1).astype(np.float32) / np.float32(max_exact)) / np.float32(math.log(128 / max_exact))
          * np.float32(nb - max_exact)).astype(np.float32)
    large = max_exact + lg.astype(np.int32)
    large = np.minimum(large, nb - 1)
    return ret + np.where(n < max_exact, n, large)


def _bias_consts():
    rel = np.arange(383) - 255
    bk = _t5_bucket(rel)
    oh = np.zeros((32, 384), np.float32)
    oh[bk, np.arange(383)] = 1.0
    kk = np.arange(128)[:, None]
    qq = np.arange(256)[None, :]
    maskw = ((kk // 64) <= (qq // 64)).astype(np.float32)
    return oh, maskw


def build_program(phases=('A', 'B', 'C'), cfg=None):
    cfg = cfg or {}
    nc = bass.Bass("TRN2", target_bir_lowering=False)
    S = Sched()
    S.max_ops = cfg.get('max_ops', 10 ** 9)

    def din(name, shape, dt=F32):
        return nc.dram_tensor(name, list(shape), dt, kind="ExternalInput")

    def dout(name, shape, dt=F32):
        return nc.dram_tensor(name, list(shape), dt, kind="ExternalOutput")

    def dscr(name, shape, dt=BF16):
        if cfg.get('dbg_scratch') and name in ('oa_s', 'ob_s', 'gates_s'):
            return nc.dram_tensor(name, list(shape), dt, kind="ExternalOutput")
        return nc.dram_tensor(name, list(shape), dt)

    xp = din("xp", [SP, D]); xs = din("xs", [SS, D])
    cdk = din("cdk", [PAST, D]); cdv = din("cdv", [PAST, D])
    cckv = din("cckv", [PAST, 128]); ckpe = din("ckpe", [PAST, 32])
    rel_bias = din("rel_bias", [32, 8])
    norm_mix = din("norm_mix", [D]); w_in = din("w_in", [D, INW])
    diff_lambda = din("diff_lambda", [4 * 64]); diff_subln = din("diff_subln", [128])
    mla_q_norm = din("mla_q_norm", [256]); w_uq = din("w_uq", [256, 1536])
    mla_kv_norm = din("mla_kv_norm", [128]); w_uk = din("w_uk", [128, 1024]); w_uv = din("w_uv", [128, 1024])
    w_ba = din("w_ba", [D, D]); w_bb = din("w_bb", [D, D]); w_o = din("w_o", [D, D])
    norm_ffn = din("norm_ffn", [D]); peer_wq = din("peer_wq", [D, D])
    peer_keys = din("peer_keys", [16, 128, 64])
    peer_u = din("peer_u", [16384, D]); peer_v = din("peer_v", [16384, D])
    norm_final = din("norm_final", [D])
    c_ident = din("c_ident", [128, 128]); c_rope = din("c_rope", [NQ, 32])
    c_oh = din("c_oh", [32, 384]); c_maskw = din("c_maskw", [128, 256])
    c_iota = din("c_iota", [128, 256])

    yp = dout("yp", [SP, D]); ys = dout("ys", [SS, D])
    kp = dout("kp", [SP, D]); vp = dout("vp", [SP, D]); cp = dout("cp", [SP, 128]); ep = dout("ep", [SP, 32])
    ks_ = dout("ks", [SS, D]); vs_ = dout("vs", [SS, D]); cs_ = dout("cs", [SS, 128]); es_ = dout("es", [SS, 32])

    dqT_s = dscr("dqT_s", [128, 8, NQ]); dkT_s = dscr("dkT_s", [128, 8, NT]); dv_s = dscr("dv_s", [NT, D])
    mqT_s = dscr("mqT_s", [96, 16, NQ]); mkT_s = dscr("mkT_s", [128, 8, NT]); mkpeT_s = dscr("mkpeT_s", [32, NT])
    mv_s = dscr("mv_s", [NT, D]); gates_s = dscr("gates_s", [NQ, 2 * D])
    oa_s = dscr("oa_s", [NQ, D]); ob_s = dscr("ob_s", [NQ, D])
    fbias_s = dscr("fbias_s", [8, 384], F32)
    DENSE = cfg.get('peer', 'dense') == 'dense'
    TG = cfg.get('tg', 256)
    UT_s = dscr("UT_s", [64, 128, 2, 1024]); V_s = dscr("V_s", [64, 128, 2, 1024])
    x2_s = dscr("x2_s", [NQ, D], F32); h2T_s = dscr("h2T_s", [128, 8, NQ])
    r_s = dscr("r_s", [128, 3, NQ], F32)

    with ExitStack() as gs:
        sems = {}

        def ensure_sems():
            for k in S.semkeys:
                if k not in sems:
                    sems[k] = gs.enter_context(nc.semaphore("s_" + k))

        if 'A' in phases:
            with ExitStack() as es:
                def sb(name, shape, dt):
                    return es.enter_context(nc.sbuf_tensor(name, list(shape), dt))

                def ps(name, shape, dt):
                    return es.enter_context(nc.psum_tensor(name, list(shape), dt))

                def sb1(name, shape, dt):
                    t = sb(name, shape, dt)
                    return [t, t]
                S.banks = ('pz0', 'pz1', 'pTa', 'pTb0', 'pTb1', 'pmm0', 'pmm1', 'pTs')
                S.single = ('zk_f', 'zv_f', 'gates_b', 'qh_f', 'rtmp', 'hb', 'cq_b', 'vm_b')

                win_sb = sb("win_sb", [128, 8, INW], BF16)
                wuq_sb = sb("wuq_sb", [128, 2, 1536], BF16)
                wuk_sb = sb("wuk_sb", [128, 1024], BF16)
                wuv_sb = sb("wuv_sb", [128, 1024], BF16)
                identf = sb("identf", [128, 128], F32)
                identb = sb("identb", [128, 128], BF16)
                gmix = sb("gmix", [128, D], F32)
                gq = sb("gq", [128, 256], F32)
                gkv = sb("gkv", [128, 128], F32)
                rope_sb = sb("rope_sb", [128, 33, 32], F32)
                xt = [sb("xt%d" % i, [128, D], F32) for i in range(2)]
                junk = sb("junk", [128, D], BF16)
                sgtmp = sb("sgtmp", [128, D], F32)
                st = [sb("st%d" % i, [128, 8], F32) for i in range(2)]
                hb = sb1("hb", [128, D], BF16)
                hT = [sb("hT%d" % i, [128, 8, 128], BF16) for i in range(2)]
                zq_b = [sb("zq_b%d" % i, [128, D], BF16) for i in range(2)]
                zk_f = sb1("zk_f", [128, D], F32)
                zk_b = [sb("zk_b%d" % i, [128, D], BF16) for i in range(2)]
                zv_f = sb1("zv_f", [128, D], F32)
                zv_b = [sb("zv_b%d" % i, [128, D], BF16) for i in range(2)]
                zm = [sb("zm%d" % i, [128, 416], F32) for i in range(2)]
                gates_b = sb1("gates_b", [128, 2 * D], BF16)
                qT_sb = [sb("qT_sb%d" % i, [128, 8, 128], BF16) for i in range(2)]
                kT_sb = [sb("kT_sb%d" % i, [128, 8, 128], BF16) for i in range(2)]
                cq_b = sb1("cq_b", [128, 256], BF16)
                cqT = [sb("cqT%d" % i, [128, 2, 128], BF16) for i in range(2)]
                qh_f = sb1("qh_f", [128, 16, 96], F32)
                qh_b = [sb("qh_b%d" % i, [128, 16, 96], BF16) for i in range(2)]
                rtmp = sb1("rtmp", [128, 4, 16, 16], F32)
                mqT_sb = [sb("mqT_sb%d" % i, [128, 16, 128], BF16) for i in range(2)]
                ckv_f = [sb("ckv_f%d" % i, [128, 128], F32) for i in range(2)]
                ckv_b = [sb("ckv_b%d" % i, [128, 128], BF16) for i in range(2)]
                ckvT = [sb("ckvT%d" % i, [128, 128], BF16) for i in range(2)]
                kn_b = [sb("kn_b%d" % i, [128, D], BF16) for i in range(2)]
                vm_b = sb1("vm_b", [128, D], BF16)
                knT_sb = [sb("knT_sb%d" % i, [128, 8, 128], BF16) for i in range(2)]
                kpe_f = [sb("kpe_f%d" % i, [128, 32], F32) for i in range(2)]
                kpe_b = [sb("kpe_b%d" % i, [128, 32], BF16) for i in range(2)]
                kpeT_sb = [sb("kpeT_sb%d" % i, [32, 128], BF16) for i in range(2)]
                pTa = ps("pTa", [128, 8, 128], BF16)
                pz = [ps("pz%d" % i, [128, 512], F32) for i in range(2)]
                pTb = [ps("pTb%d" % i, [128, 8, 128], BF16) for i in range(2)]
                pmm = [ps("pmm%d" % i, [128, 512], F32) for i in range(2)]
                pTs = ps("pTs", [128, 4, 128], BF16)

                for gi_, (n0, nw) in enumerate([(0, 1024), (1024, 2048), (3072, 416), (3488, 2048)]):
                    for kc in range(8):
                        S.dma('gpsimd', 'w_in%d' % gi_, L('dma_start',
                            out=win_sb[:, kc, n0:n0 + nw], in_=w_in[kc * 128:(kc + 1) * 128, n0:n0 + nw]),
                            writes=['win%d' % gi_])
                for kc in range(2):
                    S.dma('gpsimd', 'w_uq', L('dma_start', out=wuq_sb[:, kc, :], in_=w_uq[kc * 128:(kc + 1) * 128, :]), writes=['wuq'])
                S.dma('gpsimd', 'w_uk', L('dma_start', out=wuk_sb[:], in_=w_uk.ap()), writes=['wuk'])
                S.dma('gpsimd', 'w_uv', L('dma_start', out=wuv_sb[:], in_=w_uv.ap()), writes=['wuv'])
                S.dma('sync', 'c_identf', L('dma_start', out=identf[:], in_=c_ident.ap()), writes=['identf'])
                S.dma('sync', 'c_gmix', L('dma_start', out=gmix[:], in_=bass.AP(norm_mix, 0, [[0, 128], [1, D]])), writes=['gmix'])
                S.dma('sync', 'c_gq', L('dma_start', out=gq[:], in_=bass.AP(mla_q_norm, 0, [[0, 128], [1, 256]])), writes=['gq'])
                S.dma('sync', 'c_gkv', L('dma_start', out=gkv[:], in_=bass.AP(mla_kv_norm, 0, [[0, 128], [1, 128]])), writes=['gkv'])
                S.dma('sync', 'c_rope', L('dma_start', out=rope_sb[:], in_=c_rope.ap().rearrange("(t p) c -> p t c", p=128)), writes=['rope'])
                S.op('vector', L('tensor_copy', out=identb[:], in_=identf[:]), reads=['identf'], writes=['identb'])

                evac_rr = [0]

                def evac(out, in_, reads, writes, func=AF.Copy, eng=None):
                    if eng is None:
                        eng = 'scalar' if (evac_rr[0] % 2 == 0) else 'vector'
                        evac_rr[0] += 1
                    if eng == 'scalar':
                        S.op('scalar', L('activation', out=out, in_=in_, func=func), reads=reads, writes=writes)
                    else:
                        S.op('vector', L('tensor_copy', out=out, in_=in_), reads=reads, writes=writes)

                def scale_gain(out, in0, scalar, in1, tmp, reads, writes):
                    S.op('vector', L('tensor_scalar', out=tmp, in0=in0, scalar1=scalar, scalar2=None, op0=ALU.mult),
                         reads=reads, writes=['sgtmp'])
                    S.op('vector', L('tensor_tensor', out=out, in0=tmp, in1=in1, op=ALU.mult),
                         reads=['sgtmp'] + list(reads), writes=writes)

                def rms_stats(src_ap, ncols, stt, col, sl):
                    key = 'st%d_%d' % (sl, col)
                    S.op('scalar', L('activation', out=junk[:, 0:ncols], in_=src_ap, func=AF.Square,
                                                          accum_out=stt[:, col:col + 1]),
                         reads=rms_stats.reads, writes=['junk', key])
                    S.op('vector', L('tensor_scalar', out=stt[:, col:col + 1], in0=stt[:, col:col + 1],
                                                             scalar1=1.0 / ncols, scalar2=EPS, op0=ALU.mult, op1=ALU.add),
                         reads=[key], writes=[key])
                    S.op('scalar', L('activation', out=stt[:, col:col + 1], in_=stt[:, col:col + 1], func=AF.Sqrt),
                         reads=[key], writes=[key])
                    S.op('vector', L('reciprocal', out=stt[:, col:col + 1], in_=stt[:, col:col + 1]),
                         reads=[key], writes=[key])
                    return key

                def transposes(src_fn, n, pst, pkey, reads, rows=128):
                    for j in range(n):
                        S.op('tensor', L('transpose', out=pst[0:rows, j, :], in_=src_fn(j), identity=identb[:]),
                             reads=list(reads) + ['identb'], writes=['%s_%d' % (pkey, j)])

                def k_side(sl, kcol, zkb_keys, zvb_keys, ckvb_key, kpeb_key):
                    transposes(lambda j: zk_b[sl][:, j * 128:(j + 1) * 128], 8, pTb[1], 'pTb1', zkb_keys)
                    evac(kT_sb[sl][:], pTb[1][:], ['pTb1_%d' % j for j in range(8)], ['kT_sb%d' % sl])
                    S.dma('gpsimd', 'st_kT%d' % sl, L('dma_start', out=dkT_s[:, :, kcol:kcol + 128], in_=kT_sb[sl][:]),
                          reads=['kT_sb%d' % sl])
                    S.dma('gpsimd', 'st_v%d' % sl, L('dma_start', out=dv_s[kcol:kcol + 128, :], in_=zv_b[sl][:]),
                          reads=zvb_keys)
                    yield
                    S.op('tensor', L('transpose', out=pTs[:, 2, :], in_=ckv_b[sl][:], identity=identb[:]),
                         reads=[ckvb_key, 'identb'], writes=['pTs_2'])
                    evac(ckvT[sl][:], pTs[:, 2, :], ['pTs_2'], ['ckvT%d' % sl])
                    yield
                    for (wsb, wkey, dst, dkey) in [(wuk_sb, 'wuk', kn_b, 'kn_b'), (wuv_sb, 'wuv', vm_b, 'vm_b')]:
                        for c in range(2):
                            S.op('tensor', L('matmul', pmm[c][:], lhsT=ckvT[sl][:], rhs=wsb[:, c * 512:(c + 1) * 512],
                                                                    start=True, stop=True),
                                 reads=['ckvT%d' % sl, wkey], writes=['pmm%d' % c])
                            evac(dst[sl][:, c * 512:(c + 1) * 512], pmm[c][:], ['pmm%d' % c], ['%s%d_%d' % (dkey, sl, c)])
                    S.dma('gpsimd', 'st_mv%d' % sl, L('dma_start', out=mv_s[kcol:kcol + 128, :], in_=vm_b[sl][:]),
                          reads=['vm_b%d_0' % sl, 'vm_b%d_1' % sl])
                    yield
                    transposes(lambda j: kn_b[sl][:, j * 128:(j + 1) * 128], 8, pTb[0], 'pTb0',
                               ['kn_b%d_0' % sl, 'kn_b%d_1' % sl])
                    evac(knT_sb[sl][:], pTb[0][:], ['pTb0_%d' % j for j in range(8)], ['knT_sb%d' % sl])
                    S.dma('gpsimd', 'st_mk%d' % sl, L('dma_start', out=mkT_s[:, :, kcol:kcol + 128], in_=knT_sb[sl][:]),
                          reads=['knT_sb%d' % sl])
                    S.op('tensor', L('transpose', out=pTs[0:32, 3, :], in_=kpe_b[sl][:], identity=identb[:]),
                         reads=[kpeb_key, 'identb'], writes=['pTs_3'])
                    evac(kpeT_sb[sl][:], pTs[0:32, 3, :], ['pTs_3'], ['kpeT_sb%d' % sl])
                    S.dma('gpsimd', 'st_kpe%d' % sl, L('dma_start', out=mkpeT_s[:, kcol:kcol + 128], in_=kpeT_sb[sl][:]),
                          reads=['kpeT_sb%d' % sl])

                tiles = [('p', i) for i in range(32)] + [('s', 0)] + [('c', i) for i in range(16)]
                tiles = cfg.get('a_tiles', tiles)
                pending = [None]

                def step_pending():
                    if pending[0] is not None:
                        try:
                            next(pending[0])
                        except StopIteration:
                            pending[0] = None

                def drain():
                    while pending[0] is not None:
                        step_pending()
                for tix, (kind, i) in enumerate(tiles):
                    sl = tix % 2
                    if kind == 'c':
                        kcol = KS0 + i * 128
                        r0 = i * 128
                        S.dma('gpsimd', 'ld_ck%d' % sl, L('dma_start', out=zk_b[sl][:], in_=cdk[r0:r0 + 128, :]),
                              writes=['zk_b%d_2' % sl, 'zk_b%d_3' % sl])
                        S.dma('gpsimd', 'ld_cv%d' % sl, L('dma_start', out=zv_b[sl][:], in_=cdv[r0:r0 + 128, :]),
                              writes=['zv_b%d_4' % sl, 'zv_b%d_5' % sl])
                        S.dma('gpsimd', 'ld_cc%d' % sl, L('dma_start', out=ckv_b[sl][:], in_=cckv[r0:r0 + 128, :]),
                              writes=['ckv_b%d' % sl])
                        S.dma('gpsimd', 'ld_ce%d' % sl, L('dma_start', out=kpe_b[sl][:], in_=ckpe[r0:r0 + 128, :]),
                              writes=['kpe_b%d' % sl])
                        drain()
                        for _ in k_side(sl, kcol, ['zk_b%d_2' % sl, 'zk_b%d_3' % sl], ['zv_b%d_4' % sl, 'zv_b%d_5' % sl], 'ckv_b%d' % sl, 'kpe_b%d' % sl):
                            pass
                        continue
                    if kind == 'p':
                        rows = 128; qcol = i * 128; kcol = i * 128; rt = i
                        xsrc = xp[i * 128:(i + 1) * 128, :]
                        o_k, o_v, o_c, o_e = kp[qcol:qcol + 128, :], vp[qcol:qcol + 128, :], cp[qcol:qcol + 128, :], ep[qcol:qcol + 128, :]
                    else:
                        rows = 64; qcol = SP; kcol = KS0 + PAST; rt = 32
                        xsrc = xs.ap()
                        o_k, o_v, o_c, o_e = ks_.ap(), vs_.ap(), cs_.ap(), es_.ap()
                        S.op('vector', L('memset', xt[sl][:], 0.0), writes=['xt%d' % sl])
                    S.dma('sync', 'ld_x%d' % sl, L('dma_start', out=xt[sl][0:rows, :], in_=xsrc),
                          writes=['xt%d' % sl])
                    rms_stats.reads = ['xt%d' % sl]
                    k0 = rms_stats(xt[sl][:], D, st[sl], 0, sl)
                    scale_gain(hb[sl][:], xt[sl][:], st[sl][:, 0:1], gmix[:], sgtmp[:, 0:1024], ['xt%d' % sl, k0, 'gmix'], ['hb%d' % sl])
                    transposes(lambda j: hb[sl][:, j * 128:(j + 1) * 128], 8, pTa, 'pTa', ['hb%d' % sl])
                    evac(hT[sl][:], pTa[:], ['pTa_%d' % j for j in range(8)], ['hT%d' % sl])

                    chunks = [(0, 512), (512, 512), (1024, 512), (1536, 512), (2048, 512), (2560, 512), (3072, 416),
                              (3488, 512), (4000, 512), (4512, 512), (5024, 512)]
                    for ci, (n0, nw) in enumerate(chunks):
                        pb = ci % 2
                        for kc in range(8):
                            S.op('tensor', L('matmul', pz[pb][:, 0:nw], lhsT=hT[sl][:, kc, :], rhs=win_sb[:, kc, n0:n0 + nw],
                                start=(kc == 0), stop=(kc == 7)),
                                reads=['hT%d' % sl, 'win%d' % (0 if n0 < 1024 else 1 if n0 < 3072 else 2 if n0 < 3488 else 3)], writes=['pz%d' % pb])
                        if ci < 2:
                            evac(zq_b[sl][:, n0:n0 + 512], pz[pb][:], ['pz%d' % pb], ['zq_b%d_%d' % (sl, ci)])
                        elif ci < 4:
                            c0 = n0 - 1024
                            evac(zk_f[sl][:, c0:c0 + 512], pz[pb][:], ['pz%d' % pb], ['zk_f%d_%d' % (sl, ci)], eng='scalar')
                            S.op('vector', L('tensor_copy', out=zk_b[sl][:, c0:c0 + 512], in_=pz[pb][:]),
                                 reads=['pz%d' % pb], writes=['zk_b%d_%d' % (sl, ci)])
                        elif ci < 6:
                            c0 = n0 - 2048
                            evac(zv_f[sl][:, c0:c0 + 512], pz[pb][:], ['pz%d' % pb], ['zv_f%d_%d' % (sl, ci)], eng='scalar')
                            S.op('vector', L('tensor_copy', out=zv_b[sl][:, c0:c0 + 512], in_=pz[pb][:]),
                                 reads=['pz%d' % pb], writes=['zv_b%d_%d' % (sl, ci)])
                        elif ci == 6:
                            evac(zm[sl][:], pz[pb][:, 0:416], ['pz%d' % pb], ['zm%d' % sl], eng='vector')
                        else:
                            c0 = n0 - 3488
                            S.op('scalar', L('activation', out=gates_b[sl][:, c0:c0 + 512], in_=pz[pb][:],
                                                                                 func=AF.Sigmoid),
                                 reads=['pz%d' % pb], writes=['gates_b%d_%d' % (sl, ci)])
                        step_pending()
                    drain()
                    S.dma('gpsimd', 'st_ok%d' % sl, L('dma_start', out=o_k, in_=zk_f[sl][0:rows, :]),
                          reads=['zk_f%d_2' % sl, 'zk_f%d_3' % sl])
                    S.dma('gpsimd', 'st_ov%d' % sl, L('dma_start', out=o_v, in_=zv_f[sl][0:rows, :]),
                          reads=['zv_f%d_4' % sl, 'zv_f%d_5' % sl])
                    S.dma('gpsimd', 'st_g%d' % sl, L('dma_start', out=gates_s[qcol:qcol + 128, :], in_=gates_b[sl][:]),
                          reads=['gates_b%d_%d' % (sl, c) for c in range(7, 11)])
                    def epi(sl=sl, qcol=qcol, kcol=kcol, rows=rows, rt=rt, o_c=o_c, o_e=o_e):
                        tmpk = 'rtmp%d' % sl
                        rms_stats.reads = ['zm%d' % sl]
                        k1 = rms_stats(zm[sl][:, 0:256], 256, st[sl], 1, sl)
                        scale_gain(cq_b[sl][:], zm[sl][:, 0:256], st[sl][:, 1:2], gq[:], sgtmp[:, 0:256], ['zm%d' % sl, k1, 'gq'], ['cq_b%d' % sl])
                        k2 = rms_stats(zm[sl][:, 256:384], 128, st[sl], 2, sl)
                        scale_gain(ckv_f[sl][:], zm[sl][:, 256:384], st[sl][:, 2:3], gkv[:], sgtmp[:, 0:128], ['zm%d' % sl, k2, 'gkv'], ['ckv_f%d' % sl])
                        S.op('vector', L('tensor_copy', out=ckv_b[sl][:], in_=ckv_f[sl][:]), reads=['ckv_f%d' % sl], writes=['ckv_b%d' % sl])
                        S.dma('gpsimd', 'st_oc%d' % sl, L('dma_start', out=o_c, in_=ckv_f[sl][0:rows, :]),
                              reads=['ckv_f%d' % sl])
                        cos1 = bass.AP(rope_sb, rt * 32, [[33 * 32, 128], [1, 16]])
                        sin1 = bass.AP(rope_sb, rt * 32 + 16, [[33 * 32, 128], [1, 16]])
                        y1 = zm[sl][:, 384:400]
                        y2 = zm[sl][:, 400:416]
                        for ti_, (a, b_) in enumerate([(y1, cos1), (y2, sin1), (y1, sin1), (y2, cos1)]):
                            S.op('vector', L('tensor_tensor', out=rtmp[sl][:, ti_, 0, :], in0=a, in1=b_, op=ALU.mult),
                                 reads=['zm%d' % sl, 'rope'], writes=[tmpk + '_%d' % ti_])
                        S.op('vector', L('tensor_tensor', out=kpe_f[sl][:, 0:16], in0=rtmp[sl][:, 0, 0, :], in1=rtmp[sl][:, 1, 0, :], op=ALU.subtract),
                             reads=[tmpk + '_0', tmpk + '_1'], writes=['kpe_f%d_a' % sl])
                        S.op('vector', L('tensor_tensor', out=kpe_f[sl][:, 16:32], in0=rtmp[sl][:, 2, 0, :], in1=rtmp[sl][:, 3, 0, :], op=ALU.add),
                             reads=[tmpk + '_2', tmpk + '_3'], writes=['kpe_f%d_b' % sl])
                        S.op('vector', L('tensor_copy', out=kpe_b[sl][:], in_=kpe_f[sl][:]),
                             reads=['kpe_f%d_a' % sl, 'kpe_f%d_b' % sl], writes=['kpe_b%d' % sl])
                        S.dma('gpsimd', 'st_oe%d' % sl, L('dma_start', out=o_e, in_=kpe_f[sl][0:rows, :]),
                              reads=['kpe_f%d_a' % sl, 'kpe_f%d_b' % sl])
                        yield
                        transposes(lambda j: zq_b[sl][:, j * 128:(j + 1) * 128], 8, pTb[0], 'pTb0', ['zq_b%d_0' % sl, 'zq_b%d_1' % sl])
                        evac(qT_sb[sl][:], pTb[0][:], ['pTb0_%d' % j for j in range(8)], ['qT_sb%d' % sl])
                        S.dma('gpsimd', 'st_qT%d' % sl, L('dma_start', out=dqT_s[:, :, qcol:qcol + 128], in_=qT_sb[sl][:]),
                              reads=['qT_sb%d' % sl])
                        yield
                        for j in range(2):
                            S.op('tensor', L('transpose', out=pTs[:, j, :], in_=cq_b[sl][:, j * 128:(j + 1) * 128], identity=identb[:]),
                                 reads=['cq_b%d' % sl, 'identb'], writes=['pTs_%d' % j])
                        evac(cqT[sl][:], pTs[:, 0:2, :], ['pTs_0', 'pTs_1'], ['cqT%d' % sl])
                        yield
                        qh_flat = qh_f[sl][:].rearrange("p h e -> p (h e)")
                        for c in range(3):
                            pb = c % 2
                            for kc in range(2):
                                S.op('tensor', L('matmul', pmm[pb][:], lhsT=cqT[sl][:, kc, :], rhs=wuq_sb[:, kc, c * 512:(c + 1) * 512],
                                    start=(kc == 0), stop=(kc == 1)),
                                    reads=['cqT%d' % sl, 'wuq'], writes=['pmm%d' % pb])
                            evac(qh_flat[:, c * 512:(c + 1) * 512], pmm[pb][:], ['pmm%d' % pb], ['qh_f%d_%d' % (sl, c)])
                        yield
                        qhr = ['qh_f%d_%d' % (sl, c) for c in range(3)]
                        cosb = bass.AP(rope_sb, rt * 32, [[33 * 32, 128], [0, 16], [1, 16]])
                        sinb = bass.AP(rope_sb, rt * 32 + 16, [[33 * 32, 128], [0, 16], [1, 16]])
                        x1 = qh_f[sl][:, :, 64:80]
                        x2 = qh_f[sl][:, :, 80:96]
                        tmpk = 'rtmp%d' % sl
                        for ti_, (a, b_) in enumerate([(x1, cosb), (x2, sinb), (x1, sinb), (x2, cosb)]):
                            S.op('vector', L('tensor_tensor', out=rtmp[sl][:, ti_, :, :], in0=a, in1=b_, op=ALU.mult),
                                 reads=qhr + ['rope'], writes=[tmpk + '_%d' % ti_])
                        S.op('vector', L('tensor_copy', out=qh_b[sl][:, :, 0:64], in_=qh_f[sl][:, :, 0:64]),
                             reads=qhr, writes=['qh_b%d_n' % sl])
                        S.op('vector', L('tensor_tensor', out=qh_b[sl][:, :, 64:80], in0=rtmp[sl][:, 0, :, :], in1=rtmp[sl][:, 1, :, :],
                                                                 op=ALU.subtract),
                             reads=[tmpk + '_0', tmpk + '_1'], writes=['qh_b%d_a' % sl])
                        S.op('vector', L('tensor_tensor', out=qh_b[sl][:, :, 80:96], in0=rtmp[sl][:, 2, :, :], in1=rtmp[sl][:, 3, :, :],
                                                                 op=ALU.add),
                             reads=[tmpk + '_2', tmpk + '_3'], writes=['qh_b%d_b' % sl])
                        yield
                        ks = k_side(sl, kcol, ['zk_b%d_2' % sl, 'zk_b%d_3' % sl], ['zv_b%d_4' % sl, 'zv_b%d_5' % sl], 'ckv_b%d' % sl, 'kpe_b%d' % sl)
                        next(ks, None)
                        yield
                        qhb_keys = ['qh_b%d_n' % sl, 'qh_b%d_a' % sl, 'qh_b%d_b' % sl]
                        for half in range(2):
                            transposes(lambda j, half=half: qh_b[sl][:, half * 8 + j, :], 8, pTb[half], 'pTb%d' % half, qhb_keys, rows=96)
                            evac(mqT_sb[sl][0:96, half * 8:(half + 1) * 8, :], pTb[half][0:96, :, :],
                                 ['pTb%d_%d' % (half, j) for j in range(8)], ['mqT_sb%d_%d' % (sl, half)])
                        S.dma('gpsimd', 'st_mq%d' % sl, L('dma_start', out=mqT_s[:, :, qcol:qcol + 128], in_=mqT_sb[sl][0:96, :, :]),
                              reads=['mqT_sb%d_0' % sl, 'mqT_sb%d_1' % sl])
                        yield
                        for _ in ks:
                            yield
                    pending[0] = epi()

                drain()
                S.end_phase()
                ensure_sems()
                S.emit(nc, sems)

        if 'B' in phases:
            with ExitStack() as es:
                def sb(name, shape, dt):
                    return es.enter_context(nc.sbuf_tensor(name, list(shape), dt))

                def ps(name, shape, dt):
                    return es.enter_context(nc.psum_tensor(name, list(shape), dt))
                S.single = ()
                S.banks = ('sc0', 'sc1', 'oacc0', 'oacc1', 'oacc2', 'oacc3', 'pset')
                NKT = NT // 128
                QT = [sb("QT%d" % i, [128, NQ], BF16) for i in range(2)]
                KT = [sb("KT%d" % i, [128, NT], BF16) for i in range(2)]
                VA = [sb("VA%d" % i, [128, NKT, 129], BF16) for i in range(2)]
                VM = [sb("VM%d" % i, [128, NKT, 65], BF16) for i in range(2)]
                et = [sb("et%d" % i, [128, 512], BF16) for i in range(3)]
                rb_sb = sb("rb_sb", [32, 8], F32)
                oh_sb = sb("oh_sb", [32, 384], F32)
                rb15 = sb("rb15", [8, 1], F32)
                fexp = sb("fexp", [8, 384], F32)
                hank = sb("hank", [128, 8, 256], F32)
                maskw = sb("maskw", [128, 256], F32)
                ebm = sb("ebm", [128, 8, 256], BF16)
                dl = sb("dl", [128, 256], F32)
                dlj = sb("dlj", [128, 64], F32)
                lam = sb("lam", [128, 4], F32)
                gsub = sb("gsub", [128, 128], F32)
                o0 = [sb("o0_%d" % i, [128, 129], F32) for i in range(4)]
                o1 = [sb("o1_%d" % i, [128, 129], F32) for i in range(4)]
                mhalfB = sb("mhalfB", [128, 1], F32)
                sm = [sb("smB%d" % i, [128, 8], F32) for i in range(4)]
                of = [sb("of%d" % i, [128, 128], F32) for i in range(4)]
                of2 = [sb("of2_%d" % i, [128, 128], F32) for i in range(4)]
                jk = sb("jkB", [128, 128], BF16)
                oa_t = [sb("oa_t%d" % i, [128, 128], BF16) for i in range(4)]
                ob_t = [sb("ob_t%d" % i, [128, 64], BF16) for i in range(4)]
                sc = [ps("sc%d" % i, [128, 512], F32) for i in range(2)]
                oacc = [ps("oacc%d" % i, [128, 512], F32) for i in range(4)]
                pset = ps("pset", [128, 512], F32)
                P_IN_B = DENSE and 'C' in phases and cfg.get('p_in_b', True)
                n_ecB = cfg.get('n_ec', 128) if P_IN_B else 0
                if P_IN_B:
                    S.banks = S.banks + ('ppTB',)
                    identfB = sb("identfB", [128, 128], F32); identbB = sb("identbB", [128, 128], BF16)
                    ubB = [sb("ubB%d" % i, [128, D], BF16) for i in range(3)]
                    vbB = [sb("vbB%d" % i, [128, D], BF16) for i in range(3)]
                    uTB = [sb("uTB%d" % i, [128, 8, 128], BF16) for i in range(3)]
                    ppTB = ps("ppTB", [128, 8, 128], BF16)
                    S.dma('sync', 'pb_identf', L('dma_start', out=identfB[:], in_=c_ident.ap()), writes=['identfB'])
                    S.op('vector', L('tensor_copy', out=identbB[:], in_=identfB[:]), reads=['identfB'], writes=['identbB'])
                p_state = [0, 0]

                def p_load():
                    ec = p_state[0]
                    if ec >= n_ecB:
                        return
                    p_state[0] += 1
                    s3 = ec % 3
                    S.dma('gpsimd', 'pb_u%d' % s3, L('dma_start', out=ubB[s3][:], in_=peer_u[ec * 128:(ec + 1) * 128, :]), writes=['ubB%d' % s3])
                    S.dma('gpsimd', 'pb_v%d' % s3, L('dma_start', out=vbB[s3][:], in_=peer_v[ec * 128:(ec + 1) * 128, :]), writes=['vbB%d' % s3])

                def p_proc():
                    ec = p_state[1]
                    if ec >= n_ecB:
                        return
                    p_state[1] += 1
                    s3 = ec % 3
                    for dc in range(8):
                        S.op('tensor', L('transpose', out=ppTB[:, dc, :], in_=ubB[s3][:, dc * 128:(dc + 1) * 128], identity=identbB[:]),
                             reads=['ubB%d' % s3, 'identbB'], writes=['ppTB_%d' % dc])
                    S.op('vector', L('tensor_copy', out=uTB[s3][:], in_=ppTB[:]), reads=['ppTB_%d' % dc for dc in range(8)], writes=['uTB%d' % s3])
                    S.dma('sync', 'pb_su%d' % s3, L('dma_start', out=UT_s[ec // 2][:, ec % 2, :].rearrange("p (a b) -> p a b", a=8), in_=uTB[s3][:]),
                          reads=['uTB%d' % s3])
                    S.dma('sync', 'pb_sv%d' % s3, L('dma_start', out=V_s[ec // 2][:, ec % 2, :], in_=vbB[s3][:]), reads=['vbB%d' % s3])

                def p_iter():
                    if p_state[1] < n_ecB:
                        while p_state[0] < min(p_state[1] + 3, n_ecB):
                            p_load()
                        p_proc()

                S.dma('sync', 'b_rb', L('dma_start', out=rb_sb[:], in_=rel_bias.ap()), writes=['rb_sb'])
                S.dma('sync', 'b_oh', L('dma_start', out=oh_sb[:], in_=c_oh.ap()), writes=['oh_sb'])
                S.dma('sync', 'b_rb15', L('dma_start', out=rb15[:], in_=rel_bias[15:16, :].rearrange("a h -> h a")), writes=['rb15'])
                S.dma('sync', 'b_mw', L('dma_start', out=maskw[:], in_=c_maskw.ap()), writes=['maskw'])
                S.dma('sync', 'b_dl', L('dma_start', out=dl[:], in_=bass.AP(diff_lambda, 0, [[0, 128], [1, 256]])), writes=['dl'])
                S.dma('sync', 'b_gs', L('dma_start', out=gsub[:], in_=bass.AP(diff_subln, 0, [[0, 128], [1, 128]])), writes=['gsub'])
                S.op('tensor', L('matmul', pset[0:8, 0:384], lhsT=rb_sb[:], rhs=oh_sb[:], start=True, stop=True),
                     reads=['rb_sb', 'oh_sb'], writes=['pset'])
                S.op('vector', L('tensor_scalar', out=fexp[:], in0=pset[0:8, 0:384], scalar1=rb15[:, 0:1], scalar2=None, op0=ALU.subtract),
                     reads=['pset', 'rb15'], writes=['fexp'])
                S.op('scalar', L('activation', out=fexp[:], in_=fexp[:], func=AF.Exp), reads=['fexp'], writes=['fexp'])
                S.dma('sync', 'b_fs', L('dma_start', out=fbias_s.ap(), in_=fexp[:]), reads=['fexp'], writes=['fbias_s'])
                for h in range(8):
                    S.dma('sync', 'b_hk', L('dma_start', out=hank[:, h, :], in_=bass.AP(fbias_s, h * 384, [[1, 128], [1, 256]])),
                          reads=['fbias_s'], writes=['hank'])
                for h in range(8):
                    S.op('vector', L('tensor_tensor', out=ebm[:, h, :], in0=hank[:, h, ::-1], in1=maskw[:], op=ALU.mult),
                         reads=['hank', 'maskw'], writes=['ebm'])
                for i_, (a, b_) in enumerate([(0, 1), (2, 3)]):
                    S.op('vector', L('tensor_tensor', out=dlj[:], in0=dl[:, a * 64:(a + 1) * 64], in1=dl[:, b_ * 64:(b_ + 1) * 64], op=ALU.mult),
                         reads=['dl'], writes=['dlj'])
                    S.op('vector', L('reduce_sum', out=lam[:, i_:i_ + 1], in_=dlj[:], axis=mybir.AxisListType.X),
                         reads=['dlj'], writes=['lam%d' % i_])
                    S.op('scalar', L('activation', out=lam[:, i_:i_ + 1], in_=lam[:, i_:i_ + 1], func=AF.Exp),
                         reads=['lam%d' % i_], writes=['lam%d' % i_])
                S.op('vector', L('tensor_tensor', out=lam[:, 2:3], in0=lam[:, 1:2], in1=lam[:, 0:1], op=ALU.subtract),
                     reads=['lam0', 'lam1'], writes=['lam2'])
                S.op('vector', L('tensor_scalar', out=lam[:, 2:3], in0=lam[:, 2:3], scalar1=-LAMBDA_INIT, scalar2=None, op0=ALU.add),
                     reads=['lam2'], writes=['lam2'])
                S.op('vector', L('tensor_scalar', out=gsub[:], in0=gsub[:], scalar1=1.0 - LAMBDA_INIT, scalar2=None, op0=ALU.mult),
                     reads=['gsub'], writes=['gsub'])
                S.op('gpsimd', L('memset', mhalfB[:], -0.5), writes=['mhalfB'])
                for i in range(2):
                    S.op('gpsimd', L('memset', VA[i][:, :, 128:129], 1.0), writes=['VA%d_one' % i])
                    S.op('gpsimd', L('memset', VA[i][64:128, NKT - 1, 128:129], 0.0), writes=['VA%d_one' % i])
                    S.op('gpsimd', L('memset', VM[i][:, :, 64:65], 1.0), writes=['VM%d_one' % i])
                    S.op('gpsimd', L('memset', VM[i][64:128, NKT - 1, 64:65], 0.0), writes=['VM%d_one' % i])

                groups = [dict(qcol0=g * 512, nq=4, kts=list(range(4 * g + 4)), qabs0=4 * g) for g in range(8)]
                groups.append(dict(qcol0=SP, nq=1, kts=list(range(32, 49)), qabs0=48))
                groups = cfg.get('b_groups', groups)
                st_ctr = [0]
                sc_ctr = [0]
                scb = [sc[0], sc[1], pset]
                sckey = ['sc0', 'sc1', 'pset']

                def attention(kind, h, slot, r0, r1, E, scale, vt, vkey, grp, finalize):
                    nq = grp['nq']; kts = grp['kts']; qabs0 = grp['qabs0']; qcol0 = grp['qcol0']
                    p_iter()

                    def qlo_of(kt):
                        return max(kt - qabs0, 0) if nq > 1 else 0

                    def score(idx):
                        kt = kts[idx]; qlo = qlo_of(kt); N = (nq - qlo) * 128
                        s = sc_ctr[0] % 3
                        sc_ctr[0] += 1
                        S.op('tensor', L('matmul', scb[s][:, 0:N], lhsT=KT[slot][r0:r1, kt * 128:(kt + 1) * 128],
                                         rhs=QT[slot][r0:r1, qcol0 + qlo * 128:qcol0 + nq * 128], start=True, stop=True),
                             reads=['KT%d' % slot, 'KT%d_pe' % slot, 'QT%d' % slot], writes=[sckey[s]])
                        return s
                    pend = [score(i_) for i_ in range(min(2, len(kts)))]
                    for idx, kt in enumerate(kts):
                        s = pend.pop(0)
                        st_ctr[0] += 1
                        if idx + 2 < len(kts):
                            pend.append(score(idx + 2))
                        qlo = qlo_of(kt); N = (nq - qlo) * 128
                        t = st_ctr[0] % 3
                        S.op('scalar', L('activation', out=et[t][:, 0:N], in_=scb[s][:, 0:N], func=AF.Exp, scale=scale),
                             reads=[sckey[s]], writes=['et%d' % t])
                        d = qabs0 + qlo - kt
                        if kind == 'd' and d in (0, 1):
                            W = min(256 - d * 128, N)
                            S.op('vector', L('tensor_tensor', out=et[t][:, 0:W], in0=et[t][:, 0:W], in1=ebm[:, h, d * 128:d * 128 + W], op=ALU.mult),
                                 reads=['et%d' % t, 'ebm'], writes=['et%d' % t])
                        if kind == 'm' and d == 0:
                            S.op('gpsimd', L('memset', et[t][64:128, 0:64], 0.0), reads=['et%d' % t], writes=['et%d' % t])
                        for j in range(qlo, nq):
                            last = (qabs0 + j) if nq > 1 else kts[-1]
                            S.op('tensor', L('matmul', oacc[j][:, 0:E + 1], lhsT=et[t][:, (j - qlo) * 128:(j - qlo + 1) * 128],
                                             rhs=vt[:, kt, 0:E + 1], start=(idx == 0), stop=(kt == last)),
                                 reads=['et%d' % t, vkey, vkey + '_one'], writes=['oacc%d' % j])
                    finalize([(j, qcol0 + j * 128) for j in range(nq)])

                def rstd_small(smt, col, key, n):
                    S.op('vector', L('tensor_scalar', out=smt[:, col:col + 1], in0=smt[:, col:col + 1], scalar1=1.0 / n, scalar2=EPS,
                                     op0=ALU.mult, op1=ALU.add), reads=[key], writes=[key])
                    S.op('scalar', L('activation', out=smt[:, col:col + 1], in_=smt[:, col:col + 1], func=AF.Sqrt), reads=[key], writes=[key])
                    S.op('vector', L('reciprocal', out=smt[:, col:col + 1], in_=smt[:, col:col + 1]), reads=[key], writes=[key])

                dheads = cfg.get('b_dheads', list(range(8)))
                for hi, h in enumerate(dheads):
                    slot = hi % 2
                    S.dma('sync', 'b_q%d' % slot, L('dma_start', out=QT[slot][:], in_=dqT_s[:, h, :]), writes=['QT%d' % slot])
                    S.dma('sync', 'b_k%d' % slot, L('dma_start', out=KT[slot][:], in_=dkT_s[:, h, :]), writes=['KT%d' % slot, 'KT%d_pe' % slot])
                    for t0 in range(0, NKT, 13):
                        t1 = min(t0 + 13, NKT)
                        S.dma('sync', 'b_v%d' % slot, L('dma_start', out=VA[slot][:, t0:t1, 0:128],
                                                         in_=dv_s.ap().rearrange("(t p) f -> p t f", p=128)[:, t0:t1, h * 128:(h + 1) * 128]),
                              writes=['VA%d' % slot])
                    for grp in groups:
                        def fin0(items):
                            for (j, qrow) in items:
                                S.op('vector', L('tensor_copy', out=o0[j][:], in_=oacc[j][:, 0:129]), reads=['oacc%d' % j], writes=['o0_%d' % j])

                        def fin1(items, h=h):
                            for (j, qrow) in items:
                                S.op('vector', L('tensor_copy', out=o1[j][:], in_=oacc[j][:, 0:129]), reads=['oacc%d' % j], writes=['o1_%d' % j])
                            for (j, qrow) in items:
                                k_ = 'smB%d' % j
                                S.op('vector', L('reciprocal', out=sm[j][:, 0:1], in_=o0[j][:, 128:129]), reads=['o0_%d' % j], writes=[k_ + 'a'])
                                S.op('vector', L('reciprocal', out=sm[j][:, 1:2], in_=o1[j][:, 128:129]), reads=['o1_%d' % j], writes=[k_ + 'b'])
                                S.op('vector', L('tensor_tensor', out=sm[j][:, 1:2], in0=sm[j][:, 1:2], in1=lam[:, 2:3], op=ALU.mult),
                                     reads=[k_ + 'b', 'lam2'], writes=[k_ + 'b'])
                                S.op('vector', L('tensor_scalar', out=of[j][:], in0=o0[j][:, 0:128], scalar1=sm[j][:, 0:1], scalar2=None, op0=ALU.mult),
                                     reads=['o0_%d' % j, k_ + 'a'], writes=['of%d' % j])
                                S.op('vector', L('tensor_scalar', out=of2[j][:], in0=o1[j][:, 0:128], scalar1=sm[j][:, 1:2], scalar2=None, op0=ALU.mult),
                                     reads=['o1_%d' % j, k_ + 'b'], writes=['of2_%d' % j])
                                S.op('vector', L('tensor_tensor', out=of[j][:], in0=of[j][:], in1=of2[j][:], op=ALU.add),
                                     reads=['of%d' % j, 'of2_%d' % j], writes=['of%d' % j])
                                S.op('vector', L('tensor_tensor', out=of2[j][:], in0=of[j][:], in1=of[j][:], op=ALU.mult),
                                     reads=['of%d' % j], writes=['of2_%d' % j])
                                S.op('vector', L('reduce_sum', out=sm[j][:, 2:3], in_=of2[j][:], axis=mybir.AxisListType.X),
                                     reads=['of2_%d' % j], writes=[k_ + 'c'])
                                S.op('gpsimd', L('tensor_scalar', out=sm[j][:, 2:3], in0=sm[j][:, 2:3], scalar1=1.0 / 128, scalar2=EPS, op0=ALU.mult, op1=ALU.add),
                                     reads=[k_ + 'c'], writes=[k_ + 'c'])
                                S.op('gpsimd', L('tensor_tensor', out=sm[j][:, 2:3], in0=sm[j][:, 2:3], in1=mhalfB[:], op=ALU.pow),
                                     reads=[k_ + 'c', 'mhalfB'], writes=[k_ + 'c'])
                                S.op('vector', L('tensor_scalar', out=of[j][:], in0=of[j][:], scalar1=sm[j][:, 2:3], scalar2=None, op0=ALU.mult),
                                     reads=['of%d' % j, k_ + 'c'], writes=['of%d' % j])
                                S.op('vector', L('tensor_tensor', out=oa_t[j][:], in0=of[j][:], in1=gsub[:], op=ALU.mult),
                                     reads=['of%d' % j, 'gsub'], writes=['oa_t%d' % j])
                                S.dma('gpsimd', 'b_so%d' % j, L('dma_start', out=oa_s[qrow:qrow + 128, h * 128:(h + 1) * 128], in_=oa_t[j][:]),
                                      reads=['oa_t%d' % j])
                        attention('d', h, slot, 0, 64, 128, 0.125, VA[slot], 'VA%d' % slot, grp, fin0)
                        attention('d', h, slot, 64, 128, 128, 0.125, VA[slot], 'VA%d' % slot, grp, fin1)

                mheads = cfg.get('b_mheads', list(range(16)))
                for hi, h in enumerate(mheads):
                    slot = hi % 2
                    S.dma('sync', 'b_q%d' % slot, L('dma_start', out=QT[slot][0:96, :], in_=mqT_s[:, h, :]), writes=['QT%d' % slot])
                    S.dma('sync', 'b_k%d' % slot, L('dma_start', out=KT[slot][0:64, :], in_=mkT_s[(h % 2) * 64:(h % 2) * 64 + 64, h // 2, :]),
                          writes=['KT%d' % slot])
                    if hi < 2:
                        S.dma('sync', 'b_kpe%d' % slot, L('dma_start', out=KT[slot][64:96, :], in_=mkpeT_s.ap()), writes=['KT%d_pe' % slot])
                    for t0 in range(0, NKT, 13):
                        t1 = min(t0 + 13, NKT)
                        S.dma('sync', 'b_vm%d' % slot, L('dma_start', out=VM[slot][:, t0:t1, 0:64],
                                                          in_=mv_s.ap().rearrange("(t p) f -> p t f", p=128)[:, t0:t1, h * 64:(h + 1) * 64]),
                              writes=['VM%d' % slot])
                    for grp in groups:
                        def finm(items, h=h):
                            for (j, qrow) in items:
                                S.op('vector', L('tensor_copy', out=o1[j][:, 0:65], in_=oacc[j][:, 0:65]), reads=['oacc%d' % j], writes=['o1_%d' % j])
                            for (j, qrow) in items:
                                k_ = 'smB%d' % j
                                S.op('vector', L('reciprocal', out=sm[j][:, 0:1], in_=o1[j][:, 64:65]), reads=['o1_%d' % j], writes=[k_ + 'a'])
                                S.op('vector', L('tensor_scalar', out=ob_t[j][:], in0=o1[j][:, 0:64], scalar1=sm[j][:, 0:1], scalar2=None, op0=ALU.mult),
                                     reads=['o1_%d' % j, k_ + 'a'], writes=['ob_t%d' % j])
                                S.dma('gpsimd', 'b_sb%d' % j, L('dma_start', out=ob_s[qrow:qrow + 128, h * 64:(h + 1) * 64], in_=ob_t[j][:]),
                                      reads=['ob_t%d' % j])
                        attention('m', h, slot, 0, 96, 64, 96.0 ** -0.5, VM[slot], 'VM%d' % slot, grp, finm)

                while p_state[1] < n_ecB:
                    p_iter()
                S.end_phase()
                ensure_sems()
                S.emit(nc, sems)

        if 'C' in phases:
            with ExitStack() as es:
                def sb(name, shape, dt):
                    return es.enter_context(nc.sbuf_tensor(name, list(shape), dt))

                def ps(name, shape, dt):
                    return es.enter_context(nc.psum_tensor(name, list(shape), dt))
                S.single = ()
                S.banks = ('pT0', 'pT1', 'pacc0', 'pacc1', 'pacc2', 'pacc3', 'psS0', 'psS1')
                NB = 4
                wa_sb = sb("wa_sb", [128, 8, D], BF16); wb_sb = sb("wb_sb", [128, 8, D], BF16)
                wo_sb = sb("wo_sb", [128, 8, D], BF16); wq_sb = sb("wq_sb", [128, 8, D], BF16)
                identf = sb("identfC", [128, 128], F32); identb = sb("identbC", [128, 128], BF16)
                gffn = sb("gffn", [128, D], F32); gfin = sb("gfin", [128, D], F32)
                keys_f = sb("keys_f", [128, 16, 64], F32); keys_b = sb("keys_b", [128, 16, 64], BF16)
                keysT = sb("keysT", [128, 8, 128], BF16)
                iota_t = sb("iota_t", [128, 256], F32)
                thr = sb("thr", [128, 16], F32)
                oa_t = sb("oa_tC", [128, D], BF16); ob_t = sb("ob_tC", [128, D], BF16)
                gt = sb("gtC", [128, 2 * D], BF16); xt = sb("xtC", [128, D], F32)
                oaT = sb("oaT", [128, 8, 128], BF16); obT = sb("obT", [128, 8, 128], BF16)
                m1 = sb("m1", [128, D], F32); m2 = sb("m2", [128, D], F32); mb = sb("mb", [128, D], BF16)
                mT = sb("mT", [128, 8, 128], BF16)
                h2f = sb("h2f", [128, D], F32); h2b = sb("h2b", [128, D], BF16)
                pq_b = sb("pq_b", [128, D], BF16); pqT = sb("pqT", [128, 8, 128], BF16)
                S_all = sb("S_all", [128, 16, 128], F32); S_wk = sb("S_wk", [128, 256], F32)
                s_top = sb("s_top", [128, 16, 16], F32); i_top = sb("i_top", [128, 16, 16], U32); i_topf = sb("i_topf", [128, 16, 16], F32)
                cand = sb("cand", [128, 256], F32)
                c_top = sb("c_top", [128, 16], F32); c_pos = sb("c_pos", [128, 16], U32); c_posf = sb("c_posf", [128, 16], F32)
                t3 = sb("t3", [128, 16, 16], F32)
                av = sb("av", [128, 16], F32); bv = sb("bv", [128, 16], F32)
                i1v = sb("i1v", [128, 16], F32); i2v = sb("i2v", [128, 16], F32)
                eidf = sb("eidf", [128, 128], F32); eidi = sb("eidi", [128, 128], I32)
                g_all = sb("g_all", [128, 128], F32)
                smc = sb("smc", [128, 16], F32)
                act = sb("act", [128, 128], F32); ga = sb("ga", [128, 128], F32)
                i1_all = sb("i1_all", [128, 128], F32); i2_all = sb("i2_all", [128, 128], F32)
                rT = sb("rT", [128, 3, 128], F32)
                junkb = sb("junkbC", [128, D], BF16)
                pT = [ps("pT%d" % i, [128, 8, 128], BF16) for i in range(2)]
                pacc = [ps("pacc%d" % i, [128, 512], F32) for i in range(4)]
                psS = [ps("psS%d" % i, [128, 4, 128], F32) for i in range(2)]

                for (wsb, wsrc, wk) in [(wa_sb, w_ba, 'wa'), (wb_sb, w_bb, 'wb'), (wo_sb, w_o, 'wo'), (wq_sb, peer_wq, 'wq')]:
                    for kc in range(8):
                        S.dma('gpsimd', 'c_' + wk, L('dma_start', out=wsb[:, kc, :], in_=wsrc[kc * 128:(kc + 1) * 128, :]), writes=[wk])
                S.dma('sync', 'c_identf', L('dma_start', out=identf[:], in_=c_ident.ap()), writes=['identf'])
                S.dma('sync', 'c_gffn', L('dma_start', out=gffn[:], in_=bass.AP(norm_ffn, 0, [[0, 128], [1, D]])), writes=['gffn'])
                S.dma('sync', 'c_gfin', L('dma_start', out=gfin[:], in_=bass.AP(norm_final, 0, [[0, 128], [1, D]])), writes=['gfin'])
                S.dma('sync', 'c_keys', L('dma_start', out=keys_f[:], in_=peer_keys.ap().rearrange("a n d -> n a d")), writes=['keys_f'])
                S.dma('sync', 'c_iota', L('dma_start', out=iota_t[:], in_=c_iota.ap()), writes=['iota'])
                S.op('vector', L('tensor_copy', out=identb[:], in_=identf[:]), reads=['identf'], writes=['identb'])
                S.op('vector', L('tensor_copy', out=keys_b[:], in_=keys_f[:]), reads=['keys_f'], writes=['keys_b'])
                S.op('vector', L('tensor_scalar', out=thr[:], in0=iota_t[:, 0:16], scalar1=16.0, scalar2=None, op0=ALU.mult),
                     reads=['iota'], writes=['thr'])
                for h in range(8):
                    S.op('tensor', L('transpose', out=pT[0][:, h, :], in_=keys_b[:, 2 * h:2 * h + 2, :].rearrange("p a d -> p (a d)"), identity=identb[:]),
                         reads=['keys_b', 'identb'], writes=['pT0_%d' % h])
                S.op('vector', L('tensor_copy', out=keysT[:], in_=pT[0][:]), reads=['pT0_%d' % h for h in range(8)], writes=['keysT'])

                def transposes8(src, skey, pti, dst, dkey, eng):
                    for j in range(8):
                        S.op('tensor', L('transpose', out=pT[pti][:, j, :], in_=src[:, j * 128:(j + 1) * 128], identity=identb[:]),
                             reads=list(skey) + ['identb'], writes=['pT%d_%d' % (pti, j)])
                    rk = ['pT%d_%d' % (pti, j) for j in range(8)]
                    if eng == 'scalar':
                        S.op('scalar', L('activation', out=dst[:], in_=pT[pti][:], func=AF.Copy), reads=rk, writes=[dkey])
                    else:
                        S.op('vector', L('tensor_copy', out=dst[:], in_=pT[pti][:]), reads=rk, writes=[dkey])

                def proj(srcT, skey, wsb, wk, c, pb):
                    for kc in range(8):
                        S.op('tensor', L('matmul', pacc[pb][:], lhsT=srcT[:, kc, :], rhs=wsb[:, kc, c * 512:(c + 1) * 512],
                                         start=(kc == 0), stop=(kc == 7)), reads=[skey, wk], writes=['pacc%d' % pb])

                def rstd_c(col, key, n):
                    S.op('vector', L('tensor_scalar', out=smc[:, col:col + 1], in0=smc[:, col:col + 1], scalar1=1.0 / n, scalar2=EPS,
                                     op0=ALU.mult, op1=ALU.add), reads=[key], writes=[key])
                    S.op('scalar', L('activation', out=smc[:, col:col + 1], in_=smc[:, col:col + 1], func=AF.Sqrt), reads=[key], writes=[key])
                    S.op('vector', L('reciprocal', out=smc[:, col:col + 1], in_=smc[:, col:col + 1]), reads=[key], writes=[key])

                S_wk2 = [S_wk, sb("S_wk_b", [128, 256], F32), sb("S_wk_c", [128, 256], F32), sb("S_wk_d", [128, 256], F32)]

                def top16_multi(probs):
                    for g0 in range(0, len(probs), 4):
                        grp_ = probs[g0:g0 + 4]
                        for (src_ap, skey, vals, vkey, idxs, ikey) in grp_:
                            S.op('vector', L('max', out=vals[:, 0:8], in_=src_ap), reads=[skey], writes=[vkey + 'a'])
                        for (src_ap, skey, vals, vkey, idxs, ikey) in grp_:
                            S.op('vector', L('max_index', out=idxs[:, 0:8], in_max=vals[:, 0:8], in_values=src_ap), reads=[skey, vkey + 'a'], writes=[ikey + 'a'])
                        for w_, (src_ap, skey, vals, vkey, idxs, ikey) in enumerate(grp_):
                            n = src_ap.shape[-1]
                            S.op('vector', L('match_replace', out=S_wk2[w_][:, 0:n], in_to_replace=vals[:, 0:8], in_values=src_ap, imm_value=-1e30),
                                 reads=[skey, vkey + 'a'], writes=['S_wk%d' % w_])
                        for w_, (src_ap, skey, vals, vkey, idxs, ikey) in enumerate(grp_):
                            n = src_ap.shape[-1]
                            S.op('vector', L('max', out=vals[:, 8:16], in_=S_wk2[w_][:, 0:n]), reads=['S_wk%d' % w_], writes=[vkey + 'b'])
                        for w_, (src_ap, skey, vals, vkey, idxs, ikey) in enumerate(grp_):
                            n = src_ap.shape[-1]
                            S.op('vector', L('max_index', out=idxs[:, 8:16], in_max=vals[:, 8:16], in_values=S_wk2[w_][:, 0:n]),
                                 reads=['S_wk%d' % w_, vkey + 'b'], writes=[ikey + 'b'])

                ctiles = [('p', i) for i in range(32)] + [('s', 0)]
                ctiles = cfg.get('c_tiles', ctiles)
                mhalf = sb("mhalf", [128, 1], F32)
                S.op('gpsimd', L('memset', mhalf[:], -0.5), writes=['mhalf'])
                x2d = [sb("x2d%d" % i, [128, D], F32) for i in range(2)]
                h2Td = [sb("h2Td%d" % i, [128, 8, 128], BF16) for i in range(2)]
                S_alld = [S_all, sb("S_all1", [128, 16, 128], F32)]
                c_top_all = sb("c_top_all", [128, 8, 16], F32); c_pos_all = sb("c_pos_all", [128, 8, 16], U32)
                c_posf_all = sb("c_posf_all", [128, 128], F32)
                t3b = sb("t3b", [128, 128, 16], F32)
                av_all = sb("av_all", [128, 128], F32); bv_all = sb("bv_all", [128, 128], F32)
                zsum = sb("zsum", [128, 8], F32)
                cand4 = [cand, sb("cand_b", [128, 256], F32), sb("cand_c", [128, 256], F32), sb("cand_d", [128, 256], F32)]
                psX = psS[1]
                S.banks = ('pT0', 'pT1', 'pacc0', 'pacc1', 'pacc2', 'pacc3', 'psS0', 'psS1')

                def t8(src, skey, dst, dkey):
                    for j in range(8):
                        S.op('tensor', L('transpose', out=pT[0][:, j, :], in_=src[:, j * 128:(j + 1) * 128], identity=identb[:]),
                             reads=list(skey) + ['identb'], writes=['pT0_%d' % j])
                    S.op('scalar', L('activation', out=dst[:], in_=pT[0][:], func=AF.Copy), reads=['pT0_%d' % j for j in range(8)], writes=[dkey])

                def front(ti):
                    kind, i = ctiles[ti]
                    sl = ti % 2
                    if kind == 'p':
                        rows = 128; qrow = i * 128; xsrc = xp[qrow:qrow + 128, :]
                    else:
                        rows = 64; qrow = SP; xsrc = xs.ap()
                        S.op('gpsimd', L('memset', xt[:], 0.0), writes=['xt'])
                    S.dma('sync', 'cl_x', L('dma_start', out=xt[0:rows, :], in_=xsrc), writes=['xt'])
                    S.dma('sync', 'cl_oa', L('dma_start', out=oa_t[:], in_=oa_s[qrow:qrow + 128, :]), writes=['oa_t'])
                    S.dma('sync', 'cl_ob', L('dma_start', out=ob_t[:], in_=ob_s[qrow:qrow + 128, :]), writes=['ob_t'])
                    S.dma('sync', 'cl_g', L('dma_start', out=gt[:], in_=gates_s[qrow:qrow + 128, :]), writes=['gt'])
                    t8(oa_t, ['oa_t'], oaT, 'oaT')
                    t8(ob_t, ['ob_t'], obT, 'obT')
                    for (srcT, skey, wsb, wk, dst, dkey, pb0) in [(oaT, 'oaT', wa_sb, 'wa', m1, 'm1', 0), (obT, 'obT', wb_sb, 'wb', m2, 'm2', 2)]:
                        for c in range(2):
                            proj(srcT, skey, wsb, wk, c, pb0 + c)
                            S.op('scalar', L('activation', out=dst[:, c * 512:(c + 1) * 512], in_=pacc[pb0 + c][:], func=AF.Copy),
                                 reads=['pacc%d' % (pb0 + c)], writes=['%s_%d' % (dkey, c)])
                    S.op('gpsimd', L('tensor_tensor', out=m1[:], in0=m1[:], in1=gt[:, 0:D], op=ALU.mult), reads=['m1_0', 'm1_1', 'gt'], writes=['m1_0', 'm1_1'])
                    S.op('gpsimd', L('tensor_tensor', out=m2[:], in0=m2[:], in1=gt[:, D:2 * D], op=ALU.mult), reads=['m2_0', 'm2_1', 'gt'], writes=['m2_0', 'm2_1'])
                    S.op('gpsimd', L('tensor_tensor', out=mb[:], in0=m1[:], in1=m2[:], op=ALU.add),
                         reads=['m1_0', 'm1_1', 'm2_0', 'm2_1'], writes=['mb'])
                    t8(mb, ['mb'], mT, 'mT')
                    x2 = x2d[sl]
                    for c in range(2):
                        proj(mT, 'mT', wo_sb, 'wo', c, c)
                        S.op('scalar', L('activation', out=x2[:, c * 512:(c + 1) * 512], in_=pacc[c][:], func=AF.Copy),
                             reads=['pacc%d' % c], writes=['x2d%d_%d' % (sl, c)])
                    x2k = ['x2d%d_0' % sl, 'x2d%d_1' % sl]
                    S.op('gpsimd', L('tensor_tensor', out=x2[:], in0=x2[:], in1=xt[:], op=ALU.add), reads=x2k + ['xt'], writes=x2k)
                    S.op('scalar', L('activation', out=junkb[:], in_=x2[:], func=AF.Square, accum_out=smc[:, 0:1]), reads=x2k, writes=['junkb', 'smc0'])
                    S.op('gpsimd', L('tensor_scalar', out=smc[:, 0:1], in0=smc[:, 0:1], scalar1=1.0 / D, scalar2=EPS, op0=ALU.mult, op1=ALU.add),
                         reads=['smc0'], writes=['smc0'])
                    S.op('gpsimd', L('tensor_tensor', out=smc[:, 0:1], in0=smc[:, 0:1], in1=mhalf[:], op=ALU.pow), reads=['smc0', 'mhalf'], writes=['smc0'])
                    S.op('scalar', L('activation', out=h2f[:], in_=x2[:], func=AF.Copy, scale=smc[:, 0:1]), reads=x2k + ['smc0'], writes=['h2f'])
                    S.op('gpsimd', L('tensor_tensor', out=h2b[:], in0=h2f[:], in1=gffn[:], op=ALU.mult), reads=['h2f', 'gffn'], writes=['h2b'])
                    t8(h2b, ['h2b'], h2Td[sl], 'h2Td%d' % sl)
                    for c in range(2):
                        proj(h2Td[sl], 'h2Td%d' % sl, wq_sb, 'wq', c, 2 + c)
                        S.op('scalar', L('activation', out=pq_b[:, c * 512:(c + 1) * 512], in_=pacc[2 + c][:], func=AF.Copy),
                             reads=['pacc%d' % (2 + c)], writes=['pq_b%d' % c])
                    t8(pq_b, ['pq_b0', 'pq_b1'], pqT, 'pqT')
                    for q4 in range(4):
                        c = q4 % 2; h0 = (q4 // 2) * 4
                        bank, bkey = (psS[0], 'psS0') if c == 0 else (pacc[3].rearrange("p (a b) -> p a b", a=4), 'pacc3')
                        for i4 in range(4):
                            h = h0 + i4
                            S.op('tensor', L('matmul', bank[:, i4, :], lhsT=pqT[c * 64:(c + 1) * 64, h, :], rhs=keysT[c * 64:(c + 1) * 64, h, :],
                                             start=True, stop=True), reads=['pqT', 'keysT'], writes=[bkey + ('_%d' % i4 if c == 0 else '')])
                        lo = 2 * h0 + c
                        S.op('scalar', L('activation', out=S_alld[sl][:, lo:min(lo + 8, 16):2, :], in_=bank[:, :, :], func=AF.Copy),
                             reads=([bkey + '_%d' % i4 for i4 in range(4)] if c == 0 else [bkey]),
                             writes=['S_all%d_hc%d' % (sl, lo + 2 * i4) for i4 in range(4)])

                def back(ti):
                    sl = ti % 2
                    Sa = S_alld[sl]
                    top16_multi([(Sa[:, hc, :], 'S_all%d_hc%d' % (sl, hc), s_top[:, hc, :], 's_top%d' % hc, i_top[:, hc, :], 'i_top%d' % hc)
                                 for hc in range(16)])
                    S.op('vector', L('tensor_copy', out=i_topf[:], in_=i_top[:]),
                         reads=['i_top%d%s' % (hc, ab) for hc in range(16) for ab in 'ab'], writes=['i_topf'])
                    pstr = 16 * 16
                    for h0 in range(0, 8, 4):
                        probs = []
                        for h in range(h0, h0 + 4):
                            stk = ['s_top%d%s' % (hc, ab) for hc in (2 * h, 2 * h + 1) for ab in 'ab']
                            in0 = bass.AP(s_top, (2 * h) * 16, [[pstr, 128], [1, 16], [0, 16]])
                            in1 = bass.AP(s_top, (2 * h + 1) * 16, [[pstr, 128], [0, 16], [1, 16]])
                            cw = cand4[h - h0]
                            S.op('vector', L('tensor_tensor', out=cw[:].rearrange("p (a b) -> p a b", a=16), in0=in0, in1=in1, op=ALU.add),
                                 reads=stk, writes=['cand%d' % (h - h0)])
                            probs.append((cw[:], 'cand%d' % (h - h0), c_top_all[:, h, :], 'c_top%d' % h, c_pos_all[:, h, :], 'c_pos%d' % h))
                        top16_multi(probs)
                    ctk = ['c_top%d%s' % (h, ab) for h in range(8) for ab in 'ab']
                    cpk = ['c_pos%d%s' % (h, ab) for h in range(8) for ab in 'ab']
                    S.op('vector', L('tensor_copy', out=c_posf_all[:], in_=c_pos_all[:].rearrange("p h k -> p (h k)")), reads=cpk, writes=['c_posf_all'])
                    S.op('vector', L('tensor_tensor', out=t3b[:, :, 0:15], in0=bass.AP(c_posf_all, 0, [[128, 128], [1, 128], [0, 15]]),
                                     in1=bass.AP(thr, 1, [[16, 128], [0, 128], [1, 15]]), op=ALU.is_ge), reads=['c_posf_all', 'thr'], writes=['t3b'])
                    S.op('vector', L('reduce_sum', out=av_all[:], in_=t3b[:, :, 0:15], axis=mybir.AxisListType.X), reads=['t3b'], writes=['av_all'])
                    S.op('vector', L('scalar_tensor_tensor', out=bv_all[:], in0=av_all[:], scalar=-16.0, in1=c_posf_all[:], op0=ALU.mult, op1=ALU.add),
                         reads=['av_all', 'c_posf_all'], writes=['bv_all'])
                    for (sel, skey, c, dst, dkey) in [(av_all, 'av_all', 0, i1_all, 'i1_all'), (bv_all, 'bv_all', 1, i2_all, 'i2_all')]:
                        S.op('vector', L('tensor_tensor', out=t3b[:], in0=bass.AP(sel, 0, [[128, 128], [1, 128], [0, 16]]),
                                         in1=bass.AP(iota_t, 0, [[256, 128], [0, 128], [1, 16]]), op=ALU.is_equal), reads=[skey, 'iota'], writes=['t3b'])
                        S.op('vector', L('tensor_tensor', out=t3b[:].rearrange("p (h k) a -> p h k a", h=8), in0=t3b[:].rearrange("p (h k) a -> p h k a", h=8),
                                         in1=bass.AP(i_topf, c * 16, [[pstr, 128], [32, 8], [0, 16], [1, 16]]), op=ALU.mult),
                             reads=['t3b', 'i_topf'], writes=['t3b'])
                        S.op('vector', L('reduce_sum', out=dst[:], in_=t3b[:], axis=mybir.AxisListType.X), reads=['t3b'], writes=[dkey])
                    S.op('vector', L('tensor_tensor', out=g_all[:].rearrange("p (h k) -> p h k", h=8), in0=c_top_all[:],
                                     in1=bass.AP(c_top_all, 0, [[128, 128], [16, 8], [0, 16]]), op=ALU.subtract), reads=ctk, writes=['g_all'])
                    S.op('scalar', L('activation', out=g_all[:], in_=g_all[:], func=AF.Exp), reads=['g_all'], writes=['g_all'])
                    S.op('vector', L('reduce_sum', out=zsum[:], in_=g_all[:].rearrange("p (h k) -> p h k", h=8), axis=mybir.AxisListType.X),
                         reads=['g_all'], writes=['zsum'])
                    S.op('vector', L('reciprocal', out=zsum[:], in_=zsum[:]), reads=['zsum'], writes=['zsum'])
                    S.op('vector', L('tensor_tensor', out=g_all[:].rearrange("p (h k) -> p h k", h=8), in0=g_all[:].rearrange("p (h k) -> p h k", h=8),
                                     in1=bass.AP(zsum, 0, [[8, 128], [1, 8], [0, 16]]), op=ALU.mult), reads=['g_all', 'zsum'], writes=['g_all'])

                def export(ti):
                    kind, i = ctiles[ti]
                    sl = ti % 2
                    qrow = i * 128 if kind == 'p' else SP
                    for ri, (src, key_) in enumerate([(i1_all, 'i1_all'), (i2_all, 'i2_all'), (g_all, 'g_all')]):
                        S.op('tensor', L('transpose', out=psX[:, ri, :], in_=src[:], identity=identf[:]),
                             reads=[key_, 'identf'], writes=['psS1_%d' % ri])
                    S.op('scalar', L('activation', out=rT[:], in_=psX[:, 0:3, :], func=AF.Copy), reads=['psS1_0', 'psS1_1', 'psS1_2'], writes=['rT'])
                    S.dma('sync', 'cs_r', L('dma_start', out=r_s[:, :, qrow:qrow + 128], in_=rT[:]), reads=['rT'])
                    S.dma('sync', 'cs_x2', L('dma_start', out=x2_s[qrow:qrow + 128, :], in_=x2d[sl][:]), reads=['x2d%d_0' % sl, 'x2d%d_1' % sl])
                    S.dma('sync', 'cs_h2', L('dma_start', out=h2T_s[:, :, qrow:qrow + 128], in_=h2Td[sl][:]), reads=['h2Td%d' % sl])

                if ctiles:
                    front(0)
                for ti in range(len(ctiles)):
                    if ti + 1 < len(ctiles):
                        front(ti + 1)
                    back(ti)
                    export(ti)

                S.end_phase()
                ensure_sems()
                S.emit(nc, sems)

        if 'C' in phases and DENSE:
            n_ec = cfg.get('n_ec', 128)
            with ExitStack() as es:
                def sb(name, shape, dt):
                    return es.enter_context(nc.sbuf_tensor(name, list(shape), dt))

                def ps(name, shape, dt):
                    return es.enter_context(nc.psum_tensor(name, list(shape), dt))
                S.single = ()
                S.banks = ('ppT0', 'ppT1')
                identf = sb("identfP", [128, 128], F32); identb = sb("identbP", [128, 128], BF16)
                ub = [sb("ub%d" % i, [128, D], BF16) for i in range(3)]
                vb = [sb("vbD%d" % i, [128, D], BF16) for i in range(3)]
                uT = [sb("uT%d" % i, [128, 8, 128], BF16) for i in range(3)]
                ppT = [ps("ppT%d" % i, [128, 8, 128], BF16) for i in range(2)]
                S.dma('sync', 'p_identf', L('dma_start', out=identf[:], in_=c_ident.ap()), writes=['identf'])
                S.op('vector', L('tensor_copy', out=identb[:], in_=identf[:]), reads=['identf'], writes=['identb'])
                for ec in range(0 if ('B' in phases and cfg.get('p_in_b', True)) else n_ec):
                    s2 = ec % 3
                    pp = ec % 2
                    S.dma('gpsimd', 'p_u%d' % s2, L('dma_start', out=ub[s2][:], in_=peer_u[ec * 128:(ec + 1) * 128, :]), writes=['ub%d' % s2])
                    S.dma('gpsimd', 'p_v%d' % s2, L('dma_start', out=vb[s2][:], in_=peer_v[ec * 128:(ec + 1) * 128, :]), writes=['vb%d' % s2])
                    for dc in range(8):
                        S.op('tensor', L('transpose', out=ppT[pp][:, dc, :], in_=ub[s2][:, dc * 128:(dc + 1) * 128], identity=identb[:]),
                             reads=['ub%d' % s2, 'identb'], writes=['ppT%d_%d' % (pp, dc)])
                    if ec % 2 == 0:
                        S.op('vector', L('tensor_copy', out=uT[s2][:], in_=ppT[pp][:]), reads=['ppT%d_%d' % (pp, dc) for dc in range(8)], writes=['uT%d' % s2])
                    else:
                        S.op('scalar', L('activation', out=uT[s2][:], in_=ppT[pp][:], func=AF.Copy), reads=['ppT%d_%d' % (pp, dc) for dc in range(8)], writes=['uT%d' % s2])
                    S.dma('sync', 'p_su%d' % s2, L('dma_start', out=UT_s[ec // 2][:, ec % 2, :].rearrange("p (a b) -> p a b", a=8), in_=uT[s2][:]),
                          reads=['uT%d' % s2], writes=['UT_s'])
                    S.dma('sync', 'p_sv%d' % s2, L('dma_start', out=V_s[ec // 2][:, ec % 2, :], in_=vb[s2][:]), reads=['vb%d' % s2], writes=['V_s'])
                S.end_phase()
                ensure_sems()
                S.emit(nc, sems)
            with ExitStack() as es:
                def sb(name, shape, dt):
                    return es.enter_context(nc.sbuf_tensor(name, list(shape), dt))

                def ps(name, shape, dt):
                    return es.enter_context(nc.psum_tensor(name, list(shape), dt))
                S.single = ()
                S.banks = ('pact0', 'pact1', 'pout0', 'pout1', 'pout2', 'pout3', 'pout4', 'pout5', 'pw0', 'pw1')
                NTT = TG // 128
                gfin = sb("gfinD", [128, D], F32)
                iota_b = sb("iota_b", [128, 128], F32)
                NBUF = 4
                utp = [sb("utp%d" % i, [128, 2, 8, 128], BF16) for i in range(NBUF)]
                vtp = [sb("vtp%d" % i, [128, 2, D], BF16) for i in range(NBUF)]
                h2g = [sb("h2g%d" % i, [128, 8, TG], BF16) for i in range(2)]
                rg = [sb("rg%d" % i, [128, 3, TG], F32) for i in range(2)]
                oh2 = [sb("oh2_%d" % i, [128, 128], BF16) for i in range(2)]
                oh1 = [sb("oh1_%d" % i, [128, 128], BF16) for i in range(2)]
                WT = [sb("WT%d" % i, [128, TG, 128], BF16) for i in range(2)]
                gl = [sb("gl%d" % i, [128, TG], BF16) for i in range(2)]
                gad = [sb("gad%d" % i, [128, TG], BF16) for i in range(2)]
                x2t = sb("x2t", [128, D], F32); accd = sb("accd", [128, D], F32); ytd = sb("ytd", [128, D], F32)
                junkd = sb("junkd", [128, D], BF16); smd = sb("smd", [128, 4], F32)
                npout = 2 * NTT
                pact = [ps("pact%d" % i, [128, 512], F32) for i in range(2 if npout <= 4 else 1)]
                pout = [ps("pout%d" % i, [128, 512], F32) for i in range(npout)]
                pw = [ps("pw%d" % i, [128, 4, 128], F32) for i in range(1)]

                S.dma('sync', 'd_gfin', L('dma_start', out=gfin[:], in_=bass.AP(norm_final, 0, [[0, 128], [1, D]])), writes=['gfin'])
                S.dma('sync', 'd_iota', L('dma_start', out=iota_b[:], in_=c_iota[:, 0:128]), writes=['iota_b'])
                dgroups = [(g * TG, TG, [('p', g * TG + j * 128) for j in range(NTT)]) for g in range(SP // TG)] + [(SP, 128, [('s', SP)])]
                dgroups = cfg.get('d_groups', dgroups)
                pairs_total = len(dgroups) * (n_ec // 2)
                issued = [0]

                def ensure_loaded(upto):
                    while issued[0] < min(upto, pairs_total):
                        gp = issued[0]; p = gp % (n_ec // 2); b = gp % NBUF
                        S.dma('sync', 'd_ut%d' % b, L('dma_start', out=utp[b][:].rearrange("p j a b -> p j (a b)"), in_=UT_s[p]), writes=['utp%d' % b])
                        S.dma('sync', 'd_vt%d' % b, L('dma_start', out=vtp[b][:], in_=V_s[p]), writes=['vtp%d' % b])
                        issued[0] += 1

                def load_group(gi):
                    q0, tg, _ = dgroups[gi]
                    w = gi % 2
                    S.dma('sync', 'd_h2_%d' % w, L('dma_start', out=h2g[w][:, :, 0:tg], in_=h2T_s[:, :, q0:q0 + tg]), writes=['h2g%d' % w])
                    S.dma('sync', 'd_r%d' % w, L('dma_start', out=rg[w][:, :, 0:tg], in_=r_s[:, :, q0:q0 + tg]), writes=['rg%d' % w])

                def wbuild(gi, t):
                    w = gi % 2
                    o = t % 2
                    S.op('vector', L('tensor_scalar', out=oh2[o][:], in0=iota_b[:], scalar1=rg[w][:, 1, t:t + 1], scalar2=None, op0=ALU.is_equal),
                         reads=['iota_b', 'rg%d' % w], writes=['oh2_%d' % o])
                    S.op('vector', L('tensor_scalar', out=oh1[o][:], in0=iota_b[:], scalar1=rg[w][:, 0, t:t + 1], scalar2=rg[w][:, 2, t:t + 1],
                                     op0=ALU.is_equal, op1=ALU.mult), reads=['iota_b', 'rg%d' % w], writes=['oh1_%d' % o])
                    S.op('tensor', L('matmul', pw[0][:, t % 4, :], lhsT=oh2[o][:], rhs=oh1[o][:], start=True, stop=True),
                         reads=['oh2_%d' % o, 'oh1_%d' % o], writes=['pw0_%d' % (t % 4)])
                    if t % 4 == 3:
                        S.op('scalar', L('activation', out=WT[w][:, t - 3:t + 1, :], in_=pw[0][:], func=AF.Copy),
                             reads=['pw0_%d' % j for j in range(4)], writes=['WT%d' % w])

                if dgroups:
                    ensure_loaded(3)
                    load_group(0)
                    for t in range(dgroups[0][1]):
                        wbuild(0, t)
                for gi, (q0, tg, ttiles) in enumerate(dgroups):
                    ntt = tg // 128
                    w = gi % 2
                    nxt = gi + 1 if gi + 1 < len(dgroups) else None
                    if nxt is not None:
                        load_group(nxt)
                        ntok_next = dgroups[nxt][1]
                        per = (ntok_next + n_ec - 1) // n_ec
                    wb_t = [0]

                    def act_mm(ec, gi=gi, w=w, tg=tg):
                        gp = gi * (n_ec // 2) + ec // 2
                        b = gp % NBUF; j = ec % 2
                        pa = ec % len(pact)
                        for dc in range(8):
                            S.op('tensor', L('matmul', pact[pa][:, 0:tg], lhsT=utp[b][:, j, dc, :], rhs=h2g[w][:, dc, 0:tg], start=(dc == 0), stop=(dc == 7)),
                                 reads=['utp%d' % b, 'h2g%d' % w], writes=['pact%d' % pa])
                        return (b, j)
                    pend_b = act_mm(0) if n_ec > 0 else None
                    for ec in range(n_ec):
                        b = pend_b
                        pa = ec % len(pact)
                        gb = ec % 2
                        S.op('scalar', L('activation', out=gl[gb][:, 0:tg], in_=pact[pa][:, 0:tg], func=AF.Gelu), reads=['pact%d' % pa], writes=['gl%d' % gb])
                        S.op('vector', L('tensor_tensor', out=gad[gb][:, 0:tg], in0=gl[gb][:, 0:tg], in1=WT[w][:, 0:tg, ec], op=ALU.mult),
                             reads=['gl%d' % gb, 'WT%d' % w], writes=['gad%d' % gb])
                        if ec + 1 < n_ec:
                            pend_b = act_mm(ec + 1)
                        for tt in range(ntt):
                            for dh in range(2):
                                S.op('tensor', L('matmul', pout[tt * 2 + dh][:], lhsT=gad[gb][:, tt * 128:(tt + 1) * 128], rhs=vtp[b[0]][:, b[1], dh * 512:(dh + 1) * 512],
                                                 start=(ec == 0), stop=(ec == n_ec - 1)),
                                     reads=['gad%d' % gb, 'vtp%d' % b[0]], writes=['pout%d' % (tt * 2 + dh)])
                        if ec % 2 == 1:
                            ensure_loaded(gi * (n_ec // 2) + ec // 2 + 4)
                        if nxt is not None:
                            for _ in range(per):
                                if wb_t[0] < ntok_next:
                                    wbuild(nxt, wb_t[0])
                                    wb_t[0] += 1
                    if nxt is not None:
                        while wb_t[0] < ntok_next:
                            wbuild(nxt, wb_t[0])
                            wb_t[0] += 1
                    for tt, (kind, qrow) in enumerate(ttiles):
                        rows = 128 if kind == 'p' else 64
                        ydst = yp[qrow:qrow + 128, :] if kind == 'p' else ys.ap()
                        S.dma('sync', 'd_x2', L('dma_start', out=x2t[:], in_=x2_s[qrow:qrow + 128, :]), writes=['x2t'])
                        for dh in range(2):
                            S.op('vector', L('tensor_tensor', out=accd[:, dh * 512:(dh + 1) * 512], in0=pout[tt * 2 + dh][:], in1=x2t[:, dh * 512:(dh + 1) * 512], op=ALU.add),
                                 reads=['pout%d' % (tt * 2 + dh), 'x2t'], writes=['accd%d' % dh])
                        S.op('scalar', L('activation', out=junkd[:], in_=accd[:], func=AF.Square, accum_out=smd[:, 0:1]),
                             reads=['accd0', 'accd1'], writes=['junkd', 'smd0'])
                        S.op('vector', L('tensor_scalar', out=smd[:, 0:1], in0=smd[:, 0:1], scalar1=1.0 / D, scalar2=EPS, op0=ALU.mult, op1=ALU.add),
                             reads=['smd0'], writes=['smd0'])
                        S.op('scalar', L('activation', out=smd[:, 0:1], in_=smd[:, 0:1], func=AF.Sqrt), reads=['smd0'], writes=['smd0'])
                        S.op('vector', L('reciprocal', out=smd[:, 0:1], in_=smd[:, 0:1]), reads=['smd0'], writes=['smd0'])
                        S.op('vector', L('tensor_scalar', out=ytd[:], in0=accd[:], scalar1=smd[:, 0:1], scalar2=None, op0=ALU.mult),
                             reads=['accd0', 'accd1', 'smd0'], writes=['ytd'])
                        S.op('gpsimd', L('tensor_tensor', out=ytd[:], in0=ytd[:], in1=gfin[:], op=ALU.mult), reads=['ytd', 'gfin'], writes=['ytd'])
                        S.dma('sync', 'd_y', L('dma_start', out=ydst, in_=ytd[0:rows, :]), reads=['ytd'])

                S.end_phase()
                ensure_sems()
                S.emit(nc, sems)

    return nc


_CACHE = {}


def _consts():
    if 'c' not in _CACHE:
        oh, maskw = _bias_consts()
        _CACHE['c'] = dict(
            c_ident=np.eye(128, dtype=np.float32),
            c_rope=_rope_table(),
            c_oh=oh, c_maskw=maskw,
            c_iota=np.tile(np.arange(256, dtype=np.float32)[None, :], (128, 1)),
        )
    return _CACHE['c']


def kernel(x_prompt, x_sample, cache_diff_k, cache_diff_v, cache_mla_ckv, cache_mla_kpe,
           rel_bias, norm_mix, w_in, diff_lambda, diff_subln, mla_q_norm, mla_w_uq, mla_kv_norm,
           mla_w_uk, mla_w_uv, w_branch_a, w_branch_b, w_out, norm_ffn, peer_w_q, peer_keys,
           peer_u, peer_v, norm_final):
    f = lambda a: np.ascontiguousarray(np.asarray(a, dtype=np.float32))
    shared = dict(
        rel_bias=f(rel_bias), norm_mix=f(norm_mix).reshape(D), w_in=f(w_in).reshape(D, INW),
        diff_lambda=f(diff_lambda).reshape(256), diff_subln=f(diff_subln).reshape(128),
        mla_q_norm=f(mla_q_norm).reshape(256), w_uq=f(mla_w_uq).reshape(256, 1536),
        mla_kv_norm=f(mla_kv_norm).reshape(128), w_uk=f(mla_w_uk).reshape(128, 1024), w_uv=f(mla_w_uv).reshape(128, 1024),
        w_ba=f(w_branch_a).reshape(D, D), w_bb=f(w_branch_b).reshape(D, D), w_o=f(w_out).reshape(D, D),
        norm_ffn=f(norm_ffn).reshape(D), peer_wq=f(peer_w_q).reshape(D, D),
        peer_keys=f(peer_keys).reshape(16, 128, 64), peer_u=f(peer_u).reshape(16384, D), peer_v=f(peer_v).reshape(16384, D),
        norm_final=f(norm_final).reshape(D),
    )
    shared.update(_consts())
    xpf = f(x_prompt); xsf = f(x_sample)
    cdkf = f(cache_diff_k).reshape(NCORES, PAST, D); cdvf = f(cache_diff_v).reshape(NCORES, PAST, D)
    cckvf = f(cache_mla_ckv).reshape(NCORES, PAST, 128); ckpef = f(cache_mla_kpe).reshape(NCORES, PAST, 32)
    in_maps = []
    for c in range(NCORES):
        m = dict(shared)
        m.update(xp=xpf[c], xs=xsf[c], cdk=cdkf[c], cdv=cdvf[c], cckv=cckvf[c], ckpe=ckpef[c])
        in_maps.append(m)
    nc = build_program()
    res = run_bass_kernel_spmd(nc, in_maps, core_ids=list(range(NCORES)))
    R = res.results

    def g(name, shape):
        return np.stack([np.asarray(R[c][name], dtype=np.float32) for c in range(NCORES)], 0).reshape(shape)
    return (g("yp", (8, SP, D)), g("ys", (8, SS, D)),
            g("kp", (1, 8, SP, 8, 2, 64)), g("vp", (1, 8, SP, 8, 128)), g("cp", (1, 8, SP, 128)), g("ep", (1, 8, SP, 32)),
            g("ks", (1, 8, SS, 8, 2, 64)), g("vs", (1, 8, SS, 8, 128)), g("cs", (1, 8, SS, 128)), g("es", (1, 8, SS, 32)))
```

```python
import math
from contextlib import ExitStack

import numpy as np
import ml_dtypes

import concourse.bass as bass
import concourse.mybir as mybir
from concourse.bass_utils import run_bass_kernel_spmd

F32 = mybir.dt.float32
BF16 = mybir.dt.bfloat16
U32 = mybir.dt.uint32
I32 = mybir.dt.int32
ALU = mybir.AluOpType
AF = mybir.ActivationFunctionType

NCORES = 8
D = 1024
SP = 4096
SS = 64
PAST = 2048
INW = 5536
NQ = SP + 128
NT = SP + PAST + 128
KS0 = SP
EPS = 1e-6
LAMBDA_INIT = 0.8 - 0.6 * math.exp(0.0)


def L(name, *a, **kw):
    return (name, a, kw)


class Sched:
    ENGS = ['tensor', 'vector', 'scalar', 'gpsimd', 'sync']

    def __init__(self):
        self.prog = {e: [] for e in self.ENGS}
        self.cnt = {e: 0 for e in self.ENGS}
        self.waited = {e: {} for e in self.ENGS}
        self.last_w = {}
        self.readers = {}
        self.semkeys = list(self.ENGS)
        self.phys_of = {}
        self.free = {}
        self.phys_q = {}
        self.nphys = 0
        self.nops = 0
        self.single = ()

    banks = ()

    def with_banks(self, reads, writes):
        extra = []
        for k in list(reads) + list(writes):
            for b in self.banks:
                if k.startswith(b):
                    bk = 'BANK:' + b
                    if bk not in extra:
                        extra.append(bk)
                    break
        return list(writes) + extra

    def canon(self, keys):
        out = []
        for k in keys:
            for p in self.single:
                if k.startswith(p + '1'):
                    k = p + '0' + k[len(p) + 1:]
                    break
            out.append(k)
        return out

    def _need(self, eng, tok, same_ok=False):
        if tok is None:
            return
        key, val = tok
        if same_ok and key == eng and eng == 'tensor':
            return
        if self.waited[eng].get(key, 0) >= val:
            return
        self.waited[eng][key] = val
        self.prog[eng].append(('wait', key, val))

    def _deps(self, eng, reads, writes):
        for r in reads:
            self._need(eng, self.last_w.get(r))
        for w in writes:
            self._need(eng, self.last_w.get(w), same_ok=True)
            for tok in self.readers.get(w, ()):
                self._need(eng, tok, same_ok=True)

    def _commit(self, tok, reads, writes):
        for r in reads:
            lst = self.readers.setdefault(r, [])
            lst.append(tok)
            if len(lst) > 48:
                best = {}
                for k, v in lst:
                    best[k] = max(best.get(k, 0), v)
                self.readers[r] = list(best.items())
        for w in writes:
            self.last_w[w] = tok
            self.readers[w] = []

    max_ops = 10 ** 9
    allow_quiet = True

    def op(self, eng, fn, reads=(), writes=(), quiet=False):
        if self.nops >= self.max_ops:
            return None
        reads = self.canon(reads); writes = self.with_banks(reads, self.canon(writes))
        self._deps(eng, reads, writes)
        if quiet and eng == 'tensor' and self.allow_quiet:
            tok = (eng, self.cnt[eng] + 1)
            self.prog[eng].append(('opq', fn))
            self._commit(tok, reads, writes)
            self.nops += 1
            return tok
        self.cnt[eng] += 1
        tok = (eng, self.cnt[eng])
        self.prog[eng].append(('op', fn, eng, 1))
        self._commit(tok, reads, writes)
        self.nops += 1
        return tok

    def dma(self, q, semkey, fn, reads=(), writes=()):
        if self.nops >= self.max_ops:
            return None
        phys = self.phys_of.get(semkey)
        if phys is None:
            if self.free.get(q):
                phys = self.free[q].pop()
            else:
                phys = 'dma%s%d' % (q[0], self.nphys)
                self.nphys += 1
                self.cnt[phys] = 0
                self.semkeys.append(phys)
            self.phys_of[semkey] = phys
            self.phys_q[phys] = q
        reads = self.canon(reads); writes = self.with_banks(reads, self.canon(writes))
        self._deps(q, reads, writes)
        self.cnt[phys] += 16
        tok = (phys, self.cnt[phys])
        self.prog[q].append(('op', fn, phys, 16))
        self._commit(tok, reads, writes)
        self.nops += 1
        return tok

    def wait_dma(self, eng, semkey):
        phys = self.phys_of.get(semkey)
        if phys is not None:
            self._need(eng, (phys, self.cnt[phys]))

    def end_phase(self):
        for k in sorted(self.phys_of):
            phys = self.phys_of[k]
            self._need('sync', (phys, self.cnt[phys]))
            self.free.setdefault(self.phys_q[phys], []).append(phys)
        self.phys_of = {}

    def emit(self, nc, sems):
        prog = self.prog
        self.prog = {e: [] for e in self.ENGS}
        self.last_w = {}
        self.readers = {}

        def run(engname, e):
            for it in prog[engname]:
                if it[0] == 'wait':
                    e.wait_ge(sems[it[1]], it[2])
                elif it[0] == 'opq':
                    name, a, kw = it[1]
                    getattr(e, name)(*a, **kw)
                else:
                    _, fn, key, inc = it
                    name, a, kw = fn
                    getattr(e, name)(*a, **kw).then_inc(sems[key], inc)

        with nc.Block() as block:
            @block.tensor
            def _(e):
                run('tensor', e)

            @block.vector
            def _(e):
                run('vector', e)

            @block.scalar
            def _(e):
                run('scalar', e)

            @block.gpsimd
            def _(e):
                run('gpsimd', e)

            @block.sync
            def _(e):
                run('sync', e)


def _rope_table():
    half = 16
    inv = (np.float32(10000.0) ** (-np.arange(half, dtype=np.float32) / np.float32(half))).astype(np.float32)
    pos = np.concatenate([np.arange(SP), PAST + np.arange(128)]).astype(np.float32)
    ang = (pos[:, None] * inv[None, :]).astype(np.float32)
    return np.concatenate([np.cos(ang), np.sin(ang)], axis=1).astype(np.float32)


def _t5_bucket(rel):
    nb = 16
    max_exact = 8
    ret = np.where(rel > 0, nb, 0)
    n = np.abs(rel)
    lg = (np.log(np.maximum(n, 1).astype(np.float32) / np.float32(max_exact)) / np.float32(math.log(128 / max_exact))
          * np.float32(nb - max_exact)).astype(np.float32)
    large = max_exact + lg.astype(np.int32)
    large = np.minimum(large, nb - 1)
    return ret + np.where(n < max_exact, n, large)


def _bias_consts():
    rel = np.arange(383) - 255
    bk = _t5_bucket(rel)
    oh = np.zeros((32, 384), np.float32)
    oh[bk, np.arange(383)] = 1.0
    kk = np.arange(128)[:, None]
    qq = np.arange(256)[None, :]
    maskw = ((kk // 64) <= (qq // 64)).astype(np.float32)
    return oh, maskw


def build_program(phases=('A', 'B', 'C'), cfg=None):
    cfg = cfg or {}
    nc = bass.Bass("TRN2", target_bir_lowering=False)
    S = Sched()
    S.max_ops = cfg.get('max_ops', 10 ** 9)
    S.allow_quiet = cfg.get('quiet', True)

    def din(name, shape, dt=F32):
        return nc.dram_tensor(name, list(shape), dt, kind="ExternalInput")

    def dout(name, shape, dt=F32):
        return nc.dram_tensor(name, list(shape), dt, kind="ExternalOutput")

    def dscr(name, shape, dt=BF16):
        if cfg.get('dbg_scratch') and name in ('oa_s', 'ob_s', 'gates_s'):
            return nc.dram_tensor(name, list(shape), dt, kind="ExternalOutput")
        return nc.dram_tensor(name, list(shape), dt)

    xp = din("xp", [SP, D]); xs = din("xs", [SS, D])
    cdk = din("cdk", [PAST, D]); cdv = din("cdv", [PAST, D])
    cckv = din("cckv", [PAST, 128]); ckpe = din("ckpe", [PAST, 32])
    rel_bias = din("rel_bias", [32, 8])
    norm_mix = din("norm_mix", [D]); w_in = din("w_in", [D, INW])
    diff_lambda = din("diff_lambda", [4 * 64]); diff_subln = din("diff_subln", [128])
    mla_q_norm = din("mla_q_norm", [256]); w_uq = din("w_uq", [256, 1536])
    mla_kv_norm = din("mla_kv_norm", [128]); w_uk = din("w_uk", [128, 1024]); w_uv = din("w_uv", [128, 1024])
    w_ba = din("w_ba", [D, D]); w_bb = din("w_bb", [D, D]); w_o = din("w_o", [D, D])
    norm_ffn = din("norm_ffn", [D]); peer_wq = din("peer_wq", [D, D])
    peer_keys = din("peer_keys", [16, 128, 64])
    peer_u = din("peer_u", [16384, D]); peer_v = din("peer_v", [16384, D])
    norm_final = din("norm_final", [D])
    c_ident = din("c_ident", [128, 128]); c_rope = din("c_rope", [NQ, 32])
    c_oh = din("c_oh", [32, 384]); c_maskw = din("c_maskw", [128, 256])
    c_iota = din("c_iota", [128, 256])

    yp = dout("yp", [SP, D]); ys = dout("ys", [SS, D])
    kp = dout("kp", [SP, D]); vp = dout("vp", [SP, D]); cp = dout("cp", [SP, 128]); ep = dout("ep", [SP, 32])
    ks_ = dout("ks", [SS, D]); vs_ = dout("vs", [SS, D]); cs_ = dout("cs", [SS, 128]); es_ = dout("es", [SS, 32])

    dqT_s = dscr("dqT_s", [128, 8, NQ]); dkT_s = dscr("dkT_s", [128, 8, NT]); dv_s = dscr("dv_s", [NT, D])
    mqT_s = dscr("mqT_s", [96, 16, NQ]); mkT_s = dscr("mkT_s", [128, 8, NT]); mkpeT_s = dscr("mkpeT_s", [32, NT])
    mv_s = dscr("mv_s", [NT, D]); gates_s = dscr("gates_s", [NQ, 2 * D])
    oa_s = dscr("oa_s", [NQ, D]); ob_s = dscr("ob_s", [NQ, D])
    fbias_s = dscr("fbias_s", [8, 384], F32)
    DENSE = cfg.get('peer', 'dense') == 'dense'
    TG = cfg.get('tg', 256)
    UT_s = dscr("UT_s", [64, 128, 2, 1024]); V_s = dscr("V_s", [64, 128, 2, 1024])
    x2_s = dscr("x2_s", [NQ, D], F32); h2T_s = dscr("h2T_s", [128, 8, NQ])
    r_s = dscr("r_s", [128, 3, NQ], F32)

    with ExitStack() as gs:
        sems = {}

        def ensure_sems():
            for k in S.semkeys:
                if k not in sems:
                    sems[k] = gs.enter_context(nc.semaphore("s_" + k))

        if 'A' in phases:
            with ExitStack() as es:
                def sb(name, shape, dt):
                    return es.enter_context(nc.sbuf_tensor(name, list(shape), dt))

                def ps(name, shape, dt):
                    return es.enter_context(nc.psum_tensor(name, list(shape), dt))

                def sb1(name, shape, dt):
                    t = sb(name, shape, dt)
                    return [t, t]
                S.banks = ('pz0', 'pz1', 'pTa', 'pTb0', 'pTb1', 'pmm0', 'pmm1', 'pTs')
                S.single = ('zk_f', 'zv_f', 'gates_b', 'qh_f', 'rtmp', 'hb', 'cq_b', 'vm_b')

                win_sb = sb("win_sb", [128, 8, INW], BF16)
                wuq_sb = sb("wuq_sb", [128, 2, 1536], BF16)
                wuk_sb = sb("wuk_sb", [128, 1024], BF16)
                wuv_sb = sb("wuv_sb", [128, 1024], BF16)
                identf = sb("identf", [128, 128], F32)
                identb = sb("identb", [128, 128], BF16)
                gmix = sb("gmix", [128, D], F32)
                gq = sb("gq", [128, 256], F32)
                gkv = sb("gkv", [128, 128], F32)
                rope_sb = sb("rope_sb", [128, 33, 32], F32)
                xt = [sb("xt%d" % i, [128, D], F32) for i in range(2)]
                junk = sb("junk", [128, D], BF16)
                sgtmp = sb("sgtmp", [128, D], F32)
                st = [sb("st%d" % i, [128, 8], F32) for i in range(2)]
                hb = sb1("hb", [128, D], BF16)
                hT = [sb("hT%d" % i, [128, 8, 128], BF16) for i in range(2)]
                zq_b = [sb("zq_b%d" % i, [128, D], BF16) for i in range(2)]
                zk_f = sb1("zk_f", [128, D], F32)
                zk_b = [sb("zk_b%d" % i, [128, D], BF16) for i in range(2)]
                zv_f = sb1("zv_f", [128, D], F32)
                zv_b = [sb("zv_b%d" % i, [128, D], BF16) for i in range(2)]
                zm = [sb("zm%d" % i, [128, 416], F32) for i in range(2)]
                gates_b = sb1("gates_b", [128, 2 * D], BF16)
                qT_sb = [sb("qT_sb%d" % i, [128, 8, 128], BF16) for i in range(2)]
                kT_sb = [sb("kT_sb%d" % i, [128, 8, 128], BF16) for i in range(2)]
                cq_b = sb1("cq_b", [128, 256], BF16)
                cqT = [sb("cqT%d" % i, [128, 2, 128], BF16) for i in range(2)]
                qh_f = sb1("qh_f", [128, 16, 96], F32)
                qh_b = [sb("qh_b%d" % i, [128, 16, 96], BF16) for i in range(2)]
                rtmp = sb1("rtmp", [128, 4, 16, 16], F32)
                mqT_sb = [sb("mqT_sb%d" % i, [128, 16, 128], BF16) for i in range(2)]
                ckv_f = [sb("ckv_f%d" % i, [128, 128], F32) for i in range(2)]
                ckv_b = [sb("ckv_b%d" % i, [128, 128], BF16) for i in range(2)]
                ckvT = [sb("ckvT%d" % i, [128, 128], BF16) for i in range(2)]
                kn_b = [sb("kn_b%d" % i, [128, D], BF16) for i in range(2)]
                vm_b = sb1("vm_b", [128, D], BF16)
                knT_sb = [sb("knT_sb%d" % i, [128, 8, 128], BF16) for i in range(2)]
                kpe_f = [sb("kpe_f%d" % i, [128, 32], F32) for i in range(2)]
                kpe_b = [sb("kpe_b%d" % i, [128, 32], BF16) for i in range(2)]
                kpeT_sb = [sb("kpeT_sb%d" % i, [32, 128], BF16) for i in range(2)]
                pTa = ps("pTa", [128, 8, 128], BF16)
                pz = [ps("pz%d" % i, [128, 512], F32) for i in range(2)]
                pTb = [ps("pTb%d" % i, [128, 8, 128], BF16) for i in range(2)]
                pmm = [ps("pmm%d" % i, [128, 512], F32) for i in range(2)]
                pTs = ps("pTs", [128, 4, 128], BF16)

                for gi_, (n0, nw) in enumerate([(0, 1024), (1024, 2048), (3072, 416), (3488, 2048)]):
                    for kc in range(8):
                        S.dma('gpsimd', 'w_in%d' % gi_, L('dma_start',
                            out=win_sb[:, kc, n0:n0 + nw], in_=w_in[kc * 128:(kc + 1) * 128, n0:n0 + nw]),
                            writes=['win%d' % gi_])
                for kc in range(2):
                    S.dma('gpsimd', 'w_uq', L('dma_start', out=wuq_sb[:, kc, :], in_=w_uq[kc * 128:(kc + 1) * 128, :]), writes=['wuq'])
                S.dma('gpsimd', 'w_uk', L('dma_start', out=wuk_sb[:], in_=w_uk.ap()), writes=['wuk'])
                S.dma('gpsimd', 'w_uv', L('dma_start', out=wuv_sb[:], in_=w_uv.ap()), writes=['wuv'])
                S.dma('sync', 'c_identf', L('dma_start', out=identf[:], in_=c_ident.ap()), writes=['identf'])
                S.dma('sync', 'c_gmix', L('dma_start', out=gmix[:], in_=bass.AP(norm_mix, 0, [[0, 128], [1, D]])), writes=['gmix'])
                S.dma('sync', 'c_gq', L('dma_start', out=gq[:], in_=bass.AP(mla_q_norm, 0, [[0, 128], [1, 256]])), writes=['gq'])
                S.dma('sync', 'c_gkv', L('dma_start', out=gkv[:], in_=bass.AP(mla_kv_norm, 0, [[0, 128], [1, 128]])), writes=['gkv'])
                S.dma('sync', 'c_rope', L('dma_start', out=rope_sb[:], in_=c_rope.ap().rearrange("(t p) c -> p t c", p=128)), writes=['rope'])
                S.op('vector', L('tensor_copy', out=identb[:], in_=identf[:]), reads=['identf'], writes=['identb'])

                evac_rr = [0]

                def evac(out, in_, reads, writes, func=AF.Copy, eng=None):
                    if eng is None:
                        eng = 'scalar' if (evac_rr[0] % 2 == 0) else 'vector'
                        evac_rr[0] += 1
                    if eng == 'scalar':
                        S.op('scalar', L('activation', out=out, in_=in_, func=func), reads=reads, writes=writes)
                    else:
                        S.op('vector', L('tensor_copy', out=out, in_=in_), reads=reads, writes=writes)

                def scale_gain(out, in0, scalar, in1, tmp, reads, writes):
                    S.op('vector', L('tensor_scalar', out=tmp, in0=in0, scalar1=scalar, scalar2=None, op0=ALU.mult),
                         reads=reads, writes=['sgtmp'])
                    S.op('vector', L('tensor_tensor', out=out, in0=tmp, in1=in1, op=ALU.mult),
                         reads=['sgtmp'] + list(reads), writes=writes)

                def rms_stats(src_ap, ncols, stt, col, sl):
                    key = 'st%d_%d' % (sl, col)
                    S.op('scalar', L('activation', out=junk[:, 0:ncols], in_=src_ap, func=AF.Square,
                                                          accum_out=stt[:, col:col + 1]),
                         reads=rms_stats.reads, writes=['junk', key])
                    S.op('vector', L('tensor_scalar', out=stt[:, col:col + 1], in0=stt[:, col:col + 1],
                                                             scalar1=1.0 / ncols, scalar2=EPS, op0=ALU.mult, op1=ALU.add),
                         reads=[key], writes=[key])
                    S.op('scalar', L('activation', out=stt[:, col:col + 1], in_=stt[:, col:col + 1], func=AF.Sqrt),
                         reads=[key], writes=[key])
                    S.op('vector', L('reciprocal', out=stt[:, col:col + 1], in_=stt[:, col:col + 1]),
                         reads=[key], writes=[key])
                    return key

                def transposes(src_fn, n, pst, pkey, reads, rows=128):
                    for j in range(n):
                        S.op('tensor', L('transpose', out=pst[0:rows, j, :], in_=src_fn(j), identity=identb[:]),
                             reads=list(reads) + ['identb'], writes=['%s_%d' % (pkey, j)], quiet=(j < n - 1))

                def k_side(sl, kcol, zkb_keys, zvb_keys, ckvb_key, kpeb_key):
                    transposes(lambda j: zk_b[sl][:, j * 128:(j + 1) * 128], 8, pTb[1], 'pTb1', zkb_keys)
                    evac(kT_sb[sl][:], pTb[1][:], ['pTb1_%d' % j for j in range(8)], ['kT_sb%d' % sl])
                    S.dma('gpsimd', 'st_kT%d' % sl, L('dma_start', out=dkT_s[:, :, kcol:kcol + 128], in_=kT_sb[sl][:]),
                          reads=['kT_sb%d' % sl])
                    S.dma('gpsimd', 'st_v%d' % sl, L('dma_start', out=dv_s[kcol:kcol + 128, :], in_=zv_b[sl][:]),
                          reads=zvb_keys)
                    yield
                    S.op('tensor', L('transpose', out=pTs[:, 2, :], in_=ckv_b[sl][:], identity=identb[:]),
                         reads=[ckvb_key, 'identb'], writes=['pTs_2'])
                    evac(ckvT[sl][:], pTs[:, 2, :], ['pTs_2'], ['ckvT%d' % sl])
                    yield
                    for (wsb, wkey, dst, dkey) in [(wuk_sb, 'wuk', kn_b, 'kn_b'), (wuv_sb, 'wuv', vm_b, 'vm_b')]:
                        for c in range(2):
                            S.op('tensor', L('matmul', pmm[c][:], lhsT=ckvT[sl][:], rhs=wsb[:, c * 512:(c + 1) * 512],
                                                                    start=True, stop=True),
                                 reads=['ckvT%d' % sl, wkey], writes=['pmm%d' % c])
                            evac(dst[sl][:, c * 512:(c + 1) * 512], pmm[c][:], ['pmm%d' % c], ['%s%d_%d' % (dkey, sl, c)])
                    S.dma('gpsimd', 'st_mv%d' % sl, L('dma_start', out=mv_s[kcol:kcol + 128, :], in_=vm_b[sl][:]),
                          reads=['vm_b%d_0' % sl, 'vm_b%d_1' % sl])
                    yield
                    transposes(lambda j: kn_b[sl][:, j * 128:(j + 1) * 128], 8, pTb[0], 'pTb0',
                               ['kn_b%d_0' % sl, 'kn_b%d_1' % sl])
                    evac(knT_sb[sl][:], pTb[0][:], ['pTb0_%d' % j for j in range(8)], ['knT_sb%d' % sl])
                    S.dma('gpsimd', 'st_mk%d' % sl, L('dma_start', out=mkT_s[:, :, kcol:kcol + 128], in_=knT_sb[sl][:]),
                          reads=['knT_sb%d' % sl])
                    S.op('tensor', L('transpose', out=pTs[0:32, 3, :], in_=kpe_b[sl][:], identity=identb[:]),
                         reads=[kpeb_key, 'identb'], writes=['pTs_3'])
                    evac(kpeT_sb[sl][:], pTs[0:32, 3, :], ['pTs_3'], ['kpeT_sb%d' % sl])
                    S.dma('gpsimd', 'st_kpe%d' % sl, L('dma_start', out=mkpeT_s[:, kcol:kcol + 128], in_=kpeT_sb[sl][:]),
                          reads=['kpeT_sb%d' % sl])

                tiles = [('p', i) for i in range(32)] + [('s', 0)] + [('c', i) for i in range(16)]
                tiles = cfg.get('a_tiles', tiles)
                pending = [None]
                head_done = {}

                def head_elem(tix_):
                    kind_, i_ = tiles[tix_]
                    sl_ = tix_ % 2
                    head_done[tix_] = True
                    if kind_ == 'p':
                        rows_ = 128; xsrc_ = xp[i_ * 128:(i_ + 1) * 128, :]
                    else:
                        rows_ = 64; xsrc_ = xs.ap()
                        S.op('vector', L('memset', xt[sl_][:], 0.0), writes=['xt%d' % sl_])
                    S.dma('sync', 'ld_x%d' % sl_, L('dma_start', out=xt[sl_][0:rows_, :], in_=xsrc_), writes=['xt%d' % sl_])
                    rms_stats.reads = ['xt%d' % sl_]
                    k0_ = rms_stats(xt[sl_][:], D, st[sl_], 0, sl_)
                    scale_gain(hb[sl_][:], xt[sl_][:], st[sl_][:, 0:1], gmix[:], sgtmp[:, 0:1024], ['xt%d' % sl_, k0_, 'gmix'], ['hb%d' % sl_])

                def head_pe(tix_):
                    sl_ = tix_ % 2
                    transposes(lambda j: hb[sl_][:, j * 128:(j + 1) * 128], 8, pTa, 'pTa', ['hb%d' % sl_])
                    evac(hT[sl_][:], pTa[:], ['pTa_%d' % j for j in range(8)], ['hT%d' % sl_])

                def step_pending():
                    if pending[0] is not None:
                        try:
                            next(pending[0])
                        except StopIteration:
                            pending[0] = None

                def drain():
                    while pending[0] is not None:
                        step_pending()
                for tix, (kind, i) in enumerate(tiles):
                    sl = tix % 2
                    if kind == 'c':
                        kcol = KS0 + i * 128
                        r0 = i * 128
                        S.dma('gpsimd', 'ld_ck%d' % sl, L('dma_start', out=zk_b[sl][:], in_=cdk[r0:r0 + 128, :]),
                              writes=['zk_b%d_2' % sl, 'zk_b%d_3' % sl])
                        S.dma('gpsimd', 'ld_cv%d' % sl, L('dma_start', out=zv_b[sl][:], in_=cdv[r0:r0 + 128, :]),
                              writes=['zv_b%d_4' % sl, 'zv_b%d_5' % sl])
                        S.dma('gpsimd', 'ld_cc%d' % sl, L('dma_start', out=ckv_b[sl][:], in_=cckv[r0:r0 + 128, :]),
                              writes=['ckv_b%d' % sl])
                        S.dma('gpsimd', 'ld_ce%d' % sl, L('dma_start', out=kpe_b[sl][:], in_=ckpe[r0:r0 + 128, :]),
                              writes=['kpe_b%d' % sl])
                        drain()
                        for _ in k_side(sl, kcol, ['zk_b%d_2' % sl, 'zk_b%d_3' % sl], ['zv_b%d_4' % sl, 'zv_b%d_5' % sl], 'ckv_b%d' % sl, 'kpe_b%d' % sl):
                            pass
                        continue
                    if kind == 'p':
                        rows = 128; qcol = i * 128; kcol = i * 128; rt = i
                        xsrc = xp[i * 128:(i + 1) * 128, :]
                        o_k, o_v, o_c, o_e = kp[qcol:qcol + 128, :], vp[qcol:qcol + 128, :], cp[qcol:qcol + 128, :], ep[qcol:qcol + 128, :]
                    else:
                        rows = 64; qcol = SP; kcol = KS0 + PAST; rt = 32
                        xsrc = xs.ap()
                        o_k, o_v, o_c, o_e = ks_.ap(), vs_.ap(), cs_.ap(), es_.ap()
                    if not head_done.get(tix):
                        head_elem(tix); head_pe(tix)
                    nxt_full = (tix + 1 < len(tiles)) and tiles[tix + 1][0] != 'c'
                    if nxt_full:
                        head_elem(tix + 1)

                    chunks = [(0, 512), (512, 512), (1024, 512), (1536, 512), (2048, 512), (2560, 512), (3072, 416),
                              (3488, 512), (4000, 512), (4512, 512), (5024, 512)]
                    for ci, (n0, nw) in enumerate(chunks):
                        pb = ci % 2
                        for kc in range(8):
                            S.op('tensor', L('matmul', pz[pb][:, 0:nw], lhsT=hT[sl][:, kc, :], rhs=win_sb[:, kc, n0:n0 + nw],
                                start=(kc == 0), stop=(kc == 7)),
                                reads=['hT%d' % sl, 'win%d' % (0 if n0 < 1024 else 1 if n0 < 3072 else 2 if n0 < 3488 else 3)], writes=['pz%d' % pb],
                                quiet=(kc < 7))
                        if ci < 2:
                            evac(zq_b[sl][:, n0:n0 + 512], pz[pb][:], ['pz%d' % pb], ['zq_b%d_%d' % (sl, ci)])
                        elif ci < 4:
                            c0 = n0 - 1024
                            evac(zk_f[sl][:, c0:c0 + 512], pz[pb][:], ['pz%d' % pb], ['zk_f%d_%d' % (sl, ci)], eng='scalar')
                            S.op('vector', L('tensor_copy', out=zk_b[sl][:, c0:c0 + 512], in_=pz[pb][:]),
                                 reads=['pz%d' % pb], writes=['zk_b%d_%d' % (sl, ci)])
                        elif ci < 6:
                            c0 = n0 - 2048
                            evac(zv_f[sl][:, c0:c0 + 512], pz[pb][:], ['pz%d' % pb], ['zv_f%d_%d' % (sl, ci)], eng='scalar')
                            S.op('vector', L('tensor_copy', out=zv_b[sl][:, c0:c0 + 512], in_=pz[pb][:]),
                                 reads=['pz%d' % pb], writes=['zv_b%d_%d' % (sl, ci)])
                        elif ci == 6:
                            evac(zm[sl][:], pz[pb][:, 0:416], ['pz%d' % pb], ['zm%d' % sl], eng='vector')
                        else:
                            c0 = n0 - 3488
                            S.op('scalar', L('activation', out=gates_b[sl][:, c0:c0 + 512], in_=pz[pb][:],
                                                                                 func=AF.Sigmoid),
                                 reads=['pz%d' % pb], writes=['gates_b%d_%d' % (sl, ci)])
                        step_pending()
                    drain()
                    S.dma('gpsimd', 'st_ok%d' % sl, L('dma_start', out=o_k, in_=zk_f[sl][0:rows, :]),
                          reads=['zk_f%d_2' % sl, 'zk_f%d_3' % sl])
                    S.dma('gpsimd', 'st_ov%d' % sl, L('dma_start', out=o_v, in_=zv_f[sl][0:rows, :]),
                          reads=['zv_f%d_4' % sl, 'zv_f%d_5' % sl])
                    S.dma('gpsimd', 'st_g%d' % sl, L('dma_start', out=gates_s[qcol:qcol + 128, :], in_=gates_b[sl][:]),
                          reads=['gates_b%d_%d' % (sl, c) for c in range(7, 11)])
                    if nxt_full:
                        head_pe(tix + 1)

                    def epi(sl=sl, qcol=qcol, kcol=kcol, rows=rows, rt=rt, o_c=o_c, o_e=o_e):
                        tmpk = 'rtmp%d' % sl
                        rms_stats.reads = ['zm%d' % sl]
                        k1 = rms_stats(zm[sl][:, 0:256], 256, st[sl], 1, sl)
                        scale_gain(cq_b[sl][:], zm[sl][:, 0:256], st[sl][:, 1:2], gq[:], sgtmp[:, 0:256], ['zm%d' % sl, k1, 'gq'], ['cq_b%d' % sl])
                        k2 = rms_stats(zm[sl][:, 256:384], 128, st[sl], 2, sl)
                        scale_gain(ckv_f[sl][:], zm[sl][:, 256:384], st[sl][:, 2:3], gkv[:], sgtmp[:, 0:128], ['zm%d' % sl, k2, 'gkv'], ['ckv_f%d' % sl])
                        S.op('vector', L('tensor_copy', out=ckv_b[sl][:], in_=ckv_f[sl][:]), reads=['ckv_f%d' % sl], writes=['ckv_b%d' % sl])
                        S.dma('gpsimd', 'st_oc%d' % sl, L('dma_start', out=o_c, in_=ckv_f[sl][0:rows, :]),
                              reads=['ckv_f%d' % sl])
                        cos1 = bass.AP(rope_sb, rt * 32, [[33 * 32, 128], [1, 16]])
                        sin1 = bass.AP(rope_sb, rt * 32 + 16, [[33 * 32, 128], [1, 16]])
                        y1 = zm[sl][:, 384:400]
                        y2 = zm[sl][:, 400:416]
                        for ti_, (a, b_) in enumerate([(y1, cos1), (y2, sin1), (y1, sin1), (y2, cos1)]):
                            S.op('vector', L('tensor_tensor', out=rtmp[sl][:, ti_, 0, :], in0=a, in1=b_, op=ALU.mult),
                                 reads=['zm%d' % sl, 'rope'], writes=[tmpk + '_%d' % ti_])
                        S.op('vector', L('tensor_tensor', out=kpe_f[sl][:, 0:16], in0=rtmp[sl][:, 0, 0, :], in1=rtmp[sl][:, 1, 0, :], op=ALU.subtract),
                             reads=[tmpk + '_0', tmpk + '_1'], writes=['kpe_f%d_a' % sl])
                        S.op('vector', L('tensor_tensor', out=kpe_f[sl][:, 16:32], in0=rtmp[sl][:, 2, 0, :], in1=rtmp[sl][:, 3, 0, :], op=ALU.add),
                             reads=[tmpk + '_2', tmpk + '_3'], writes=['kpe_f%d_b' % sl])
                        S.op('vector', L('tensor_copy', out=kpe_b[sl][:], in_=kpe_f[sl][:]),
                             reads=['kpe_f%d_a' % sl, 'kpe_f%d_b' % sl], writes=['kpe_b%d' % sl])
                        S.dma('gpsimd', 'st_oe%d' % sl, L('dma_start', out=o_e, in_=kpe_f[sl][0:rows, :]),
                              reads=['kpe_f%d_a' % sl, 'kpe_f%d_b' % sl])
                        yield
                        transposes(lambda j: zq_b[sl][:, j * 128:(j + 1) * 128], 8, pTb[0], 'pTb0', ['zq_b%d_0' % sl, 'zq_b%d_1' % sl])
                        evac(qT_sb[sl][:], pTb[0][:], ['pTb0_%d' % j for j in range(8)], ['qT_sb%d' % sl])
                        S.dma('gpsimd', 'st_qT%d' % sl, L('dma_start', out=dqT_s[:, :, qcol:qcol + 128], in_=qT_sb[sl][:]),
                              reads=['qT_sb%d' % sl])
                        yield
                        for j in range(2):
                            S.op('tensor', L('transpose', out=pTs[:, j, :], in_=cq_b[sl][:, j * 128:(j + 1) * 128], identity=identb[:]),
                                 reads=['cq_b%d' % sl, 'identb'], writes=['pTs_%d' % j])
                        evac(cqT[sl][:], pTs[:, 0:2, :], ['pTs_0', 'pTs_1'], ['cqT%d' % sl])
                        yield
                        qh_flat = qh_f[sl][:].rearrange("p h e -> p (h e)")
                        for c in range(3):
                            pb = c % 2
                            for kc in range(2):
                                S.op('tensor', L('matmul', pmm[pb][:], lhsT=cqT[sl][:, kc, :], rhs=wuq_sb[:, kc, c * 512:(c + 1) * 512],
                                    start=(kc == 0), stop=(kc == 1)),
                                    reads=['cqT%d' % sl, 'wuq'], writes=['pmm%d' % pb])
                            evac(qh_flat[:, c * 512:(c + 1) * 512], pmm[pb][:], ['pmm%d' % pb], ['qh_f%d_%d' % (sl, c)])
                        yield
                        qhr = ['qh_f%d_%d' % (sl, c) for c in range(3)]
                        cosb = bass.AP(rope_sb, rt * 32, [[33 * 32, 128], [0, 16], [1, 16]])
                        sinb = bass.AP(rope_sb, rt * 32 + 16, [[33 * 32, 128], [0, 16], [1, 16]])
                        x1 = qh_f[sl][:, :, 64:80]
                        x2 = qh_f[sl][:, :, 80:96]
                        tmpk = 'rtmp%d' % sl
                        for ti_, (a, b_) in enumerate([(x1, cosb), (x2, sinb), (x1, sinb), (x2, cosb)]):
                            S.op('vector', L('tensor_tensor', out=rtmp[sl][:, ti_, :, :], in0=a, in1=b_, op=ALU.mult),
                                 reads=qhr + ['rope'], writes=[tmpk + '_%d' % ti_])
                        S.op('vector', L('tensor_copy', out=qh_b[sl][:, :, 0:64], in_=qh_f[sl][:, :, 0:64]),
                             reads=qhr, writes=['qh_b%d_n' % sl])
                        S.op('vector', L('tensor_tensor', out=qh_b[sl][:, :, 64:80], in0=rtmp[sl][:, 0, :, :], in1=rtmp[sl][:, 1, :, :],
                                                                 op=ALU.subtract),
                             reads=[tmpk + '_0', tmpk + '_1'], writes=['qh_b%d_a' % sl])
                        S.op('vector', L('tensor_tensor', out=qh_b[sl][:, :, 80:96], in0=rtmp[sl][:, 2, :, :], in1=rtmp[sl][:, 3, :, :],
                                                                 op=ALU.add),
                             reads=[tmpk + '_2', tmpk + '_3'], writes=['qh_b%d_b' % sl])
                        yield
                        ks = k_side(sl, kcol, ['zk_b%d_2' % sl, 'zk_b%d_3' % sl], ['zv_b%d_4' % sl, 'zv_b%d_5' % sl], 'ckv_b%d' % sl, 'kpe_b%d' % sl)
                        next(ks, None)
                        yield
                        qhb_keys = ['qh_b%d_n' % sl, 'qh_b%d_a' % sl, 'qh_b%d_b' % sl]
                        for half in range(2):
                            transposes(lambda j, half=half: qh_b[sl][:, half * 8 + j, :], 8, pTb[half], 'pTb%d' % half, qhb_keys, rows=96)
                            evac(mqT_sb[sl][0:96, half * 8:(half + 1) * 8, :], pTb[half][0:96, :, :],
                                 ['pTb%d_%d' % (half, j) for j in range(8)], ['mqT_sb%d_%d' % (sl, half)])
                        S.dma('gpsimd', 'st_mq%d' % sl, L('dma_start', out=mqT_s[:, :, qcol:qcol + 128], in_=mqT_sb[sl][0:96, :, :]),
                              reads=['mqT_sb%d_0' % sl, 'mqT_sb%d_1' % sl])
                        yield
                        for _ in ks:
                            yield
                    pending[0] = epi()

                drain()
                S.end_phase()
                ensure_sems()
                S.emit(nc, sems)

        if 'B' in phases:
            with ExitStack() as es:
                def sb(name, shape, dt):
                    return es.enter_context(nc.sbuf_tensor(name, list(shape), dt))

                def ps(name, shape, dt):
                    return es.enter_context(nc.psum_tensor(name, list(shape), dt))
                S.single = ()
                S.banks = ('sc0', 'sc1', 'oacc0', 'oacc1', 'oacc2', 'oacc3', 'pset')
                NKT = NT // 128
                QT = [sb("QT%d" % i, [128, NQ], BF16) for i in range(2)]
                KT = [sb("KT%d" % i, [128, NT], BF16) for i in range(2)]
                VA = [sb("VA%d" % i, [128, NKT, 129], BF16) for i in range(2)]
                VM = [sb("VM%d" % i, [128, NKT, 65], BF16) for i in range(2)]
                et = [sb("et%d" % i, [128, 512], BF16) for i in range(3)]
                rb_sb = sb("rb_sb", [32, 8], F32)
                oh_sb = sb("oh_sb", [32, 384], F32)
                rb15 = sb("rb15", [8, 1], F32)
                fexp = sb("fexp", [8, 384], F32)
                hank = sb("hank", [128, 8, 256], F32)
                maskw = sb("maskw", [128, 256], F32)
                ebm = sb("ebm", [128, 8, 256], BF16)
                dl = sb("dl", [128, 256], F32)
                dlj = sb("dlj", [128, 64], F32)
                lam = sb("lam", [128, 4], F32)
                gsub = sb("gsub", [128, 128], F32)
                o0 = [sb("o0_%d" % i, [128, 129], F32) for i in range(4)]
                o1 = [sb("o1_%d" % i, [128, 129], F32) for i in range(4)]
                mhalfB = sb("mhalfB", [128, 1], F32)
                sm = [sb("smB%d" % i, [128, 8], F32) for i in range(4)]
                of = [sb("of%d" % i, [128, 128], F32) for i in range(4)]
                of2 = [sb("of2_%d" % i, [128, 128], F32) for i in range(4)]
                jk = sb("jkB", [128, 128], BF16)
                oa_t = [sb("oa_t%d" % i, [128, 128], BF16) for i in range(4)]
                ob_t = [sb("ob_t%d" % i, [128, 64], BF16) for i in range(4)]
                sc = [ps("sc%d" % i, [128, 512], F32) for i in range(2)]
                oacc = [ps("oacc%d" % i, [128, 512], F32) for i in range(4)]
                pset = ps("pset", [128, 512], F32)
                P_IN_B = DENSE and 'C' in phases and cfg.get('p_in_b', True)
                n_ecB = cfg.get('n_ec', 128) if P_IN_B else 0
                if P_IN_B:
                    S.banks = S.banks + ('ppTB',)
                    identfB = sb("identfB", [128, 128], F32); identbB = sb("identbB", [128, 128], BF16)
                    ubB = [sb("ubB%d" % i, [128, D], BF16) for i in range(3)]
                    vbB = [sb("vbB%d" % i, [128, D], BF16) for i in range(3)]
                    uTB = [sb("uTB%d" % i, [128, 8, 128], BF16) for i in range(3)]
                    ppTB = ps("ppTB", [128, 8, 128], BF16)
                    S.dma('sync', 'pb_identf', L('dma_start', out=identfB[:], in_=c_ident.ap()), writes=['identfB'])
                    S.op('vector', L('tensor_copy', out=identbB[:], in_=identfB[:]), reads=['identfB'], writes=['identbB'])
                p_state = [0, 0]

                def p_load():
                    ec = p_state[0]
                    if ec >= n_ecB:
                        return
                    p_state[0] += 1
                    s3 = ec % 3
                    S.dma('gpsimd', 'pb_u%d' % s3, L('dma_start', out=ubB[s3][:], in_=peer_u[ec * 128:(ec + 1) * 128, :]), writes=['ubB%d' % s3])
                    S.dma('gpsimd', 'pb_v%d' % s3, L('dma_start', out=vbB[s3][:], in_=peer_v[ec * 128:(ec + 1) * 128, :]), writes=['vbB%d' % s3])

                def p_proc():
                    ec = p_state[1]
                    if ec >= n_ecB:
                        return
                    p_state[1] += 1
                    s3 = ec % 3
                    for dc in range(8):
                        S.op('tensor', L('transpose', out=ppTB[:, dc, :], in_=ubB[s3][:, dc * 128:(dc + 1) * 128], identity=identbB[:]),
                             reads=['ubB%d' % s3, 'identbB'], writes=['ppTB_%d' % dc])
                    S.op('vector', L('tensor_copy', out=uTB[s3][:], in_=ppTB[:]), reads=['ppTB_%d' % dc for dc in range(8)], writes=['uTB%d' % s3])
                    S.dma('sync', 'pb_su%d' % s3, L('dma_start', out=UT_s[ec // 2][:, ec % 2, :].rearrange("p (a b) -> p a b", a=8), in_=uTB[s3][:]),
                          reads=['uTB%d' % s3])
                    S.dma('sync', 'pb_sv%d' % s3, L('dma_start', out=V_s[ec // 2][:, ec % 2, :], in_=vbB[s3][:]), reads=['vbB%d' % s3])

                def p_iter():
                    if p_state[1] < n_ecB:
                        while p_state[0] < min(p_state[1] + 3, n_ecB):
                            p_load()
                        p_proc()

                S.dma('sync', 'b_rb', L('dma_start', out=rb_sb[:], in_=rel_bias.ap()), writes=['rb_sb'])
                S.dma('sync', 'b_oh', L('dma_start', out=oh_sb[:], in_=c_oh.ap()), writes=['oh_sb'])
                S.dma('sync', 'b_rb15', L('dma_start', out=rb15[:], in_=rel_bias[15:16, :].rearrange("a h -> h a")), writes=['rb15'])
                S.dma('sync', 'b_mw', L('dma_start', out=maskw[:], in_=c_maskw.ap()), writes=['maskw'])
                S.dma('sync', 'b_dl', L('dma_start', out=dl[:], in_=bass.AP(diff_lambda, 0, [[0, 128], [1, 256]])), writes=['dl'])
                S.dma('sync', 'b_gs', L('dma_start', out=gsub[:], in_=bass.AP(diff_subln, 0, [[0, 128], [1, 128]])), writes=['gsub'])
                S.op('tensor', L('matmul', pset[0:8, 0:384], lhsT=rb_sb[:], rhs=oh_sb[:], start=True, stop=True),
                     reads=['rb_sb', 'oh_sb'], writes=['pset'])
                S.op('vector', L('tensor_scalar', out=fexp[:], in0=pset[0:8, 0:384], scalar1=rb15[:, 0:1], scalar2=None, op0=ALU.subtract),
                     reads=['pset', 'rb15'], writes=['fexp'])
                S.op('scalar', L('activation', out=fexp[:], in_=fexp[:], func=AF.Exp), reads=['fexp'], writes=['fexp'])
                S.dma('sync', 'b_fs', L('dma_start', out=fbias_s.ap(), in_=fexp[:]), reads=['fexp'], writes=['fbias_s'])
                for h in range(8):
                    S.dma('sync', 'b_hk', L('dma_start', out=hank[:, h, :], in_=bass.AP(fbias_s, h * 384, [[1, 128], [1, 256]])),
                          reads=['fbias_s'], writes=['hank'])
                for h in range(8):
                    S.op('vector', L('tensor_tensor', out=ebm[:, h, :], in0=hank[:, h, ::-1], in1=maskw[:], op=ALU.mult),
                         reads=['hank', 'maskw'], writes=['ebm'])
                for i_, (a, b_) in enumerate([(0, 1), (2, 3)]):
                    S.op('vector', L('tensor_tensor', out=dlj[:], in0=dl[:, a * 64:(a + 1) * 64], in1=dl[:, b_ * 64:(b_ + 1) * 64], op=ALU.mult),
                         reads=['dl'], writes=['dlj'])
                    S.op('vector', L('reduce_sum', out=lam[:, i_:i_ + 1], in_=dlj[:], axis=mybir.AxisListType.X),
                         reads=['dlj'], writes=['lam%d' % i_])
                    S.op('scalar', L('activation', out=lam[:, i_:i_ + 1], in_=lam[:, i_:i_ + 1], func=AF.Exp),
                         reads=['lam%d' % i_], writes=['lam%d' % i_])
                S.op('vector', L('tensor_tensor', out=lam[:, 2:3], in0=lam[:, 1:2], in1=lam[:, 0:1], op=ALU.subtract),
                     reads=['lam0', 'lam1'], writes=['lam2'])
                S.op('vector', L('tensor_scalar', out=lam[:, 2:3], in0=lam[:, 2:3], scalar1=-LAMBDA_INIT, scalar2=None, op0=ALU.add),
                     reads=['lam2'], writes=['lam2'])
                S.op('vector', L('tensor_scalar', out=gsub[:], in0=gsub[:], scalar1=1.0 - LAMBDA_INIT, scalar2=None, op0=ALU.mult),
                     reads=['gsub'], writes=['gsub'])
                S.op('gpsimd', L('memset', mhalfB[:], -0.5), writes=['mhalfB'])
                for i in range(2):
                    S.op('gpsimd', L('memset', VA[i][:, :, 128:129], 1.0), writes=['VA%d_one' % i])
                    S.op('gpsimd', L('memset', VA[i][64:128, NKT - 1, 128:129], 0.0), writes=['VA%d_one' % i])
                    S.op('gpsimd', L('memset', VM[i][:, :, 64:65], 1.0), writes=['VM%d_one' % i])
                    S.op('gpsimd', L('memset', VM[i][64:128, NKT - 1, 64:65], 0.0), writes=['VM%d_one' % i])

                groups = [dict(qcol0=g * 512, nq=4, kts=list(range(4 * g + 4)), qabs0=4 * g) for g in range(8)]
                groups.append(dict(qcol0=SP, nq=1, kts=list(range(32, 49)), qabs0=48))
                groups = cfg.get('b_groups', groups)
                st_ctr = [0]
                sc_ctr = [0]
                scb = [sc[0], sc[1], pset]
                sckey = ['sc0', 'sc1', 'pset']

                def attention(kind, h, slot, r0, r1, E, scale, vt, vkey, grp, finalize):
                    nq = grp['nq']; kts = grp['kts']; qabs0 = grp['qabs0']; qcol0 = grp['qcol0']
                    p_iter()

                    def qlo_of(kt):
                        return max(kt - qabs0, 0) if nq > 1 else 0

                    def score(idx):
                        kt = kts[idx]; qlo = qlo_of(kt); N = (nq - qlo) * 128
                        s = sc_ctr[0] % 3
                        sc_ctr[0] += 1
                        S.op('tensor', L('matmul', scb[s][:, 0:N], lhsT=KT[slot][r0:r1, kt * 128:(kt + 1) * 128],
                                         rhs=QT[slot][r0:r1, qcol0 + qlo * 128:qcol0 + nq * 128], start=True, stop=True),
                             reads=['KT%d' % slot, 'KT%d_pe' % slot, 'QT%d' % slot], writes=[sckey[s]])
                        return s
                    pend = [score(i_) for i_ in range(min(2, len(kts)))]
                    for idx, kt in enumerate(kts):
                        s = pend.pop(0)
                        st_ctr[0] += 1
                        if idx + 2 < len(kts):
                            pend.append(score(idx + 2))
                        qlo = qlo_of(kt); N = (nq - qlo) * 128
                        t = st_ctr[0] % 3
                        S.op('scalar', L('activation', out=et[t][:, 0:N], in_=scb[s][:, 0:N], func=AF.Exp, scale=scale),
                             reads=[sckey[s]], writes=['et%d' % t])
                        d = qabs0 + qlo - kt
                        if kind == 'd' and d in (0, 1):
                            W = min(256 - d * 128, N)
                            S.op('vector', L('tensor_tensor', out=et[t][:, 0:W], in0=et[t][:, 0:W], in1=ebm[:, h, d * 128:d * 128 + W], op=ALU.mult),
                                 reads=['et%d' % t, 'ebm'], writes=['et%d' % t])
                        if kind == 'm' and d == 0:
                            S.op('gpsimd', L('memset', et[t][64:128, 0:64], 0.0), reads=['et%d' % t], writes=['et%d' % t])
                        for j in range(qlo, nq):
                            last = (qabs0 + j) if nq > 1 else kts[-1]
                            S.op('tensor', L('matmul', oacc[j][:, 0:E + 1], lhsT=et[t][:, (j - qlo) * 128:(j - qlo + 1) * 128],
                                             rhs=vt[:, kt, 0:E + 1], start=(idx == 0), stop=(kt == last)),
                                 reads=['et%d' % t, vkey, vkey + '_one'], writes=['oacc%d' % j], quiet=(j < nq - 1))
                    finalize([(j, qcol0 + j * 128) for j in range(nq)])

                def rstd_small(smt, col, key, n):
                    S.op('vector', L('tensor_scalar', out=smt[:, col:col + 1], in0=smt[:, col:col + 1], scalar1=1.0 / n, scalar2=EPS,
                                     op0=ALU.mult, op1=ALU.add), reads=[key], writes=[key])
                    S.op('scalar', L('activation', out=smt[:, col:col + 1], in_=smt[:, col:col + 1], func=AF.Sqrt), reads=[key], writes=[key])
                    S.op('vector', L('reciprocal', out=smt[:, col:col + 1], in_=smt[:, col:col + 1]), reads=[key], writes=[key])

                dheads = cfg.get('b_dheads', list(range(8)))
                for hi, h in enumerate(dheads):
                    slot = hi % 2
                    S.dma('sync', 'b_q%d' % slot, L('dma_start', out=QT[slot][:], in_=dqT_s[:, h, :]), writes=['QT%d' % slot])
                    S.dma('sync', 'b_k%d' % slot, L('dma_start', out=KT[slot][:], in_=dkT_s[:, h, :]), writes=['KT%d' % slot, 'KT%d_pe' % slot])
                    for t0 in range(0, NKT, 13):
                        t1 = min(t0 + 13, NKT)
                        S.dma('sync', 'b_v%d' % slot, L('dma_start', out=VA[slot][:, t0:t1, 0:128],
                                                         in_=dv_s.ap().rearrange("(t p) f -> p t f", p=128)[:, t0:t1, h * 128:(h + 1) * 128]),
                              writes=['VA%d' % slot])
                    for grp in groups:
                        def fin0(items):
                            for (j, qrow) in items:
                                S.op('vector', L('tensor_copy', out=o0[j][:], in_=oacc[j][:, 0:129]), reads=['oacc%d' % j], writes=['o0_%d' % j])

                        def fin1(items, h=h):
                            for (j, qrow) in items:
                                S.op('vector', L('tensor_copy', out=o1[j][:], in_=oacc[j][:, 0:129]), reads=['oacc%d' % j], writes=['o1_%d' % j])
                            for (j, qrow) in items:
                                k_ = 'smB%d' % j
                                S.op('vector', L('reciprocal', out=sm[j][:, 0:1], in_=o0[j][:, 128:129]), reads=['o0_%d' % j], writes=[k_ + 'a'])
                                S.op('vector', L('reciprocal', out=sm[j][:, 1:2], in_=o1[j][:, 128:129]), reads=['o1_%d' % j], writes=[k_ + 'b'])
                                S.op('vector', L('tensor_tensor', out=sm[j][:, 1:2], in0=sm[j][:, 1:2], in1=lam[:, 2:3], op=ALU.mult),
                                     reads=[k_ + 'b', 'lam2'], writes=[k_ + 'b'])
                                S.op('vector', L('tensor_scalar', out=of[j][:], in0=o0[j][:, 0:128], scalar1=sm[j][:, 0:1], scalar2=None, op0=ALU.mult),
                                     reads=['o0_%d' % j, k_ + 'a'], writes=['of%d' % j])
                                S.op('vector', L('tensor_scalar', out=of2[j][:], in0=o1[j][:, 0:128], scalar1=sm[j][:, 1:2], scalar2=None, op0=ALU.mult),
                                     reads=['o1_%d' % j, k_ + 'b'], writes=['of2_%d' % j])
                                S.op('vector', L('tensor_tensor', out=of[j][:], in0=of[j][:], in1=of2[j][:], op=ALU.add),
                                     reads=['of%d' % j, 'of2_%d' % j], writes=['of%d' % j])
                                S.op('vector', L('tensor_tensor', out=of2[j][:], in0=of[j][:], in1=of[j][:], op=ALU.mult),
                                     reads=['of%d' % j], writes=['of2_%d' % j])
                                S.op('vector', L('reduce_sum', out=sm[j][:, 2:3], in_=of2[j][:], axis=mybir.AxisListType.X),
                                     reads=['of2_%d' % j], writes=[k_ + 'c'])
                                S.op('gpsimd', L('tensor_scalar', out=sm[j][:, 2:3], in0=sm[j][:, 2:3], scalar1=1.0 / 128, scalar2=EPS, op0=ALU.mult, op1=ALU.add),
                                     reads=[k_ + 'c'], writes=[k_ + 'c'])
                                S.op('gpsimd', L('tensor_tensor', out=sm[j][:, 2:3], in0=sm[j][:, 2:3], in1=mhalfB[:], op=ALU.pow),
                                     reads=[k_ + 'c', 'mhalfB'], writes=[k_ + 'c'])
                                S.op('vector', L('tensor_scalar', out=of[j][:], in0=of[j][:], scalar1=sm[j][:, 2:3], scalar2=None, op0=ALU.mult),
                                     reads=['of%d' % j, k_ + 'c'], writes=['of%d' % j])
                                S.op('vector', L('tensor_tensor', out=oa_t[j][:], in0=of[j][:], in1=gsub[:], op=ALU.mult),
                                     reads=['of%d' % j, 'gsub'], writes=['oa_t%d' % j])
                                S.dma('gpsimd', 'b_so%d' % j, L('dma_start', out=oa_s[qrow:qrow + 128, h * 128:(h + 1) * 128], in_=oa_t[j][:]),
                                      reads=['oa_t%d' % j])
                        attention('d', h, slot, 0, 64, 128, 0.125, VA[slot], 'VA%d' % slot, grp, fin0)
                        attention('d', h, slot, 64, 128, 128, 0.125, VA[slot], 'VA%d' % slot, grp, fin1)

                mheads = cfg.get('b_mheads', list(range(16)))
                for hi, h in enumerate(mheads):
                    slot = hi % 2
                    S.dma('sync', 'b_q%d' % slot, L('dma_start', out=QT[slot][0:96, :], in_=mqT_s[:, h, :]), writes=['QT%d' % slot])
                    S.dma('sync', 'b_k%d' % slot, L('dma_start', out=KT[slot][0:64, :], in_=mkT_s[(h % 2) * 64:(h % 2) * 64 + 64, h // 2, :]),
                          writes=['KT%d' % slot])
                    if hi < 2:
                        S.dma('sync', 'b_kpe%d' % slot, L('dma_start', out=KT[slot][64:96, :], in_=mkpeT_s.ap()), writes=['KT%d_pe' % slot])
                    for t0 in range(0, NKT, 13):
                        t1 = min(t0 + 13, NKT)
                        S.dma('sync', 'b_vm%d' % slot, L('dma_start', out=VM[slot][:, t0:t1, 0:64],
                                                          in_=mv_s.ap().rearrange("(t p) f -> p t f", p=128)[:, t0:t1, h * 64:(h + 1) * 64]),
                              writes=['VM%d' % slot])
                    for grp in groups:
                        def finm(items, h=h):
                            for (j, qrow) in items:
                                S.op('vector', L('tensor_copy', out=o1[j][:, 0:65], in_=oacc[j][:, 0:65]), reads=['oacc%d' % j], writes=['o1_%d' % j])
                            for (j, qrow) in items:
                                k_ = 'smB%d' % j
                                S.op('vector', L('reciprocal', out=sm[j][:, 0:1], in_=o1[j][:, 64:65]), reads=['o1_%d' % j], writes=[k_ + 'a'])
                                S.op('vector', L('tensor_scalar', out=ob_t[j][:], in0=o1[j][:, 0:64], scalar1=sm[j][:, 0:1], scalar2=None, op0=ALU.mult),
                                     reads=['o1_%d' % j, k_ + 'a'], writes=['ob_t%d' % j])
                                S.dma('gpsimd', 'b_sb%d' % j, L('dma_start', out=ob_s[qrow:qrow + 128, h * 64:(h + 1) * 64], in_=ob_t[j][:]),
                                      reads=['ob_t%d' % j])
                        attention('m', h, slot, 0, 96, 64, 96.0 ** -0.5, VM[slot], 'VM%d' % slot, grp, finm)

                while p_state[1] < n_ecB:
                    p_iter()
                S.end_phase()
                ensure_sems()
                S.emit(nc, sems)

        if 'C' in phases:
            with ExitStack() as es:
                def sb(name, shape, dt):
                    return es.enter_context(nc.sbuf_tensor(name, list(shape), dt))

                def ps(name, shape, dt):
                    return es.enter_context(nc.psum_tensor(name, list(shape), dt))
                S.single = ()
                S.banks = ('pT0', 'pT1', 'pacc0', 'pacc1', 'pacc2', 'pacc3', 'psS0', 'psS1')
                NB = 4
                wa_sb = sb("wa_sb", [128, 8, D], BF16); wb_sb = sb("wb_sb", [128, 8, D], BF16)
                wo_sb = sb("wo_sb", [128, 8, D], BF16); wq_sb = sb("wq_sb", [128, 8, D], BF16)
                identf = sb("identfC", [128, 128], F32); identb = sb("identbC", [128, 128], BF16)
                gffn = sb("gffn", [128, D], F32); gfin = sb("gfin", [128, D], F32)
                keys_f = sb("keys_f", [128, 16, 64], F32); keys_b = sb("keys_b", [128, 16, 64], BF16)
                keysT = sb("keysT", [128, 8, 128], BF16)
                iota_t = sb("iota_t", [128, 256], F32)
                thr = sb("thr", [128, 16], F32)
                oa_t = sb("oa_tC", [128, D], BF16); ob_t = sb("ob_tC", [128, D], BF16)
                gt = sb("gtC", [128, 2 * D], BF16); xt = sb("xtC", [128, D], F32)
                oaT = sb("oaT", [128, 8, 128], BF16); obT = sb("obT", [128, 8, 128], BF16)
                m1 = sb("m1", [128, D], F32); m2 = sb("m2", [128, D], F32); mb = sb("mb", [128, D], BF16)
                mT = sb("mT", [128, 8, 128], BF16)
                h2f = sb("h2f", [128, D], F32); h2b = sb("h2b", [128, D], BF16)
                pq_b = sb("pq_b", [128, D], BF16); pqT = sb("pqT", [128, 8, 128], BF16)
                S_all = sb("S_all", [128, 16, 128], F32); S_wk = sb("S_wk", [128, 256], F32)
                s_top = sb("s_top", [128, 16, 16], F32); i_top = sb("i_top", [128, 16, 16], U32); i_topf = sb("i_topf", [128, 16, 16], F32)
                cand = sb("cand", [128, 256], F32)
                c_top = sb("c_top", [128, 16], F32); c_pos = sb("c_pos", [128, 16], U32); c_posf = sb("c_posf", [128, 16], F32)
                t3 = sb("t3", [128, 16, 16], F32)
                av = sb("av", [128, 16], F32); bv = sb("bv", [128, 16], F32)
                i1v = sb("i1v", [128, 16], F32); i2v = sb("i2v", [128, 16], F32)
                eidf = sb("eidf", [128, 128], F32); eidi = sb("eidi", [128, 128], I32)
                g_all = sb("g_all", [128, 128], F32)
                smc = sb("smc", [128, 16], F32)
                act = sb("act", [128, 128], F32); ga = sb("ga", [128, 128], F32)
                i1_all = sb("i1_all", [128, 128], F32); i2_all = sb("i2_all", [128, 128], F32)
                rT = sb("rT", [128, 3, 128], F32)
                junkb = sb("junkbC", [128, D], BF16)
                pT = [ps("pT%d" % i, [128, 8, 128], BF16) for i in range(2)]
                pacc = [ps("pacc%d" % i, [128, 512], F32) for i in range(4)]
                psS = [ps("psS%d" % i, [128, 4, 128], F32) for i in range(2)]

                for (wsb, wsrc, wk) in [(wa_sb, w_ba, 'wa'), (wb_sb, w_bb, 'wb'), (wo_sb, w_o, 'wo'), (wq_sb, peer_wq, 'wq')]:
                    for kc in range(8):
                        S.dma('gpsimd', 'c_' + wk, L('dma_start', out=wsb[:, kc, :], in_=wsrc[kc * 128:(kc + 1) * 128, :]), writes=[wk])
                S.dma('sync', 'c_identf', L('dma_start', out=identf[:], in_=c_ident.ap()), writes=['identf'])
                S.dma('sync', 'c_gffn', L('dma_start', out=gffn[:], in_=bass.AP(norm_ffn, 0, [[0, 128], [1, D]])), writes=['gffn'])
                S.dma('sync', 'c_gfin', L('dma_start', out=gfin[:], in_=bass.AP(norm_final, 0, [[0, 128], [1, D]])), writes=['gfin'])
                S.dma('sync', 'c_keys', L('dma_start', out=keys_f[:], in_=peer_keys.ap().rearrange("a n d -> n a d")), writes=['keys_f'])
                S.dma('sync', 'c_iota', L('dma_start', out=iota_t[:], in_=c_iota.ap()), writes=['iota'])
                S.op('vector', L('tensor_copy', out=identb[:], in_=identf[:]), reads=['identf'], writes=['identb'])
                S.op('vector', L('tensor_copy', out=keys_b[:], in_=keys_f[:]), reads=['keys_f'], writes=['keys_b'])
                S.op('vector', L('tensor_scalar', out=thr[:], in0=iota_t[:, 0:16], scalar1=16.0, scalar2=None, op0=ALU.mult),
                     reads=['iota'], writes=['thr'])
                for h in range(8):
                    S.op('tensor', L('transpose', out=pT[0][:, h, :], in_=keys_b[:, 2 * h:2 * h + 2, :].rearrange("p a d -> p (a d)"), identity=identb[:]),
                         reads=['keys_b', 'identb'], writes=['pT0_%d' % h])
                S.op('vector', L('tensor_copy', out=keysT[:], in_=pT[0][:]), reads=['pT0_%d' % h for h in range(8)], writes=['keysT'])

                def transposes8(src, skey, pti, dst, dkey, eng):
                    for j in range(8):
                        S.op('tensor', L('transpose', out=pT[pti][:, j, :], in_=src[:, j * 128:(j + 1) * 128], identity=identb[:]),
                             reads=list(skey) + ['identb'], writes=['pT%d_%d' % (pti, j)])
                    rk = ['pT%d_%d' % (pti, j) for j in range(8)]
                    if eng == 'scalar':
                        S.op('scalar', L('activation', out=dst[:], in_=pT[pti][:], func=AF.Copy), reads=rk, writes=[dkey])
                    else:
                        S.op('vector', L('tensor_copy', out=dst[:], in_=pT[pti][:]), reads=rk, writes=[dkey])

                def proj(srcT, skey, wsb, wk, c, pb):
                    for kc in range(8):
                        S.op('tensor', L('matmul', pacc[pb][:], lhsT=srcT[:, kc, :], rhs=wsb[:, kc, c * 512:(c + 1) * 512],
                                         start=(kc == 0), stop=(kc == 7)), reads=[skey, wk], writes=['pacc%d' % pb], quiet=(kc < 7))

                def rstd_c(col, key, n):
                    S.op('vector', L('tensor_scalar', out=smc[:, col:col + 1], in0=smc[:, col:col + 1], scalar1=1.0 / n, scalar2=EPS,
                                     op0=ALU.mult, op1=ALU.add), reads=[key], writes=[key])
                    S.op('scalar', L('activation', out=smc[:, col:col + 1], in_=smc[:, col:col + 1], func=AF.Sqrt), reads=[key], writes=[key])
                    S.op('vector', L('reciprocal', out=smc[:, col:col + 1], in_=smc[:, col:col + 1]), reads=[key], writes=[key])

                S_wk2 = [S_wk, sb("S_wk_b", [128, 256], F32), sb("S_wk_c", [128, 256], F32), sb("S_wk_d", [128, 256], F32)]

                def top16_multi(probs):
                    for g0 in range(0, len(probs), 4):
                        grp_ = probs[g0:g0 + 4]
                        for (src_ap, skey, vals, vkey, idxs, ikey) in grp_:
                            S.op('vector', L('max', out=vals[:, 0:8], in_=src_ap), reads=[skey], writes=[vkey + 'a'])
                        for (src_ap, skey, vals, vkey, idxs, ikey) in grp_:
                            S.op('vector', L('max_index', out=idxs[:, 0:8], in_max=vals[:, 0:8], in_values=src_ap), reads=[skey, vkey + 'a'], writes=[ikey + 'a'])
                        for w_, (src_ap, skey, vals, vkey, idxs, ikey) in enumerate(grp_):
                            n = src_ap.shape[-1]
                            S.op('vector', L('match_replace', out=S_wk2[w_][:, 0:n], in_to_replace=vals[:, 0:8], in_values=src_ap, imm_value=-1e30),
                                 reads=[skey, vkey + 'a'], writes=['S_wk%d' % w_])
                        for w_, (src_ap, skey, vals, vkey, idxs, ikey) in enumerate(grp_):
                            n = src_ap.shape[-1]
                            S.op('vector', L('max', out=vals[:, 8:16], in_=S_wk2[w_][:, 0:n]), reads=['S_wk%d' % w_], writes=[vkey + 'b'])
                        for w_, (src_ap, skey, vals, vkey, idxs, ikey) in enumerate(grp_):
                            n = src_ap.shape[-1]
                            S.op('vector', L('max_index', out=idxs[:, 8:16], in_max=vals[:, 8:16], in_values=S_wk2[w_][:, 0:n]),
                                 reads=['S_wk%d' % w_, vkey + 'b'], writes=[ikey + 'b'])

                ctiles = [('p', i) for i in range(32)] + [('s', 0)]
                ctiles = cfg.get('c_tiles', ctiles)
                mhalf = sb("mhalf", [128, 1], F32)
                S.op('gpsimd', L('memset', mhalf[:], -0.5), writes=['mhalf'])
                x2d = [sb("x2d%d" % i, [128, D], F32) for i in range(2)]
                h2Td = [sb("h2Td%d" % i, [128, 8, 128], BF16) for i in range(2)]
                S_alld = [S_all, sb("S_all1", [128, 16, 128], F32)]
                c_top_all = sb("c_top_all", [128, 8, 16], F32); c_pos_all = sb("c_pos_all", [128, 8, 16], U32)
                c_posf_all = sb("c_posf_all", [128, 128], F32)
                t3b = sb("t3b", [128, 128, 16], F32)
                av_all = sb("av_all", [128, 128], F32); bv_all = sb("bv_all", [128, 128], F32)
                zsum = sb("zsum", [128, 8], F32)
                cand4 = [cand, sb("cand_b", [128, 256], F32), sb("cand_c", [128, 256], F32), sb("cand_d", [128, 256], F32)]
                psX = psS[1]
                S.banks = ('pT0', 'pT1', 'pacc0', 'pacc1', 'pacc2', 'pacc3', 'psS0', 'psS1')

                def t8(src, skey, dst, dkey):
                    for j in range(8):
                        S.op('tensor', L('transpose', out=pT[0][:, j, :], in_=src[:, j * 128:(j + 1) * 128], identity=identb[:]),
                             reads=list(skey) + ['identb'], writes=['pT0_%d' % j], quiet=(j < 7))
                    S.op('scalar', L('activation', out=dst[:], in_=pT[0][:], func=AF.Copy), reads=['pT0_%d' % j for j in range(8)], writes=[dkey])

                def front(ti):
                    kind, i = ctiles[ti]
                    sl = ti % 2
                    if kind == 'p':
                        rows = 128; qrow = i * 128; xsrc = xp[qrow:qrow + 128, :]
                    else:
                        rows = 64; qrow = SP; xsrc = xs.ap()
                        S.op('gpsimd', L('memset', xt[:], 0.0), writes=['xt'])
                    S.dma('sync', 'cl_x', L('dma_start', out=xt[0:rows, :], in_=xsrc), writes=['xt'])
                    S.dma('sync', 'cl_oa', L('dma_start', out=oa_t[:], in_=oa_s[qrow:qrow + 128, :]), writes=['oa_t'])
                    S.dma('sync', 'cl_ob', L('dma_start', out=ob_t[:], in_=ob_s[qrow:qrow + 128, :]), writes=['ob_t'])
                    S.dma('sync', 'cl_g', L('dma_start', out=gt[:], in_=gates_s[qrow:qrow + 128, :]), writes=['gt'])
                    t8(oa_t, ['oa_t'], oaT, 'oaT')
                    t8(ob_t, ['ob_t'], obT, 'obT')
                    for (srcT, skey, wsb, wk, dst, dkey, pb0) in [(oaT, 'oaT', wa_sb, 'wa', m1, 'm1', 0), (obT, 'obT', wb_sb, 'wb', m2, 'm2', 2)]:
                        for c in range(2):
                            proj(srcT, skey, wsb, wk, c, pb0 + c)
                            S.op('scalar', L('activation', out=dst[:, c * 512:(c + 1) * 512], in_=pacc[pb0 + c][:], func=AF.Copy),
                                 reads=['pacc%d' % (pb0 + c)], writes=['%s_%d' % (dkey, c)])
                    S.op('gpsimd', L('tensor_tensor', out=m1[:], in0=m1[:], in1=gt[:, 0:D], op=ALU.mult), reads=['m1_0', 'm1_1', 'gt'], writes=['m1_0', 'm1_1'])
                    S.op('gpsimd', L('tensor_tensor', out=m2[:], in0=m2[:], in1=gt[:, D:2 * D], op=ALU.mult), reads=['m2_0', 'm2_1', 'gt'], writes=['m2_0', 'm2_1'])
                    S.op('gpsimd', L('tensor_tensor', out=mb[:], in0=m1[:], in1=m2[:], op=ALU.add),
                         reads=['m1_0', 'm1_1', 'm2_0', 'm2_1'], writes=['mb'])
                    t8(mb, ['mb'], mT, 'mT')
                    x2 = x2d[sl]
                    for c in range(2):
                        proj(mT, 'mT', wo_sb, 'wo', c, c)
                        S.op('scalar', L('activation', out=x2[:, c * 512:(c + 1) * 512], in_=pacc[c][:], func=AF.Copy),
                             reads=['pacc%d' % c], writes=['x2d%d_%d' % (sl, c)])
                    x2k = ['x2d%d_0' % sl, 'x2d%d_1' % sl]
                    S.op('gpsimd', L('tensor_tensor', out=x2[:], in0=x2[:], in1=xt[:], op=ALU.add), reads=x2k + ['xt'], writes=x2k)
                    S.op('scalar', L('activation', out=junkb[:], in_=x2[:], func=AF.Square, accum_out=smc[:, 0:1]), reads=x2k, writes=['junkb', 'smc0'])
                    S.op('gpsimd', L('tensor_scalar', out=smc[:, 0:1], in0=smc[:, 0:1], scalar1=1.0 / D, scalar2=EPS, op0=ALU.mult, op1=ALU.add),
                         reads=['smc0'], writes=['smc0'])
                    S.op('gpsimd', L('tensor_tensor', out=smc[:, 0:1], in0=smc[:, 0:1], in1=mhalf[:], op=ALU.pow), reads=['smc0', 'mhalf'], writes=['smc0'])
                    S.op('scalar', L('activation', out=h2f[:], in_=x2[:], func=AF.Copy, scale=smc[:, 0:1]), reads=x2k + ['smc0'], writes=['h2f'])
                    S.op('gpsimd', L('tensor_tensor', out=h2b[:], in0=h2f[:], in1=gffn[:], op=ALU.mult), reads=['h2f', 'gffn'], writes=['h2b'])
                    t8(h2b, ['h2b'], h2Td[sl], 'h2Td%d' % sl)
                    for c in range(2):
                        proj(h2Td[sl], 'h2Td%d' % sl, wq_sb, 'wq', c, 2 + c)
                        S.op('scalar', L('activation', out=pq_b[:, c * 512:(c + 1) * 512], in_=pacc[2 + c][:], func=AF.Copy),
                             reads=['pacc%d' % (2 + c)], writes=['pq_b%d' % c])
                    t8(pq_b, ['pq_b0', 'pq_b1'], pqT, 'pqT')
                    for q4 in range(4):
                        c = q4 % 2; h0 = (q4 // 2) * 4
                        bank, bkey = (psS[0], 'psS0') if c == 0 else (pacc[3].rearrange("p (a b) -> p a b", a=4), 'pacc3')
                        for i4 in range(4):
                            h = h0 + i4
                            S.op('tensor', L('matmul', bank[:, i4, :], lhsT=pqT[c * 64:(c + 1) * 64, h, :], rhs=keysT[c * 64:(c + 1) * 64, h, :],
                                             start=True, stop=True), reads=['pqT', 'keysT'], writes=[bkey + ('_%d' % i4 if c == 0 else '')])
                        lo = 2 * h0 + c
                        S.op('scalar', L('activation', out=S_alld[sl][:, lo:min(lo + 8, 16):2, :], in_=bank[:, :, :], func=AF.Copy),
                             reads=([bkey + '_%d' % i4 for i4 in range(4)] if c == 0 else [bkey]),
                             writes=['S_all%d_hc%d' % (sl, lo + 2 * i4) for i4 in range(4)])

                def back(ti):
                    sl = ti % 2
                    Sa = S_alld[sl]
                    top16_multi([(Sa[:, hc, :], 'S_all%d_hc%d' % (sl, hc), s_top[:, hc, :], 's_top%d' % hc, i_top[:, hc, :], 'i_top%d' % hc)
                                 for hc in range(16)])
                    S.op('vector', L('tensor_copy', out=i_topf[:], in_=i_top[:]),
                         reads=['i_top%d%s' % (hc, ab) for hc in range(16) for ab in 'ab'], writes=['i_topf'])
                    pstr = 16 * 16
                    for h0 in range(0, 8, 4):
                        probs = []
                        for h in range(h0, h0 + 4):
                            stk = ['s_top%d%s' % (hc, ab) for hc in (2 * h, 2 * h + 1) for ab in 'ab']
                            in0 = bass.AP(s_top, (2 * h) * 16, [[pstr, 128], [1, 16], [0, 16]])
                            in1 = bass.AP(s_top, (2 * h + 1) * 16, [[pstr, 128], [0, 16], [1, 16]])
                            cw = cand4[h - h0]
                            S.op('vector', L('tensor_tensor', out=cw[:].rearrange("p (a b) -> p a b", a=16), in0=in0, in1=in1, op=ALU.add),
                                 reads=stk, writes=['cand%d' % (h - h0)])
                            probs.append((cw[:], 'cand%d' % (h - h0), c_top_all[:, h, :], 'c_top%d' % h, c_pos_all[:, h, :], 'c_pos%d' % h))
                        top16_multi(probs)
                    ctk = ['c_top%d%s' % (h, ab) for h in range(8) for ab in 'ab']
                    cpk = ['c_pos%d%s' % (h, ab) for h in range(8) for ab in 'ab']
                    S.op('vector', L('tensor_copy', out=c_posf_all[:], in_=c_pos_all[:].rearrange("p h k -> p (h k)")), reads=cpk, writes=['c_posf_all'])
                    S.op('vector', L('tensor_tensor', out=t3b[:, :, 0:15], in0=bass.AP(c_posf_all, 0, [[128, 128], [1, 128], [0, 15]]),
                                     in1=bass.AP(thr, 1, [[16, 128], [0, 128], [1, 15]]), op=ALU.is_ge), reads=['c_posf_all', 'thr'], writes=['t3b'])
                    S.op('vector', L('reduce_sum', out=av_all[:], in_=t3b[:, :, 0:15], axis=mybir.AxisListType.X), reads=['t3b'], writes=['av_all'])
                    S.op('vector', L('scalar_tensor_tensor', out=bv_all[:], in0=av_all[:], scalar=-16.0, in1=c_posf_all[:], op0=ALU.mult, op1=ALU.add),
                         reads=['av_all', 'c_posf_all'], writes=['bv_all'])
                    for (sel, skey, c, dst, dkey) in [(av_all, 'av_all', 0, i1_all, 'i1_all'), (bv_all, 'bv_all', 1, i2_all, 'i2_all')]:
                        S.op('vector', L('tensor_tensor', out=t3b[:], in0=bass.AP(sel, 0, [[128, 128], [1, 128], [0, 16]]),
                                         in1=bass.AP(iota_t, 0, [[256, 128], [0, 128], [1, 16]]), op=ALU.is_equal), reads=[skey, 'iota'], writes=['t3b'])
                        S.op('vector', L('tensor_tensor', out=t3b[:].rearrange("p (h k) a -> p h k a", h=8), in0=t3b[:].rearrange("p (h k) a -> p h k a", h=8),
                                         in1=bass.AP(i_topf, c * 16, [[pstr, 128], [32, 8], [0, 16], [1, 16]]), op=ALU.mult),
                             reads=['t3b', 'i_topf'], writes=['t3b'])
                        S.op('vector', L('reduce_sum', out=dst[:], in_=t3b[:], axis=mybir.AxisListType.X), reads=['t3b'], writes=[dkey])
                    S.op('vector', L('tensor_tensor', out=g_all[:].rearrange("p (h k) -> p h k", h=8), in0=c_top_all[:],
                                     in1=bass.AP(c_top_all, 0, [[128, 128], [16, 8], [0, 16]]), op=ALU.subtract), reads=ctk, writes=['g_all'])
                    S.op('scalar', L('activation', out=g_all[:], in_=g_all[:], func=AF.Exp), reads=['g_all'], writes=['g_all'])
                    S.op('vector', L('reduce_sum', out=zsum[:], in_=g_all[:].rearrange("p (h k) -> p h k", h=8), axis=mybir.AxisListType.X),
                         reads=['g_all'], writes=['zsum'])
                    S.op('vector', L('reciprocal', out=zsum[:], in_=zsum[:]), reads=['zsum'], writes=['zsum'])
                    S.op('vector', L('tensor_tensor', out=g_all[:].rearrange("p (h k) -> p h k", h=8), in0=g_all[:].rearrange("p (h k) -> p h k", h=8),
                                     in1=bass.AP(zsum, 0, [[8, 128], [1, 8], [0, 16]]), op=ALU.mult), reads=['g_all', 'zsum'], writes=['g_all'])

                def export(ti):
                    kind, i = ctiles[ti]
                    sl = ti % 2
                    qrow = i * 128 if kind == 'p' else SP
                    for ri, (src, key_) in enumerate([(i1_all, 'i1_all'), (i2_all, 'i2_all'), (g_all, 'g_all')]):
                        S.op('tensor', L('transpose', out=psX[:, ri, :], in_=src[:], identity=identf[:]),
                             reads=[key_, 'identf'], writes=['psS1_%d' % ri])
                    S.op('scalar', L('activation', out=rT[:], in_=psX[:, 0:3, :], func=AF.Copy), reads=['psS1_0', 'psS1_1', 'psS1_2'], writes=['rT'])
                    S.dma('sync', 'cs_r', L('dma_start', out=r_s[:, :, qrow:qrow + 128], in_=rT[:]), reads=['rT'])
                    S.dma('sync', 'cs_x2', L('dma_start', out=x2_s[qrow:qrow + 128, :], in_=x2d[sl][:]), reads=['x2d%d_0' % sl, 'x2d%d_1' % sl])
                    S.dma('sync', 'cs_h2', L('dma_start', out=h2T_s[:, :, qrow:qrow + 128], in_=h2Td[sl][:]), reads=['h2Td%d' % sl])

                if ctiles:
                    front(0)
                for ti in range(len(ctiles)):
                    if ti + 1 < len(ctiles):
                        front(ti + 1)
                    back(ti)
                    export(ti)

                S.end_phase()
                ensure_sems()
                S.emit(nc, sems)

        if 'C' in phases and DENSE:
            n_ec = cfg.get('n_ec', 128)
            with ExitStack() as es:
                def sb(name, shape, dt):
                    return es.enter_context(nc.sbuf_tensor(name, list(shape), dt))

                def ps(name, shape, dt):
                    return es.enter_context(nc.psum_tensor(name, list(shape), dt))
                S.single = ()
                S.banks = ('ppT0', 'ppT1')
                identf = sb("identfP", [128, 128], F32); identb = sb("identbP", [128, 128], BF16)
                ub = [sb("ub%d" % i, [128, D], BF16) for i in range(3)]
                vb = [sb("vbD%d" % i, [128, D], BF16) for i in range(3)]
                uT = [sb("uT%d" % i, [128, 8, 128], BF16) for i in range(3)]
                ppT = [ps("ppT%d" % i, [128, 8, 128], BF16) for i in range(2)]
                S.dma('sync', 'p_identf', L('dma_start', out=identf[:], in_=c_ident.ap()), writes=['identf'])
                S.op('vector', L('tensor_copy', out=identb[:], in_=identf[:]), reads=['identf'], writes=['identb'])
                for ec in range(0 if ('B' in phases and cfg.get('p_in_b', True)) else n_ec):
                    s2 = ec % 3
                    pp = ec % 2
                    S.dma('gpsimd', 'p_u%d' % s2, L('dma_start', out=ub[s2][:], in_=peer_u[ec * 128:(ec + 1) * 128, :]), writes=['ub%d' % s2])
                    S.dma('gpsimd', 'p_v%d' % s2, L('dma_start', out=vb[s2][:], in_=peer_v[ec * 128:(ec + 1) * 128, :]), writes=['vb%d' % s2])
                    for dc in range(8):
                        S.op('tensor', L('transpose', out=ppT[pp][:, dc, :], in_=ub[s2][:, dc * 128:(dc + 1) * 128], identity=identb[:]),
                             reads=['ub%d' % s2, 'identb'], writes=['ppT%d_%d' % (pp, dc)])
                    if ec % 2 == 0:
                        S.op('vector', L('tensor_copy', out=uT[s2][:], in_=ppT[pp][:]), reads=['ppT%d_%d' % (pp, dc) for dc in range(8)], writes=['uT%d' % s2])
                    else:
                        S.op('scalar', L('activation', out=uT[s2][:], in_=ppT[pp][:], func=AF.Copy), reads=['ppT%d_%d' % (pp, dc) for dc in range(8)], writes=['uT%d' % s2])
                    S.dma('sync', 'p_su%d' % s2, L('dma_start', out=UT_s[ec // 2][:, ec % 2, :].rearrange("p (a b) -> p a b", a=8), in_=uT[s2][:]),
                          reads=['uT%d' % s2], writes=['UT_s'])
                    S.dma('sync', 'p_sv%d' % s2, L('dma_start', out=V_s[ec // 2][:, ec % 2, :], in_=vb[s2][:]), reads=['vb%d' % s2], writes=['V_s'])
                S.end_phase()
                ensure_sems()
                S.emit(nc, sems)
            with ExitStack() as es:
                def sb(name, shape, dt):
                    return es.enter_context(nc.sbuf_tensor(name, list(shape), dt))

                def ps(name, shape, dt):
                    return es.enter_context(nc.psum_tensor(name, list(shape), dt))
                S.single = ()
                S.banks = ('pact0', 'pact1', 'pout0', 'pout1', 'pout2', 'pout3', 'pout4', 'pout5', 'pw0', 'pw1')
                NTT = TG // 128
                gfin = sb("gfinD", [128, D], F32)
                iota_b = sb("iota_b", [128, 128], F32)
                NBUF = 4
                utp = [sb("utp%d" % i, [128, 2, 8, 128], BF16) for i in range(NBUF)]
                vtp = [sb("vtp%d" % i, [128, 2, D], BF16) for i in range(NBUF)]
                h2g = [sb("h2g%d" % i, [128, 8, TG], BF16) for i in range(2)]
                rg = [sb("rg%d" % i, [128, 3, TG], F32) for i in range(2)]
                oh2 = [sb("oh2_%d" % i, [128, 128], BF16) for i in range(2)]
                oh1 = [sb("oh1_%d" % i, [128, 128], BF16) for i in range(2)]
                WT = [sb("WT%d" % i, [128, TG, 128], BF16) for i in range(2)]
                gl = [sb("gl%d" % i, [128, TG], BF16) for i in range(2)]
                gad = [sb("gad%d" % i, [128, TG], BF16) for i in range(2)]
                x2t = sb("x2t", [128, D], F32); accd = sb("accd", [128, D], F32); ytd = sb("ytd", [128, D], F32)
                junkd = sb("junkd", [128, D], BF16); smd = sb("smd", [128, 4], F32)
                npout = 2 * NTT
                pact = [ps("pact%d" % i, [128, 512], F32) for i in range(2 if npout <= 4 else 1)]
                pout = [ps("pout%d" % i, [128, 512], F32) for i in range(npout)]
                pw = [ps("pw%d" % i, [128, 4, 128], F32) for i in range(1)]

                S.dma('sync', 'd_gfin', L('dma_start', out=gfin[:], in_=bass.AP(norm_final, 0, [[0, 128], [1, D]])), writes=['gfin'])
                S.dma('sync', 'd_iota', L('dma_start', out=iota_b[:], in_=c_iota[:, 0:128]), writes=['iota_b'])
                dgroups = [(g * TG, TG, [('p', g * TG + j * 128) for j in range(NTT)]) for g in range(SP // TG)] + [(SP, 128, [('s', SP)])]
                dgroups = cfg.get('d_groups', dgroups)
                pairs_total = len(dgroups) * (n_ec // 2)
                issued = [0]

                def ensure_loaded(upto):
                    while issued[0] < min(upto, pairs_total):
                        gp = issued[0]; p = gp % (n_ec // 2); b = gp % NBUF
                        S.dma('sync', 'd_ut%d' % b, L('dma_start', out=utp[b][:].rearrange("p j a b -> p j (a b)"), in_=UT_s[p]), writes=['utp%d' % b])
                        S.dma(cfg.get('vt_q', 'sync'), 'd_vt%d' % b, L('dma_start', out=vtp[b][:], in_=V_s[p]), writes=['vtp%d' % b])
                        issued[0] += 1

                def load_group(gi):
                    q0, tg, _ = dgroups[gi]
                    w = gi % 2
                    S.dma('sync', 'd_h2_%d' % w, L('dma_start', out=h2g[w][:, :, 0:tg], in_=h2T_s[:, :, q0:q0 + tg]), writes=['h2g%d' % w])
                    S.dma('sync', 'd_r%d' % w, L('dma_start', out=rg[w][:, :, 0:tg], in_=r_s[:, :, q0:q0 + tg]), writes=['rg%d' % w])

                def wbuild(gi, t):
                    w = gi % 2
                    o = t % 2
                    S.op('vector', L('tensor_scalar', out=oh2[o][:], in0=iota_b[:], scalar1=rg[w][:, 1, t:t + 1], scalar2=None, op0=ALU.is_equal),
                         reads=['iota_b', 'rg%d' % w], writes=['oh2_%d' % o])
                    S.op('vector', L('tensor_scalar', out=oh1[o][:], in0=iota_b[:], scalar1=rg[w][:, 0, t:t + 1], scalar2=rg[w][:, 2, t:t + 1],
                                     op0=ALU.is_equal, op1=ALU.mult), reads=['iota_b', 'rg%d' % w], writes=['oh1_%d' % o])
                    S.op('tensor', L('matmul', pw[0][:, t % 4, :], lhsT=oh2[o][:], rhs=oh1[o][:], start=True, stop=True),
                         reads=['oh2_%d' % o, 'oh1_%d' % o], writes=['pw0_%d' % (t % 4)])
                    if t % 4 == 3:
                        S.op('scalar', L('activation', out=WT[w][:, t - 3:t + 1, :], in_=pw[0][:], func=AF.Copy),
                             reads=['pw0_%d' % j for j in range(4)], writes=['WT%d' % w])

                if dgroups:
                    ensure_loaded(3)
                    load_group(0)
                    for t in range(dgroups[0][1]):
                        wbuild(0, t)
                for gi, (q0, tg, ttiles) in enumerate(dgroups):
                    ntt = tg // 128
                    w = gi % 2
                    nxt = gi + 1 if gi + 1 < len(dgroups) else None
                    if nxt is not None:
                        load_group(nxt)
                        ntok_next = dgroups[nxt][1]
                        per = (ntok_next + n_ec - 1) // n_ec
                    wb_t = [0]

                    def act_mm(ec, gi=gi, w=w, tg=tg):
                        gp = gi * (n_ec // 2) + ec // 2
                        b = gp % NBUF; j = ec % 2
                        pa = ec % len(pact)
                        for dc in range(8):
                            S.op('tensor', L('matmul', pact[pa][:, 0:tg], lhsT=utp[b][:, j, dc, :], rhs=h2g[w][:, dc, 0:tg], start=(dc == 0), stop=(dc == 7)),
                                 reads=['utp%d' % b, 'h2g%d' % w], writes=['pact%d' % pa], quiet=(dc < 7))
                        return (b, j)
                    pend_b = act_mm(0) if n_ec > 0 else None
                    for ec in range(n_ec):
                        b = pend_b
                        pa = ec % len(pact)
                        gb = ec % 2
                        S.op('scalar', L('activation', out=gl[gb][:, 0:tg], in_=pact[pa][:, 0:tg], func=AF.Gelu), reads=['pact%d' % pa], writes=['gl%d' % gb])
                        S.op('vector', L('tensor_tensor', out=gad[gb][:, 0:tg], in0=gl[gb][:, 0:tg], in1=WT[w][:, 0:tg, ec], op=ALU.mult),
                             reads=['gl%d' % gb, 'WT%d' % w], writes=['gad%d' % gb])
                        if ec + 1 < n_ec:
                            pend_b = act_mm(ec + 1)
                        for tt in range(ntt):
                            for dh in range(2):
                                S.op('tensor', L('matmul', pout[tt * 2 + dh][:], lhsT=gad[gb][:, tt * 128:(tt + 1) * 128], rhs=vtp[b[0]][:, b[1], dh * 512:(dh + 1) * 512],
                                                 start=(ec == 0), stop=(ec == n_ec - 1)),
                                     reads=['gad%d' % gb, 'vtp%d' % b[0]], writes=['pout%d' % (tt * 2 + dh)],
                                     quiet=not (tt == ntt - 1 and dh == 1))
                        if ec % 2 == 1:
                            ensure_loaded(gi * (n_ec // 2) + ec // 2 + 4)
                        if nxt is not None:
                            for _ in range(per):
                                if wb_t[0] < ntok_next:
                                    wbuild(nxt, wb_t[0])
                                    wb_t[0] += 1
                    if nxt is not None:
                        while wb_t[0] < ntok_next:
                            wbuild(nxt, wb_t[0])
                            wb_t[0] += 1
                    for tt, (kind, qrow) in enumerate(ttiles):
                        rows = 128 if kind == 'p' else 64
                        ydst = yp[qrow:qrow + 128, :] if kind == 'p' else ys.ap()
                        S.dma('sync', 'd_x2', L('dma_start', out=x2t[:], in_=x2_s[qrow:qrow + 128, :]), writes=['x2t'])
                        for dh in range(2):
                            S.op('vector', L('tensor_tensor', out=accd[:, dh * 512:(dh + 1) * 512], in0=pout[tt * 2 + dh][:], in1=x2t[:, dh * 512:(dh + 1) * 512], op=ALU.add),
                                 reads=['pout%d' % (tt * 2 + dh), 'x2t'], writes=['accd%d' % dh])
                        S.op('scalar', L('activation', out=junkd[:], in_=accd[:], func=AF.Square, accum_out=smd[:, 0:1]),
                             reads=['accd0', 'accd1'], writes=['junkd', 'smd0'])
                        S.op('vector', L('tensor_scalar', out=smd[:, 0:1], in0=smd[:, 0:1], scalar1=1.0 / D, scalar2=EPS, op0=ALU.mult, op1=ALU.add),
                             reads=['smd0'], writes=['smd0'])
                        S.op('scalar', L('activation', out=smd[:, 0:1], in_=smd[:, 0:1], func=AF.Sqrt), reads=['smd0'], writes=['smd0'])
                        S.op('vector', L('reciprocal', out=smd[:, 0:1], in_=smd[:, 0:1]), reads=['smd0'], writes=['smd0'])
                        S.op('vector', L('tensor_scalar', out=ytd[:], in0=accd[:], scalar1=smd[:, 0:1], scalar2=None, op0=ALU.mult),
                             reads=['accd0', 'accd1', 'smd0'], writes=['ytd'])
                        S.op('gpsimd', L('tensor_tensor', out=ytd[:], in0=ytd[:], in1=gfin[:], op=ALU.mult), reads=['ytd', 'gfin'], writes=['ytd'])
                        S.dma('sync', 'd_y', L('dma_start', out=ydst, in_=ytd[0:rows, :]), reads=['ytd'])

                S.end_phase()
                ensure_sems()
                S.emit(nc, sems)

    return nc


_CACHE = {}


def _consts():
    if 'c' not in _CACHE:
        oh, maskw = _bias_consts()
        _CACHE['c'] = dict(
            c_ident=np.eye(128, dtype=np.float32),
            c_rope=_rope_table(),
            c_oh=oh, c_maskw=maskw,
            c_iota=np.tile(np.arange(256, dtype=np.float32)[None, :], (128, 1)),
        )
    return _CACHE['c']


def kernel(x_prompt, x_sample, cache_diff_k, cache_diff_v, cache_mla_ckv, cache_mla_kpe,
           rel_bias, norm_mix, w_in, diff_lambda, diff_subln, mla_q_norm, mla_w_uq, mla_kv_norm,
           mla_w_uk, mla_w_uv, w_branch_a, w_branch_b, w_out, norm_ffn, peer_w_q, peer_keys,
           peer_u, peer_v, norm_final):
    f = lambda a: np.ascontiguousarray(np.asarray(a, dtype=np.float32))
    shared = dict(
        rel_bias=f(rel_bias), norm_mix=f(norm_mix).reshape(D), w_in=f(w_in).reshape(D, INW),
        diff_lambda=f(diff_lambda).reshape(256), diff_subln=f(diff_subln).reshape(128),
        mla_q_norm=f(mla_q_norm).reshape(256), w_uq=f(mla_w_uq).reshape(256, 1536),
        mla_kv_norm=f(mla_kv_norm).reshape(128), w_uk=f(mla_w_uk).reshape(128, 1024), w_uv=f(mla_w_uv).reshape(128, 1024),
        w_ba=f(w_branch_a).reshape(D, D), w_bb=f(w_branch_b).reshape(D, D), w_o=f(w_out).reshape(D, D),
        norm_ffn=f(norm_ffn).reshape(D), peer_wq=f(peer_w_q).reshape(D, D),
        peer_keys=f(peer_keys).reshape(16, 128, 64), peer_u=f(peer_u).reshape(16384, D), peer_v=f(peer_v).reshape(16384, D),
        norm_final=f(norm_final).reshape(D),
    )
    shared.update(_consts())
    xpf = f(x_prompt); xsf = f(x_sample)
    cdkf = f(cache_diff_k).reshape(NCORES, PAST, D); cdvf = f(cache_diff_v).reshape(NCORES, PAST, D)
    cckvf = f(cache_mla_ckv).reshape(NCORES, PAST, 128); ckpef = f(cache_mla_kpe).reshape(NCORES, PAST, 32)
    in_maps = []
    for c in range(NCORES):
        m = dict(shared)
        m.update(xp=xpf[c], xs=xsf[c], cdk=cdkf[c], cdv=cdvf[c], cckv=cckvf[c], ckpe=ckpef[c])
        in_maps.append(m)
    nc = build_program()
    res = run_bass_kernel_spmd(nc, in_maps, core_ids=list(range(NCORES)))
    R = res.results

    def g(name, shape):
        return np.stack([np.asarray(R[c][name], dtype=np.float32) for c in range(NCORES)], 0).reshape(shape)
    return (g("yp", (8, SP, D)), g("ys", (8, SS, D)),
            g("kp", (1, 8, SP, 8, 2, 64)), g("vp", (1, 8, SP, 8, 128)), g("cp", (1, 8, SP, 128)), g("ep", (1, 8, SP, 32)),
            g("ks", (1, 8, SS, 8, 2, 64)), g("vs", (1, 8, SS, 8, 128)), g("cs", (1, 8, SS, 128)), g("es", (1, 8, SS, 32)))
```
